# Optimizing a Trainium2 kernel written in Bass

```python
import jax, jax.numpy as jnp
from jax import lax
import numpy as np

D_MODEL = 2048
BATCH = 32
SEQ = 256
DEPTH = 4
DEC_BATCH = 2
DEC_SEQ = 1024
PAST_LEN = 512

GRID_W = 64
ADA_CHUNKS = 6
NORM_EPS = 1e-6
Q_BLOCK = 128
ROPE_THETA = 10000.0

MLA_HEADS = 8
MLA_NOPE = 128
MLA_ROPE = 64
MLA_QK_DIM = MLA_NOPE + MLA_ROPE
MLA_V_DIM = 128
MLA_Q_LORA = 512
MLA_KV_LORA = 256
MLA_WIDTH = MLA_HEADS * MLA_V_DIM

NA_HEADS = 4
NA_HEAD_DIM = 128
NA_WIDTH = NA_HEADS * NA_HEAD_DIM
NA_KR = 8
NA_KC = 16
NA_BAND = 2 * NA_KC
NA_NCB = GRID_W // NA_KC

CONV_CH = 512
CONV_K = 3

D_MIX = MLA_WIDTH + NA_WIDTH + CONV_CH
IN_SIZES = (MLA_Q_LORA, MLA_KV_LORA, MLA_ROPE, NA_WIDTH, NA_WIDTH, NA_WIDTH, CONV_CH, CONV_CH, CONV_CH)
D_IN = MLA_Q_LORA + MLA_KV_LORA + MLA_ROPE + 3 * NA_WIDTH + 3 * CONV_CH

PEER_HEADS = 8
PEER_N_KEYS = 128
PEER_N_EXPERTS = PEER_N_KEYS * PEER_N_KEYS
PEER_KEY_DIM = 256
PEER_KEY_HALF = PEER_KEY_DIM // 2
PEER_TOPK = 16
PEER_TOKEN_BLOCK = 128

kernel_name = "hybrid_mla_natten_shortconv_peer_dit_step"


def rms_norm(x, g):
    xf = x.astype(jnp.float32)
    y = xf * lax.rsqrt(jnp.mean(xf * xf, axis=-1, keepdims=True) + NORM_EPS)
    return (y * g.astype(jnp.float32)).astype(x.dtype)


def split_heads(x, n_heads):
    b, s, _ = x.shape
    return x.reshape(b, s, n_heads, -1).transpose(0, 2, 1, 3)


def merge_heads(x):
    b, n, s, d = x.shape
    return x.transpose(0, 2, 1, 3).reshape(b, s, n * d)


def ada_modulation(cvec, w, b):
    m = jnp.dot(jax.nn.silu(cvec), w) + b
    return jnp.split(m[:, None, :], ADA_CHUNKS, axis=-1)


def modulate(x, g, shift, scale):
    return rms_norm(x, g) * (1 + scale) + shift


def axial_rope_tables(n_tokens):
    t = jnp.arange(n_tokens)
    row = (t // GRID_W).astype(jnp.float32)
    col = (t % GRID_W).astype(jnp.float32)
    n_freq = MLA_ROPE // 4
    inv = ROPE_THETA ** (-jnp.arange(n_freq, dtype=jnp.float32) / n_freq)
    ang = jnp.concatenate([row[:, None] * inv, col[:, None] * inv], axis=-1)
    return jnp.cos(ang), jnp.sin(ang)


def rope_tail(x, cos, sin):
    x_pass, x_rot = x[..., :-MLA_ROPE], x[..., -MLA_ROPE:]
    xf = x_rot.astype(jnp.float32)
    half = MLA_ROPE // 2
    x1, x2 = xf[..., :half], xf[..., half:]
    rot = jnp.concatenate([x1 * cos - x2 * sin, x1 * sin + x2 * cos], axis=-1)
    return jnp.concatenate([x_pass, rot.astype(x.dtype)], axis=-1)


def blocked_attention(q, k, v, scale):
    b, h, sq, dq = q.shape
    nb = sq // Q_BLOCK
    qb = q.reshape(b, h, nb, Q_BLOCK, dq).transpose(2, 0, 1, 3, 4)

    def one_block(q_blk):
        s = jnp.einsum('bhqd,bhkd->bhqk', q_blk, k).astype(jnp.float32) * scale
        p = jax.nn.softmax(s, axis=-1).astype(v.dtype)
        return jnp.einsum('bhqk,bhkd->bhqd', p, v)

    o = lax.map(one_block, qb)
    return o.transpose(1, 2, 0, 3, 4).reshape(b, h, sq, v.shape[-1])


def neighbourhood_attention(q, k, v, k_ctx, v_ctx, rel_bias):
    b, h, t, d = q.shape
    rows = t // GRID_W
    kr = min(NA_KR, rows)
    r = jnp.arange(rows)
    row_start = jnp.clip(r - kr // 2, 0, rows - kr)
    row_idx = row_start[:, None] + jnp.arange(kr)[None, :]
    row_off = row_idx - r[:, None] + (NA_KR - 1)
    j = jnp.arange(NA_NCB)
    band_start = jnp.clip(j * NA_KC - NA_KC // 2, 0, GRID_W - NA_BAND)
    key_col = band_start[:, None] + jnp.arange(NA_BAND)[None, :]
    q_col = j[:, None] * NA_KC + jnp.arange(NA_KC)[None, :]
    col_start = jnp.clip(q_col - NA_KC // 2, 0, GRID_W - NA_KC)
    kc3 = key_col[:, None, :]
    valid = (kc3 >= col_start[..., None]) & (kc3 < col_start[..., None] + NA_KC)
    col_off = jnp.clip(kc3 - q_col[..., None], -(NA_KC - 1), NA_KC - 1) + (NA_KC - 1)
    k_grid = k.reshape(b, h, rows, GRID_W, d)
    v_grid = v.reshape(b, h, rows, GRID_W, d)
    q_rows = q.reshape(b, h, rows, NA_NCB, NA_KC, d).transpose(2, 0, 1, 3, 4, 5)
    scale = NA_HEAD_DIM ** -0.5
    n_loc = kr * NA_BAND

    def one_row(args):
        q_r, ridx, roff = args
        kb = jnp.take(k_grid, ridx, axis=2)[:, :, :, key_col]
        vb = jnp.take(v_grid, ridx, axis=2)[:, :, :, key_col]
        bias = rel_bias[:, roff[None, None, :, None], col_off[:, :, None, :]]
        s_loc = jnp.einsum('bhjqd,bhajcd->bhjqac', q_r, kb).astype(jnp.float32) * scale
        s_loc = jnp.where(valid[:, :, None, :], s_loc + bias.astype(jnp.float32), -jnp.inf)
        s_ctx = jnp.einsum('bhjqd,bhkd->bhjqk', q_r, k_ctx).astype(jnp.float32) * scale
        s = jnp.concatenate([s_loc.reshape(b, h, NA_NCB, NA_KC, n_loc), s_ctx], axis=-1)
        p = jax.nn.softmax(s, axis=-1).astype(v.dtype)
        p_loc = p[..., :n_loc].reshape(b, h, NA_NCB, NA_KC, kr, NA_BAND)
        return (jnp.einsum('bhjqac,bhajcd->bhjqd', p_loc, vb)
                + jnp.einsum('bhjqk,bhkd->bhjqd', p[..., n_loc:], v_ctx))

    o = lax.map(one_row, (q_rows, row_idx, row_off))
    return o.transpose(1, 2, 0, 3, 4, 5).reshape(b, h, t, d)


def short_conv(u, w):
    s = u.shape[1]
    up = jnp.pad(u, ((0, 0), (1, 1), (0, 0)))
    return up[:, :s] * w[:, 0] + up[:, 1:s + 1] * w[:, 1] + up[:, 2:] * w[:, 2]


def mla_queries(cq, lp):
    q = jnp.dot(rms_norm(cq, lp['mla_q_norm_g']), lp['mla_w_uq'])
    return rms_norm(split_heads(q, MLA_HEADS), lp['mla_q_head_g'])


def mla_keys_values(ckv_n, kpe, lp):
    b, l, _ = ckv_n.shape
    kv = jnp.dot(ckv_n, lp['mla_w_ukv']).reshape(b, l, MLA_HEADS, MLA_NOPE + MLA_V_DIM)
    k_nope, v = kv[..., :MLA_NOPE], kv[..., MLA_NOPE:]
    k_pe = jnp.broadcast_to(kpe[:, :, None, :], (b, l, MLA_HEADS, MLA_ROPE))
    k = jnp.concatenate([k_nope, k_pe], axis=-1).transpose(0, 2, 1, 3)
    return rms_norm(k, lp['mla_k_head_g']), v.transpose(0, 2, 1, 3)


def project_mixers(h, lp):
    z = jnp.dot(h, lp['w_in'])
    offs, acc = [], 0
    for sz in IN_SIZES[:-1]:
        acc += sz
        offs.append(acc)
    cq, ckv, kpe, na_q, na_k, na_v, g_b, g_c, u = jnp.split(z, offs, axis=-1)
    q_mla = mla_queries(cq, lp)
    ckv_n = rms_norm(ckv, lp['mla_kv_norm_g'])
    q_na = rms_norm(split_heads(na_q, NA_HEADS), lp['na_q_head_g'])
    k_na = rms_norm(split_heads(na_k, NA_HEADS), lp['na_k_head_g'])
    v_na = split_heads(na_v, NA_HEADS)
    conv_out = g_b * short_conv(g_c * u, lp['conv_w'])
    return q_mla, ckv_n, kpe, q_na, k_na, v_na, conv_out


def peer(h, lp):
    b, s, dm = h.shape
    x = h.reshape(b * s, dm)
    t = x.shape[0]
    q = jnp.dot(x, lp['peer_w_q']).reshape(t, PEER_HEADS, 2, PEER_KEY_HALF)
    sc = jnp.einsum('thpd,hpnd->thpn', q, lp['peer_sub_keys']).astype(jnp.float32)
    s1, i1 = lax.top_k(sc[:, :, 0], PEER_TOPK)
    s2, i2 = lax.top_k(sc[:, :, 1], PEER_TOPK)
    cand = (s1[..., :, None] + s2[..., None, :]).reshape(t, PEER_HEADS, PEER_TOPK * PEER_TOPK)
    cidx = (i1[..., :, None] * PEER_N_KEYS + i2[..., None, :]).reshape(t, PEER_HEADS, PEER_TOPK * PEER_TOPK)
    top_s, pos = lax.top_k(cand, PEER_TOPK)
    eidx = jnp.take_along_axis(cidx, pos, axis=-1)
    gate = jax.nn.softmax(top_s, axis=-1)
    nb = t // PEER_TOKEN_BLOCK
    ne = PEER_HEADS * PEER_TOPK

    def one_block(args):
        xb, eb, gb = args
        e = eb.reshape(PEER_TOKEN_BLOCK, ne)
        u = jnp.take(lp['peer_u'], e, axis=0)
        a = jax.nn.gelu(jnp.einsum('td,ted->te', xb, u), approximate=False)
        wgt = gb.reshape(PEER_TOKEN_BLOCK, ne).astype(a.dtype) * a
        v = jnp.take(lp['peer_v'], e, axis=0)
        return jnp.einsum('te,ted->td', wgt, v)

    out = lax.map(one_block, (x.reshape(nb, PEER_TOKEN_BLOCK, dm),
                              eidx.reshape(nb, PEER_TOKEN_BLOCK, PEER_HEADS, PEER_TOPK),
                              gate.reshape(nb, PEER_TOKEN_BLOCK, PEER_HEADS, PEER_TOPK)))
    return out.reshape(b, s, dm)


def context_layer(x, c_ctx, lp):
    sh1, sc1, g1, sh2, sc2, g2 = ada_modulation(c_ctx[None, :], lp['ada_w'], lp['ada_b'])
    h = modulate(x, lp['norm_mix_g'], sh1, sc1)
    q_mla, ckv_n, kpe, q_na, k_na, v_na, conv_out = project_mixers(h, lp)
    k_mla, v_mla = mla_keys_values(ckv_n, kpe, lp)
    o_mla = blocked_attention(q_mla, k_mla, v_mla, MLA_QK_DIM ** -0.5)
    o_na = blocked_attention(q_na, k_na, v_na, NA_HEAD_DIM ** -0.5)
    mix = jnp.concatenate([merge_heads(o_mla), merge_heads(o_na), conv_out], axis=-1)
    x = x + g1 * jnp.dot(mix, lp['w_out'])
    x = x + g2 * peer(modulate(x, lp['norm_ffn_g'], sh2, sc2), lp)
    return x, ckv_n, kpe, k_na, v_na


def latent_layer(x, c, lp, ckv_ctx, kpe_ctx, k_na_ctx, v_na_ctx, cos, sin):
    sh1, sc1, g1, sh2, sc2, g2 = ada_modulation(c, lp['ada_w'], lp['ada_b'])
    h = modulate(x, lp['norm_mix_g'], sh1, sc1)
    q_mla, ckv_n, kpe, q_na, k_na, v_na, conv_out = project_mixers(h, lp)
    q_mla = rope_tail(q_mla, cos, sin)
    k_lat, v_lat = mla_keys_values(ckv_n, kpe, lp)
    k_lat = rope_tail(k_lat, cos, sin)
    k_ctx, v_ctx = mla_keys_values(ckv_ctx, kpe_ctx, lp)
    o_mla = blocked_attention(q_mla, jnp.concatenate([k_ctx, k_lat], axis=2),
                              jnp.concatenate([v_ctx, v_lat], axis=2), MLA_QK_DIM ** -0.5)
    o_na = neighbourhood_attention(q_na, k_na, v_na, k_na_ctx, v_na_ctx, lp['na_rel_bias'])
    mix = jnp.concatenate([merge_heads(o_mla), merge_heads(o_na), conv_out], axis=-1)
    x = x + g1 * jnp.dot(mix, lp['w_out'])
    x = x + g2 * peer(modulate(x, lp['norm_ffn_g'], sh2, sc2), lp)
    return x


def setup_inputs(seed: int = 0) -> dict:
    key = jax.random.key(seed)
    ks = jax.random.split(key, 32)
    f32 = jnp.float32

    def nrm(k, shape, scale):
        return jax.random.normal(k, shape, f32) * scale

    def gain(k, shape):
        return 1.0 + 0.02 * jax.random.normal(k, shape, f32)

    return {
        "x_prompt": nrm(ks[0], (BATCH, SEQ, D_MODEL), 1.0),
        "x_sample": nrm(ks[1], (DEC_BATCH, DEC_SEQ, D_MODEL), 1.0),
        "cache_mla_ckv": nrm(ks[2], (DEC_BATCH, DEPTH, PAST_LEN, MLA_KV_LORA), 1.0),
        "cache_mla_kpe": nrm(ks[3], (DEC_BATCH, DEPTH, PAST_LEN, MLA_ROPE), 1.0),
        "cache_na_k": nrm(ks[4], (DEC_BATCH, DEPTH, NA_HEADS, PAST_LEN, NA_HEAD_DIM), 1.0),
        "cache_na_v": nrm(ks[5], (DEC_BATCH, DEPTH, NA_HEADS, PAST_LEN, NA_HEAD_DIM), 1.0),
        "c": nrm(ks[6], (DEC_BATCH, D_MODEL), 1.0),
        "c_ctx": nrm(ks[7], (D_MODEL,), 1.0),
        "ada_w": nrm(ks[8], (DEPTH, D_MODEL, ADA_CHUNKS * D_MODEL), D_MODEL ** -0.5),
        "ada_b": nrm(ks[9], (DEPTH, ADA_CHUNKS * D_MODEL), 0.02),
        "norm_mix_g": gain(ks[10], (DEPTH, D_MODEL)),
        "norm_ffn_g": gain(ks[11], (DEPTH, D_MODEL)),
        "w_in": nrm(ks[12], (DEPTH, D_MODEL, D_IN), D_MODEL ** -0.5),
        "mla_q_norm_g": gain(ks[13], (DEPTH, MLA_Q_LORA)),
        "mla_w_uq": nrm(ks[14], (DEPTH, MLA_Q_LORA, MLA_HEADS * MLA_QK_DIM), MLA_Q_LORA ** -0.5),
        "mla_kv_norm_g": gain(ks[15], (DEPTH, MLA_KV_LORA)),
        "mla_w_ukv": nrm(ks[16], (DEPTH, MLA_KV_LORA, MLA_HEADS * (MLA_NOPE + MLA_V_DIM)), MLA_KV_LORA ** -0.5),
        "mla_q_head_g": gain(ks[17], (DEPTH, MLA_QK_DIM)),
        "mla_k_head_g": gain(ks[18], (DEPTH, MLA_QK_DIM)),
        "na_q_head_g": gain(ks[19], (DEPTH, NA_HEAD_DIM)),
        "na_k_head_g": gain(ks[20], (DEPTH, NA_HEAD_DIM)),
        "na_rel_bias": nrm(ks[21], (DEPTH, NA_HEADS, 2 * NA_KR - 1, 2 * NA_KC - 1), 0.02),
        "conv_w": nrm(ks[22], (DEPTH, CONV_CH, CONV_K), CONV_K ** -0.5),
        "w_out": nrm(ks[23], (DEPTH, D_MIX, D_MODEL), D_MIX ** -0.5),
        "peer_w_q": nrm(ks[24], (DEPTH, D_MODEL, PEER_HEADS * PEER_KEY_DIM), D_MODEL ** -0.5),
        "peer_sub_keys": nrm(ks[25], (DEPTH, PEER_HEADS, 2, PEER_N_KEYS, PEER_KEY_HALF), PEER_KEY_HALF ** -0.5),
        "peer_u": nrm(ks[26], (DEPTH, PEER_N_EXPERTS, D_MODEL), D_MODEL ** -0.5),
        "peer_v": nrm(ks[27], (DEPTH, PEER_N_EXPERTS, D_MODEL), 0.05),
    }


def reference(x_prompt, x_sample, cache_mla_ckv, cache_mla_kpe, cache_na_k, cache_na_v, c, c_ctx,
              ada_w, ada_b, norm_mix_g, norm_ffn_g, w_in, mla_q_norm_g, mla_w_uq, mla_kv_norm_g,
              mla_w_ukv, mla_q_head_g, mla_k_head_g, na_q_head_g, na_k_head_g, na_rel_bias, conv_w,
              w_out, peer_w_q, peer_sub_keys, peer_u, peer_v):
    y_prompt = x_prompt
    y_sample = x_sample
    cos, sin = axial_rope_tables(x_sample.shape[1])
    ckv_list, kpe_list, nak_list, nav_list = [], [], [], []
    for i in range(DEPTH):
        lp = {
            'ada_w': ada_w[i], 'ada_b': ada_b[i], 'norm_mix_g': norm_mix_g[i], 'norm_ffn_g': norm_ffn_g[i],
            'w_in': w_in[i], 'mla_q_norm_g': mla_q_norm_g[i], 'mla_w_uq': mla_w_uq[i],
            'mla_kv_norm_g': mla_kv_norm_g[i], 'mla_w_ukv': mla_w_ukv[i],
            'mla_q_head_g': mla_q_head_g[i], 'mla_k_head_g': mla_k_head_g[i],
            'na_q_head_g': na_q_head_g[i], 'na_k_head_g': na_k_head_g[i], 'na_rel_bias': na_rel_bias[i],
            'conv_w': conv_w[i], 'w_out': w_out[i], 'peer_w_q': peer_w_q[i],
            'peer_sub_keys': peer_sub_keys[i], 'peer_u': peer_u[i], 'peer_v': peer_v[i],
        }
        y_prompt, ckv_n, kpe, k_na, v_na = context_layer(y_prompt, c_ctx, lp)
        ckv_list.append(ckv_n)
        kpe_list.append(kpe)
        nak_list.append(k_na)
        nav_list.append(v_na)
        y_sample = latent_layer(y_sample, c, lp, cache_mla_ckv[:, i], cache_mla_kpe[:, i],
                                cache_na_k[:, i], cache_na_v[:, i], cos, sin)
    new_mla_ckv = jnp.stack(ckv_list, axis=1)
    new_mla_kpe = jnp.stack(kpe_list, axis=1)
    new_na_k = jnp.stack(nak_list, axis=1)
    new_na_v = jnp.stack(nav_list, axis=1)
    return (y_prompt, y_sample, new_mla_ckv, new_mla_kpe, new_na_k, new_na_v)
```

```python
from contextlib import ExitStack
import numpy as np
import concourse.bass as bass
import concourse.mybir as mybir
from concourse.bass_utils import run_bass_kernel_spmd

F32 = mybir.dt.float32
I32 = mybir.dt.int32
AF = mybir.ActivationFunctionType
ALU = mybir.AluOpType
AX = mybir.AxisListType

D = 2048
DIN = 3904
NEG = -30000.0


class Buf:
    __slots__ = ("t", "lw", "rd", "name")

    def __init__(self, t, name=""):
        self.t = t
        self.lw = None
        self.rd = {}
        self.name = name

    def __getitem__(self, k):
        return self.t[k]


class FW:
    ENG = ("pe", "act", "dve", "pool", "sp")
    NDMA = 6

    def __init__(self, nc, es, same_engine_sync=True):
        self.nc = nc
        self.es = es
        self.same = same_engine_sync
        self.sem = {}
        self.cnt = {}
        for e in self.ENG:
            self.sem[e] = es.enter_context(nc.semaphore("s_" + e))
            self.cnt[e] = 0
        self.dq = {}
        for q in ("sp", "pool", "act"):
            slots = []
            for i in range(self.NDMA):
                k = "d_%s%d" % (q, i)
                self.sem[k] = es.enter_context(nc.semaphore(k))
                self.cnt[k] = 0
                slots.append(k)
            self.dq[q] = [slots, 0]
        self.seen = {e: {} for e in self.ENG}
        self.prog = {e: [] for e in self.ENG}
        self.ninst = 0
        self._psi = 0

    def sb(self, name, shape, dt=F32):
        t = self.es.enter_context(self.nc.sbuf_tensor("sb_" + name, list(shape), dt))
        return Buf(t, name)

    def ps(self, name, shape, dt=F32):
        t = self.es.enter_context(self.nc.psum_tensor(name, list(shape), dt))
        return Buf(t, name)

    def dram(self, name, shape, dt=F32, kind="Internal"):
        t = self.nc.dram_tensor(name, list(shape), dt, kind=kind)
        return Buf(t, name)

    def _wait(self, eng, key, val):
        if key == eng and not self.same:
            return
        if self.seen[eng].get(key, 0) >= val:
            return
        self.prog[eng].append(("w", key, val))
        self.seen[eng][key] = val

    def _deps(self, eng, reads, writes):
        for b in reads:
            if b.lw is not None:
                self._wait(eng, *b.lw)
        for b in writes:
            if b.lw is not None:
                self._wait(eng, *b.lw)
            for k, v in b.rd.items():
                self._wait(eng, k, v)

    def _mark(self, key, val, reads, writes):
        for b in reads:
            if b.rd.get(key, 0) < val:
                b.rd[key] = val
        for b in writes:
            b.lw = (key, val)
            b.rd = {}

    def op(self, eng, fn, reads=(), writes=()):
        self._deps(eng, reads, writes)
        self.cnt[eng] += 1
        self.prog[eng].append(("i", fn, eng, 1))
        self._mark(eng, self.cnt[eng], reads, writes)
        self.ninst += 1

    def dma(self, q, fn, reads=(), writes=()):
        slots, i = self.dq[q]
        k = slots[i % len(slots)]
        self.dq[q][1] = i + 1
        if self.cnt[k] > 0:
            self._wait(q, k, self.cnt[k])
        self._deps(q, reads, writes)
        self.cnt[k] += 16
        self.prog[q].append(("i", fn, k, 16))
        self._mark(k, self.cnt[k], reads, writes)
        self.ninst += 1

    def finish(self):
        for e in self.ENG:
            if e != "sp" and self.cnt[e] > 0:
                self._wait("sp", e, self.cnt[e])
        for q in self.dq:
            for k in self.dq[q][0]:
                if self.cnt[k] > 0:
                    self._wait("sp", k, self.cnt[k])

    def emit(self):
        nc = self.nc
        prog = self.prog
        sem = self.sem

        def run(e, items):
            for it in items:
                if it[0] == "w":
                    e.wait_ge(sem[it[1]], it[2])
                else:
                    it[1](e).then_inc(sem[it[2]], it[3])

        with nc.Block() as block:
            @block.tensor
            def _(e):
                run(e, prog["pe"])

            @block.scalar
            def _(e):
                run(e, prog["act"])

            @block.vector
            def _(e):
                run(e, prog["dve"])

            @block.gpsimd
            def _(e):
                run(e, prog["pool"])

            @block.sync
            def _(e):
                run(e, prog["sp"])


class Cfg:
    def __init__(self, depth=4, nseq=4, lat=1024, stages=99):
        self.depth = depth
        self.nseq = nseq
        self.lat = lat
        self.stages = stages


GOFF = {"qn": 0, "kvn": 512, "qh": 768, "kh": 960, "naq": 1152, "nak": 1280}
GLEN = 1408


CBW = 256
NEG2 = -1.0e30
U32 = mybir.dt.uint32


def build(cfg):
    L = cfg.depth
    NS = cfg.nseq
    NTC = NS * 2
    TL = cfg.lat
    NTL = TL // 128
    NT = NTC + NTL
    NTOK = NT * 128
    NCACHE = 512 if TL else 0
    nc = bass.Bass("TRN2", target_bir_lowering=False)

    def din(name, shape, dt=F32):
        return Buf(nc.dram_tensor(name, list(shape), dt, kind="ExternalInput"), name)

    def dout(name, shape, dt=F32):
        return Buf(nc.dram_tensor(name, list(shape), dt, kind="ExternalOutput"), name)

    xin = din("xin", [NTOK, D])
    cvecT = din("cvecT", [128, 16, 2])
    ada_w = din("ada_w", [L, D, 6 * D])
    ada_bT = din("ada_bT", [L, 128, 96])
    ada_b = din("ada_b", [L, 6 * D])
    gmixT = din("gmixT", [L, 128, 16])
    gffnT = din("gffnT", [L, 128, 16])
    w_in = din("w_in", [L, D, DIN])
    w_uq = din("w_uq", [L, 512, 1536])
    w_ukv = din("w_ukv", [L, 256, 2048])
    w_out = din("w_out", [L, D, D])
    gvec = din("gvec", [L, GLEN])
    conv_wT = din("conv_wT", [L, 3, 512])
    ident_d = din("ident", [128, 128])
    sel_d = din("sel", [2, 2, 128])
    peer_wq = din("peer_wq", [L, D, D])
    subT_d = din("subT", [L, 128, 16, 128])
    peer_u = [din("peer_u%d" % i, [16384, D]) for i in range(L)]
    peer_v = [din("peer_v%d" % i, [16384, D]) for i in range(L)]
    iota_d = din("iota", [128, 256])
    if TL:
        rope_d = din("rope", [TL, 64])
        c_ckv = din("c_ckv", [L, 512, 256])
        c_kpe = din("c_kpe", [L, 512, 64])
        c_nak = din("c_nak", [L, 4, 512, 128])
        c_nav = din("c_nav", [L, 4, 512, 128])
        relT_d = din("relT", [L, 31, 4, 15])
        OH_d = din("OH", [31, 64, 128])
        cm_d = din("cmask", [128, 64])

    xout = dout("xout", [NTOK, D])
    o_ckv = dout("o_ckv", [NS, L, 256, 256])
    o_kpe = dout("o_kpe", [NS, L, 256, 64])
    o_nak = dout("o_nak", [NS, L, 4, 256, 128])
    o_nav = dout("o_nav", [NS, L, 4, 256, 128])

    with ExitStack() as es:
        fw = FW(nc, es)
        NKT = NTOK + NCACHE
        X = fw.dram("Xs", [NTOK, D])
        QMT = fw.dram("QMT", [8, 192, NTOK])
        KMT = fw.dram("KMT", [8, 192, NKT])
        VM = fw.dram("VM", [NKT, 1024])
        QNT = fw.dram("QNT", [4, 128, NTOK])
        KNT = fw.dram("KNT", [4, 128, NKT])
        VN = fw.dram("VN", [NKT, 512])
        NSEQ = NS + (1 if TL else 0)
        CU = fw.dram("CU", [NTOK + 2 * NSEQ, 512])
        GBs = fw.dram("GBs", [NTOK, 512])
        GD = fw.dram("GD", [2, 2, 128, D])
        CBD = fw.dram("CBD", [4, 128, 15 * 64])

        def seq_of_tile(t):
            if t < NTC:
                si = t // 2
            else:
                si = NS
            return si, t * 128 + 2 * si + 1

        pbig = fw.ps("pbig", [128, 4096])
        psb = [Buf(pbig.t[:, i * 512:(i + 1) * 512], "ps%d" % i) for i in range(8)]
        grp_i = [0]

        def PS():
            b = psb[fw._psi % 8]
            fw._psi += 1
            return b

        def PSG():
            g = grp_i[0] % 2
            grp_i[0] += 1
            return psb[g * 4:(g + 1) * 4], pbig.t[:, g * 2048:(g + 1) * 2048]

        ident = fw.sb("ident", [128, 128])
        sel = fw.sb("sel", [2, 2, 128])
        fw.dma("sp", lambda e: e.dma_start(out=ident[:], in_=ident_d[:, :]), [ident_d], [ident])
        fw.dma("sp", lambda e: e.dma_start(out=sel[:], in_=sel_d[:, :, :]), [sel_d], [sel])
        sT = fw.sb("sT", [128, 16, 2])
        fw.dma("sp", lambda e: e.dma_start(out=sT[:], in_=cvecT[:, :, :]), [cvecT], [sT])
        fw.op("act", lambda e: e.activation(out=sT[:], in_=sT[:], func=AF.Silu), [sT], [sT])

        for t in range(NT):
            fw.dma("pool", lambda e, t=t: e.dma_start(out=X[t * 128:(t + 1) * 128, :], in_=xin[t * 128:(t + 1) * 128, :]), [xin], [X])

        B = [fw.sb("B%d" % i, [128, D]) for i in range(10)]
        wsl = [fw.sb("wsl%d" % i, [128, 16, CBW]) for i in range(2)]
        wi = [0]

        def WSL():
            b = wsl[wi[0] % 2]
            wi[0] += 1
            return b

        modT = fw.sb("modT", [128, 96, 2])
        abT = fw.sb("abT", [128, 96])
        rowb = fw.sb("rowb", [2, CBW])
        gmix = fw.sb("gmix", [128, 16])
        gffn = fw.sb("gffn", [128, 16])
        A1 = fw.sb("A1", [128, 16, 2])
        A2 = fw.sb("A2", [128, 16, 2])
        gv = fw.sb("gv", [128, GLEN])
        cw = fw.sb("cw", [128, 3, 512])
        hT = fw.sb("hT", [128, 16, 128])
        z = fw.sb("z", [128, DIN])
        ss = fw.sb("ss", [128, 16])
        rs = fw.sb("rs", [128, 16])
        tT = fw.sb("tT", [128, 4, 128])
        stg = fw.sb("stg", [128, 2, 8, 128])
        kTs = fw.sb("kTs", [128, 2, 1536])
        qTs = fw.sb("qTs", [128, 2, 128])
        mx = fw.sb("mx", [128, 4])
        subk = fw.sb("subk", [128, 16, 128])
        TV = fw.sb("TV", [128, 16, 16])
        TIu = fw.sb("TIu", [128, 16, 16], U32)
        TIf = fw.sb("TIf", [128, 16, 16])
        TS = fw.sb("TS", [128, 8, 16])
        PIu = fw.sb("PIu", [128, 8, 16], U32)
        PIf = fw.sb("PIf", [128, 8, 16])
        iota = fw.sb("iota", [128, 256])
        fw.dma("sp", lambda e: e.dma_start(out=iota[:], in_=iota_d[:, :]), [iota_d], [iota])
        EI = fw.sb("EI", [128, 128])
        GT = fw.sb("GT", [128, 128])
        IDX = fw.sb("IDX", [128, 128], I32)
        GTT = fw.sb("GTT", [128, 128])
        ACTV = fw.sb("ACTV", [128, 128])
        WT = fw.sb("WT", [128, 128])
        ZW = [fw.sb("ZW%d" % i, [128, 256]) for i in range(4)]
        for i in range(4):
            fw.op("pool", lambda e, i=i: e.memset(ZW[i][:], 0.0), [], [ZW[i]])
        zrow = fw.sb("zrow", [1, 512])
        fw.op("pool", lambda e: e.memset(zrow[:], 0.0), [], [zrow])
        for si in range(NSEQ):
            st = si * 256 if si < NS else NS * 256
            ln = 256 if si < NS else TL
            for rr in (st + 2 * si, st + 2 * si + 1 + ln):
                fw.dma("pool", lambda e, rr=rr: e.dma_start(out=CU[rr:rr + 1, :], in_=zrow[:, :]), [zrow], [CU])
        if TL:
            ropet = fw.sb("ropet", [128, NTL, 64])
            fw.dma("sp", lambda e: e.dma_start(out=ropet[:], in_=rope_d.t.ap().rearrange("(n p) c -> p n c", p=128)), [rope_d], [ropet])
            relTs = fw.sb("relTs", [31, 4, 15])
            cm = fw.sb("cm", [128, 64])
            fw.dma("sp", lambda e: e.dma_start(out=cm[:], in_=cm_d[:, :]), [cm_d], [cm])
            CB2 = fw.sb("CB2", [128, 15, 64])
            Bt = fw.sb("Bt", [128, 1088])
            fw.op("pool", lambda e: e.memset(Bt[:], 0.0), [], [Bt])

        def rms_rstd(src_ap, srcbuf, n, col, junk):
            fw.op("act", lambda e: e.activation(out=junk[:, 0:n], in_=src_ap, func=AF.Square, scale=float(n ** -0.5),
                                                accum_out=ss[:, col:col + 1]), [srcbuf], [junk, ss])
            fw.op("act", lambda e: e.activation(out=rs[:, col:col + 1], in_=ss[:, col:col + 1], func=AF.Sqrt, bias=1e-6, scale=1.0), [ss], [rs])
            fw.op("dve", lambda e: e.reciprocal(out=rs[:, col:col + 1], in_=rs[:, col:col + 1]), [rs], [rs])

        def heads_norm(src_ap, srcbuf, nh, dh, gname, dst_ap, dstbuf, junk, extra_scale=1.0):
            jv = junk[:, 0:nh * dh].rearrange("p (h d) -> p h d", h=nh)
            fw.op("dve", lambda e: e.tensor_tensor(out=jv, in0=src_ap, in1=src_ap, op=ALU.mult), [srcbuf], [junk])
            fw.op("dve", lambda e: e.tensor_reduce(out=ss[:, 0:nh], in_=jv, axis=AX.X, op=ALU.add), [junk], [ss])
            fw.op("act", lambda e: e.activation(out=rs[:, 0:nh], in_=ss[:, 0:nh], func=AF.Sqrt, bias=1e-6, scale=float(1.0 / dh)), [ss], [rs])
            fw.op("dve", lambda e: e.reciprocal(out=rs[:, 0:nh], in_=rs[:, 0:nh]), [rs], [rs])
            fw.op("dve", lambda e: e.tensor_tensor(out=dst_ap, in0=src_ap, in1=rs[:, 0:nh].unsqueeze(2).to_broadcast([128, nh, dh]), op=ALU.mult),
                  [srcbuf, rs], [dstbuf])
            g0 = GOFF[gname]
            fw.op("dve", lambda e: e.scalar_tensor_tensor(out=dst_ap, in0=dst_ap, scalar=float(extra_scale),
                                                          in1=gv[:, g0:g0 + dh].unsqueeze(1).to_broadcast([128, nh, dh]), op0=ALU.mult, op1=ALU.mult),
                  [dstbuf, gv], [dstbuf])

        def rope(buf, nh, lt, junk):
            v = buf[:, 0:nh * 192].rearrange("p (h d) -> p h d", h=nh)
            x1, x2 = v[:, :, 128:160], v[:, :, 160:192]
            cs = ropet[:, lt, 0:32].unsqueeze(1).to_broadcast([128, nh, 32])
            sn = ropet[:, lt, 32:64].unsqueeze(1).to_broadcast([128, nh, 32])
            j = junk[:, 0:nh * 128].rearrange("p (h d) -> p h d", h=nh)
            a, b_, c, d_ = j[:, :, 0:32], j[:, :, 32:64], j[:, :, 64:96], j[:, :, 96:128]
            fw.op("dve", lambda e: e.tensor_tensor(out=a, in0=x1, in1=cs, op=ALU.mult), [buf, ropet], [junk])
            fw.op("dve", lambda e: e.tensor_tensor(out=b_, in0=x2, in1=sn, op=ALU.mult), [buf, ropet], [junk])
            fw.op("dve", lambda e: e.tensor_tensor(out=c, in0=x1, in1=sn, op=ALU.mult), [buf, ropet], [junk])
            fw.op("dve", lambda e: e.tensor_tensor(out=d_, in0=x2, in1=cs, op=ALU.mult), [buf, ropet], [junk])
            fw.op("dve", lambda e: e.tensor_tensor(out=x1, in0=a, in1=b_, op=ALU.subtract), [junk], [buf])
            fw.op("dve", lambda e: e.tensor_tensor(out=x2, in0=c, in1=d_, op=ALU.add), [junk], [buf])

        def transpose_chunks(src_ap_fn, srcbuf, nch, dst_fn, dstbuf, width=128, scale=None, bias=None, rows=128):
            for k in range(nch):
                p = PS()
                fw.op("pe", lambda e, k=k, p=p: e.transpose(out=p[0:width, 0:rows], in_=src_ap_fn(k), identity=ident[0:rows, 0:rows]), [srcbuf, ident], [p])
                if scale is None:
                    fw.op("act", lambda e, k=k, p=p: e.copy(out=dst_fn(k), in_=p[0:width, 0:rows]), [p], [dstbuf])
                else:
                    sc, bi = scale(k), bias(k)
                    fw.op("act", lambda e, k=k, p=p, sc=sc, bi=bi: e.activation(out=dst_fn(k), in_=p[0:width, 0:rows], func=AF.Identity,
                                                                                 bias=bi, scale=sc), [p, A1, A2, modT], [dstbuf])

        def gemm(lhsT_fn, lbuf, nk, Wd, w_ap_fn, ncols, out_fn, kparts=128):
            c0 = 0
            while c0 < ncols:
                w = min(CBW, ncols - c0)
                slab = WSL()
                fw.dma("sp", lambda e, c0=c0, w=w, slab=slab: e.dma_start(out=slab[0:kparts, 0:nk, 0:w], in_=w_ap_fn(c0, w)), [Wd], [slab])
                p = PS()
                for k in range(nk):
                    fw.op("pe", lambda e, k=k, p=p, slab=slab, w=w: e.matmul(p[:, 0:w], lhsT=lhsT_fn(k), rhs=slab[0:kparts, k, 0:w],
                                                                            start=(k == 0), stop=(k == nk - 1)), [lbuf, slab], [p])
                out_fn(p, c0, w)
                c0 += w

        def store_T(src, nh, dq, dstT, tok0):
            sv = src[:, 0:nh * dq].rearrange("p (h d) -> p h d", h=nh)
            for h in range(nh):
                transpose_chunks(lambda k, h=h: sv[:, h, 0:128], src, 1, lambda k, h=h: stg[:, 0, h, :], stg)
                if dq > 128:
                    transpose_chunks(lambda k, h=h: sv[:, h, 128:dq], src, 1, lambda k, h=h: stg[0:dq - 128, 1, h, :], stg, width=dq - 128)
            fw.dma("pool", lambda e: e.dma_start(out=dstT[:, 0:128, tok0:tok0 + 128].rearrange("h d t -> d h t"), in_=stg[:, 0, 0:nh, :]), [stg], [dstT])
            if dq > 128:
                fw.dma("pool", lambda e: e.dma_start(out=dstT[:, 128:dq, tok0:tok0 + 128].rearrange("h d t -> d h t"), in_=stg[0:dq - 128, 1, 0:nh, :]), [stg], [dstT])

        def kv_pipeline(l, ckvn_ap, ckvn_buf, kpe_ap, kpe_buf, tok0, lt, t1, t2, junk):
            transpose_chunks(lambda k: ckvn_ap[:, k * 128:(k + 1) * 128], ckvn_buf, 2, lambda k: tT[:, k, :], tT)
            gemm(lambda k: tT[:, k, :], tT, 2, w_ukv, lambda c0, w, l=l: w_ukv[l, :, c0:c0 + w].rearrange("(k p) n -> p k n", p=128), 2048,
                 lambda p, c0, w: fw.op("act", lambda e, p=p, c0=c0, w=w: e.copy(out=t2[:, c0:c0 + w], in_=p[:, 0:w]), [p], [t2]))
            kvv = t2[:, 0:2048].rearrange("p (h d) -> p h d", h=8)
            kf = t1[:, 0:1536].rearrange("p (h d) -> p h d", h=8)
            fw.op("dve", lambda e: e.tensor_copy(out=kf[:, :, 0:128], in_=kvv[:, :, 0:128]), [t2], [t1])
            fw.op("dve", lambda e: e.tensor_copy(out=kf[:, :, 128:192], in_=kpe_ap.unsqueeze(1).to_broadcast([128, 8, 64])), [kpe_buf], [t1])
            fw.dma("pool", lambda e: e.dma_start(out=VM[tok0:tok0 + 128, :].rearrange("t (h d) -> t h d", h=8), in_=kvv[:, :, 128:256]), [t2], [VM])
            heads_norm(kf, t1, 8, 192, "kh", kf, t1, junk)
            if lt is not None:
                rope(t1, 8, lt, junk)
            store_T(t1, 8, 192, KMT, tok0)

        def attention(QT, KT, V, vw, h, dq, dv, r0, keyblocks, mix, mixcol, Ssb, PTs, Vs, bias=None):
            nkb = len(keyblocks)
            nk = sum(n for _, n in keyblocks)
            nch = 2 if dq > 128 else 1
            fw.dma("sp", lambda e: e.dma_start(out=qTs[:, 0, :], in_=QT[h, 0:128, r0:r0 + 128]), [QT], [qTs])
            if nch == 2:
                fw.dma("sp", lambda e: e.dma_start(out=qTs[0:dq - 128, 1, :], in_=QT[h, 128:dq, r0:r0 + 128]), [QT], [qTs])
            off = 0
            i = 0
            while i < nkb:
                t0, n = keyblocks[i]
                j = i + 1
                tot = n
                while j < nkb and keyblocks[j][0] == t0 + tot:
                    tot += keyblocks[j][1]
                    j += 1
                fw.dma("sp", lambda e, t0=t0, tot=tot, off=off: e.dma_start(out=kTs[:, 0, off:off + tot], in_=KT[h, 0:128, t0:t0 + tot]), [KT], [kTs])
                if nch == 2:
                    fw.dma("sp", lambda e, t0=t0, tot=tot, off=off: e.dma_start(out=kTs[0:dq - 128, 1, off:off + tot], in_=KT[h, 128:dq, t0:t0 + tot]), [KT], [kTs])
                off += tot
                i = j
            Vv = Vs[:, 0:12 * 128].rearrange("p (k d) -> p k d", k=12)
            for kb, (t0, n) in enumerate(keyblocks):
                fw.dma("pool", lambda e, kb=kb, t0=t0, n=n: e.dma_start(out=Vv[0:n, kb, 0:dv], in_=V[t0:t0 + n, h * dv:(h + 1) * dv]), [V], [Vs])
            banks, _ = PSG()
            c0 = 0
            while c0 < nk:
                w = min(512, nk - c0)
                b = banks[c0 // 512]
                fw.op("pe", lambda e, b=b, c0=c0, w=w: e.matmul(b[:, 0:w], lhsT=qTs[:, 0, :], rhs=kTs[:, 0, c0:c0 + w], start=True, stop=(nch == 1)), [qTs, kTs], [b])
                if nch == 2:
                    fw.op("pe", lambda e, b=b, c0=c0, w=w: e.matmul(b[:, 0:w], lhsT=qTs[0:dq - 128, 1, :], rhs=kTs[0:dq - 128, 1, c0:c0 + w], start=False, stop=True),
                          [qTs, kTs], [b])
                if bias is not None:
                    fw.op("dve", lambda e, b=b, c0=c0, w=w: e.tensor_tensor(out=Ssb[:, c0:c0 + w], in0=b[:, 0:w], in1=bias[:, c0:c0 + w], op=ALU.add), [b, bias], [Ssb])
                else:
                    fw.op("act", lambda e, b=b, c0=c0, w=w: e.copy(out=Ssb[:, c0:c0 + w], in_=b[:, 0:w]), [b], [Ssb])
                c0 += w
            fw.op("dve", lambda e: e.reduce_max(out=mx[:, 0:1], in_=Ssb[:, 0:nk], axis=AX.X), [Ssb], [mx])
            fw.op("dve", lambda e: e.tensor_scalar(out=mx[:, 1:2], in0=mx[:, 0:1], scalar1=-1.0, scalar2=None, op0=ALU.mult), [mx], [mx])
            fw.op("act", lambda e: e.activation(out=Ssb[:, 0:nk], in_=Ssb[:, 0:nk], func=AF.Exp, bias=mx[:, 1:2], scale=1.0, accum_out=mx[:, 2:3]), [Ssb, mx], [Ssb, mx])
            fw.op("dve", lambda e: e.reciprocal(out=mx[:, 3:4], in_=mx[:, 2:3]), [mx], [mx])
            PTv = PTs[:, 0:12 * 128].rearrange("p (k d) -> p k d", k=12)
            off = 0
            for kb, (t0, n) in enumerate(keyblocks):
                p = PS()
                fw.op("pe", lambda e, p=p, off=off, n=n: e.transpose(out=p[0:n, 0:128], in_=Ssb[:, off:off + n], identity=ident[:]), [Ssb, ident], [p])
                fw.op("act", lambda e, p=p, kb=kb, n=n: e.copy(out=PTv[0:n, kb, :], in_=p[0:n, 0:128]), [p], [PTs])
                off += n
            po = PS()
            for kb, (t0, n) in enumerate(keyblocks):
                fw.op("pe", lambda e, kb=kb, n=n, po=po: e.matmul(po[:, 0:dv], lhsT=PTv[0:n, kb, :], rhs=Vv[0:n, kb, 0:dv], start=(kb == 0), stop=(kb == nkb - 1)),
                      [PTs, Vs], [po])
            fw.op("dve", lambda e, po=po: e.tensor_scalar(out=mix[:, mixcol:mixcol + dv], in0=po[:, 0:dv], scalar1=mx[:, 3:4], scalar2=None, op0=ALU.mult), [po, mx], [mix])


        def peer_phase(l):
            xt, xn, junk, Hh, t2, G2t = B[0], B[1], B[2], B[3], B[4], B[9]
            Ug = [B[5], B[6]]
            Vg = [B[7], B[8]]
            SC = z
            SCv = SC[:, 0:2048].rearrange("p (a n) -> p a n", a=16)
            cand = z[:, 2048:3904]
            for t in range(NT):
                grp = 0 if t < NTC else 1
                r0 = t * 128
                fw.dma("sp", lambda e, r0=r0: e.dma_start(out=xt[:], in_=X[r0:r0 + 128, :]), [X], [xt])
                fw.dma("sp", lambda e, grp=grp: e.dma_start(out=G2t[:], in_=GD[1, grp, :, :]), [GD], [G2t])
                rms_rstd(xt[:], xt, D, 0, junk)
                fw.op("dve", lambda e: e.tensor_scalar(out=xn[:], in0=xt[:], scalar1=rs[:, 0:1], scalar2=None, op0=ALU.mult), [xt, rs], [xn])
                transpose_chunks(lambda k: xn[:, k * 128:(k + 1) * 128], xn, 16, lambda k: hT[:, k, :], hT,
                                 scale=lambda k, grp=grp: A2[:, k, grp:grp + 1], bias=lambda k, grp=grp: modT[:, 48 + k, grp:grp + 1])
                gemm(lambda k: hT[:, k, :], hT, 16, peer_wq, lambda c0, w, l=l: peer_wq[l, :, c0:c0 + w].rearrange("(k p) n -> p k n", p=128), D,
                     lambda p, c0, w: fw.op("act", lambda e, p=p, c0=c0, w=w: e.copy(out=t2[:, c0:c0 + w], in_=p[:, 0:w]), [p], [t2]))
                transpose_chunks(lambda k: hT[:, k, :], hT, 16, lambda k: Hh[:, k * 128:(k + 1) * 128], Hh)
                for hp in range(16):
                    transpose_chunks(lambda k, hp=hp: t2[:, hp * 128:(hp + 1) * 128], t2, 1, lambda k, hp=hp: tT[:, hp % 4, :], tT)
                    p2 = PS()
                    fw.op("pe", lambda e, hp=hp, p2=p2: e.matmul(p2[:, 0:128], lhsT=tT[:, hp % 4, :], rhs=subk[:, hp, :], start=True, stop=True), [tT, subk], [p2])
                    fw.op("act", lambda e, hp=hp, p2=p2: e.copy(out=SCv[:, hp, :], in_=p2[:, 0:128]), [p2], [z])
                for hp in range(16):
                    fw.op("dve", lambda e, hp=hp: e.max(out=TV[:, hp, 0:8], in_=SCv[:, hp, :]), [z], [TV])
                    fw.op("dve", lambda e, hp=hp: e.max_index(out=TIu[:, hp, 0:8], in_max=TV[:, hp, 0:8], in_values=SCv[:, hp, :]), [z, TV], [TIu])
                    fw.op("dve", lambda e, hp=hp: e.match_replace(out=junk[:, 0:128], in_to_replace=TV[:, hp, 0:8], in_values=SCv[:, hp, :], imm_value=NEG2), [z, TV], [junk])
                    fw.op("dve", lambda e, hp=hp: e.max(out=TV[:, hp, 8:16], in_=junk[:, 0:128]), [junk], [TV])
                    fw.op("dve", lambda e, hp=hp: e.max_index(out=TIu[:, hp, 8:16], in_max=TV[:, hp, 8:16], in_values=junk[:, 0:128]), [junk, TV], [TIu])
                fw.op("dve", lambda e: e.tensor_copy(out=TIf[:], in_=TIu[:]), [TIu], [TIf])
                TVv = TV[:].rearrange("p (h two) k -> p h two k", two=2)
                TIv = TIf[:].rearrange("p (h two) k -> p h two k", two=2)
                cand = xn[:, 0:2048].rearrange("p (h a b) -> p h a b", h=8, a=16)
                cidx = junk[:, 0:2048].rearrange("p (h a b) -> p h a b", h=8, a=16)
                for h in range(8):
                    fw.op("dve", lambda e, h=h: e.tensor_tensor(out=cand[:, h, :, :], in0=TVv[:, h, 0, :].unsqueeze(2).to_broadcast([128, 16, 16]),
                                                                in1=TVv[:, h, 1, :].unsqueeze(1).to_broadcast([128, 16, 16]), op=ALU.add), [TV], [xn])
                    fw.op("dve", lambda e, h=h: e.scalar_tensor_tensor(out=cidx[:, h, :, :], in0=TIv[:, h, 0, :].unsqueeze(2).to_broadcast([128, 16, 16]), scalar=128.0,
                                                                       in1=TIv[:, h, 1, :].unsqueeze(1).to_broadcast([128, 16, 16]), op0=ALU.mult, op1=ALU.add), [TIf], [junk])
                candf = xn[:, 0:2048].rearrange("p (h c) -> p h c", h=8)
                cidxf = junk[:, 0:2048].rearrange("p (h c) -> p h c", h=8)
                scr = z[:, 2048:2304]
                scr2 = z[:, 2304:2560]
                for h in range(8):
                    fw.op("dve", lambda e, h=h: e.max(out=TS[:, h, 0:8], in_=candf[:, h, :]), [xn], [TS])
                    fw.op("dve", lambda e, h=h: e.max_index(out=PIu[:, h, 0:8], in_max=TS[:, h, 0:8], in_values=candf[:, h, :]), [xn, TS], [PIu])
                    fw.op("dve", lambda e, h=h: e.match_replace(out=scr, in_to_replace=TS[:, h, 0:8], in_values=candf[:, h, :], imm_value=NEG2), [xn, TS], [z])
                    fw.op("dve", lambda e, h=h: e.max(out=TS[:, h, 8:16], in_=scr), [z], [TS])
                    fw.op("dve", lambda e, h=h: e.max_index(out=PIu[:, h, 8:16], in_max=TS[:, h, 8:16], in_values=scr), [z, TS], [PIu])
                fw.op("dve", lambda e: e.tensor_copy(out=PIf[:], in_=PIu[:]), [PIu], [PIf])
                for h in range(8):
                    for k in range(16):
                        fw.op("dve", lambda e, h=h, k=k: e.scalar_tensor_tensor(out=scr2, in0=iota[:, :], scalar=PIf[:, h, k:k + 1], in1=cidxf[:, h, :],
                                                                                 op0=ALU.is_equal, op1=ALU.mult, accum_out=EI[:, h * 16 + k:h * 16 + k + 1]), [iota, PIf, junk], [z, EI])
                GTv = GT[:].rearrange("p (h k) -> p h k", h=8)
                fw.op("dve", lambda e: e.tensor_tensor(out=GTv, in0=TS[:], in1=TS[:, :, 0:1].to_broadcast([128, 8, 16]), op=ALU.subtract), [TS], [GT])
                fw.op("act", lambda e: e.activation(out=GT[:], in_=GT[:], func=AF.Exp), [GT], [GT])
                fw.op("dve", lambda e: e.tensor_reduce(out=ss[:, 0:8], in_=GTv, axis=AX.X, op=ALU.add), [GT], [ss])
                fw.op("dve", lambda e: e.reciprocal(out=rs[:, 0:8], in_=ss[:, 0:8]), [ss], [rs])
                fw.op("dve", lambda e: e.tensor_tensor(out=GTv, in0=GTv, in1=rs[:, 0:8].unsqueeze(2).to_broadcast([128, 8, 16]), op=ALU.mult), [GT, rs], [GT])
                pe_ = PS()
                fw.op("pe", lambda e, pe_=pe_: e.transpose(out=pe_[:, 0:128], in_=EI[:], identity=ident[:]), [EI, ident], [pe_])
                fw.op("dve", lambda e, pe_=pe_: e.tensor_copy(out=IDX[:], in_=pe_[:, 0:128]), [pe_], [IDX])
                pg = PS()
                fw.op("pe", lambda e, pg=pg: e.transpose(out=pg[:, 0:128], in_=GT[:], identity=ident[:]), [GT, ident], [pg])
                fw.op("act", lambda e, pg=pg: e.copy(out=GTT[:], in_=pg[:, 0:128]), [pg], [GTT])
                for tok in range(128):
                    ug = Ug[tok % 2]
                    fw.dma("pool", lambda e, ug=ug, tok=tok, l=l: e.indirect_dma_start(
                        out=ug[:, :], out_offset=None, in_=peer_u[l][:, :],
                        in_offset=bass.IndirectOffsetOnAxis(ap=IDX[:, tok:tok + 1], axis=0)), [peer_u[l], IDX], [ug])
                    banks, gap = PSG()
                    for c in range(4):
                        fw.op("pe", lambda e, c=c, tok=tok, banks=banks: e.matmul(banks[c][:, :], lhsT=ident[:, tok:tok + 1].to_broadcast([128, 128]),
                                                                                  rhs=Hh[:, c * 512:(c + 1) * 512], start=True, stop=True), [ident, Hh], [banks[c]])
                    fw.op("dve", lambda e, ug=ug, gap=gap, tok=tok: e.scalar_tensor_tensor(out=xn[:], in0=ug[:], scalar=1.0, in1=gap, op0=ALU.mult, op1=ALU.mult,
                                                                                           accum_out=ACTV[:, tok:tok + 1]), [ug] + banks, [xn, ACTV])
                fw.op("act", lambda e: e.activation(out=WT[:], in_=ACTV[:], func=AF.Gelu), [ACTV], [WT])
                fw.op("dve", lambda e: e.tensor_tensor(out=WT[:], in0=WT[:], in1=GTT[:], op=ALU.mult), [WT, GTT], [WT])
                banks, gap = PSG()
                for tok in range(128):
                    vg = Vg[tok % 2]
                    zw = ZW[tok % 4]
                    fw.dma("pool", lambda e, vg=vg, tok=tok, l=l: e.indirect_dma_start(
                        out=vg[:, :], out_offset=None, in_=peer_v[l][:, :],
                        in_offset=bass.IndirectOffsetOnAxis(ap=IDX[:, tok:tok + 1], axis=0)), [peer_v[l], IDX], [vg])
                    fw.op("act", lambda e, zw=zw, tok=tok: e.copy(out=zw[:, 127:128], in_=WT[:, tok:tok + 1]), [WT], [zw])
                    for c in range(4):
                        fw.op("pe", lambda e, c=c, tok=tok, banks=banks, zw=zw, vg=vg: e.matmul(banks[c][:, :], lhsT=zw[:, 127 - tok:255 - tok], rhs=vg[:, c * 512:(c + 1) * 512],
                                                                                              start=(tok == 0), stop=(tok == 127)), [zw, vg], [banks[c]])
                fw.op("dve", lambda e, gap=gap: e.tensor_tensor(out=xn[:], in0=gap, in1=G2t[:], op=ALU.mult), banks + [G2t], [xn])
                fw.op("dve", lambda e: e.tensor_tensor(out=xt[:], in0=xt[:], in1=xn[:], op=ALU.add), [xt, xn], [xt])
                fw.dma("sp", lambda e, r0=r0: e.dma_start(out=X[r0:r0 + 128, :], in_=xt[:]), [xt], [X])

        for l in range(L):
            abrow0, abrow1 = B[8], B[9]
            fw.dma("sp", lambda e, l=l: e.dma_start(out=abT[:], in_=ada_bT[l, :, :]), [ada_bT], [abT])
            fw.dma("sp", lambda e, l=l: e.dma_start(out=abrow0[0:2, :], in_=ada_b[l:l + 1, 2 * D:3 * D].partition_broadcast(2)), [ada_b], [abrow0])
            fw.dma("sp", lambda e, l=l: e.dma_start(out=abrow1[0:2, :], in_=ada_b[l:l + 1, 5 * D:6 * D].partition_broadcast(2)), [ada_b], [abrow1])
            fw.dma("sp", lambda e, l=l: e.dma_start(out=gmix[:], in_=gmixT[l, :, :]), [gmixT], [gmix])
            fw.dma("sp", lambda e, l=l: e.dma_start(out=gffn[:], in_=gffnT[l, :, :]), [gffnT], [gffn])
            fw.dma("sp", lambda e, l=l: e.dma_start(out=gv[:], in_=gvec[l:l + 1, :].partition_broadcast(128)), [gvec], [gv])
            fw.dma("sp", lambda e, l=l: e.dma_start(out=cw[:], in_=conv_wT[l:l + 1, :, :].partition_broadcast(128)), [conv_wT], [cw])
            fw.dma("sp", lambda e, l=l: e.dma_start(out=subk[:], in_=subT_d[l, :, :, :]), [subT_d], [subk])
            Gst = B[7]
            NJB = 6 * D // CBW
            for jb in range(NJB):
                slab = WSL()
                fw.dma("sp", lambda e, l=l, jb=jb, slab=slab: e.dma_start(
                    out=slab[:, :, :], in_=ada_w[l, :, jb * CBW:(jb + 1) * CBW].rearrange("(k p) n -> p k n", p=128)), [ada_w], [slab])
                for sub in range(CBW // 128):
                    j = jb * (CBW // 128) + sub
                    pm = PS()
                    for k in range(16):
                        fw.op("pe", lambda e, k=k, pm=pm, slab=slab, sub=sub: e.matmul(pm[:, 0:2], lhsT=slab[:, k, sub * 128:(sub + 1) * 128], rhs=sT[:, k, :],
                                                                                       start=(k == 0), stop=(k == 15)), [slab, sT], [pm])
                    fw.op("dve", lambda e, j=j, pm=pm: e.tensor_scalar(out=modT[:, j, :], in0=pm[:, 0:2], scalar1=abT[:, j:j + 1], scalar2=None, op0=ALU.add),
                          [pm, abT], [modT])
                c0 = jb * CBW
                which = 0 if 2 * D <= c0 < 3 * D else (1 if c0 >= 5 * D else None)
                if which is not None:
                    off = c0 - (2 * D if which == 0 else 5 * D)
                    abr = abrow0 if which == 0 else abrow1
                    pr = PS()
                    for k in range(16):
                        fw.op("pe", lambda e, k=k, pr=pr, slab=slab: e.matmul(pr[0:2, 0:CBW], lhsT=sT[:, k, :], rhs=slab[:, k, :], start=(k == 0), stop=(k == 15)),
                              [slab, sT], [pr])
                    fw.op("dve", lambda e, pr=pr, abr=abr, off=off: e.tensor_tensor(out=rowb[:], in0=pr[0:2, 0:CBW], in1=abr[0:2, off:off + CBW], op=ALU.add),
                          [pr, abr], [rowb])
                    for grp in range(2):
                        pb = PS()
                        fw.op("pe", lambda e, pb=pb, grp=grp: e.matmul(pb[:, 0:CBW], lhsT=sel[:, grp, :], rhs=rowb[:, :], start=True, stop=True), [sel, rowb], [pb])
                        fw.op("act", lambda e, pb=pb: e.copy(out=Gst[:, 0:CBW], in_=pb[:, 0:CBW]), [pb], [Gst])
                        fw.dma("sp", lambda e, which=which, grp=grp, off=off: e.dma_start(out=GD[which, grp, :, off:off + CBW], in_=Gst[:, 0:CBW]), [Gst], [GD])
            for (A, g, j0) in ((A1, gmix, 16), (A2, gffn, 64)):
                fw.op("dve", lambda e, A=A, j0=j0: e.tensor_scalar(out=A[:], in0=modT[:, j0:j0 + 16, :], scalar1=1.0, scalar2=None, op0=ALU.add), [modT], [A])
                fw.op("dve", lambda e, A=A, g=g: e.tensor_tensor(out=A[:], in0=A[:], in1=g[:].unsqueeze(2).to_broadcast([128, 16, 2]), op=ALU.mult), [A, g], [A])

            xt, xn, junk, t1, t2 = B[0], B[1], B[2], B[3], B[4]
            for t in range(NT):
                grp = 0 if t < NTC else 1
                lt = None if t < NTC else t - NTC
                r0 = t * 128
                fw.dma("sp", lambda e, r0=r0: e.dma_start(out=xt[:], in_=X[r0:r0 + 128, :]), [X], [xt])
                rms_rstd(xt[:], xt, D, 0, junk)
                fw.op("dve", lambda e: e.tensor_scalar(out=xn[:], in0=xt[:], scalar1=rs[:, 0:1], scalar2=None, op0=ALU.mult), [xt, rs], [xn])
                transpose_chunks(lambda k: xn[:, k * 128:(k + 1) * 128], xn, 16, lambda k: hT[:, k, :], hT,
                                 scale=lambda k, grp=grp: A1[:, k, grp:grp + 1], bias=lambda k, grp=grp: modT[:, k, grp:grp + 1])
                gemm(lambda k: hT[:, k, :], hT, 16, w_in, lambda c0, w, l=l: w_in[l, :, c0:c0 + w].rearrange("(k p) n -> p k n", p=128), DIN,
                     lambda p, c0, w: fw.op("act", lambda e, p=p, c0=c0, w=w: e.copy(out=z[:, c0:c0 + w], in_=p[:, 0:w]), [p], [z]))
                rms_rstd(z[:, 0:512], z, 512, 1, junk)
                fw.op("dve", lambda e: e.scalar_tensor_tensor(out=t1[:, 0:512], in0=z[:, 0:512], scalar=rs[:, 1:2], in1=gv[:, 0:512], op0=ALU.mult, op1=ALU.mult),
                      [z, rs, gv], [t1])
                transpose_chunks(lambda k: t1[:, k * 128:(k + 1) * 128], t1, 4, lambda k: tT[:, k, :], tT)
                gemm(lambda k: tT[:, k, :], tT, 4, w_uq, lambda c0, w, l=l: w_uq[l, :, c0:c0 + w].rearrange("(k p) n -> p k n", p=128), 1536,
                     lambda p, c0, w: fw.op("act", lambda e, p=p, c0=c0, w=w: e.copy(out=t2[:, c0:c0 + w], in_=p[:, 0:w]), [p], [t2]))
                heads_norm(t2[:, 0:1536].rearrange("p (h d) -> p h d", h=8), t2, 8, 192, "qh",
                           t1[:, 0:1536].rearrange("p (h d) -> p h d", h=8), t1, junk, extra_scale=192 ** -0.5)
                if lt is not None:
                    rope(t1, 8, lt, junk)
                store_T(t1, 8, 192, QMT, r0)
                rms_rstd(z[:, 512:768], z, 256, 2, junk)
                fw.op("dve", lambda e: e.scalar_tensor_tensor(out=xn[:, 0:256], in0=z[:, 512:768], scalar=rs[:, 2:3], in1=gv[:, 512:768], op0=ALU.mult, op1=ALU.mult),
                      [z, rs, gv], [xn])
                if t < NTC:
                    s_, h_ = t // 2, t % 2
                    fw.dma("pool", lambda e, s_=s_, h_=h_, l=l: e.dma_start(out=o_ckv[s_, l, h_ * 128:(h_ + 1) * 128, :], in_=xn[:, 0:256]), [xn], [o_ckv])
                    fw.dma("pool", lambda e, s_=s_, h_=h_, l=l: e.dma_start(out=o_kpe[s_, l, h_ * 128:(h_ + 1) * 128, :], in_=z[:, 768:832]), [z], [o_kpe])
                    fw.dma("pool", lambda e, s_=s_, h_=h_, l=l: e.dma_start(
                        out=o_nav[s_, l, :, h_ * 128:(h_ + 1) * 128, :].rearrange("h t d -> t h d"),
                        in_=z[:, 1856:2368].rearrange("p (h d) -> p h d", h=4)), [z], [o_nav])
                kv_pipeline(l, xn[:, 0:256], xn, z[:, 768:832], z, r0, lt, t1, t2, junk)
                fw.dma("pool", lambda e, r0=r0: e.dma_start(out=VN[r0:r0 + 128, :], in_=z[:, 1856:2368]), [z], [VN])
                heads_norm(z[:, 832:1344].rearrange("p (h d) -> p h d", h=4), z, 4, 128, "naq",
                           t1[:, 0:512].rearrange("p (h d) -> p h d", h=4), t1, junk, extra_scale=128 ** -0.5)
                store_T(t1, 4, 128, QNT, r0)
                heads_norm(z[:, 1344:1856].rearrange("p (h d) -> p h d", h=4), z, 4, 128, "nak",
                           t2[:, 0:512].rearrange("p (h d) -> p h d", h=4), t2, junk)
                if t < NTC:
                    fw.dma("pool", lambda e, s_=s_, h_=h_, l=l: e.dma_start(
                        out=o_nak[s_, l, :, h_ * 128:(h_ + 1) * 128, :].rearrange("h t d -> t h d"),
                        in_=t2[:, 0:512].rearrange("p (h d) -> p h d", h=4)), [t2], [o_nak])
                store_T(t2, 4, 128, KNT, r0)
                si, cur = seq_of_tile(t)
                fw.op("dve", lambda e: e.tensor_tensor(out=t1[:, 1024:1536], in0=z[:, 2880:3392], in1=z[:, 3392:3904], op=ALU.mult), [z], [t1])
                fw.dma("pool", lambda e, cur=cur: e.dma_start(out=CU[cur:cur + 128, :], in_=t1[:, 1024:1536]), [t1], [CU])
                fw.dma("pool", lambda e, r0=r0: e.dma_start(out=GBs[r0:r0 + 128, :], in_=z[:, 2368:2880]), [z], [GBs])
            if TL:
                for ct in range(4):
                    tok0 = NTOK + ct * 128
                    fw.dma("sp", lambda e, ct=ct, l=l: e.dma_start(out=xn[:, 0:256], in_=c_ckv[l, ct * 128:(ct + 1) * 128, :]), [c_ckv], [xn])
                    fw.dma("sp", lambda e, ct=ct, l=l: e.dma_start(out=xn[:, 256:320], in_=c_kpe[l, ct * 128:(ct + 1) * 128, :]), [c_kpe], [xn])
                    kv_pipeline(l, xn[:, 0:256], xn, xn[:, 256:320], xn, tok0, None, t1, t2, junk)
                    fw.dma("sp", lambda e, ct=ct, l=l: e.dma_start(out=t2[:, 0:512].rearrange("p (h d) -> p h d", h=4),
                                                                   in_=c_nak[l, :, ct * 128:(ct + 1) * 128, :].rearrange("h t d -> t h d")), [c_nak], [t2])
                    store_T(t2, 4, 128, KNT, tok0)
                    fw.dma("pool", lambda e, ct=ct, l=l, tok0=tok0: e.dma_start(out=VN[tok0:tok0 + 128, :].rearrange("t (h d) -> t h d", h=4),
                                                                                 in_=c_nav[l, :, ct * 128:(ct + 1) * 128, :].rearrange("h t d -> t h d")), [c_nav], [VN])
            if cfg.stages <= 1:
                continue

            mix, Ssb, Vs, PTs, xt2, G1t = B[0], B[1], B[3], B[4], B[5], B[6]
            cb_ = B[7]
            junk = B[2]
            if TL:
                fw.dma("sp", lambda e, l=l: e.dma_start(out=relTs[:], in_=relT_d[l, :, :, :]), [relT_d], [relTs])

            def na_bias(h):
                pbk = [PS(), PS()]
                for sl in range(4):
                    slab = WSL()
                    ov = slab[0:31, 0:8, :].rearrange("p a (b c) -> p (a b) c", c=128)
                    fw.dma("sp", lambda e, sl=sl, ov=ov: e.dma_start(out=ov, in_=OH_d[:, sl * 16:(sl + 1) * 16, :]), [OH_d], [slab])
                    for kci in range(16):
                        kc = sl * 16 + kci
                        pb = pbk[kc // 32]
                        fw.op("pe", lambda e, pb=pb, kc=kc, kci=kci, ov=ov: e.matmul(pb[:, (kc % 32) * 15:(kc % 32) * 15 + 15], lhsT=ov[:, kci, :], rhs=relTs[:, h, :],
                                                                                  start=True, stop=True), [slab, relTs], [pb])
                for half in range(2):
                    pb = pbk[half]
                    fw.op("dve", lambda e, pb=pb, half=half: e.tensor_tensor(
                        out=CB2[:, :, half * 32:(half + 1) * 32], in0=pb[:, 0:480].rearrange("p (k a) -> p a k", a=15),
                        in1=cm[:, half * 32:(half + 1) * 32].unsqueeze(1).to_broadcast([128, 15, 32]), op=ALU.add), [pb, cm], [CB2])

            def phaseB_tail(t):
                grp = 0 if t < NTC else 1
                r0 = t * 128
                si, cur = seq_of_tile(t)
                fw.dma("sp", lambda e: e.dma_start(out=cb_[:, 0:512], in_=CU[cur - 1:cur + 127, :]), [CU], [cb_])
                fw.dma("sp", lambda e: e.dma_start(out=cb_[:, 1024:1536], in_=CU[cur:cur + 128, :]), [CU], [cb_])
                fw.dma("sp", lambda e: e.dma_start(out=cb_[:, 1536:2048], in_=CU[cur + 1:cur + 129, :]), [CU], [cb_])
                fw.dma("sp", lambda e: e.dma_start(out=cb_[:, 512:1024], in_=GBs[r0:r0 + 128, :]), [GBs], [cb_])
                fw.op("dve", lambda e: e.tensor_tensor(out=cb_[:, 0:512], in0=cb_[:, 0:512], in1=cw[:, 0, :], op=ALU.mult), [cb_, cw], [cb_])
                fw.op("dve", lambda e: e.tensor_tensor(out=cb_[:, 1024:1536], in0=cb_[:, 1024:1536], in1=cw[:, 1, :], op=ALU.mult), [cb_, cw], [cb_])
                fw.op("dve", lambda e: e.tensor_tensor(out=cb_[:, 1536:2048], in0=cb_[:, 1536:2048], in1=cw[:, 2, :], op=ALU.mult), [cb_, cw], [cb_])
                fw.op("dve", lambda e: e.tensor_tensor(out=cb_[:, 1024:1536], in0=cb_[:, 1024:1536], in1=cb_[:, 0:512], op=ALU.add), [cb_], [cb_])
                fw.op("dve", lambda e: e.tensor_tensor(out=cb_[:, 1024:1536], in0=cb_[:, 1024:1536], in1=cb_[:, 1536:2048], op=ALU.add), [cb_], [cb_])
                fw.op("dve", lambda e: e.tensor_tensor(out=mix[:, 1536:2048], in0=cb_[:, 1024:1536], in1=cb_[:, 512:1024], op=ALU.mult), [cb_], [mix])
                transpose_chunks(lambda k: mix[:, k * 128:(k + 1) * 128], mix, 16, lambda k: hT[:, k, :], hT)
                fw.dma("sp", lambda e: e.dma_start(out=xt2[:], in_=X[r0:r0 + 128, :]), [X], [xt2])
                fw.dma("sp", lambda e: e.dma_start(out=G1t[:], in_=GD[0, grp, :, :]), [GD], [G1t])

                def ofn(p, c0, w):
                    fw.op("dve", lambda e, p=p, c0=c0, w=w: e.tensor_tensor(out=junk[:, c0:c0 + w], in0=p[:, 0:w], in1=G1t[:, c0:c0 + w], op=ALU.mult), [p, G1t], [junk])
                    fw.op("dve", lambda e, c0=c0, w=w: e.tensor_tensor(out=xt2[:, c0:c0 + w], in0=xt2[:, c0:c0 + w], in1=junk[:, c0:c0 + w], op=ALU.add), [xt2, junk], [xt2])
                gemm(lambda k: hT[:, k, :], hT, 16, w_out, lambda c0, w, l=l: w_out[l, :, c0:c0 + w].rearrange("(k p) n -> p k n", p=128), D, ofn)
                fw.dma("sp", lambda e: e.dma_start(out=X[r0:r0 + 128, :], in_=xt2[:]), [xt2], [X])
            if TL:
                for h in range(4):
                    na_bias(h)
                    fw.dma("sp", lambda e, h=h: e.dma_start(out=CBD[h, :, :], in_=CB2[:].rearrange("p a k -> p (a k)")), [CB2], [CBD])
            for t in range(NT):
                r0 = t * 128
                if t < NTC:
                    sq0 = (t // 2) * 256
                    kb_m = [(sq0, 128), (sq0 + 128, 128)]
                    for h in range(8):
                        attention(QMT, KMT, VM, 1024, h, 192, 128, r0, kb_m, mix, h * 128, Ssb, PTs, Vs)
                    for h in range(4):
                        attention(QNT, KNT, VN, 512, h, 128, 128, r0, kb_m, mix, 1024 + h * 128, Ssb, PTs, Vs)
                else:
                    l0 = NTC * 128
                    kb_c = [(NTOK + i * 128, 128) for i in range(4)]
                    kb_m = kb_c + [(l0 + i * 128, 128) for i in range(NTL)]
                    for h in range(8):
                        attention(QMT, KMT, VM, 1024, h, 192, 128, r0, kb_m, mix, h * 128, Ssb, PTs, Vs)
                    rows = TL // 64
                    rg = 2 * (t - NTC)
                    rsf = lambda r: min(max(r - 4, 0), rows - 8)
                    ks = min(rsf(rg), rows - 9)
                    kb_l = [(l0 + ks * 64 + i * 128, 128) for i in range(4)] + [(l0 + ks * 64 + 512, 64)]
                    for h in range(4):
                        fw.op("pool", lambda e: e.memset(Bt[:, 0:576], NEG), [], [Bt])
                        for dr in range(2):
                            r = rg + dr
                            j0 = rsf(r) - ks
                            a0 = rsf(r) - r + 7
                            fw.dma("sp", lambda e, h=h, dr=dr, j0=j0, a0=a0: e.dma_start(out=Bt[dr * 64:(dr + 1) * 64, j0 * 64:(j0 + 8) * 64],
                                                                                          in_=CBD[h, dr * 64:(dr + 1) * 64, a0 * 64:(a0 + 8) * 64]), [CBD], [Bt])
                        attention(QNT, KNT, VN, 512, h, 128, 128, r0, kb_l + kb_c, mix, 1024 + h * 128, Ssb, PTs, Vs, bias=Bt)
                phaseB_tail(t)
            if cfg.stages <= 2:
                continue
            peer_phase(l)
        for t in range(NT):
            fw.dma("pool", lambda e, t=t: e.dma_start(out=xout[t * 128:(t + 1) * 128, :], in_=X[t * 128:(t + 1) * 128, :]), [X], [xout])
        fw.finish()
        fw.emit()
        print("instructions:", fw.ninst)
    return nc


def host_inputs(cfg, core, inp):
    L, NS = cfg.depth, cfg.nseq
    f = np.float32
    b = core // 4
    xs = [np.asarray(inp["x_prompt"][core * NS:(core + 1) * NS], f).reshape(NS * 256, D)]
    if cfg.lat:
        xs.append(np.asarray(inp["x_sample"][b], f)[:cfg.lat])
    m = {}
    m["xin"] = np.ascontiguousarray(np.concatenate(xs, 0))
    cv = np.stack([np.asarray(inp["c_ctx"], f), np.asarray(inp["c"][b], f)], 0)
    m["cvecT"] = np.ascontiguousarray(cv.reshape(2, 16, 128).transpose(2, 1, 0))
    m["ada_w"] = np.asarray(inp["ada_w"][:L], f)
    m["ada_b"] = np.asarray(inp["ada_b"][:L], f)
    m["ada_bT"] = np.ascontiguousarray(np.asarray(inp["ada_b"][:L], f).reshape(L, 96, 128).transpose(0, 2, 1))
    m["gmixT"] = np.ascontiguousarray(np.asarray(inp["norm_mix_g"][:L], f).reshape(L, 16, 128).transpose(0, 2, 1))
    m["gffnT"] = np.ascontiguousarray(np.asarray(inp["norm_ffn_g"][:L], f).reshape(L, 16, 128).transpose(0, 2, 1))
    m["w_in"] = np.asarray(inp["w_in"][:L], f)
    m["w_uq"] = np.asarray(inp["mla_w_uq"][:L], f)
    m["w_ukv"] = np.asarray(inp["mla_w_ukv"][:L], f)
    m["w_out"] = np.asarray(inp["w_out"][:L], f)
    m["gvec"] = np.ascontiguousarray(np.concatenate([np.asarray(inp[k][:L], f) for k in
                                                     ("mla_q_norm_g", "mla_kv_norm_g", "mla_q_head_g", "mla_k_head_g", "na_q_head_g", "na_k_head_g")], 1))
    m["conv_wT"] = np.ascontiguousarray(np.asarray(inp["conv_w"][:L], f).transpose(0, 2, 1))
    m["ident"] = np.eye(128, dtype=f)
    sel = np.zeros((2, 2, 128), f)
    sel[0, 0, :] = 1
    sel[1, 1, :] = 1
    m["sel"] = sel
    m["peer_wq"] = np.asarray(inp["peer_w_q"][:L], f)
    m["iota"] = np.ascontiguousarray(np.broadcast_to(np.arange(256, dtype=f)[None, :], (128, 256)))
    m["subT"] = np.ascontiguousarray(np.asarray(inp["peer_sub_keys"][:L], f).transpose(0, 4, 1, 2, 3).reshape(L, 128, 16, 128))
    for i in range(L):
        m["peer_u%d" % i] = np.asarray(inp["peer_u"][i], f)
        m["peer_v%d" % i] = np.asarray(inp["peer_v"][i], f)
    if cfg.lat:
        TL = cfg.lat
        tt = np.arange(TL)
        row = (tt // 64).astype(f)
        col = (tt % 64).astype(f)
        inv = (np.float32(10000.0) ** (-np.arange(16, dtype=f) / np.float32(16))).astype(f)
        ang = np.concatenate([row[:, None] * inv, col[:, None] * inv], -1).astype(f)
        m["rope"] = np.ascontiguousarray(np.concatenate([np.cos(ang), np.sin(ang)], -1).astype(f))
        m["c_ckv"] = np.ascontiguousarray(np.asarray(inp["cache_mla_ckv"][b, :L], f))
        m["c_kpe"] = np.ascontiguousarray(np.asarray(inp["cache_mla_kpe"][b, :L], f))
        m["c_nak"] = np.ascontiguousarray(np.asarray(inp["cache_na_k"][b, :L], f))
        m["c_nav"] = np.ascontiguousarray(np.asarray(inp["cache_na_v"][b, :L], f))
        m["relT"] = np.ascontiguousarray(np.asarray(inp["na_rel_bias"][:L], f).transpose(0, 3, 1, 2))
        OH = np.zeros((31, 64, 128), f)
        cmk = np.full((128, 64), NEG, f)
        for qc in range(64):
            cs = min(max(qc - 8, 0), 48)
            for kc in range(64):
                bb = kc - qc + 15
                if 0 <= bb < 31:
                    OH[bb, kc, qc] = 1
                    OH[bb, kc, 64 + qc] = 1
                if cs <= kc < cs + 16:
                    cmk[qc, kc] = 0
                    cmk[64 + qc, kc] = 0
        m["OH"] = OH
        m["cmask"] = cmk
    return m


_NC_CACHE = {}


def kernel(**inp):
    cfg = Cfg()
    n = 8
    if "nc" not in _NC_CACHE:
        _NC_CACHE["nc"] = build(cfg)
    nc = _NC_CACHE["nc"]
    in_maps = [host_inputs(cfg, c, inp) for c in range(n)]
    res = run_bass_kernel_spmd(nc, in_maps, core_ids=list(range(n)))
    R = res.results
    NS, L = cfg.nseq, cfg.depth
    y_prompt = np.concatenate([R[c]["xout"][:NS * 256].reshape(NS, 256, D) for c in range(n)], 0)
    y_sample = np.stack([np.concatenate([R[b * 4 + q]["xout"][NS * 256 + q * 256:NS * 256 + (q + 1) * 256] for q in range(4)], 0) for b in range(2)], 0)
    ckv = np.concatenate([R[c]["o_ckv"] for c in range(n)], 0)
    kpe = np.concatenate([R[c]["o_kpe"] for c in range(n)], 0)
    nak = np.concatenate([R[c]["o_nak"] for c in range(n)], 0)
    nav = np.concatenate([R[c]["o_nav"] for c in range(n)], 0)
    return (y_prompt.astype(np.float32), y_sample.astype(np.float32), ckv, kpe, nak, nav)
```

```python
from contextlib import ExitStack
import numpy as np
import concourse.bass as bass
import concourse.mybir as mybir
from concourse.bass_utils import run_bass_kernel_spmd

F32 = mybir.dt.float32
I32 = mybir.dt.int32
AF = mybir.ActivationFunctionType
ALU = mybir.AluOpType
AX = mybir.AxisListType

D = 2048
DIN = 3904
NEG = -30000.0


class Buf:
    __slots__ = ("t", "lw", "rd", "name")

    def __init__(self, t, name=""):
        self.t = t
        self.lw = None
        self.rd = {}
        self.name = name

    def __getitem__(self, k):
        return self.t[k]


class FW:
    ENG = ("pe", "act", "dve", "pool", "sp")
    NDMA = 6

    def __init__(self, nc, es, same_engine_sync=True):
        self.nc = nc
        self.es = es
        self.same = same_engine_sync
        self.sem = {}
        self.cnt = {}
        for e in self.ENG:
            self.sem[e] = es.enter_context(nc.semaphore("s_" + e))
            self.cnt[e] = 0
        self.dq = {}
        for q in ("sp", "pool", "act"):
            slots = []
            for i in range(self.NDMA):
                k = "d_%s%d" % (q, i)
                self.sem[k] = es.enter_context(nc.semaphore(k))
                self.cnt[k] = 0
                slots.append(k)
            self.dq[q] = [slots, 0]
        self.seen = {e: {} for e in self.ENG}
        self.prog = {e: [] for e in self.ENG}
        self.ninst = 0
        self._psi = 0

    def sb(self, name, shape, dt=F32):
        t = self.es.enter_context(self.nc.sbuf_tensor("sb_" + name, list(shape), dt))
        return Buf(t, name)

    def ps(self, name, shape, dt=F32):
        t = self.es.enter_context(self.nc.psum_tensor(name, list(shape), dt))
        return Buf(t, name)

    def dram(self, name, shape, dt=F32, kind="Internal"):
        t = self.nc.dram_tensor(name, list(shape), dt, kind=kind)
        return Buf(t, name)

    def _wait(self, eng, key, val):
        if key == eng and not self.same:
            return
        if self.seen[eng].get(key, 0) >= val:
            return
        self.prog[eng].append(("w", key, val))
        self.seen[eng][key] = val

    def _deps(self, eng, reads, writes):
        for b in reads:
            if b.lw is not None:
                self._wait(eng, *b.lw)
        for b in writes:
            if b.lw is not None:
                self._wait(eng, *b.lw)
            for k, v in b.rd.items():
                self._wait(eng, k, v)

    def _mark(self, key, val, reads, writes):
        for b in reads:
            if b.rd.get(key, 0) < val:
                b.rd[key] = val
        for b in writes:
            b.lw = (key, val)
            b.rd = {}

    def op(self, eng, fn, reads=(), writes=()):
        self._deps(eng, reads, writes)
        self.cnt[eng] += 1
        self.prog[eng].append(("i", fn, eng, 1))
        self._mark(eng, self.cnt[eng], reads, writes)
        self.ninst += 1

    def dma(self, q, fn, reads=(), writes=()):
        slots, i = self.dq[q]
        k = slots[i % len(slots)]
        self.dq[q][1] = i + 1
        if self.cnt[k] > 0:
            self._wait(q, k, self.cnt[k])
        self._deps(q, reads, writes)
        self.cnt[k] += 16
        self.prog[q].append(("i", fn, k, 16))
        self._mark(k, self.cnt[k], reads, writes)
        self.ninst += 1

    def finish(self):
        for e in self.ENG:
            if e != "sp" and self.cnt[e] > 0:
                self._wait("sp", e, self.cnt[e])
        for q in self.dq:
            for k in self.dq[q][0]:
                if self.cnt[k] > 0:
                    self._wait("sp", k, self.cnt[k])

    def emit(self):
        nc = self.nc
        prog = self.prog
        sem = self.sem

        def run(e, items):
            for it in items:
                if it[0] == "w":
                    e.wait_ge(sem[it[1]], it[2])
                else:
                    it[1](e).then_inc(sem[it[2]], it[3])

        with nc.Block() as block:
            @block.tensor
            def _(e):
                run(e, prog["pe"])

            @block.scalar
            def _(e):
                run(e, prog["act"])

            @block.vector
            def _(e):
                run(e, prog["dve"])

            @block.gpsimd
            def _(e):
                run(e, prog["pool"])

            @block.sync
            def _(e):
                run(e, prog["sp"])


class Cfg:
    def __init__(self, depth=4, nseq=4, lat=1024, stages=99):
        self.depth = depth
        self.nseq = nseq
        self.lat = lat
        self.stages = stages


GOFF = {"qn": 0, "kvn": 512, "qh": 768, "kh": 960, "naq": 1152, "nak": 1280}
GLEN = 1408


CBW = 256
NEG2 = -1.0e30
U32 = mybir.dt.uint32
BF16 = mybir.dt.bfloat16


def build(cfg):
    L = cfg.depth
    NS = cfg.nseq
    NTC = NS * 2
    TL = cfg.lat
    NTL = TL // 128
    NT = NTC + NTL
    NTOK = NT * 128
    NCACHE = 512 if TL else 0
    nc = bass.Bass("TRN2", target_bir_lowering=False)

    def din(name, shape, dt=F32):
        return Buf(nc.dram_tensor(name, list(shape), dt, kind="ExternalInput"), name)

    def dout(name, shape, dt=F32):
        return Buf(nc.dram_tensor(name, list(shape), dt, kind="ExternalOutput"), name)

    xin = din("xin", [NTOK, D])
    cvecT = din("cvecT", [128, 16, 2])
    ada_w = din("ada_w", [L, D, 6 * D])
    ada_bT = din("ada_bT", [L, 128, 96])
    ada_b = din("ada_b", [L, 6 * D])
    gmixT = din("gmixT", [L, 128, 16])
    gffnT = din("gffnT", [L, 128, 16])
    w_in = din("w_in", [L, D, DIN])
    w_uq = din("w_uq", [L, 512, 1536])
    w_ukv = din("w_ukv", [L, 256, 2048])
    w_out = din("w_out", [L, D, D])
    gvec = din("gvec", [L, GLEN])
    conv_wT = din("conv_wT", [L, 3, 512])
    ident_d = din("ident", [128, 128])
    sel_d = din("sel", [2, 2, 128])
    peer_wq = din("peer_wq", [L, D, D])
    subT_d = din("subT", [L, 128, 16, 128])
    peer_u = [din("peer_u%d" % i, [16384, D]) for i in range(L)]
    peer_v = [din("peer_v%d" % i, [16384, D]) for i in range(L)]
    iota_d = din("iota", [128, 256])
    if TL:
        rope_d = din("rope", [TL, 64])
        c_ckv = din("c_ckv", [L, 512, 256])
        c_kpe = din("c_kpe", [L, 512, 64])
        c_nak = din("c_nak", [L, 4, 512, 128])
        c_nav = din("c_nav", [L, 4, 512, 128])
        relT_d = din("relT", [L, 31, 4, 15])
        OH_d = din("OH", [31, 64, 128])
        cm_d = din("cmask", [128, 64])

    xout = dout("xout", [NTOK, D])
    o_ckv = dout("o_ckv", [NS, L, 256, 256])
    o_kpe = dout("o_kpe", [NS, L, 256, 64])
    o_nak = dout("o_nak", [NS, L, 4, 256, 128])
    o_nav = dout("o_nav", [NS, L, 4, 256, 128])

    with ExitStack() as es:
        fw = FW(nc, es)
        NKT = NTOK + NCACHE
        X = fw.dram("Xs", [NTOK, D])
        QMT = fw.dram("QMT", [8, 192, NTOK])
        KMT = fw.dram("KMT", [8, 192, NKT])
        VM = fw.dram("VM", [NKT, 1024])
        QNT = fw.dram("QNT", [4, 128, NTOK])
        KNT = fw.dram("KNT", [4, 128, NKT])
        VN = fw.dram("VN", [NKT, 512])
        NSEQ = NS + (1 if TL else 0)
        CU = fw.dram("CU", [NTOK + 2 * NSEQ, 512])
        GBs = fw.dram("GBs", [NTOK, 512])
        GD = fw.dram("GD", [2, 2, 128, D])
        CBD = fw.dram("CBD", [4, 128, 15 * 64])

        def seq_of_tile(t):
            if t < NTC:
                si = t // 2
            else:
                si = NS
            return si, t * 128 + 2 * si + 1

        pbig = fw.ps("pbig", [128, 4096])
        psb = [Buf(pbig.t[:, i * 512:(i + 1) * 512], "ps%d" % i) for i in range(8)]
        grp_i = [0]

        def PS():
            b = psb[fw._psi % 8]
            fw._psi += 1
            return b

        def PSG():
            g = grp_i[0] % 2
            grp_i[0] += 1
            return psb[g * 4:(g + 1) * 4], pbig.t[:, g * 2048:(g + 1) * 2048]

        ident = fw.sb("ident", [128, 128])
        sel = fw.sb("sel", [2, 2, 128])
        fw.dma("sp", lambda e: e.dma_start(out=ident[:], in_=ident_d[:, :]), [ident_d], [ident])
        fw.dma("sp", lambda e: e.dma_start(out=sel[:], in_=sel_d[:, :, :]), [sel_d], [sel])
        sT = fw.sb("sT", [128, 16, 2])
        fw.dma("sp", lambda e: e.dma_start(out=sT[:], in_=cvecT[:, :, :]), [cvecT], [sT])
        fw.op("act", lambda e: e.activation(out=sT[:], in_=sT[:], func=AF.Silu), [sT], [sT])

        for t in range(NT):
            fw.dma("pool", lambda e, t=t: e.dma_start(out=X[t * 128:(t + 1) * 128, :], in_=xin[t * 128:(t + 1) * 128, :]), [xin], [X])

        B = [fw.sb("B%d" % i, [128, D]) for i in range(10)]
        Vgb = [fw.sb("Vgb%d" % i, [128, D], BF16) for i in range(2)]
        Hhb = fw.sb("Hhb", [128, D], BF16)
        identb = fw.sb("identb", [128, 128], BF16)
        fw.op("dve", lambda e: e.tensor_copy(out=identb[:], in_=ident[:]), [ident], [identb])
        wsl = [fw.sb("wsl%d" % i, [128, 16, CBW]) for i in range(2)]
        wi = [0]

        def WSL():
            b = wsl[wi[0] % 2]
            wi[0] += 1
            return b

        modT = fw.sb("modT", [128, 96, 2])
        abT = fw.sb("abT", [128, 96])
        rowb = fw.sb("rowb", [2, CBW])
        gmix = fw.sb("gmix", [128, 16])
        gffn = fw.sb("gffn", [128, 16])
        A1 = fw.sb("A1", [128, 16, 2])
        A2 = fw.sb("A2", [128, 16, 2])
        gv = fw.sb("gv", [128, GLEN])
        cw = fw.sb("cw", [128, 3, 512])
        hT = fw.sb("hT", [128, 16, 128])
        z = fw.sb("z", [128, DIN])
        ss = fw.sb("ss", [128, 16])
        rs = fw.sb("rs", [128, 16])
        tT = fw.sb("tT", [128, 4, 128])
        stg = B[5].t[:, :].rearrange("p (a h t) -> p a h t", a=2, h=8)
        stgb = B[5]
        kTs = fw.sb("kTs", [128, 2, 1536])
        qTs = fw.sb("qTs", [128, 2, 128])
        mx = fw.sb("mx", [128, 4])
        subk = fw.sb("subk", [128, 16, 128])
        TV = fw.sb("TV", [128, 16, 16])
        TIu = fw.sb("TIu", [128, 16, 16], U32)
        TIf = fw.sb("TIf", [128, 16, 16])
        TS = fw.sb("TS", [128, 8, 16])
        PIu = fw.sb("PIu", [128, 8, 16], U32)
        PIf = fw.sb("PIf", [128, 8, 16])
        iota = fw.sb("iota", [128, 256])
        fw.dma("sp", lambda e: e.dma_start(out=iota[:], in_=iota_d[:, :]), [iota_d], [iota])
        EI = fw.sb("EI", [128, 128])
        GT = fw.sb("GT", [128, 128])
        IDX = fw.sb("IDX", [128, 128], I32)
        GTT = fw.sb("GTT", [128, 128])
        ACTV = fw.sb("ACTV", [128, 128])
        WT = fw.sb("WT", [128, 128])
        ZW = [fw.sb("ZW%d" % i, [128, 256], BF16) for i in range(4)]
        for i in range(4):
            fw.op("pool", lambda e, i=i: e.memset(ZW[i][:], 0.0), [], [ZW[i]])
        zrow = fw.sb("zrow", [1, 512])
        fw.op("pool", lambda e: e.memset(zrow[:], 0.0), [], [zrow])
        for si in range(NSEQ):
            st = si * 256 if si < NS else NS * 256
            ln = 256 if si < NS else TL
            for rr in (st + 2 * si, st + 2 * si + 1 + ln):
                fw.dma("pool", lambda e, rr=rr: e.dma_start(out=CU[rr:rr + 1, :], in_=zrow[:, :]), [zrow], [CU])
        if TL:
            ropet = fw.sb("ropet", [128, NTL, 64])
            fw.dma("sp", lambda e: e.dma_start(out=ropet[:], in_=rope_d.t.ap().rearrange("(n p) c -> p n c", p=128)), [rope_d], [ropet])
            relTs = fw.sb("relTs", [31, 4, 15])
            cm = fw.sb("cm", [128, 64])
            fw.dma("sp", lambda e: e.dma_start(out=cm[:], in_=cm_d[:, :]), [cm_d], [cm])
            CB2 = fw.sb("CB2", [128, 15, 64])
            Bt = B[8]

        def rms_rstd(src_ap, srcbuf, n, col, junk):
            fw.op("act", lambda e: e.activation(out=junk[:, 0:n], in_=src_ap, func=AF.Square, scale=float(n ** -0.5),
                                                accum_out=ss[:, col:col + 1]), [srcbuf], [junk, ss])
            fw.op("act", lambda e: e.activation(out=rs[:, col:col + 1], in_=ss[:, col:col + 1], func=AF.Sqrt, bias=1e-6, scale=1.0), [ss], [rs])
            fw.op("dve", lambda e: e.reciprocal(out=rs[:, col:col + 1], in_=rs[:, col:col + 1]), [rs], [rs])

        def heads_norm(src_ap, srcbuf, nh, dh, gname, dst_ap, dstbuf, junk, extra_scale=1.0):
            jv = junk[:, 0:nh * dh].rearrange("p (h d) -> p h d", h=nh)
            fw.op("dve", lambda e: e.tensor_tensor(out=jv, in0=src_ap, in1=src_ap, op=ALU.mult), [srcbuf], [junk])
            fw.op("dve", lambda e: e.tensor_reduce(out=ss[:, 0:nh], in_=jv, axis=AX.X, op=ALU.add), [junk], [ss])
            fw.op("act", lambda e: e.activation(out=rs[:, 0:nh], in_=ss[:, 0:nh], func=AF.Sqrt, bias=1e-6, scale=float(1.0 / dh)), [ss], [rs])
            fw.op("dve", lambda e: e.reciprocal(out=rs[:, 0:nh], in_=rs[:, 0:nh]), [rs], [rs])
            fw.op("dve", lambda e: e.tensor_tensor(out=dst_ap, in0=src_ap, in1=rs[:, 0:nh].unsqueeze(2).to_broadcast([128, nh, dh]), op=ALU.mult),
                  [srcbuf, rs], [dstbuf])
            g0 = GOFF[gname]
            fw.op("dve", lambda e: e.scalar_tensor_tensor(out=dst_ap, in0=dst_ap, scalar=float(extra_scale),
                                                          in1=gv[:, g0:g0 + dh].unsqueeze(1).to_broadcast([128, nh, dh]), op0=ALU.mult, op1=ALU.mult),
                  [dstbuf, gv], [dstbuf])

        def rope(buf, nh, lt, junk):
            v = buf[:, 0:nh * 192].rearrange("p (h d) -> p h d", h=nh)
            x1, x2 = v[:, :, 128:160], v[:, :, 160:192]
            cs = ropet[:, lt, 0:32].unsqueeze(1).to_broadcast([128, nh, 32])
            sn = ropet[:, lt, 32:64].unsqueeze(1).to_broadcast([128, nh, 32])
            j = junk[:, 0:nh * 128].rearrange("p (h d) -> p h d", h=nh)
            a, b_, c, d_ = j[:, :, 0:32], j[:, :, 32:64], j[:, :, 64:96], j[:, :, 96:128]
            fw.op("dve", lambda e: e.tensor_tensor(out=a, in0=x1, in1=cs, op=ALU.mult), [buf, ropet], [junk])
            fw.op("dve", lambda e: e.tensor_tensor(out=b_, in0=x2, in1=sn, op=ALU.mult), [buf, ropet], [junk])
            fw.op("dve", lambda e: e.tensor_tensor(out=c, in0=x1, in1=sn, op=ALU.mult), [buf, ropet], [junk])
            fw.op("dve", lambda e: e.tensor_tensor(out=d_, in0=x2, in1=cs, op=ALU.mult), [buf, ropet], [junk])
            fw.op("dve", lambda e: e.tensor_tensor(out=x1, in0=a, in1=b_, op=ALU.subtract), [junk], [buf])
            fw.op("dve", lambda e: e.tensor_tensor(out=x2, in0=c, in1=d_, op=ALU.add), [junk], [buf])

        def transpose_chunks(src_ap_fn, srcbuf, nch, dst_fn, dstbuf, width=128, scale=None, bias=None, rows=128):
            for k in range(nch):
                p = PS()
                fw.op("pe", lambda e, k=k, p=p: e.transpose(out=p[0:width, 0:rows], in_=src_ap_fn(k), identity=ident[0:rows, 0:rows]), [srcbuf, ident], [p])
                if scale is None:
                    fw.op("act", lambda e, k=k, p=p: e.copy(out=dst_fn(k), in_=p[0:width, 0:rows]), [p], [dstbuf])
                else:
                    sc, bi = scale(k), bias(k)
                    fw.op("act", lambda e, k=k, p=p, sc=sc, bi=bi: e.activation(out=dst_fn(k), in_=p[0:width, 0:rows], func=AF.Identity,
                                                                                 bias=bi, scale=sc), [p, A1, A2, modT], [dstbuf])

        def gemm(lhsT_fn, lbuf, nk, Wd, w_ap_fn, ncols, out_fn, kparts=128):
            c0 = 0
            while c0 < ncols:
                w = min(CBW, ncols - c0)
                slab = WSL()
                fw.dma("sp", lambda e, c0=c0, w=w, slab=slab: e.dma_start(out=slab[0:kparts, 0:nk, 0:w], in_=w_ap_fn(c0, w)), [Wd], [slab])
                p = PS()
                for k in range(nk):
                    fw.op("pe", lambda e, k=k, p=p, slab=slab, w=w: e.matmul(p[:, 0:w], lhsT=lhsT_fn(k), rhs=slab[0:kparts, k, 0:w],
                                                                            start=(k == 0), stop=(k == nk - 1)), [lbuf, slab], [p])
                out_fn(p, c0, w)
                c0 += w

        def store_T(src, nh, dq, dstT, tok0):
            sv = src[:, 0:nh * dq].rearrange("p (h d) -> p h d", h=nh)
            for h in range(nh):
                transpose_chunks(lambda k, h=h: sv[:, h, 0:128], src, 1, lambda k, h=h: stg[:, 0, h, :], stgb)
                if dq > 128:
                    transpose_chunks(lambda k, h=h: sv[:, h, 128:dq], src, 1, lambda k, h=h: stg[0:dq - 128, 1, h, :], stgb, width=dq - 128)
            fw.dma("pool", lambda e: e.dma_start(out=dstT[:, 0:128, tok0:tok0 + 128].rearrange("h d t -> d h t"), in_=stg[:, 0, 0:nh, :]), [stgb], [dstT])
            if dq > 128:
                fw.dma("pool", lambda e: e.dma_start(out=dstT[:, 128:dq, tok0:tok0 + 128].rearrange("h d t -> d h t"), in_=stg[0:dq - 128, 1, 0:nh, :]), [stgb], [dstT])

        def kv_pipeline(l, ckvn_ap, ckvn_buf, kpe_ap, kpe_buf, tok0, lt, t1, t2, junk):
            transpose_chunks(lambda k: ckvn_ap[:, k * 128:(k + 1) * 128], ckvn_buf, 2, lambda k: tT[:, k, :], tT)
            gemm(lambda k: tT[:, k, :], tT, 2, w_ukv, lambda c0, w, l=l: w_ukv[l, :, c0:c0 + w].rearrange("(k p) n -> p k n", p=128), 2048,
                 lambda p, c0, w: fw.op("act", lambda e, p=p, c0=c0, w=w: e.copy(out=t2[:, c0:c0 + w], in_=p[:, 0:w]), [p], [t2]))
            kvv = t2[:, 0:2048].rearrange("p (h d) -> p h d", h=8)
            kf = t1[:, 0:1536].rearrange("p (h d) -> p h d", h=8)
            fw.op("dve", lambda e: e.tensor_copy(out=kf[:, :, 0:128], in_=kvv[:, :, 0:128]), [t2], [t1])
            fw.op("dve", lambda e: e.tensor_copy(out=kf[:, :, 128:192], in_=kpe_ap.unsqueeze(1).to_broadcast([128, 8, 64])), [kpe_buf], [t1])
            fw.dma("pool", lambda e: e.dma_start(out=VM[tok0:tok0 + 128, :].rearrange("t (h d) -> t h d", h=8), in_=kvv[:, :, 128:256]), [t2], [VM])
            heads_norm(kf, t1, 8, 192, "kh", kf, t1, junk)
            if lt is not None:
                rope(t1, 8, lt, junk)
            store_T(t1, 8, 192, KMT, tok0)

        def attention(QT, KT, V, vw, h, dq, dv, r0, keyblocks, mix, mixcol, Ssb, PTs, Vs, bias=None):
            nkb = len(keyblocks)
            nk = sum(n for _, n in keyblocks)
            nch = 2 if dq > 128 else 1
            fw.dma("sp", lambda e: e.dma_start(out=qTs[:, 0, :], in_=QT[h, 0:128, r0:r0 + 128]), [QT], [qTs])
            if nch == 2:
                fw.dma("sp", lambda e: e.dma_start(out=qTs[0:dq - 128, 1, :], in_=QT[h, 128:dq, r0:r0 + 128]), [QT], [qTs])
            off = 0
            i = 0
            while i < nkb:
                t0, n = keyblocks[i]
                j = i + 1
                tot = n
                while j < nkb and keyblocks[j][0] == t0 + tot:
                    tot += keyblocks[j][1]
                    j += 1
                fw.dma("sp", lambda e, t0=t0, tot=tot, off=off: e.dma_start(out=kTs[:, 0, off:off + tot], in_=KT[h, 0:128, t0:t0 + tot]), [KT], [kTs])
                if nch == 2:
                    fw.dma("sp", lambda e, t0=t0, tot=tot, off=off: e.dma_start(out=kTs[0:dq - 128, 1, off:off + tot], in_=KT[h, 128:dq, t0:t0 + tot]), [KT], [kTs])
                off += tot
                i = j
            Vv = Vs[:, 0:12 * 128].rearrange("p (k d) -> p k d", k=12)
            for kb, (t0, n) in enumerate(keyblocks):
                fw.dma("pool", lambda e, kb=kb, t0=t0, n=n: e.dma_start(out=Vv[0:n, kb, 0:dv], in_=V[t0:t0 + n, h * dv:(h + 1) * dv]), [V], [Vs])
            banks, _ = PSG()
            c0 = 0
            while c0 < nk:
                w = min(512, nk - c0)
                b = banks[c0 // 512]
                fw.op("pe", lambda e, b=b, c0=c0, w=w: e.matmul(b[:, 0:w], lhsT=qTs[:, 0, :], rhs=kTs[:, 0, c0:c0 + w], start=True, stop=(nch == 1)), [qTs, kTs], [b])
                if nch == 2:
                    fw.op("pe", lambda e, b=b, c0=c0, w=w: e.matmul(b[:, 0:w], lhsT=qTs[0:dq - 128, 1, :], rhs=kTs[0:dq - 128, 1, c0:c0 + w], start=False, stop=True),
                          [qTs, kTs], [b])
                if bias is not None:
                    fw.op("dve", lambda e, b=b, c0=c0, w=w: e.tensor_tensor(out=Ssb[:, c0:c0 + w], in0=b[:, 0:w], in1=bias[:, c0:c0 + w], op=ALU.add), [b, bias], [Ssb])
                else:
                    fw.op("act", lambda e, b=b, c0=c0, w=w: e.copy(out=Ssb[:, c0:c0 + w], in_=b[:, 0:w]), [b], [Ssb])
                c0 += w
            fw.op("dve", lambda e: e.reduce_max(out=mx[:, 0:1], in_=Ssb[:, 0:nk], axis=AX.X), [Ssb], [mx])
            fw.op("dve", lambda e: e.tensor_scalar(out=mx[:, 1:2], in0=mx[:, 0:1], scalar1=-1.0, scalar2=None, op0=ALU.mult), [mx], [mx])
            fw.op("act", lambda e: e.activation(out=Ssb[:, 0:nk], in_=Ssb[:, 0:nk], func=AF.Exp, bias=mx[:, 1:2], scale=1.0, accum_out=mx[:, 2:3]), [Ssb, mx], [Ssb, mx])
            fw.op("dve", lambda e: e.reciprocal(out=mx[:, 3:4], in_=mx[:, 2:3]), [mx], [mx])
            PTv = PTs[:, 0:12 * 128].rearrange("p (k d) -> p k d", k=12)
            off = 0
            for kb, (t0, n) in enumerate(keyblocks):
                p = PS()
                fw.op("pe", lambda e, p=p, off=off, n=n: e.transpose(out=p[0:n, 0:128], in_=Ssb[:, off:off + n], identity=ident[:]), [Ssb, ident], [p])
                fw.op("act", lambda e, p=p, kb=kb, n=n: e.copy(out=PTv[0:n, kb, :], in_=p[0:n, 0:128]), [p], [PTs])
                off += n
            po = PS()
            for kb, (t0, n) in enumerate(keyblocks):
                fw.op("pe", lambda e, kb=kb, n=n, po=po: e.matmul(po[:, 0:dv], lhsT=PTv[0:n, kb, :], rhs=Vv[0:n, kb, 0:dv], start=(kb == 0), stop=(kb == nkb - 1)),
                      [PTs, Vs], [po])
            fw.op("dve", lambda e, po=po: e.tensor_scalar(out=mix[:, mixcol:mixcol + dv], in0=po[:, 0:dv], scalar1=mx[:, 3:4], scalar2=None, op0=ALU.mult), [po, mx], [mix])


        def peer_phase(l):
            xt, xn, junk, Hh, t2, G2t = B[0], B[1], B[2], B[3], B[4], B[9]
            Ug = [B[5], B[6], B[7]]
            Vg = [B[5], B[6], B[7]]
            SC = z
            SCv = SC[:, 0:2048].rearrange("p (a n) -> p a n", a=16)
            cand = z[:, 2048:3904]
            for t in range(NT):
                grp = 0 if t < NTC else 1
                r0 = t * 128
                fw.dma("sp", lambda e, r0=r0: e.dma_start(out=xt[:], in_=X[r0:r0 + 128, :]), [X], [xt])
                fw.dma("sp", lambda e, grp=grp: e.dma_start(out=G2t[:], in_=GD[1, grp, :, :]), [GD], [G2t])
                rms_rstd(xt[:], xt, D, 0, junk)
                fw.op("dve", lambda e: e.tensor_scalar(out=xn[:], in0=xt[:], scalar1=rs[:, 0:1], scalar2=None, op0=ALU.mult), [xt, rs], [xn])
                transpose_chunks(lambda k: xn[:, k * 128:(k + 1) * 128], xn, 16, lambda k: hT[:, k, :], hT,
                                 scale=lambda k, grp=grp: A2[:, k, grp:grp + 1], bias=lambda k, grp=grp: modT[:, 48 + k, grp:grp + 1])
                gemm(lambda k: hT[:, k, :], hT, 16, peer_wq, lambda c0, w, l=l: peer_wq[l, :, c0:c0 + w].rearrange("(k p) n -> p k n", p=128), D,
                     lambda p, c0, w: fw.op("act", lambda e, p=p, c0=c0, w=w: e.copy(out=t2[:, c0:c0 + w], in_=p[:, 0:w]), [p], [t2]))
                transpose_chunks(lambda k: hT[:, k, :], hT, 16, lambda k: Hhb[:, k * 128:(k + 1) * 128], Hhb)
                for hp in range(16):
                    transpose_chunks(lambda k, hp=hp: t2[:, hp * 128:(hp + 1) * 128], t2, 1, lambda k, hp=hp: tT[:, hp % 4, :], tT)
                    p2 = PS()
                    fw.op("pe", lambda e, hp=hp, p2=p2: e.matmul(p2[:, 0:128], lhsT=tT[:, hp % 4, :], rhs=subk[:, hp, :], start=True, stop=True), [tT, subk], [p2])
                    fw.op("act", lambda e, hp=hp, p2=p2: e.copy(out=SCv[:, hp, :], in_=p2[:, 0:128]), [p2], [z])
                for hp in range(16):
                    fw.op("dve", lambda e, hp=hp: e.max(out=TV[:, hp, 0:8], in_=SCv[:, hp, :]), [z], [TV])
                    fw.op("dve", lambda e, hp=hp: e.max_index(out=TIu[:, hp, 0:8], in_max=TV[:, hp, 0:8], in_values=SCv[:, hp, :]), [z, TV], [TIu])
                    fw.op("dve", lambda e, hp=hp: e.match_replace(out=junk[:, 0:128], in_to_replace=TV[:, hp, 0:8], in_values=SCv[:, hp, :], imm_value=NEG2), [z, TV], [junk])
                    fw.op("dve", lambda e, hp=hp: e.max(out=TV[:, hp, 8:16], in_=junk[:, 0:128]), [junk], [TV])
                    fw.op("dve", lambda e, hp=hp: e.max_index(out=TIu[:, hp, 8:16], in_max=TV[:, hp, 8:16], in_values=junk[:, 0:128]), [junk, TV], [TIu])
                fw.op("dve", lambda e: e.tensor_copy(out=TIf[:], in_=TIu[:]), [TIu], [TIf])
                TVv = TV[:].rearrange("p (h two) k -> p h two k", two=2)
                TIv = TIf[:].rearrange("p (h two) k -> p h two k", two=2)
                cand = xn[:, 0:2048].rearrange("p (h a b) -> p h a b", h=8, a=16)
                cidx = junk[:, 0:2048].rearrange("p (h a b) -> p h a b", h=8, a=16)
                for h in range(8):
                    fw.op("dve", lambda e, h=h: e.tensor_tensor(out=cand[:, h, :, :], in0=TVv[:, h, 0, :].unsqueeze(2).to_broadcast([128, 16, 16]),
                                                                in1=TVv[:, h, 1, :].unsqueeze(1).to_broadcast([128, 16, 16]), op=ALU.add), [TV], [xn])
                    fw.op("dve", lambda e, h=h: e.scalar_tensor_tensor(out=cidx[:, h, :, :], in0=TIv[:, h, 0, :].unsqueeze(2).to_broadcast([128, 16, 16]), scalar=128.0,
                                                                       in1=TIv[:, h, 1, :].unsqueeze(1).to_broadcast([128, 16, 16]), op0=ALU.mult, op1=ALU.add), [TIf], [junk])
                candf = xn[:, 0:2048].rearrange("p (h c) -> p h c", h=8)
                cidxf = junk[:, 0:2048].rearrange("p (h c) -> p h c", h=8)
                scr = z[:, 2048:2304]
                scr2 = z[:, 2304:2560]
                for h in range(8):
                    fw.op("dve", lambda e, h=h: e.max(out=TS[:, h, 0:8], in_=candf[:, h, :]), [xn], [TS])
                    fw.op("dve", lambda e, h=h: e.max_index(out=PIu[:, h, 0:8], in_max=TS[:, h, 0:8], in_values=candf[:, h, :]), [xn, TS], [PIu])
                    fw.op("dve", lambda e, h=h: e.match_replace(out=scr, in_to_replace=TS[:, h, 0:8], in_values=candf[:, h, :], imm_value=NEG2), [xn, TS], [z])
                    fw.op("dve", lambda e, h=h: e.max(out=TS[:, h, 8:16], in_=scr), [z], [TS])
                    fw.op("dve", lambda e, h=h: e.max_index(out=PIu[:, h, 8:16], in_max=TS[:, h, 8:16], in_values=scr), [z, TS], [PIu])
                fw.op("dve", lambda e: e.tensor_copy(out=PIf[:], in_=PIu[:]), [PIu], [PIf])
                for h in range(8):
                    for k in range(16):
                        fw.op("dve", lambda e, h=h, k=k: e.scalar_tensor_tensor(out=scr2, in0=iota[:, :], scalar=PIf[:, h, k:k + 1], in1=cidxf[:, h, :],
                                                                                 op0=ALU.is_equal, op1=ALU.mult, accum_out=EI[:, h * 16 + k:h * 16 + k + 1]), [iota, PIf, junk], [z, EI])
                GTv = GT[:].rearrange("p (h k) -> p h k", h=8)
                fw.op("dve", lambda e: e.tensor_tensor(out=GTv, in0=TS[:], in1=TS[:, :, 0:1].to_broadcast([128, 8, 16]), op=ALU.subtract), [TS], [GT])
                fw.op("act", lambda e: e.activation(out=GT[:], in_=GT[:], func=AF.Exp), [GT], [GT])
                fw.op("dve", lambda e: e.tensor_reduce(out=ss[:, 0:8], in_=GTv, axis=AX.X, op=ALU.add), [GT], [ss])
                fw.op("dve", lambda e: e.reciprocal(out=rs[:, 0:8], in_=ss[:, 0:8]), [ss], [rs])
                fw.op("dve", lambda e: e.tensor_tensor(out=GTv, in0=GTv, in1=rs[:, 0:8].unsqueeze(2).to_broadcast([128, 8, 16]), op=ALU.mult), [GT, rs], [GT])
                pe_ = PS()
                fw.op("pe", lambda e, pe_=pe_: e.transpose(out=pe_[:, 0:128], in_=EI[:], identity=ident[:]), [EI, ident], [pe_])
                fw.op("dve", lambda e, pe_=pe_: e.tensor_copy(out=IDX[:], in_=pe_[:, 0:128]), [pe_], [IDX])
                pg = PS()
                fw.op("pe", lambda e, pg=pg: e.transpose(out=pg[:, 0:128], in_=GT[:], identity=ident[:]), [GT, ident], [pg])
                fw.op("act", lambda e, pg=pg: e.copy(out=GTT[:], in_=pg[:, 0:128]), [pg], [GTT])
                for tok in range(128):
                    ug = Ug[tok % 3]
                    fw.dma("pool", lambda e, ug=ug, tok=tok, l=l: e.indirect_dma_start(
                        out=ug[:, :], out_offset=None, in_=peer_u[l][:, :],
                        in_offset=bass.IndirectOffsetOnAxis(ap=IDX[:, tok:tok + 1], axis=0)), [peer_u[l], IDX], [ug])
                    banks, gap = PSG()
                    for c in range(4):
                        fw.op("pe", lambda e, c=c, tok=tok, banks=banks: e.matmul(banks[c][:, :], lhsT=identb[:, tok:tok + 1].to_broadcast([128, 128]),
                                                                                  rhs=Hhb[:, c * 512:(c + 1) * 512], start=True, stop=True), [identb, Hhb], [banks[c]])
                    fw.op("dve", lambda e, ug=ug, gap=gap, tok=tok: e.scalar_tensor_tensor(out=xn[:], in0=ug[:], scalar=1.0, in1=gap, op0=ALU.mult, op1=ALU.mult,
                                                                                           accum_out=ACTV[:, tok:tok + 1]), [ug] + banks, [xn, ACTV])
                fw.op("act", lambda e: e.activation(out=WT[:], in_=ACTV[:], func=AF.Gelu), [ACTV], [WT])
                fw.op("dve", lambda e: e.tensor_tensor(out=WT[:], in0=WT[:], in1=GTT[:], op=ALU.mult), [WT, GTT], [WT])
                banks, gap = PSG()
                for tok in range(128):
                    vg = Vg[tok % 3]
                    vgb = Vgb[tok % 2]
                    zw = ZW[tok % 4]
                    fw.dma("pool", lambda e, vg=vg, tok=tok, l=l: e.indirect_dma_start(
                        out=vg[:, :], out_offset=None, in_=peer_v[l][:, :],
                        in_offset=bass.IndirectOffsetOnAxis(ap=IDX[:, tok:tok + 1], axis=0)), [peer_v[l], IDX], [vg])
                    fw.op("act", lambda e, zw=zw, tok=tok: e.copy(out=zw[:, 127:128], in_=WT[:, tok:tok + 1]), [WT], [zw])
                    fw.op("act", lambda e, vg=vg, vgb=vgb: e.copy(out=vgb[:], in_=vg[:]), [vg], [vgb])
                    for c in range(4):
                        fw.op("pe", lambda e, c=c, tok=tok, banks=banks, zw=zw, vgb=vgb: e.matmul(banks[c][:, :], lhsT=zw[:, 127 - tok:255 - tok], rhs=vgb[:, c * 512:(c + 1) * 512],
                                                                                               start=(tok == 0), stop=(tok == 127)), [zw, vgb], [banks[c]])
                fw.op("dve", lambda e, gap=gap: e.tensor_tensor(out=xn[:], in0=gap, in1=G2t[:], op=ALU.mult), banks + [G2t], [xn])
                fw.op("dve", lambda e: e.tensor_tensor(out=xt[:], in0=xt[:], in1=xn[:], op=ALU.add), [xt, xn], [xt])
                fw.dma("sp", lambda e, r0=r0: e.dma_start(out=X[r0:r0 + 128, :], in_=xt[:]), [xt], [X])

        for l in range(L):
            abrow0, abrow1 = B[8], B[9]
            fw.dma("sp", lambda e, l=l: e.dma_start(out=abT[:], in_=ada_bT[l, :, :]), [ada_bT], [abT])
            fw.dma("sp", lambda e, l=l: e.dma_start(out=abrow0[0:2, :], in_=ada_b[l:l + 1, 2 * D:3 * D].partition_broadcast(2)), [ada_b], [abrow0])
            fw.dma("sp", lambda e, l=l: e.dma_start(out=abrow1[0:2, :], in_=ada_b[l:l + 1, 5 * D:6 * D].partition_broadcast(2)), [ada_b], [abrow1])
            fw.dma("sp", lambda e, l=l: e.dma_start(out=gmix[:], in_=gmixT[l, :, :]), [gmixT], [gmix])
            fw.dma("sp", lambda e, l=l: e.dma_start(out=gffn[:], in_=gffnT[l, :, :]), [gffnT], [gffn])
            fw.dma("sp", lambda e, l=l: e.dma_start(out=gv[:], in_=gvec[l:l + 1, :].partition_broadcast(128)), [gvec], [gv])
            fw.dma("sp", lambda e, l=l: e.dma_start(out=cw[:], in_=conv_wT[l:l + 1, :, :].partition_broadcast(128)), [conv_wT], [cw])
            fw.dma("sp", lambda e, l=l: e.dma_start(out=subk[:], in_=subT_d[l, :, :, :]), [subT_d], [subk])
            Gst = B[7]
            NJB = 6 * D // CBW
            for jb in range(NJB):
                slab = WSL()
                fw.dma("sp", lambda e, l=l, jb=jb, slab=slab: e.dma_start(
                    out=slab[:, :, :], in_=ada_w[l, :, jb * CBW:(jb + 1) * CBW].rearrange("(k p) n -> p k n", p=128)), [ada_w], [slab])
                for sub in range(CBW // 128):
                    j = jb * (CBW // 128) + sub
                    pm = PS()
                    for k in range(16):
                        fw.op("pe", lambda e, k=k, pm=pm, slab=slab, sub=sub: e.matmul(pm[:, 0:2], lhsT=slab[:, k, sub * 128:(sub + 1) * 128], rhs=sT[:, k, :],
                                                                                       start=(k == 0), stop=(k == 15)), [slab, sT], [pm])
                    fw.op("dve", lambda e, j=j, pm=pm: e.tensor_scalar(out=modT[:, j, :], in0=pm[:, 0:2], scalar1=abT[:, j:j + 1], scalar2=None, op0=ALU.add),
                          [pm, abT], [modT])
                c0 = jb * CBW
                which = 0 if 2 * D <= c0 < 3 * D else (1 if c0 >= 5 * D else None)
                if which is not None:
                    off = c0 - (2 * D if which == 0 else 5 * D)
                    abr = abrow0 if which == 0 else abrow1
                    pr = PS()
                    for k in range(16):
                        fw.op("pe", lambda e, k=k, pr=pr, slab=slab: e.matmul(pr[0:2, 0:CBW], lhsT=sT[:, k, :], rhs=slab[:, k, :], start=(k == 0), stop=(k == 15)),
                              [slab, sT], [pr])
                    fw.op("dve", lambda e, pr=pr, abr=abr, off=off: e.tensor_tensor(out=rowb[:], in0=pr[0:2, 0:CBW], in1=abr[0:2, off:off + CBW], op=ALU.add),
                          [pr, abr], [rowb])
                    for grp in range(2):
                        pb = PS()
                        fw.op("pe", lambda e, pb=pb, grp=grp: e.matmul(pb[:, 0:CBW], lhsT=sel[:, grp, :], rhs=rowb[:, :], start=True, stop=True), [sel, rowb], [pb])
                        fw.op("act", lambda e, pb=pb: e.copy(out=Gst[:, 0:CBW], in_=pb[:, 0:CBW]), [pb], [Gst])
                        fw.dma("sp", lambda e, which=which, grp=grp, off=off: e.dma_start(out=GD[which, grp, :, off:off + CBW], in_=Gst[:, 0:CBW]), [Gst], [GD])
            for (A, g, j0) in ((A1, gmix, 16), (A2, gffn, 64)):
                fw.op("dve", lambda e, A=A, j0=j0: e.tensor_scalar(out=A[:], in0=modT[:, j0:j0 + 16, :], scalar1=1.0, scalar2=None, op0=ALU.add), [modT], [A])
                fw.op("dve", lambda e, A=A, g=g: e.tensor_tensor(out=A[:], in0=A[:], in1=g[:].unsqueeze(2).to_broadcast([128, 16, 2]), op=ALU.mult), [A, g], [A])

            xt, xn, junk, t1, t2 = B[0], B[1], B[2], B[3], B[4]
            for t in range(NT):
                grp = 0 if t < NTC else 1
                lt = None if t < NTC else t - NTC
                r0 = t * 128
                fw.dma("sp", lambda e, r0=r0: e.dma_start(out=xt[:], in_=X[r0:r0 + 128, :]), [X], [xt])
                rms_rstd(xt[:], xt, D, 0, junk)
                fw.op("dve", lambda e: e.tensor_scalar(out=xn[:], in0=xt[:], scalar1=rs[:, 0:1], scalar2=None, op0=ALU.mult), [xt, rs], [xn])
                transpose_chunks(lambda k: xn[:, k * 128:(k + 1) * 128], xn, 16, lambda k: hT[:, k, :], hT,
                                 scale=lambda k, grp=grp: A1[:, k, grp:grp + 1], bias=lambda k, grp=grp: modT[:, k, grp:grp + 1])
                gemm(lambda k: hT[:, k, :], hT, 16, w_in, lambda c0, w, l=l: w_in[l, :, c0:c0 + w].rearrange("(k p) n -> p k n", p=128), DIN,
                     lambda p, c0, w: fw.op("act", lambda e, p=p, c0=c0, w=w: e.copy(out=z[:, c0:c0 + w], in_=p[:, 0:w]), [p], [z]))
                rms_rstd(z[:, 0:512], z, 512, 1, junk)
                fw.op("dve", lambda e: e.scalar_tensor_tensor(out=t1[:, 0:512], in0=z[:, 0:512], scalar=rs[:, 1:2], in1=gv[:, 0:512], op0=ALU.mult, op1=ALU.mult),
                      [z, rs, gv], [t1])
                transpose_chunks(lambda k: t1[:, k * 128:(k + 1) * 128], t1, 4, lambda k: tT[:, k, :], tT)
                gemm(lambda k: tT[:, k, :], tT, 4, w_uq, lambda c0, w, l=l: w_uq[l, :, c0:c0 + w].rearrange("(k p) n -> p k n", p=128), 1536,
                     lambda p, c0, w: fw.op("act", lambda e, p=p, c0=c0, w=w: e.copy(out=t2[:, c0:c0 + w], in_=p[:, 0:w]), [p], [t2]))
                heads_norm(t2[:, 0:1536].rearrange("p (h d) -> p h d", h=8), t2, 8, 192, "qh",
                           t1[:, 0:1536].rearrange("p (h d) -> p h d", h=8), t1, junk, extra_scale=192 ** -0.5)
                if lt is not None:
                    rope(t1, 8, lt, junk)
                store_T(t1, 8, 192, QMT, r0)
                rms_rstd(z[:, 512:768], z, 256, 2, junk)
                fw.op("dve", lambda e: e.scalar_tensor_tensor(out=xn[:, 0:256], in0=z[:, 512:768], scalar=rs[:, 2:3], in1=gv[:, 512:768], op0=ALU.mult, op1=ALU.mult),
                      [z, rs, gv], [xn])
                if t < NTC:
                    s_, h_ = t // 2, t % 2
                    fw.dma("pool", lambda e, s_=s_, h_=h_, l=l: e.dma_start(out=o_ckv[s_, l, h_ * 128:(h_ + 1) * 128, :], in_=xn[:, 0:256]), [xn], [o_ckv])
                    fw.dma("pool", lambda e, s_=s_, h_=h_, l=l: e.dma_start(out=o_kpe[s_, l, h_ * 128:(h_ + 1) * 128, :], in_=z[:, 768:832]), [z], [o_kpe])
                    fw.dma("pool", lambda e, s_=s_, h_=h_, l=l: e.dma_start(
                        out=o_nav[s_, l, :, h_ * 128:(h_ + 1) * 128, :].rearrange("h t d -> t h d"),
                        in_=z[:, 1856:2368].rearrange("p (h d) -> p h d", h=4)), [z], [o_nav])
                kv_pipeline(l, xn[:, 0:256], xn, z[:, 768:832], z, r0, lt, t1, t2, junk)
                fw.dma("pool", lambda e, r0=r0: e.dma_start(out=VN[r0:r0 + 128, :], in_=z[:, 1856:2368]), [z], [VN])
                heads_norm(z[:, 832:1344].rearrange("p (h d) -> p h d", h=4), z, 4, 128, "naq",
                           t1[:, 0:512].rearrange("p (h d) -> p h d", h=4), t1, junk, extra_scale=128 ** -0.5)
                store_T(t1, 4, 128, QNT, r0)
                heads_norm(z[:, 1344:1856].rearrange("p (h d) -> p h d", h=4), z, 4, 128, "nak",
                           t2[:, 0:512].rearrange("p (h d) -> p h d", h=4), t2, junk)
                if t < NTC:
                    fw.dma("pool", lambda e, s_=s_, h_=h_, l=l: e.dma_start(
                        out=o_nak[s_, l, :, h_ * 128:(h_ + 1) * 128, :].rearrange("h t d -> t h d"),
                        in_=t2[:, 0:512].rearrange("p (h d) -> p h d", h=4)), [t2], [o_nak])
                store_T(t2, 4, 128, KNT, r0)
                si, cur = seq_of_tile(t)
                fw.op("dve", lambda e: e.tensor_tensor(out=t1[:, 1024:1536], in0=z[:, 2880:3392], in1=z[:, 3392:3904], op=ALU.mult), [z], [t1])
                fw.dma("pool", lambda e, cur=cur: e.dma_start(out=CU[cur:cur + 128, :], in_=t1[:, 1024:1536]), [t1], [CU])
                fw.dma("pool", lambda e, r0=r0: e.dma_start(out=GBs[r0:r0 + 128, :], in_=z[:, 2368:2880]), [z], [GBs])
            if TL:
                for ct in range(4):
                    tok0 = NTOK + ct * 128
                    fw.dma("sp", lambda e, ct=ct, l=l: e.dma_start(out=xn[:, 0:256], in_=c_ckv[l, ct * 128:(ct + 1) * 128, :]), [c_ckv], [xn])
                    fw.dma("sp", lambda e, ct=ct, l=l: e.dma_start(out=xn[:, 256:320], in_=c_kpe[l, ct * 128:(ct + 1) * 128, :]), [c_kpe], [xn])
                    kv_pipeline(l, xn[:, 0:256], xn, xn[:, 256:320], xn, tok0, None, t1, t2, junk)
                    fw.dma("sp", lambda e, ct=ct, l=l: e.dma_start(out=t2[:, 0:512].rearrange("p (h d) -> p h d", h=4),
                                                                   in_=c_nak[l, :, ct * 128:(ct + 1) * 128, :].rearrange("h t d -> t h d")), [c_nak], [t2])
                    store_T(t2, 4, 128, KNT, tok0)
                    fw.dma("pool", lambda e, ct=ct, l=l, tok0=tok0: e.dma_start(out=VN[tok0:tok0 + 128, :].rearrange("t (h d) -> t h d", h=4),
                                                                                 in_=c_nav[l, :, ct * 128:(ct + 1) * 128, :].rearrange("h t d -> t h d")), [c_nav], [VN])
            if cfg.stages <= 1:
                continue

            mix, Ssb, Vs, PTs, xt2, G1t = B[0], B[1], B[3], B[4], B[5], B[6]
            cb_ = B[7]
            junk = B[2]
            if TL:
                fw.dma("sp", lambda e, l=l: e.dma_start(out=relTs[:], in_=relT_d[l, :, :, :]), [relT_d], [relTs])
                fw.op("pool", lambda e: e.memset(Bt[:, 576:1088], 0.0), [], [Bt])

            def na_bias(h):
                pbk = [PS(), PS()]
                for sl in range(4):
                    slab = WSL()
                    ov = slab[0:31, 0:8, :].rearrange("p a (b c) -> p (a b) c", c=128)
                    fw.dma("sp", lambda e, sl=sl, ov=ov: e.dma_start(out=ov, in_=OH_d[:, sl * 16:(sl + 1) * 16, :]), [OH_d], [slab])
                    for kci in range(16):
                        kc = sl * 16 + kci
                        pb = pbk[kc // 32]
                        fw.op("pe", lambda e, pb=pb, kc=kc, kci=kci, ov=ov: e.matmul(pb[:, (kc % 32) * 15:(kc % 32) * 15 + 15], lhsT=ov[:, kci, :], rhs=relTs[:, h, :],
                                                                                  start=True, stop=True), [slab, relTs], [pb])
                for half in range(2):
                    pb = pbk[half]
                    fw.op("dve", lambda e, pb=pb, half=half: e.tensor_tensor(
                        out=CB2[:, :, half * 32:(half + 1) * 32], in0=pb[:, 0:480].rearrange("p (k a) -> p a k", a=15),
                        in1=cm[:, half * 32:(half + 1) * 32].unsqueeze(1).to_broadcast([128, 15, 32]), op=ALU.add), [pb, cm], [CB2])

            def phaseB_tail(t):
                grp = 0 if t < NTC else 1
                r0 = t * 128
                si, cur = seq_of_tile(t)
                fw.dma("sp", lambda e: e.dma_start(out=cb_[:, 0:512], in_=CU[cur - 1:cur + 127, :]), [CU], [cb_])
                fw.dma("sp", lambda e: e.dma_start(out=cb_[:, 1024:1536], in_=CU[cur:cur + 128, :]), [CU], [cb_])
                fw.dma("sp", lambda e: e.dma_start(out=cb_[:, 1536:2048], in_=CU[cur + 1:cur + 129, :]), [CU], [cb_])
                fw.dma("sp", lambda e: e.dma_start(out=cb_[:, 512:1024], in_=GBs[r0:r0 + 128, :]), [GBs], [cb_])
                fw.op("dve", lambda e: e.tensor_tensor(out=cb_[:, 0:512], in0=cb_[:, 0:512], in1=cw[:, 0, :], op=ALU.mult), [cb_, cw], [cb_])
                fw.op("dve", lambda e: e.tensor_tensor(out=cb_[:, 1024:1536], in0=cb_[:, 1024:1536], in1=cw[:, 1, :], op=ALU.mult), [cb_, cw], [cb_])
                fw.op("dve", lambda e: e.tensor_tensor(out=cb_[:, 1536:2048], in0=cb_[:, 1536:2048], in1=cw[:, 2, :], op=ALU.mult), [cb_, cw], [cb_])
                fw.op("dve", lambda e: e.tensor_tensor(out=cb_[:, 1024:1536], in0=cb_[:, 1024:1536], in1=cb_[:, 0:512], op=ALU.add), [cb_], [cb_])
                fw.op("dve", lambda e: e.tensor_tensor(out=cb_[:, 1024:1536], in0=cb_[:, 1024:1536], in1=cb_[:, 1536:2048], op=ALU.add), [cb_], [cb_])
                fw.op("dve", lambda e: e.tensor_tensor(out=mix[:, 1536:2048], in0=cb_[:, 1024:1536], in1=cb_[:, 512:1024], op=ALU.mult), [cb_], [mix])
                transpose_chunks(lambda k: mix[:, k * 128:(k + 1) * 128], mix, 16, lambda k: hT[:, k, :], hT)
                fw.dma("sp", lambda e: e.dma_start(out=xt2[:], in_=X[r0:r0 + 128, :]), [X], [xt2])
                fw.dma("sp", lambda e: e.dma_start(out=G1t[:], in_=GD[0, grp, :, :]), [GD], [G1t])

                def ofn(p, c0, w):
                    fw.op("dve", lambda e, p=p, c0=c0, w=w: e.tensor_tensor(out=junk[:, c0:c0 + w], in0=p[:, 0:w], in1=G1t[:, c0:c0 + w], op=ALU.mult), [p, G1t], [junk])
                    fw.op("dve", lambda e, c0=c0, w=w: e.tensor_tensor(out=xt2[:, c0:c0 + w], in0=xt2[:, c0:c0 + w], in1=junk[:, c0:c0 + w], op=ALU.add), [xt2, junk], [xt2])
                gemm(lambda k: hT[:, k, :], hT, 16, w_out, lambda c0, w, l=l: w_out[l, :, c0:c0 + w].rearrange("(k p) n -> p k n", p=128), D, ofn)
                fw.dma("sp", lambda e: e.dma_start(out=X[r0:r0 + 128, :], in_=xt2[:]), [xt2], [X])
            if TL:
                for h in range(4):
                    na_bias(h)
                    fw.dma("sp", lambda e, h=h: e.dma_start(out=CBD[h, :, :], in_=CB2[:].rearrange("p a k -> p (a k)")), [CB2], [CBD])
            for t in range(NT):
                r0 = t * 128
                if t < NTC:
                    sq0 = (t // 2) * 256
                    kb_m = [(sq0, 128), (sq0 + 128, 128)]
                    for h in range(8):
                        attention(QMT, KMT, VM, 1024, h, 192, 128, r0, kb_m, mix, h * 128, Ssb, PTs, Vs)
                    for h in range(4):
                        attention(QNT, KNT, VN, 512, h, 128, 128, r0, kb_m, mix, 1024 + h * 128, Ssb, PTs, Vs)
                else:
                    l0 = NTC * 128
                    kb_c = [(NTOK + i * 128, 128) for i in range(4)]
                    kb_m = kb_c + [(l0 + i * 128, 128) for i in range(NTL)]
                    for h in range(8):
                        attention(QMT, KMT, VM, 1024, h, 192, 128, r0, kb_m, mix, h * 128, Ssb, PTs, Vs)
                    rows = TL // 64
                    rg = 2 * (t - NTC)
                    rsf = lambda r: min(max(r - 4, 0), rows - 8)
                    ks = min(rsf(rg), rows - 9)
                    kb_l = [(l0 + ks * 64 + i * 128, 128) for i in range(4)] + [(l0 + ks * 64 + 512, 64)]
                    for h in range(4):
                        fw.op("pool", lambda e: e.memset(Bt[:, 0:576], NEG), [], [Bt])
                        for dr in range(2):
                            r = rg + dr
                            j0 = rsf(r) - ks
                            a0 = rsf(r) - r + 7
                            fw.dma("sp", lambda e, h=h, dr=dr, j0=j0, a0=a0: e.dma_start(out=Bt[dr * 64:(dr + 1) * 64, j0 * 64:(j0 + 8) * 64],
                                                                                          in_=CBD[h, dr * 64:(dr + 1) * 64, a0 * 64:(a0 + 8) * 64]), [CBD], [Bt])
                        attention(QNT, KNT, VN, 512, h, 128, 128, r0, kb_l + kb_c, mix, 1024 + h * 128, Ssb, PTs, Vs, bias=Bt)
                phaseB_tail(t)
            if cfg.stages <= 2:
                continue
            peer_phase(l)
        for t in range(NT):
            fw.dma("pool", lambda e, t=t: e.dma_start(out=xout[t * 128:(t + 1) * 128, :], in_=X[t * 128:(t + 1) * 128, :]), [X], [xout])
        fw.finish()
        fw.emit()
        print("instructions:", fw.ninst)
    return nc


def host_inputs(cfg, core, inp):
    L, NS = cfg.depth, cfg.nseq
    f = np.float32
    b = core // 4
    xs = [np.asarray(inp["x_prompt"][core * NS:(core + 1) * NS], f).reshape(NS * 256, D)]
    if cfg.lat:
        xs.append(np.asarray(inp["x_sample"][b], f)[:cfg.lat])
    m = {}
    m["xin"] = np.ascontiguousarray(np.concatenate(xs, 0))
    cv = np.stack([np.asarray(inp["c_ctx"], f), np.asarray(inp["c"][b], f)], 0)
    m["cvecT"] = np.ascontiguousarray(cv.reshape(2, 16, 128).transpose(2, 1, 0))
    m["ada_w"] = np.asarray(inp["ada_w"][:L], f)
    m["ada_b"] = np.asarray(inp["ada_b"][:L], f)
    m["ada_bT"] = np.ascontiguousarray(np.asarray(inp["ada_b"][:L], f).reshape(L, 96, 128).transpose(0, 2, 1))
    m["gmixT"] = np.ascontiguousarray(np.asarray(inp["norm_mix_g"][:L], f).reshape(L, 16, 128).transpose(0, 2, 1))
    m["gffnT"] = np.ascontiguousarray(np.asarray(inp["norm_ffn_g"][:L], f).reshape(L, 16, 128).transpose(0, 2, 1))
    m["w_in"] = np.asarray(inp["w_in"][:L], f)
    m["w_uq"] = np.asarray(inp["mla_w_uq"][:L], f)
    m["w_ukv"] = np.asarray(inp["mla_w_ukv"][:L], f)
    m["w_out"] = np.asarray(inp["w_out"][:L], f)
    m["gvec"] = np.ascontiguousarray(np.concatenate([np.asarray(inp[k][:L], f) for k in
                                                     ("mla_q_norm_g", "mla_kv_norm_g", "mla_q_head_g", "mla_k_head_g", "na_q_head_g", "na_k_head_g")], 1))
    m["conv_wT"] = np.ascontiguousarray(np.asarray(inp["conv_w"][:L], f).transpose(0, 2, 1))
    m["ident"] = np.eye(128, dtype=f)
    sel = np.zeros((2, 2, 128), f)
    sel[0, 0, :] = 1
    sel[1, 1, :] = 1
    m["sel"] = sel
    m["peer_wq"] = np.asarray(inp["peer_w_q"][:L], f)
    m["iota"] = np.ascontiguousarray(np.broadcast_to(np.arange(256, dtype=f)[None, :], (128, 256)))
    m["subT"] = np.ascontiguousarray(np.asarray(inp["peer_sub_keys"][:L], f).transpose(0, 4, 1, 2, 3).reshape(L, 128, 16, 128))
    for i in range(L):
        m["peer_u%d" % i] = np.asarray(inp["peer_u"][i], f)
        m["peer_v%d" % i] = np.asarray(inp["peer_v"][i], f)
    if cfg.lat:
        TL = cfg.lat
        tt = np.arange(TL)
        row = (tt // 64).astype(f)
        col = (tt % 64).astype(f)
        inv = (np.float32(10000.0) ** (-np.arange(16, dtype=f) / np.float32(16))).astype(f)
        ang = np.concatenate([row[:, None] * inv, col[:, None] * inv], -1).astype(f)
        m["rope"] = np.ascontiguousarray(np.concatenate([np.cos(ang), np.sin(ang)], -1).astype(f))
        m["c_ckv"] = np.ascontiguousarray(np.asarray(inp["cache_mla_ckv"][b, :L], f))
        m["c_kpe"] = np.ascontiguousarray(np.asarray(inp["cache_mla_kpe"][b, :L], f))
        m["c_nak"] = np.ascontiguousarray(np.asarray(inp["cache_na_k"][b, :L], f))
        m["c_nav"] = np.ascontiguousarray(np.asarray(inp["cache_na_v"][b, :L], f))
        m["relT"] = np.ascontiguousarray(np.asarray(inp["na_rel_bias"][:L], f).transpose(0, 3, 1, 2))
        OH = np.zeros((31, 64, 128), f)
        cmk = np.full((128, 64), NEG, f)
        for qc in range(64):
            cs = min(max(qc - 8, 0), 48)
            for kc in range(64):
                bb = kc - qc + 15
                if 0 <= bb < 31:
                    OH[bb, kc, qc] = 1
                    OH[bb, kc, 64 + qc] = 1
                if cs <= kc < cs + 16:
                    cmk[qc, kc] = 0
                    cmk[64 + qc, kc] = 0
        m["OH"] = OH
        m["cmask"] = cmk
    return m


_NC_CACHE = {}


def kernel(**inp):
    cfg = Cfg()
    n = 8
    if "nc" not in _NC_CACHE:
        _NC_CACHE["nc"] = build(cfg)
    nc = _NC_CACHE["nc"]
    in_maps = [host_inputs(cfg, c, inp) for c in range(n)]
    res = run_bass_kernel_spmd(nc, in_maps, core_ids=list(range(n)))
    R = res.results
    NS, L = cfg.nseq, cfg.depth
    y_prompt = np.concatenate([R[c]["xout"][:NS * 256].reshape(NS, 256, D) for c in range(n)], 0)
    y_sample = np.stack([np.concatenate([R[b * 4 + q]["xout"][NS * 256 + q * 256:NS * 256 + (q + 1) * 256] for q in range(4)], 0) for b in range(2)], 0)
    ckv = np.concatenate([R[c]["o_ckv"] for c in range(n)], 0)
    kpe = np.concatenate([R[c]["o_kpe"] for c in range(n)], 0)
    nak = np.concatenate([R[c]["o_nak"] for c in range(n)], 0)
    nav = np.concatenate([R[c]["o_nav"] for c in range(n)], 0)
    return (y_prompt.astype(np.float32), y_sample.astype(np.float32), ckv, kpe, nak, nav)
```

```python
from contextlib import ExitStack
import numpy as np
import concourse.bass as bass
import concourse.mybir as mybir
from concourse.bass_utils import run_bass_kernel_spmd

F32 = mybir.dt.float32
I32 = mybir.dt.int32
AF = mybir.ActivationFunctionType
ALU = mybir.AluOpType
AX = mybir.AxisListType

D = 2048
DIN = 3904
NEG = -30000.0


class Buf:
    __slots__ = ("t", "lw", "rd", "name")

    def __init__(self, t, name=""):
        self.t = t
        self.lw = None
        self.rd = {}
        self.name = name

    def __getitem__(self, k):
        return self.t[k]


class FW:
    ENG = ("pe", "act", "dve", "pool", "sp")
    NDMA = 6

    def __init__(self, nc, es, same_engine_sync=True):
        self.nc = nc
        self.es = es
        self.same = same_engine_sync
        self.sem = {}
        self.cnt = {}
        for e in self.ENG:
            self.sem[e] = es.enter_context(nc.semaphore("s_" + e))
            self.cnt[e] = 0
        self.dq = {}
        for q in ("sp", "pool", "act"):
            slots = []
            for i in range(self.NDMA):
                k = "d_%s%d" % (q, i)
                self.sem[k] = es.enter_context(nc.semaphore(k))
                self.cnt[k] = 0
                slots.append(k)
            self.dq[q] = [slots, 0]
        self.seen = {e: {} for e in self.ENG}
        self.prog = {e: [] for e in self.ENG}
        self.ninst = 0
        self._psi = 0

    def sb(self, name, shape, dt=F32):
        t = self.es.enter_context(self.nc.sbuf_tensor("sb_" + name, list(shape), dt))
        return Buf(t, name)

    def ps(self, name, shape, dt=F32):
        t = self.es.enter_context(self.nc.psum_tensor(name, list(shape), dt))
        return Buf(t, name)

    def dram(self, name, shape, dt=F32, kind="Internal"):
        t = self.nc.dram_tensor(name, list(shape), dt, kind=kind)
        return Buf(t, name)

    def _wait(self, eng, key, val):
        if key == eng and not self.same:
            return
        if self.seen[eng].get(key, 0) >= val:
            return
        self.prog[eng].append(("w", key, val))
        self.seen[eng][key] = val

    def _deps(self, eng, reads, writes):
        for b in reads:
            if b.lw is not None:
                self._wait(eng, *b.lw)
        for b in writes:
            if b.lw is not None:
                self._wait(eng, *b.lw)
            for k, v in b.rd.items():
                self._wait(eng, k, v)

    def _mark(self, key, val, reads, writes):
        for b in reads:
            if b.rd.get(key, 0) < val:
                b.rd[key] = val
        for b in writes:
            b.lw = (key, val)
            b.rd = {}

    def op(self, eng, fn, reads=(), writes=()):
        self._deps(eng, reads, writes)
        self.cnt[eng] += 1
        self.prog[eng].append(("i", fn, eng, 1))
        self._mark(eng, self.cnt[eng], reads, writes)
        self.ninst += 1

    def dma(self, q, fn, reads=(), writes=()):
        slots, i = self.dq[q]
        k = slots[i % len(slots)]
        self.dq[q][1] = i + 1
        if self.cnt[k] > 0:
            self._wait(q, k, self.cnt[k])
        self._deps(q, reads, writes)
        self.cnt[k] += 16
        self.prog[q].append(("i", fn, k, 16))
        self._mark(k, self.cnt[k], reads, writes)
        self.ninst += 1

    def finish(self):
        for e in self.ENG:
            if e != "sp" and self.cnt[e] > 0:
                self._wait("sp", e, self.cnt[e])
        for q in self.dq:
            for k in self.dq[q][0]:
                if self.cnt[k] > 0:
                    self._wait("sp", k, self.cnt[k])

    def emit(self):
        nc = self.nc
        prog = self.prog
        sem = self.sem

        def run(e, items):
            for it in items:
                if it[0] == "w":
                    e.wait_ge(sem[it[1]], it[2])
                else:
                    it[1](e).then_inc(sem[it[2]], it[3])

        with nc.Block() as block:
            @block.tensor
            def _(e):
                run(e, prog["pe"])

            @block.scalar
            def _(e):
                run(e, prog["act"])

            @block.vector
            def _(e):
                run(e, prog["dve"])

            @block.gpsimd
            def _(e):
                run(e, prog["pool"])

            @block.sync
            def _(e):
                run(e, prog["sp"])


class Cfg:
    def __init__(self, depth=4, nseq=4, lat=1024, stages=99):
        self.depth = depth
        self.nseq = nseq
        self.lat = lat
        self.stages = stages


GOFF = {"qn": 0, "kvn": 512, "qh": 768, "kh": 960, "naq": 1152, "nak": 1280}
GLEN = 1408


CBW = 256
NEG2 = -1.0e30
U32 = mybir.dt.uint32
BF16 = mybir.dt.bfloat16


def build(cfg):
    L = cfg.depth
    NS = cfg.nseq
    NTC = NS * 2
    TL = cfg.lat
    NTL = TL // 128
    NT = NTC + NTL
    NTOK = NT * 128
    NCACHE = 512 if TL else 0
    nc = bass.Bass("TRN2", target_bir_lowering=False)

    def din(name, shape, dt=F32):
        return Buf(nc.dram_tensor(name, list(shape), dt, kind="ExternalInput"), name)

    def dout(name, shape, dt=F32):
        return Buf(nc.dram_tensor(name, list(shape), dt, kind="ExternalOutput"), name)

    xin = din("xin", [NTOK, D])
    cvecT = din("cvecT", [128, 16, 2])
    ada_w = din("ada_w", [L, D, 6 * D])
    ada_bT = din("ada_bT", [L, 128, 96])
    ada_b = din("ada_b", [L, 6 * D])
    gmixT = din("gmixT", [L, 128, 16])
    gffnT = din("gffnT", [L, 128, 16])
    w_in = din("w_in", [L, D, DIN])
    w_uq = din("w_uq", [L, 512, 1536])
    w_ukv = din("w_ukv", [L, 256, 2048])
    w_out = din("w_out", [L, D, D])
    gvec = din("gvec", [L, GLEN])
    conv_wT = din("conv_wT", [L, 3, 512])
    ident_d = din("ident", [128, 128])
    sel_d = din("sel", [2, 2, 128])
    peer_wq = din("peer_wq", [L, D, D])
    subT_d = din("subT", [L, 128, 16, 128])
    peer_u = [din("peer_u%d" % i, [16384, D]) for i in range(L)]
    peer_v = [din("peer_v%d" % i, [16384, D]) for i in range(L)]
    iota_d = din("iota", [128, 256])
    if TL:
        rope_d = din("rope", [TL, 64])
        c_ckv = din("c_ckv", [L, 512, 256])
        c_kpe = din("c_kpe", [L, 512, 64])
        c_nak = din("c_nak", [L, 4, 512, 128])
        c_nav = din("c_nav", [L, 4, 512, 128])
        relT_d = din("relT", [L, 31, 4, 15])
        OH_d = din("OH", [31, 64, 128])
        cm_d = din("cmask", [128, 64])

    xout = dout("xout", [NTOK, D])
    o_ckv = dout("o_ckv", [NS, L, 256, 256])
    o_kpe = dout("o_kpe", [NS, L, 256, 64])
    o_nak = dout("o_nak", [NS, L, 4, 256, 128])
    o_nav = dout("o_nav", [NS, L, 4, 256, 128])

    with ExitStack() as es:
        fw = FW(nc, es)
        NKT = NTOK + NCACHE
        X = fw.dram("Xs", [NTOK, D])
        QMT = fw.dram("QMT", [8, 192, NTOK])
        KMT = fw.dram("KMT", [8, 192, NKT])
        VM = fw.dram("VM", [NKT, 1024])
        QNT = fw.dram("QNT", [4, 128, NTOK])
        KNT = fw.dram("KNT", [4, 128, NKT])
        VN = fw.dram("VN", [NKT, 512])
        NSEQ = NS + (1 if TL else 0)
        CU = fw.dram("CU", [NTOK + 2 * NSEQ, 512])
        GBs = fw.dram("GBs", [NTOK, 512])
        GD = fw.dram("GD", [2, 2, 128, D])
        CBD = fw.dram("CBD", [4, 128, 15 * 64])

        def seq_of_tile(t):
            if t < NTC:
                si = t // 2
            else:
                si = NS
            return si, t * 128 + 2 * si + 1

        pbig = fw.ps("pbig", [128, 4096])
        psb = [Buf(pbig.t[:, i * 512:(i + 1) * 512], "ps%d" % i) for i in range(8)]
        grp_i = [0]

        def PS():
            b = psb[fw._psi % 8]
            fw._psi += 1
            return b

        def PSG():
            g = grp_i[0] % 2
            grp_i[0] += 1
            return psb[g * 4:(g + 1) * 4], pbig.t[:, g * 2048:(g + 1) * 2048]

        ident = fw.sb("ident", [128, 128])
        sel = fw.sb("sel", [2, 2, 128])
        fw.dma("sp", lambda e: e.dma_start(out=ident[:], in_=ident_d[:, :]), [ident_d], [ident])
        fw.dma("sp", lambda e: e.dma_start(out=sel[:], in_=sel_d[:, :, :]), [sel_d], [sel])
        sT = fw.sb("sT", [128, 16, 2])
        fw.dma("sp", lambda e: e.dma_start(out=sT[:], in_=cvecT[:, :, :]), [cvecT], [sT])
        fw.op("act", lambda e: e.activation(out=sT[:], in_=sT[:], func=AF.Silu), [sT], [sT])

        for t in range(NT):
            fw.dma("pool", lambda e, t=t: e.dma_start(out=X[t * 128:(t + 1) * 128, :], in_=xin[t * 128:(t + 1) * 128, :]), [xin], [X])

        B = [fw.sb("B%d" % i, [128, D]) for i in range(10)]
        Hhb = fw.sb("Hhb", [128, D], BF16)
        identb = fw.sb("identb", [128, 128], BF16)
        G16 = [Buf(B[5 + i // 2].t[:, (i % 2) * 1024:(i % 2) * 1024 + 1024].bitcast(BF16), "g16_%d" % i) for i in range(4)]

        def alias_in(views, bases):
            for i, v in enumerate(views):
                b = bases[i // 2]
                v.lw = b.lw
                v.rd = dict(b.rd)

        def alias_out(views, bases):
            for i, v in enumerate(views):
                b = bases[i // 2]
                if v.lw is not None:
                    b.rd[v.lw[0]] = max(b.rd.get(v.lw[0], 0), v.lw[1])
                for kk, vv in v.rd.items():
                    b.rd[kk] = max(b.rd.get(kk, 0), vv)

        UB = [fw.dram("UB%d" % i, [16384, D], BF16) for i in range(L)]
        VB = [fw.dram("VB%d" % i, [16384, D], BF16) for i in range(L)]
        if cfg.stages > 2:
            alias_in(G16, [B[5], B[6]])
            it = 0
            for (src, dst) in [(peer_u[i], UB[i]) for i in range(L)] + [(peer_v[i], VB[i]) for i in range(L)]:
                for c in range(128):
                    fb = B[it % 4]
                    gb = G16[it % 4]
                    fw.dma("sp", lambda e, src=src, c=c, fb=fb: e.dma_start(out=fb[:], in_=src[c * 128:(c + 1) * 128, :]), [src], [fb])
                    if it % 2 == 0:
                        fw.op("act", lambda e, fb=fb, gb=gb: e.copy(out=gb[:], in_=fb[:]), [fb], [gb])
                    else:
                        fw.op("dve", lambda e, fb=fb, gb=gb: e.tensor_copy(out=gb[:], in_=fb[:]), [fb], [gb])
                    fw.dma("pool", lambda e, dst=dst, c=c, gb=gb: e.dma_start(out=dst[c * 128:(c + 1) * 128, :], in_=gb[:]), [gb], [dst])
                    it += 1
            alias_out(G16, [B[5], B[6]])
        fw.op("dve", lambda e: e.tensor_copy(out=identb[:], in_=ident[:]), [ident], [identb])
        wsl = [fw.sb("wsl%d" % i, [128, 16, CBW]) for i in range(2)]
        wi = [0]

        def WSL():
            b = wsl[wi[0] % 2]
            wi[0] += 1
            return b

        modT = fw.sb("modT", [128, 96, 2])
        abT = fw.sb("abT", [128, 96])
        rowb = fw.sb("rowb", [2, CBW])
        gmix = fw.sb("gmix", [128, 16])
        gffn = fw.sb("gffn", [128, 16])
        A1 = fw.sb("A1", [128, 16, 2])
        A2 = fw.sb("A2", [128, 16, 2])
        gv = fw.sb("gv", [128, GLEN])
        cw = fw.sb("cw", [128, 3, 512])
        hT = fw.sb("hT", [128, 16, 128])
        z = fw.sb("z", [128, DIN])
        ss = fw.sb("ss", [128, 16])
        rs = fw.sb("rs", [128, 16])
        tT = fw.sb("tT", [128, 4, 128])
        stg = B[5].t[:, :].rearrange("p (a h t) -> p a h t", a=2, h=8)
        stgb = B[5]
        kTs = fw.sb("kTs", [128, 2, 1536])
        qTs = fw.sb("qTs", [128, 2, 128])
        mx = fw.sb("mx", [128, 4])
        subk = fw.sb("subk", [128, 16, 128])
        TV = fw.sb("TV", [128, 16, 16])
        TIu = fw.sb("TIu", [128, 16, 16], U32)
        TIf = fw.sb("TIf", [128, 16, 16])
        TS = fw.sb("TS", [128, 8, 16])
        PIu = fw.sb("PIu", [128, 8, 16], U32)
        PIf = fw.sb("PIf", [128, 8, 16])
        iota = fw.sb("iota", [128, 256])
        fw.dma("sp", lambda e: e.dma_start(out=iota[:], in_=iota_d[:, :]), [iota_d], [iota])
        EI = fw.sb("EI", [128, 128])
        GT = fw.sb("GT", [128, 128])
        IDX = fw.sb("IDX", [128, 128], I32)
        GTT = fw.sb("GTT", [128, 128])
        ACTV = fw.sb("ACTV", [128, 128])
        WT = fw.sb("WT", [128, 128])
        ZW = [fw.sb("ZW%d" % i, [128, 256], BF16) for i in range(4)]
        for i in range(4):
            fw.op("pool", lambda e, i=i: e.memset(ZW[i][:], 0.0), [], [ZW[i]])
        zrow = fw.sb("zrow", [1, 512])
        fw.op("pool", lambda e: e.memset(zrow[:], 0.0), [], [zrow])
        for si in range(NSEQ):
            st = si * 256 if si < NS else NS * 256
            ln = 256 if si < NS else TL
            for rr in (st + 2 * si, st + 2 * si + 1 + ln):
                fw.dma("pool", lambda e, rr=rr: e.dma_start(out=CU[rr:rr + 1, :], in_=zrow[:, :]), [zrow], [CU])
        if TL:
            ropet = fw.sb("ropet", [128, NTL, 64])
            fw.dma("sp", lambda e: e.dma_start(out=ropet[:], in_=rope_d.t.ap().rearrange("(n p) c -> p n c", p=128)), [rope_d], [ropet])
            relTs = fw.sb("relTs", [31, 4, 15])
            cm = fw.sb("cm", [128, 64])
            fw.dma("sp", lambda e: e.dma_start(out=cm[:], in_=cm_d[:, :]), [cm_d], [cm])
            CB2 = fw.sb("CB2", [128, 15, 64])
            Bt = B[8]

        def rms_rstd(src_ap, srcbuf, n, col, junk):
            fw.op("act", lambda e: e.activation(out=junk[:, 0:n], in_=src_ap, func=AF.Square, scale=float(n ** -0.5),
                                                accum_out=ss[:, col:col + 1]), [srcbuf], [junk, ss])
            fw.op("act", lambda e: e.activation(out=rs[:, col:col + 1], in_=ss[:, col:col + 1], func=AF.Sqrt, bias=1e-6, scale=1.0), [ss], [rs])
            fw.op("dve", lambda e: e.reciprocal(out=rs[:, col:col + 1], in_=rs[:, col:col + 1]), [rs], [rs])

        def heads_norm(src_ap, srcbuf, nh, dh, gname, dst_ap, dstbuf, junk, extra_scale=1.0):
            jv = junk[:, 0:nh * dh].rearrange("p (h d) -> p h d", h=nh)
            fw.op("dve", lambda e: e.tensor_tensor(out=jv, in0=src_ap, in1=src_ap, op=ALU.mult), [srcbuf], [junk])
            fw.op("dve", lambda e: e.tensor_reduce(out=ss[:, 0:nh], in_=jv, axis=AX.X, op=ALU.add), [junk], [ss])
            fw.op("act", lambda e: e.activation(out=rs[:, 0:nh], in_=ss[:, 0:nh], func=AF.Sqrt, bias=1e-6, scale=float(1.0 / dh)), [ss], [rs])
            fw.op("dve", lambda e: e.reciprocal(out=rs[:, 0:nh], in_=rs[:, 0:nh]), [rs], [rs])
            fw.op("dve", lambda e: e.tensor_tensor(out=dst_ap, in0=src_ap, in1=rs[:, 0:nh].unsqueeze(2).to_broadcast([128, nh, dh]), op=ALU.mult),
                  [srcbuf, rs], [dstbuf])
            g0 = GOFF[gname]
            fw.op("dve", lambda e: e.scalar_tensor_tensor(out=dst_ap, in0=dst_ap, scalar=float(extra_scale),
                                                          in1=gv[:, g0:g0 + dh].unsqueeze(1).to_broadcast([128, nh, dh]), op0=ALU.mult, op1=ALU.mult),
                  [dstbuf, gv], [dstbuf])

        def rope(buf, nh, lt, junk):
            v = buf[:, 0:nh * 192].rearrange("p (h d) -> p h d", h=nh)
            x1, x2 = v[:, :, 128:160], v[:, :, 160:192]
            cs = ropet[:, lt, 0:32].unsqueeze(1).to_broadcast([128, nh, 32])
            sn = ropet[:, lt, 32:64].unsqueeze(1).to_broadcast([128, nh, 32])
            j = junk[:, 0:nh * 128].rearrange("p (h d) -> p h d", h=nh)
            a, b_, c, d_ = j[:, :, 0:32], j[:, :, 32:64], j[:, :, 64:96], j[:, :, 96:128]
            fw.op("dve", lambda e: e.tensor_tensor(out=a, in0=x1, in1=cs, op=ALU.mult), [buf, ropet], [junk])
            fw.op("dve", lambda e: e.tensor_tensor(out=b_, in0=x2, in1=sn, op=ALU.mult), [buf, ropet], [junk])
            fw.op("dve", lambda e: e.tensor_tensor(out=c, in0=x1, in1=sn, op=ALU.mult), [buf, ropet], [junk])
            fw.op("dve", lambda e: e.tensor_tensor(out=d_, in0=x2, in1=cs, op=ALU.mult), [buf, ropet], [junk])
            fw.op("dve", lambda e: e.tensor_tensor(out=x1, in0=a, in1=b_, op=ALU.subtract), [junk], [buf])
            fw.op("dve", lambda e: e.tensor_tensor(out=x2, in0=c, in1=d_, op=ALU.add), [junk], [buf])

        def transpose_chunks(src_ap_fn, srcbuf, nch, dst_fn, dstbuf, width=128, scale=None, bias=None, rows=128):
            for k in range(nch):
                p = PS()
                fw.op("pe", lambda e, k=k, p=p: e.transpose(out=p[0:width, 0:rows], in_=src_ap_fn(k), identity=ident[0:rows, 0:rows]), [srcbuf, ident], [p])
                if scale is None:
                    fw.op("act", lambda e, k=k, p=p: e.copy(out=dst_fn(k), in_=p[0:width, 0:rows]), [p], [dstbuf])
                else:
                    sc, bi = scale(k), bias(k)
                    fw.op("act", lambda e, k=k, p=p, sc=sc, bi=bi: e.activation(out=dst_fn(k), in_=p[0:width, 0:rows], func=AF.Identity,
                                                                                 bias=bi, scale=sc), [p, A1, A2, modT], [dstbuf])

        def gemm(lhsT_fn, lbuf, nk, Wd, w_ap_fn, ncols, out_fn, kparts=128):
            c0 = 0
            while c0 < ncols:
                w = min(CBW, ncols - c0)
                slab = WSL()
                fw.dma("sp", lambda e, c0=c0, w=w, slab=slab: e.dma_start(out=slab[0:kparts, 0:nk, 0:w], in_=w_ap_fn(c0, w)), [Wd], [slab])
                p = PS()
                for k in range(nk):
                    fw.op("pe", lambda e, k=k, p=p, slab=slab, w=w: e.matmul(p[:, 0:w], lhsT=lhsT_fn(k), rhs=slab[0:kparts, k, 0:w],
                                                                            start=(k == 0), stop=(k == nk - 1)), [lbuf, slab], [p])
                out_fn(p, c0, w)
                c0 += w

        def store_T(src, nh, dq, dstT, tok0):
            sv = src[:, 0:nh * dq].rearrange("p (h d) -> p h d", h=nh)
            for h in range(nh):
                transpose_chunks(lambda k, h=h: sv[:, h, 0:128], src, 1, lambda k, h=h: stg[:, 0, h, :], stgb)
                if dq > 128:
                    transpose_chunks(lambda k, h=h: sv[:, h, 128:dq], src, 1, lambda k, h=h: stg[0:dq - 128, 1, h, :], stgb, width=dq - 128)
            fw.dma("pool", lambda e: e.dma_start(out=dstT[:, 0:128, tok0:tok0 + 128].rearrange("h d t -> d h t"), in_=stg[:, 0, 0:nh, :]), [stgb], [dstT])
            if dq > 128:
                fw.dma("pool", lambda e: e.dma_start(out=dstT[:, 128:dq, tok0:tok0 + 128].rearrange("h d t -> d h t"), in_=stg[0:dq - 128, 1, 0:nh, :]), [stgb], [dstT])

        def kv_pipeline(l, ckvn_ap, ckvn_buf, kpe_ap, kpe_buf, tok0, lt, t1, t2, junk):
            transpose_chunks(lambda k: ckvn_ap[:, k * 128:(k + 1) * 128], ckvn_buf, 2, lambda k: tT[:, k, :], tT)
            gemm(lambda k: tT[:, k, :], tT, 2, w_ukv, lambda c0, w, l=l: w_ukv[l, :, c0:c0 + w].rearrange("(k p) n -> p k n", p=128), 2048,
                 lambda p, c0, w: fw.op("act", lambda e, p=p, c0=c0, w=w: e.copy(out=t2[:, c0:c0 + w], in_=p[:, 0:w]), [p], [t2]))
            kvv = t2[:, 0:2048].rearrange("p (h d) -> p h d", h=8)
            kf = t1[:, 0:1536].rearrange("p (h d) -> p h d", h=8)
            fw.op("dve", lambda e: e.tensor_copy(out=kf[:, :, 0:128], in_=kvv[:, :, 0:128]), [t2], [t1])
            fw.op("dve", lambda e: e.tensor_copy(out=kf[:, :, 128:192], in_=kpe_ap.unsqueeze(1).to_broadcast([128, 8, 64])), [kpe_buf], [t1])
            fw.dma("pool", lambda e: e.dma_start(out=VM[tok0:tok0 + 128, :].rearrange("t (h d) -> t h d", h=8), in_=kvv[:, :, 128:256]), [t2], [VM])
            heads_norm(kf, t1, 8, 192, "kh", kf, t1, junk)
            if lt is not None:
                rope(t1, 8, lt, junk)
            store_T(t1, 8, 192, KMT, tok0)

        def attention(QT, KT, V, vw, h, dq, dv, r0, keyblocks, mix, mixcol, Ssb, PTs, Vs, bias=None):
            nkb = len(keyblocks)
            nk = sum(n for _, n in keyblocks)
            nch = 2 if dq > 128 else 1
            fw.dma("sp", lambda e: e.dma_start(out=qTs[:, 0, :], in_=QT[h, 0:128, r0:r0 + 128]), [QT], [qTs])
            if nch == 2:
                fw.dma("sp", lambda e: e.dma_start(out=qTs[0:dq - 128, 1, :], in_=QT[h, 128:dq, r0:r0 + 128]), [QT], [qTs])
            off = 0
            i = 0
            while i < nkb:
                t0, n = keyblocks[i]
                j = i + 1
                tot = n
                while j < nkb and keyblocks[j][0] == t0 + tot:
                    tot += keyblocks[j][1]
                    j += 1
                fw.dma("sp", lambda e, t0=t0, tot=tot, off=off: e.dma_start(out=kTs[:, 0, off:off + tot], in_=KT[h, 0:128, t0:t0 + tot]), [KT], [kTs])
                if nch == 2:
                    fw.dma("sp", lambda e, t0=t0, tot=tot, off=off: e.dma_start(out=kTs[0:dq - 128, 1, off:off + tot], in_=KT[h, 128:dq, t0:t0 + tot]), [KT], [kTs])
                off += tot
                i = j
            Vv = Vs[:, 0:12 * 128].rearrange("p (k d) -> p k d", k=12)
            for kb, (t0, n) in enumerate(keyblocks):
                fw.dma("pool", lambda e, kb=kb, t0=t0, n=n: e.dma_start(out=Vv[0:n, kb, 0:dv], in_=V[t0:t0 + n, h * dv:(h + 1) * dv]), [V], [Vs])
            banks, _ = PSG()
            c0 = 0
            while c0 < nk:
                w = min(512, nk - c0)
                b = banks[c0 // 512]
                fw.op("pe", lambda e, b=b, c0=c0, w=w: e.matmul(b[:, 0:w], lhsT=qTs[:, 0, :], rhs=kTs[:, 0, c0:c0 + w], start=True, stop=(nch == 1)), [qTs, kTs], [b])
                if nch == 2:
                    fw.op("pe", lambda e, b=b, c0=c0, w=w: e.matmul(b[:, 0:w], lhsT=qTs[0:dq - 128, 1, :], rhs=kTs[0:dq - 128, 1, c0:c0 + w], start=False, stop=True),
                          [qTs, kTs], [b])
                if bias is not None:
                    fw.op("dve", lambda e, b=b, c0=c0, w=w: e.tensor_tensor(out=Ssb[:, c0:c0 + w], in0=b[:, 0:w], in1=bias[:, c0:c0 + w], op=ALU.add), [b, bias], [Ssb])
                else:
                    fw.op("act", lambda e, b=b, c0=c0, w=w: e.copy(out=Ssb[:, c0:c0 + w], in_=b[:, 0:w]), [b], [Ssb])
                c0 += w
            fw.op("dve", lambda e: e.reduce_max(out=mx[:, 0:1], in_=Ssb[:, 0:nk], axis=AX.X), [Ssb], [mx])
            fw.op("dve", lambda e: e.tensor_scalar(out=mx[:, 1:2], in0=mx[:, 0:1], scalar1=-1.0, scalar2=None, op0=ALU.mult), [mx], [mx])
            fw.op("act", lambda e: e.activation(out=Ssb[:, 0:nk], in_=Ssb[:, 0:nk], func=AF.Exp, bias=mx[:, 1:2], scale=1.0, accum_out=mx[:, 2:3]), [Ssb, mx], [Ssb, mx])
            fw.op("dve", lambda e: e.reciprocal(out=mx[:, 3:4], in_=mx[:, 2:3]), [mx], [mx])
            PTv = PTs[:, 0:12 * 128].rearrange("p (k d) -> p k d", k=12)
            off = 0
            for kb, (t0, n) in enumerate(keyblocks):
                p = PS()
                fw.op("pe", lambda e, p=p, off=off, n=n: e.transpose(out=p[0:n, 0:128], in_=Ssb[:, off:off + n], identity=ident[:]), [Ssb, ident], [p])
                fw.op("act", lambda e, p=p, kb=kb, n=n: e.copy(out=PTv[0:n, kb, :], in_=p[0:n, 0:128]), [p], [PTs])
                off += n
            po = PS()
            for kb, (t0, n) in enumerate(keyblocks):
                fw.op("pe", lambda e, kb=kb, n=n, po=po: e.matmul(po[:, 0:dv], lhsT=PTv[0:n, kb, :], rhs=Vv[0:n, kb, 0:dv], start=(kb == 0), stop=(kb == nkb - 1)),
                      [PTs, Vs], [po])
            fw.op("dve", lambda e, po=po: e.tensor_scalar(out=mix[:, mixcol:mixcol + dv], in0=po[:, 0:dv], scalar1=mx[:, 3:4], scalar2=None, op0=ALU.mult), [po, mx], [mix])


        def peer_phase(l):
            xt, xn, junk, Hh, t2, G2t = B[0], B[1], B[2], B[3], B[4], B[9]
            alias_in(G16, [B[5], B[6]])
            SC = z
            SCv = SC[:, 0:2048].rearrange("p (a n) -> p a n", a=16)
            cand = z[:, 2048:3904]
            for t in range(NT):
                grp = 0 if t < NTC else 1
                r0 = t * 128
                fw.dma("sp", lambda e, r0=r0: e.dma_start(out=xt[:], in_=X[r0:r0 + 128, :]), [X], [xt])
                fw.dma("sp", lambda e, grp=grp: e.dma_start(out=G2t[:], in_=GD[1, grp, :, :]), [GD], [G2t])
                rms_rstd(xt[:], xt, D, 0, junk)
                fw.op("dve", lambda e: e.tensor_scalar(out=xn[:], in0=xt[:], scalar1=rs[:, 0:1], scalar2=None, op0=ALU.mult), [xt, rs], [xn])
                transpose_chunks(lambda k: xn[:, k * 128:(k + 1) * 128], xn, 16, lambda k: hT[:, k, :], hT,
                                 scale=lambda k, grp=grp: A2[:, k, grp:grp + 1], bias=lambda k, grp=grp: modT[:, 48 + k, grp:grp + 1])
                gemm(lambda k: hT[:, k, :], hT, 16, peer_wq, lambda c0, w, l=l: peer_wq[l, :, c0:c0 + w].rearrange("(k p) n -> p k n", p=128), D,
                     lambda p, c0, w: fw.op("act", lambda e, p=p, c0=c0, w=w: e.copy(out=t2[:, c0:c0 + w], in_=p[:, 0:w]), [p], [t2]))
                transpose_chunks(lambda k: hT[:, k, :], hT, 16, lambda k: Hhb[:, k * 128:(k + 1) * 128], Hhb)
                for hp in range(16):
                    transpose_chunks(lambda k, hp=hp: t2[:, hp * 128:(hp + 1) * 128], t2, 1, lambda k, hp=hp: tT[:, hp % 4, :], tT)
                    p2 = PS()
                    fw.op("pe", lambda e, hp=hp, p2=p2: e.matmul(p2[:, 0:128], lhsT=tT[:, hp % 4, :], rhs=subk[:, hp, :], start=True, stop=True), [tT, subk], [p2])
                    fw.op("act", lambda e, hp=hp, p2=p2: e.copy(out=SCv[:, hp, :], in_=p2[:, 0:128]), [p2], [z])
                for hp in range(16):
                    fw.op("dve", lambda e, hp=hp: e.max(out=TV[:, hp, 0:8], in_=SCv[:, hp, :]), [z], [TV])
                    fw.op("dve", lambda e, hp=hp: e.max_index(out=TIu[:, hp, 0:8], in_max=TV[:, hp, 0:8], in_values=SCv[:, hp, :]), [z, TV], [TIu])
                    fw.op("dve", lambda e, hp=hp: e.match_replace(out=junk[:, 0:128], in_to_replace=TV[:, hp, 0:8], in_values=SCv[:, hp, :], imm_value=NEG2), [z, TV], [junk])
                    fw.op("dve", lambda e, hp=hp: e.max(out=TV[:, hp, 8:16], in_=junk[:, 0:128]), [junk], [TV])
                    fw.op("dve", lambda e, hp=hp: e.max_index(out=TIu[:, hp, 8:16], in_max=TV[:, hp, 8:16], in_values=junk[:, 0:128]), [junk, TV], [TIu])
                fw.op("dve", lambda e: e.tensor_copy(out=TIf[:], in_=TIu[:]), [TIu], [TIf])
                TVv = TV[:].rearrange("p (h two) k -> p h two k", two=2)
                TIv = TIf[:].rearrange("p (h two) k -> p h two k", two=2)
                cand = xn[:, 0:2048].rearrange("p (h a b) -> p h a b", h=8, a=16)
                cidx = junk[:, 0:2048].rearrange("p (h a b) -> p h a b", h=8, a=16)
                for h in range(8):
                    fw.op("dve", lambda e, h=h: e.tensor_tensor(out=cand[:, h, :, :], in0=TVv[:, h, 0, :].unsqueeze(2).to_broadcast([128, 16, 16]),
                                                                in1=TVv[:, h, 1, :].unsqueeze(1).to_broadcast([128, 16, 16]), op=ALU.add), [TV], [xn])
                    fw.op("dve", lambda e, h=h: e.scalar_tensor_tensor(out=cidx[:, h, :, :], in0=TIv[:, h, 0, :].unsqueeze(2).to_broadcast([128, 16, 16]), scalar=128.0,
                                                                       in1=TIv[:, h, 1, :].unsqueeze(1).to_broadcast([128, 16, 16]), op0=ALU.mult, op1=ALU.add), [TIf], [junk])
                candf = xn[:, 0:2048].rearrange("p (h c) -> p h c", h=8)
                cidxf = junk[:, 0:2048].rearrange("p (h c) -> p h c", h=8)
                scr = z[:, 2048:2304]
                scr2 = z[:, 2304:2560]
                for h in range(8):
                    fw.op("dve", lambda e, h=h: e.max(out=TS[:, h, 0:8], in_=candf[:, h, :]), [xn], [TS])
                    fw.op("dve", lambda e, h=h: e.max_index(out=PIu[:, h, 0:8], in_max=TS[:, h, 0:8], in_values=candf[:, h, :]), [xn, TS], [PIu])
                    fw.op("dve", lambda e, h=h: e.match_replace(out=scr, in_to_replace=TS[:, h, 0:8], in_values=candf[:, h, :], imm_value=NEG2), [xn, TS], [z])
                    fw.op("dve", lambda e, h=h: e.max(out=TS[:, h, 8:16], in_=scr), [z], [TS])
                    fw.op("dve", lambda e, h=h: e.max_index(out=PIu[:, h, 8:16], in_max=TS[:, h, 8:16], in_values=scr), [z, TS], [PIu])
                fw.op("dve", lambda e: e.tensor_copy(out=PIf[:], in_=PIu[:]), [PIu], [PIf])
                for h in range(8):
                    for k in range(16):
                        fw.op("dve", lambda e, h=h, k=k: e.scalar_tensor_tensor(out=scr2, in0=iota[:, :], scalar=PIf[:, h, k:k + 1], in1=cidxf[:, h, :],
                                                                                 op0=ALU.is_equal, op1=ALU.mult, accum_out=EI[:, h * 16 + k:h * 16 + k + 1]), [iota, PIf, junk], [z, EI])
                GTv = GT[:].rearrange("p (h k) -> p h k", h=8)
                fw.op("dve", lambda e: e.tensor_tensor(out=GTv, in0=TS[:], in1=TS[:, :, 0:1].to_broadcast([128, 8, 16]), op=ALU.subtract), [TS], [GT])
                fw.op("act", lambda e: e.activation(out=GT[:], in_=GT[:], func=AF.Exp), [GT], [GT])
                fw.op("dve", lambda e: e.tensor_reduce(out=ss[:, 0:8], in_=GTv, axis=AX.X, op=ALU.add), [GT], [ss])
                fw.op("dve", lambda e: e.reciprocal(out=rs[:, 0:8], in_=ss[:, 0:8]), [ss], [rs])
                fw.op("dve", lambda e: e.tensor_tensor(out=GTv, in0=GTv, in1=rs[:, 0:8].unsqueeze(2).to_broadcast([128, 8, 16]), op=ALU.mult), [GT, rs], [GT])
                pe_ = PS()
                fw.op("pe", lambda e, pe_=pe_: e.transpose(out=pe_[:, 0:128], in_=EI[:], identity=ident[:]), [EI, ident], [pe_])
                fw.op("dve", lambda e, pe_=pe_: e.tensor_copy(out=IDX[:], in_=pe_[:, 0:128]), [pe_], [IDX])
                pg = PS()
                fw.op("pe", lambda e, pg=pg: e.transpose(out=pg[:, 0:128], in_=GT[:], identity=ident[:]), [GT, ident], [pg])
                fw.op("act", lambda e, pg=pg: e.copy(out=GTT[:], in_=pg[:, 0:128]), [pg], [GTT])
                for tok in range(128):
                    ug = G16[tok % 4]
                    fw.dma("pool", lambda e, ug=ug, tok=tok, l=l: e.indirect_dma_start(
                        out=ug[:, :], out_offset=None, in_=UB[l][:, :],
                        in_offset=bass.IndirectOffsetOnAxis(ap=IDX[:, tok:tok + 1], axis=0)), [UB[l], IDX], [ug])
                    banks, gap = PSG()
                    for c in range(4):
                        fw.op("pe", lambda e, c=c, tok=tok, banks=banks: e.matmul(banks[c][:, :], lhsT=identb[:, tok:tok + 1].to_broadcast([128, 128]),
                                                                                  rhs=Hhb[:, c * 512:(c + 1) * 512], start=True, stop=True), [identb, Hhb], [banks[c]])
                    fw.op("dve", lambda e, ug=ug, gap=gap, tok=tok: e.scalar_tensor_tensor(out=xn[:], in0=ug[:], scalar=1.0, in1=gap, op0=ALU.mult, op1=ALU.mult,
                                                                                           accum_out=ACTV[:, tok:tok + 1]), [ug] + banks, [xn, ACTV])
                fw.op("act", lambda e: e.activation(out=WT[:], in_=ACTV[:], func=AF.Gelu), [ACTV], [WT])
                fw.op("dve", lambda e: e.tensor_tensor(out=WT[:], in0=WT[:], in1=GTT[:], op=ALU.mult), [WT, GTT], [WT])
                banks, gap = PSG()
                for tok in range(128):
                    vg = G16[tok % 4]
                    zw = ZW[tok % 4]
                    fw.dma("pool", lambda e, vg=vg, tok=tok, l=l: e.indirect_dma_start(
                        out=vg[:, :], out_offset=None, in_=VB[l][:, :],
                        in_offset=bass.IndirectOffsetOnAxis(ap=IDX[:, tok:tok + 1], axis=0)), [VB[l], IDX], [vg])
                    fw.op("act", lambda e, zw=zw, tok=tok: e.copy(out=zw[:, 127:128], in_=WT[:, tok:tok + 1]), [WT], [zw])
                    for c in range(4):
                        fw.op("pe", lambda e, c=c, tok=tok, banks=banks, zw=zw, vg=vg: e.matmul(banks[c][:, :], lhsT=zw[:, 127 - tok:255 - tok], rhs=vg[:, c * 512:(c + 1) * 512],
                                                                                              start=(tok == 0), stop=(tok == 127)), [zw, vg], [banks[c]])
                fw.op("dve", lambda e, gap=gap: e.tensor_tensor(out=xn[:], in0=gap, in1=G2t[:], op=ALU.mult), banks + [G2t], [xn])
                fw.op("dve", lambda e: e.tensor_tensor(out=xt[:], in0=xt[:], in1=xn[:], op=ALU.add), [xt, xn], [xt])
                fw.dma("sp", lambda e, r0=r0: e.dma_start(out=X[r0:r0 + 128, :], in_=xt[:]), [xt], [X])
            alias_out(G16, [B[5], B[6]])

        for l in range(L):
            abrow0, abrow1 = B[8], B[9]
            fw.dma("sp", lambda e, l=l: e.dma_start(out=abT[:], in_=ada_bT[l, :, :]), [ada_bT], [abT])
            fw.dma("sp", lambda e, l=l: e.dma_start(out=abrow0[0:2, :], in_=ada_b[l:l + 1, 2 * D:3 * D].partition_broadcast(2)), [ada_b], [abrow0])
            fw.dma("sp", lambda e, l=l: e.dma_start(out=abrow1[0:2, :], in_=ada_b[l:l + 1, 5 * D:6 * D].partition_broadcast(2)), [ada_b], [abrow1])
            fw.dma("sp", lambda e, l=l: e.dma_start(out=gmix[:], in_=gmixT[l, :, :]), [gmixT], [gmix])
            fw.dma("sp", lambda e, l=l: e.dma_start(out=gffn[:], in_=gffnT[l, :, :]), [gffnT], [gffn])
            fw.dma("sp", lambda e, l=l: e.dma_start(out=gv[:], in_=gvec[l:l + 1, :].partition_broadcast(128)), [gvec], [gv])
            fw.dma("sp", lambda e, l=l: e.dma_start(out=cw[:], in_=conv_wT[l:l + 1, :, :].partition_broadcast(128)), [conv_wT], [cw])
            fw.dma("sp", lambda e, l=l: e.dma_start(out=subk[:], in_=subT_d[l, :, :, :]), [subT_d], [subk])
            Gst = B[7]
            NJB = 6 * D // CBW
            for jb in range(NJB):
                slab = WSL()
                fw.dma("sp", lambda e, l=l, jb=jb, slab=slab: e.dma_start(
                    out=slab[:, :, :], in_=ada_w[l, :, jb * CBW:(jb + 1) * CBW].rearrange("(k p) n -> p k n", p=128)), [ada_w], [slab])
                for sub in range(CBW // 128):
                    j = jb * (CBW // 128) + sub
                    pm = PS()
                    for k in range(16):
                        fw.op("pe", lambda e, k=k, pm=pm, slab=slab, sub=sub: e.matmul(pm[:, 0:2], lhsT=slab[:, k, sub * 128:(sub + 1) * 128], rhs=sT[:, k, :],
                                                                                       start=(k == 0), stop=(k == 15)), [slab, sT], [pm])
                    fw.op("dve", lambda e, j=j, pm=pm: e.tensor_scalar(out=modT[:, j, :], in0=pm[:, 0:2], scalar1=abT[:, j:j + 1], scalar2=None, op0=ALU.add),
                          [pm, abT], [modT])
                c0 = jb * CBW
                which = 0 if 2 * D <= c0 < 3 * D else (1 if c0 >= 5 * D else None)
                if which is not None:
                    off = c0 - (2 * D if which == 0 else 5 * D)
                    abr = abrow0 if which == 0 else abrow1
                    pr = PS()
                    for k in range(16):
                        fw.op("pe", lambda e, k=k, pr=pr, slab=slab: e.matmul(pr[0:2, 0:CBW], lhsT=sT[:, k, :], rhs=slab[:, k, :], start=(k == 0), stop=(k == 15)),
                              [slab, sT], [pr])
                    fw.op("dve", lambda e, pr=pr, abr=abr, off=off: e.tensor_tensor(out=rowb[:], in0=pr[0:2, 0:CBW], in1=abr[0:2, off:off + CBW], op=ALU.add),
                          [pr, abr], [rowb])
                    for grp in range(2):
                        pb = PS()
                        fw.op("pe", lambda e, pb=pb, grp=grp: e.matmul(pb[:, 0:CBW], lhsT=sel[:, grp, :], rhs=rowb[:, :], start=True, stop=True), [sel, rowb], [pb])
                        fw.op("act", lambda e, pb=pb: e.copy(out=Gst[:, 0:CBW], in_=pb[:, 0:CBW]), [pb], [Gst])
                        fw.dma("sp", lambda e, which=which, grp=grp, off=off: e.dma_start(out=GD[which, grp, :, off:off + CBW], in_=Gst[:, 0:CBW]), [Gst], [GD])
            for (A, g, j0) in ((A1, gmix, 16), (A2, gffn, 64)):
                fw.op("dve", lambda e, A=A, j0=j0: e.tensor_scalar(out=A[:], in0=modT[:, j0:j0 + 16, :], scalar1=1.0, scalar2=None, op0=ALU.add), [modT], [A])
                fw.op("dve", lambda e, A=A, g=g: e.tensor_tensor(out=A[:], in0=A[:], in1=g[:].unsqueeze(2).to_broadcast([128, 16, 2]), op=ALU.mult), [A, g], [A])

            xt, xn, junk, t1, t2 = B[0], B[1], B[2], B[3], B[4]
            for t in range(NT):
                grp = 0 if t < NTC else 1
                lt = None if t < NTC else t - NTC
                r0 = t * 128
                fw.dma("sp", lambda e, r0=r0: e.dma_start(out=xt[:], in_=X[r0:r0 + 128, :]), [X], [xt])
                rms_rstd(xt[:], xt, D, 0, junk)
                fw.op("dve", lambda e: e.tensor_scalar(out=xn[:], in0=xt[:], scalar1=rs[:, 0:1], scalar2=None, op0=ALU.mult), [xt, rs], [xn])
                transpose_chunks(lambda k: xn[:, k * 128:(k + 1) * 128], xn, 16, lambda k: hT[:, k, :], hT,
                                 scale=lambda k, grp=grp: A1[:, k, grp:grp + 1], bias=lambda k, grp=grp: modT[:, k, grp:grp + 1])
                gemm(lambda k: hT[:, k, :], hT, 16, w_in, lambda c0, w, l=l: w_in[l, :, c0:c0 + w].rearrange("(k p) n -> p k n", p=128), DIN,
                     lambda p, c0, w: fw.op("act", lambda e, p=p, c0=c0, w=w: e.copy(out=z[:, c0:c0 + w], in_=p[:, 0:w]), [p], [z]))
                rms_rstd(z[:, 0:512], z, 512, 1, junk)
                fw.op("dve", lambda e: e.scalar_tensor_tensor(out=t1[:, 0:512], in0=z[:, 0:512], scalar=rs[:, 1:2], in1=gv[:, 0:512], op0=ALU.mult, op1=ALU.mult),
                      [z, rs, gv], [t1])
                transpose_chunks(lambda k: t1[:, k * 128:(k + 1) * 128], t1, 4, lambda k: tT[:, k, :], tT)
                gemm(lambda k: tT[:, k, :], tT, 4, w_uq, lambda c0, w, l=l: w_uq[l, :, c0:c0 + w].rearrange("(k p) n -> p k n", p=128), 1536,
                     lambda p, c0, w: fw.op("act", lambda e, p=p, c0=c0, w=w: e.copy(out=t2[:, c0:c0 + w], in_=p[:, 0:w]), [p], [t2]))
                heads_norm(t2[:, 0:1536].rearrange("p (h d) -> p h d", h=8), t2, 8, 192, "qh",
                           t1[:, 0:1536].rearrange("p (h d) -> p h d", h=8), t1, junk, extra_scale=192 ** -0.5)
                if lt is not None:
                    rope(t1, 8, lt, junk)
                store_T(t1, 8, 192, QMT, r0)
                rms_rstd(z[:, 512:768], z, 256, 2, junk)
                fw.op("dve", lambda e: e.scalar_tensor_tensor(out=xn[:, 0:256], in0=z[:, 512:768], scalar=rs[:, 2:3], in1=gv[:, 512:768], op0=ALU.mult, op1=ALU.mult),
                      [z, rs, gv], [xn])
                if t < NTC:
                    s_, h_ = t // 2, t % 2
                    fw.dma("pool", lambda e, s_=s_, h_=h_, l=l: e.dma_start(out=o_ckv[s_, l, h_ * 128:(h_ + 1) * 128, :], in_=xn[:, 0:256]), [xn], [o_ckv])
                    fw.dma("pool", lambda e, s_=s_, h_=h_, l=l: e.dma_start(out=o_kpe[s_, l, h_ * 128:(h_ + 1) * 128, :], in_=z[:, 768:832]), [z], [o_kpe])
                    fw.dma("pool", lambda e, s_=s_, h_=h_, l=l: e.dma_start(
                        out=o_nav[s_, l, :, h_ * 128:(h_ + 1) * 128, :].rearrange("h t d -> t h d"),
                        in_=z[:, 1856:2368].rearrange("p (h d) -> p h d", h=4)), [z], [o_nav])
                kv_pipeline(l, xn[:, 0:256], xn, z[:, 768:832], z, r0, lt, t1, t2, junk)
                fw.dma("pool", lambda e, r0=r0: e.dma_start(out=VN[r0:r0 + 128, :], in_=z[:, 1856:2368]), [z], [VN])
                heads_norm(z[:, 832:1344].rearrange("p (h d) -> p h d", h=4), z, 4, 128, "naq",
                           t1[:, 0:512].rearrange("p (h d) -> p h d", h=4), t1, junk, extra_scale=128 ** -0.5)
                store_T(t1, 4, 128, QNT, r0)
                heads_norm(z[:, 1344:1856].rearrange("p (h d) -> p h d", h=4), z, 4, 128, "nak",
                           t2[:, 0:512].rearrange("p (h d) -> p h d", h=4), t2, junk)
                if t < NTC:
                    fw.dma("pool", lambda e, s_=s_, h_=h_, l=l: e.dma_start(
                        out=o_nak[s_, l, :, h_ * 128:(h_ + 1) * 128, :].rearrange("h t d -> t h d"),
                        in_=t2[:, 0:512].rearrange("p (h d) -> p h d", h=4)), [t2], [o_nak])
                store_T(t2, 4, 128, KNT, r0)
                si, cur = seq_of_tile(t)
                fw.op("dve", lambda e: e.tensor_tensor(out=t1[:, 1024:1536], in0=z[:, 2880:3392], in1=z[:, 3392:3904], op=ALU.mult), [z], [t1])
                fw.dma("pool", lambda e, cur=cur: e.dma_start(out=CU[cur:cur + 128, :], in_=t1[:, 1024:1536]), [t1], [CU])
                fw.dma("pool", lambda e, r0=r0: e.dma_start(out=GBs[r0:r0 + 128, :], in_=z[:, 2368:2880]), [z], [GBs])
            if TL:
                for ct in range(4):
                    tok0 = NTOK + ct * 128
                    fw.dma("sp", lambda e, ct=ct, l=l: e.dma_start(out=xn[:, 0:256], in_=c_ckv[l, ct * 128:(ct + 1) * 128, :]), [c_ckv], [xn])
                    fw.dma("sp", lambda e, ct=ct, l=l: e.dma_start(out=xn[:, 256:320], in_=c_kpe[l, ct * 128:(ct + 1) * 128, :]), [c_kpe], [xn])
                    kv_pipeline(l, xn[:, 0:256], xn, xn[:, 256:320], xn, tok0, None, t1, t2, junk)
                    fw.dma("sp", lambda e, ct=ct, l=l: e.dma_start(out=t2[:, 0:512].rearrange("p (h d) -> p h d", h=4),
                                                                   in_=c_nak[l, :, ct * 128:(ct + 1) * 128, :].rearrange("h t d -> t h d")), [c_nak], [t2])
                    store_T(t2, 4, 128, KNT, tok0)
                    fw.dma("pool", lambda e, ct=ct, l=l, tok0=tok0: e.dma_start(out=VN[tok0:tok0 + 128, :].rearrange("t (h d) -> t h d", h=4),
                                                                                 in_=c_nav[l, :, ct * 128:(ct + 1) * 128, :].rearrange("h t d -> t h d")), [c_nav], [VN])
            if cfg.stages <= 1:
                continue

            mix, Ssb, Vs, PTs, xt2, G1t = B[0], B[1], B[3], B[4], B[5], B[6]
            cb_ = B[7]
            junk = B[2]
            if TL:
                fw.dma("sp", lambda e, l=l: e.dma_start(out=relTs[:], in_=relT_d[l, :, :, :]), [relT_d], [relTs])
                fw.op("pool", lambda e: e.memset(Bt[:, 576:1088], 0.0), [], [Bt])

            def na_bias(h):
                pbk = [PS(), PS()]
                for sl in range(4):
                    slab = WSL()
                    ov = slab[0:31, 0:8, :].rearrange("p a (b c) -> p (a b) c", c=128)
                    fw.dma("sp", lambda e, sl=sl, ov=ov: e.dma_start(out=ov, in_=OH_d[:, sl * 16:(sl + 1) * 16, :]), [OH_d], [slab])
                    for kci in range(16):
                        kc = sl * 16 + kci
                        pb = pbk[kc // 32]
                        fw.op("pe", lambda e, pb=pb, kc=kc, kci=kci, ov=ov: e.matmul(pb[:, (kc % 32) * 15:(kc % 32) * 15 + 15], lhsT=ov[:, kci, :], rhs=relTs[:, h, :],
                                                                                  start=True, stop=True), [slab, relTs], [pb])
                for half in range(2):
                    pb = pbk[half]
                    fw.op("dve", lambda e, pb=pb, half=half: e.tensor_tensor(
                        out=CB2[:, :, half * 32:(half + 1) * 32], in0=pb[:, 0:480].rearrange("p (k a) -> p a k", a=15),
                        in1=cm[:, half * 32:(half + 1) * 32].unsqueeze(1).to_broadcast([128, 15, 32]), op=ALU.add), [pb, cm], [CB2])

            def phaseB_tail(t):
                grp = 0 if t < NTC else 1
                r0 = t * 128
                si, cur = seq_of_tile(t)
                fw.dma("sp", lambda e: e.dma_start(out=cb_[:, 0:512], in_=CU[cur - 1:cur + 127, :]), [CU], [cb_])
                fw.dma("sp", lambda e: e.dma_start(out=cb_[:, 1024:1536], in_=CU[cur:cur + 128, :]), [CU], [cb_])
                fw.dma("sp", lambda e: e.dma_start(out=cb_[:, 1536:2048], in_=CU[cur + 1:cur + 129, :]), [CU], [cb_])
                fw.dma("sp", lambda e: e.dma_start(out=cb_[:, 512:1024], in_=GBs[r0:r0 + 128, :]), [GBs], [cb_])
                fw.op("dve", lambda e: e.tensor_tensor(out=cb_[:, 0:512], in0=cb_[:, 0:512], in1=cw[:, 0, :], op=ALU.mult), [cb_, cw], [cb_])
                fw.op("dve", lambda e: e.tensor_tensor(out=cb_[:, 1024:1536], in0=cb_[:, 1024:1536], in1=cw[:, 1, :], op=ALU.mult), [cb_, cw], [cb_])
                fw.op("dve", lambda e: e.tensor_tensor(out=cb_[:, 1536:2048], in0=cb_[:, 1536:2048], in1=cw[:, 2, :], op=ALU.mult), [cb_, cw], [cb_])
                fw.op("dve", lambda e: e.tensor_tensor(out=cb_[:, 1024:1536], in0=cb_[:, 1024:1536], in1=cb_[:, 0:512], op=ALU.add), [cb_], [cb_])
                fw.op("dve", lambda e: e.tensor_tensor(out=cb_[:, 1024:1536], in0=cb_[:, 1024:1536], in1=cb_[:, 1536:2048], op=ALU.add), [cb_], [cb_])
                fw.op("dve", lambda e: e.tensor_tensor(out=mix[:, 1536:2048], in0=cb_[:, 1024:1536], in1=cb_[:, 512:1024], op=ALU.mult), [cb_], [mix])
                transpose_chunks(lambda k: mix[:, k * 128:(k + 1) * 128], mix, 16, lambda k: hT[:, k, :], hT)
                fw.dma("sp", lambda e: e.dma_start(out=xt2[:], in_=X[r0:r0 + 128, :]), [X], [xt2])
                fw.dma("sp", lambda e: e.dma_start(out=G1t[:], in_=GD[0, grp, :, :]), [GD], [G1t])

                def ofn(p, c0, w):
                    fw.op("dve", lambda e, p=p, c0=c0, w=w: e.tensor_tensor(out=junk[:, c0:c0 + w], in0=p[:, 0:w], in1=G1t[:, c0:c0 + w], op=ALU.mult), [p, G1t], [junk])
                    fw.op("dve", lambda e, c0=c0, w=w: e.tensor_tensor(out=xt2[:, c0:c0 + w], in0=xt2[:, c0:c0 + w], in1=junk[:, c0:c0 + w], op=ALU.add), [xt2, junk], [xt2])
                gemm(lambda k: hT[:, k, :], hT, 16, w_out, lambda c0, w, l=l: w_out[l, :, c0:c0 + w].rearrange("(k p) n -> p k n", p=128), D, ofn)
                fw.dma("sp", lambda e: e.dma_start(out=X[r0:r0 + 128, :], in_=xt2[:]), [xt2], [X])
            if TL:
                for h in range(4):
                    na_bias(h)
                    fw.dma("sp", lambda e, h=h: e.dma_start(out=CBD[h, :, :], in_=CB2[:].rearrange("p a k -> p (a k)")), [CB2], [CBD])
            for t in range(NT):
                r0 = t * 128
                if t < NTC:
                    sq0 = (t // 2) * 256
                    kb_m = [(sq0, 128), (sq0 + 128, 128)]
                    for h in range(8):
                        attention(QMT, KMT, VM, 1024, h, 192, 128, r0, kb_m, mix, h * 128, Ssb, PTs, Vs)
                    for h in range(4):
                        attention(QNT, KNT, VN, 512, h, 128, 128, r0, kb_m, mix, 1024 + h * 128, Ssb, PTs, Vs)
                else:
                    l0 = NTC * 128
                    kb_c = [(NTOK + i * 128, 128) for i in range(4)]
                    kb_m = kb_c + [(l0 + i * 128, 128) for i in range(NTL)]
                    for h in range(8):
                        attention(QMT, KMT, VM, 1024, h, 192, 128, r0, kb_m, mix, h * 128, Ssb, PTs, Vs)
                    rows = TL // 64
                    rg = 2 * (t - NTC)
                    rsf = lambda r: min(max(r - 4, 0), rows - 8)
                    ks = min(rsf(rg), rows - 9)
                    kb_l = [(l0 + ks * 64 + i * 128, 128) for i in range(4)] + [(l0 + ks * 64 + 512, 64)]
                    for h in range(4):
                        fw.op("pool", lambda e: e.memset(Bt[:, 0:576], NEG), [], [Bt])
                        for dr in range(2):
                            r = rg + dr
                            j0 = rsf(r) - ks
                            a0 = rsf(r) - r + 7
                            fw.dma("sp", lambda e, h=h, dr=dr, j0=j0, a0=a0: e.dma_start(out=Bt[dr * 64:(dr + 1) * 64, j0 * 64:(j0 + 8) * 64],
                                                                                          in_=CBD[h, dr * 64:(dr + 1) * 64, a0 * 64:(a0 + 8) * 64]), [CBD], [Bt])
                        attention(QNT, KNT, VN, 512, h, 128, 128, r0, kb_l + kb_c, mix, 1024 + h * 128, Ssb, PTs, Vs, bias=Bt)
                phaseB_tail(t)
            if cfg.stages <= 2:
                continue
            peer_phase(l)
        for t in range(NT):
            fw.dma("pool", lambda e, t=t: e.dma_start(out=xout[t * 128:(t + 1) * 128, :], in_=X[t * 128:(t + 1) * 128, :]), [X], [xout])
        fw.finish()
        fw.emit()
        print("instructions:", fw.ninst)
    return nc


def host_inputs(cfg, core, inp):
    L, NS = cfg.depth, cfg.nseq
    f = np.float32
    b = core // 4
    xs = [np.asarray(inp["x_prompt"][core * NS:(core + 1) * NS], f).reshape(NS * 256, D)]
    if cfg.lat:
        xs.append(np.asarray(inp["x_sample"][b], f)[:cfg.lat])
    m = {}
    m["xin"] = np.ascontiguousarray(np.concatenate(xs, 0))
    cv = np.stack([np.asarray(inp["c_ctx"], f), np.asarray(inp["c"][b], f)], 0)
    m["cvecT"] = np.ascontiguousarray(cv.reshape(2, 16, 128).transpose(2, 1, 0))
    m["ada_w"] = np.asarray(inp["ada_w"][:L], f)
    m["ada_b"] = np.asarray(inp["ada_b"][:L], f)
    m["ada_bT"] = np.ascontiguousarray(np.asarray(inp["ada_b"][:L], f).reshape(L, 96, 128).transpose(0, 2, 1))
    m["gmixT"] = np.ascontiguousarray(np.asarray(inp["norm_mix_g"][:L], f).reshape(L, 16, 128).transpose(0, 2, 1))
    m["gffnT"] = np.ascontiguousarray(np.asarray(inp["norm_ffn_g"][:L], f).reshape(L, 16, 128).transpose(0, 2, 1))
    m["w_in"] = np.asarray(inp["w_in"][:L], f)
    m["w_uq"] = np.asarray(inp["mla_w_uq"][:L], f)
    m["w_ukv"] = np.asarray(inp["mla_w_ukv"][:L], f)
    m["w_out"] = np.asarray(inp["w_out"][:L], f)
    m["gvec"] = np.ascontiguousarray(np.concatenate([np.asarray(inp[k][:L], f) for k in
                                                     ("mla_q_norm_g", "mla_kv_norm_g", "mla_q_head_g", "mla_k_head_g", "na_q_head_g", "na_k_head_g")], 1))
    m["conv_wT"] = np.ascontiguousarray(np.asarray(inp["conv_w"][:L], f).transpose(0, 2, 1))
    m["ident"] = np.eye(128, dtype=f)
    sel = np.zeros((2, 2, 128), f)
    sel[0, 0, :] = 1
    sel[1, 1, :] = 1
    m["sel"] = sel
    m["peer_wq"] = np.asarray(inp["peer_w_q"][:L], f)
    m["iota"] = np.ascontiguousarray(np.broadcast_to(np.arange(256, dtype=f)[None, :], (128, 256)))
    m["subT"] = np.ascontiguousarray(np.asarray(inp["peer_sub_keys"][:L], f).transpose(0, 4, 1, 2, 3).reshape(L, 128, 16, 128))
    for i in range(L):
        m["peer_u%d" % i] = np.asarray(inp["peer_u"][i], f)
        m["peer_v%d" % i] = np.asarray(inp["peer_v"][i], f)
    if cfg.lat:
        TL = cfg.lat
        tt = np.arange(TL)
        row = (tt // 64).astype(f)
        col = (tt % 64).astype(f)
        inv = (np.float32(10000.0) ** (-np.arange(16, dtype=f) / np.float32(16))).astype(f)
        ang = np.concatenate([row[:, None] * inv, col[:, None] * inv], -1).astype(f)
        m["rope"] = np.ascontiguousarray(np.concatenate([np.cos(ang), np.sin(ang)], -1).astype(f))
        m["c_ckv"] = np.ascontiguousarray(np.asarray(inp["cache_mla_ckv"][b, :L], f))
        m["c_kpe"] = np.ascontiguousarray(np.asarray(inp["cache_mla_kpe"][b, :L], f))
        m["c_nak"] = np.ascontiguousarray(np.asarray(inp["cache_na_k"][b, :L], f))
        m["c_nav"] = np.ascontiguousarray(np.asarray(inp["cache_na_v"][b, :L], f))
        m["relT"] = np.ascontiguousarray(np.asarray(inp["na_rel_bias"][:L], f).transpose(0, 3, 1, 2))
        OH = np.zeros((31, 64, 128), f)
        cmk = np.full((128, 64), NEG, f)
        for qc in range(64):
            cs = min(max(qc - 8, 0), 48)
            for kc in range(64):
                bb = kc - qc + 15
                if 0 <= bb < 31:
                    OH[bb, kc, qc] = 1
                    OH[bb, kc, 64 + qc] = 1
                if cs <= kc < cs + 16:
                    cmk[qc, kc] = 0
                    cmk[64 + qc, kc] = 0
        m["OH"] = OH
        m["cmask"] = cmk
    return m


_NC_CACHE = {}


def kernel(**inp):
    cfg = Cfg()
    n = 8
    if "nc" not in _NC_CACHE:
        _NC_CACHE["nc"] = build(cfg)
    nc = _NC_CACHE["nc"]
    in_maps = [host_inputs(cfg, c, inp) for c in range(n)]
    res = run_bass_kernel_spmd(nc, in_maps, core_ids=list(range(n)))
    R = res.results
    NS, L = cfg.nseq, cfg.depth
    y_prompt = np.concatenate([R[c]["xout"][:NS * 256].reshape(NS, 256, D) for c in range(n)], 0)
    y_sample = np.stack([np.concatenate([R[b * 4 + q]["xout"][NS * 256 + q * 256:NS * 256 + (q + 1) * 256] for q in range(4)], 0) for b in range(2)], 0)
    ckv = np.concatenate([R[c]["o_ckv"] for c in range(n)], 0)
    kpe = np.concatenate([R[c]["o_kpe"] for c in range(n)], 0)
    nak = np.concatenate([R[c]["o_nak"] for c in range(n)], 0)
    nav = np.concatenate([R[c]["o_nav"] for c in range(n)], 0)
    return (y_prompt.astype(np.float32), y_sample.astype(np.float32), ckv, kpe, nak, nav)
```

```python
from contextlib import ExitStack
import numpy as np
import concourse.bass as bass
import concourse.mybir as mybir
from concourse.bass_utils import run_bass_kernel_spmd

F32 = mybir.dt.float32
I32 = mybir.dt.int32
AF = mybir.ActivationFunctionType
ALU = mybir.AluOpType
AX = mybir.AxisListType

D = 2048
DIN = 3904
NEG = -30000.0


class Buf:
    __slots__ = ("t", "lw", "rd", "name")

    def __init__(self, t, name=""):
        self.t = t
        self.lw = None
        self.rd = {}
        self.name = name

    def __getitem__(self, k):
        return self.t[k]


class FW:
    ENG = ("pe", "act", "dve", "pool", "sp")
    NDMA = 6

    def __init__(self, nc, es, same_engine_sync=True):
        self.nc = nc
        self.es = es
        self.same = same_engine_sync
        self.sem = {}
        self.cnt = {}
        for e in self.ENG:
            self.sem[e] = es.enter_context(nc.semaphore("s_" + e))
            self.cnt[e] = 0
        self.dq = {}
        for q in ("sp", "pool", "act"):
            slots = []
            for i in range(self.NDMA):
                k = "d_%s%d" % (q, i)
                self.sem[k] = es.enter_context(nc.semaphore(k))
                self.cnt[k] = 0
                slots.append(k)
            self.dq[q] = [slots, 0]
        self.seen = {e: {} for e in self.ENG}
        self.prog = {e: [] for e in self.ENG}
        self.ninst = 0
        self._psi = 0

    def sb(self, name, shape, dt=F32):
        t = self.es.enter_context(self.nc.sbuf_tensor("sb_" + name, list(shape), dt))
        return Buf(t, name)

    def ps(self, name, shape, dt=F32):
        t = self.es.enter_context(self.nc.psum_tensor(name, list(shape), dt))
        return Buf(t, name)

    def dram(self, name, shape, dt=F32, kind="Internal"):
        t = self.nc.dram_tensor(name, list(shape), dt, kind=kind)
        return Buf(t, name)

    def _wait(self, eng, key, val):
        if key == eng and not self.same:
            return
        if self.seen[eng].get(key, 0) >= val:
            return
        self.prog[eng].append(("w", key, val))
        self.seen[eng][key] = val

    def _deps(self, eng, reads, writes):
        for b in reads:
            if b.lw is not None:
                self._wait(eng, *b.lw)
        for b in writes:
            if b.lw is not None:
                self._wait(eng, *b.lw)
            for k, v in b.rd.items():
                self._wait(eng, k, v)

    def _mark(self, key, val, reads, writes):
        for b in reads:
            if b.rd.get(key, 0) < val:
                b.rd[key] = val
        for b in writes:
            b.lw = (key, val)
            b.rd = {}

    def op(self, eng, fn, reads=(), writes=()):
        self._deps(eng, reads, writes)
        self.cnt[eng] += 1
        self.prog[eng].append(("i", fn, eng, 1))
        self._mark(eng, self.cnt[eng], reads, writes)
        self.ninst += 1

    def dma(self, q, fn, reads=(), writes=()):
        slots, i = self.dq[q]
        k = slots[i % len(slots)]
        self.dq[q][1] = i + 1
        if self.cnt[k] > 0:
            self._wait(q, k, self.cnt[k])
        self._deps(q, reads, writes)
        self.cnt[k] += 16
        self.prog[q].append(("i", fn, k, 16))
        self._mark(k, self.cnt[k], reads, writes)
        self.ninst += 1

    def finish(self):
        for e in self.ENG:
            if e != "sp" and self.cnt[e] > 0:
                self._wait("sp", e, self.cnt[e])
        for q in self.dq:
            for k in self.dq[q][0]:
                if self.cnt[k] > 0:
                    self._wait("sp", k, self.cnt[k])

    def emit(self):
        nc = self.nc
        prog = self.prog
        sem = self.sem

        def run(e, items):
            for it in items:
                if it[0] == "w":
                    e.wait_ge(sem[it[1]], it[2])
                else:
                    it[1](e).then_inc(sem[it[2]], it[3])

        with nc.Block() as block:
            @block.tensor
            def _(e):
                run(e, prog["pe"])

            @block.scalar
            def _(e):
                run(e, prog["act"])

            @block.vector
            def _(e):
                run(e, prog["dve"])

            @block.gpsimd
            def _(e):
                run(e, prog["pool"])

            @block.sync
            def _(e):
                run(e, prog["sp"])


class Cfg:
    def __init__(self, depth=4, nseq=4, lat=1024, stages=99):
        self.depth = depth
        self.nseq = nseq
        self.lat = lat
        self.stages = stages


GOFF = {"qn": 0, "kvn": 512, "qh": 768, "kh": 960, "naq": 1152, "nak": 1280}
GLEN = 1408


CBW = 512
ABW = 256
NEG2 = -1.0e30
U32 = mybir.dt.uint32
BF16 = mybir.dt.bfloat16


def build(cfg):
    L = cfg.depth
    NS = cfg.nseq
    NTC = NS * 2
    TL = cfg.lat
    NTL = TL // 128
    NT = NTC + NTL
    NTOK = NT * 128
    NCACHE = 512 if TL else 0
    nc = bass.Bass("TRN2", target_bir_lowering=False)

    def din(name, shape, dt=F32):
        return Buf(nc.dram_tensor(name, list(shape), dt, kind="ExternalInput"), name)

    def dout(name, shape, dt=F32):
        return Buf(nc.dram_tensor(name, list(shape), dt, kind="ExternalOutput"), name)

    xin = din("xin", [NTOK, D])
    cvecT = din("cvecT", [128, 16, 2])
    ada_w = din("ada_w", [L, D, 6 * D])
    ada_bT = din("ada_bT", [L, 128, 96])
    ada_b = din("ada_b", [L, 6 * D])
    gmixT = din("gmixT", [L, 128, 16])
    gffnT = din("gffnT", [L, 128, 16])
    w_in = din("w_in", [L, D, DIN])
    w_uq = din("w_uq", [L, 512, 1536])
    w_ukv = din("w_ukv", [L, 256, 2048])
    w_out = din("w_out", [L, D, D])
    gvec = din("gvec", [L, GLEN])
    conv_wT = din("conv_wT", [L, 3, 512])
    ident_d = din("ident", [128, 128])
    sel_d = din("sel", [2, 2, 128])
    peer_wq = din("peer_wq", [L, D, D])
    subT_d = din("subT", [L, 128, 16, 128])
    peer_u = [din("peer_u%d" % i, [16384, D]) for i in range(L)]
    peer_v = [din("peer_v%d" % i, [16384, D]) for i in range(L)]
    iota_d = din("iota", [128, 256])
    if TL:
        rope_d = din("rope", [TL, 64])
        c_ckv = din("c_ckv", [L, 512, 256])
        c_kpe = din("c_kpe", [L, 512, 64])
        c_nak = din("c_nak", [L, 4, 512, 128])
        c_nav = din("c_nav", [L, 4, 512, 128])
        relT_d = din("relT", [L, 31, 4, 15])
        OH_d = din("OH", [31, 64, 128])
        cm_d = din("cmask", [128, 64])

    xout = dout("xout", [NTOK, D])
    o_ckv = dout("o_ckv", [NS, L, 256, 256])
    o_kpe = dout("o_kpe", [NS, L, 256, 64])
    o_nak = dout("o_nak", [NS, L, 4, 256, 128])
    o_nav = dout("o_nav", [NS, L, 4, 256, 128])

    with ExitStack() as es:
        fw = FW(nc, es)
        NKT = NTOK + NCACHE
        X = fw.dram("Xs", [NTOK, D])
        QMT = fw.dram("QMT", [8, 192, NTOK])
        KMT = fw.dram("KMT", [8, 192, NKT])
        VM = fw.dram("VM", [NKT, 1024])
        QNT = fw.dram("QNT", [4, 128, NTOK])
        KNT = fw.dram("KNT", [4, 128, NKT])
        VN = fw.dram("VN", [NKT, 512])
        NSEQ = NS + (1 if TL else 0)
        CU = fw.dram("CU", [NTOK + 2 * NSEQ, 512])
        GBs = fw.dram("GBs", [NTOK, 512])
        GD = fw.dram("GD", [2, 2, 128, D])
        CBD = fw.dram("CBD", [4, 128, 15 * 64])

        def seq_of_tile(t):
            if t < NTC:
                si = t // 2
            else:
                si = NS
            return si, t * 128 + 2 * si + 1

        pbig = fw.ps("pbig", [128, 4096])
        psb = [Buf(pbig.t[:, i * 512:(i + 1) * 512], "ps%d" % i) for i in range(8)]
        grp_i = [0]

        ps_hi = [False]

        def PS():
            if ps_hi[0]:
                b = psb[4 + fw._psi % 4]
            else:
                b = psb[fw._psi % 8]
            fw._psi += 1
            return b

        def PSG():
            g = grp_i[0] % 2
            grp_i[0] += 1
            return psb[g * 4:(g + 1) * 4], pbig.t[:, g * 2048:(g + 1) * 2048]

        ident = fw.sb("ident", [128, 128])
        sel = fw.sb("sel", [2, 2, 128])
        fw.dma("sp", lambda e: e.dma_start(out=ident[:], in_=ident_d[:, :]), [ident_d], [ident])
        fw.dma("sp", lambda e: e.dma_start(out=sel[:], in_=sel_d[:, :, :]), [sel_d], [sel])
        sT = fw.sb("sT", [128, 16, 2])
        fw.dma("sp", lambda e: e.dma_start(out=sT[:], in_=cvecT[:, :, :]), [cvecT], [sT])
        fw.op("act", lambda e: e.activation(out=sT[:], in_=sT[:], func=AF.Silu), [sT], [sT])

        for t in range(NT):
            fw.dma("pool", lambda e, t=t: e.dma_start(out=X[t * 128:(t + 1) * 128, :], in_=xin[t * 128:(t + 1) * 128, :]), [xin], [X])

        B = [fw.sb("B%d" % i, [128, D]) for i in range(10)]
        Hhb = fw.sb("Hhb", [128, D], BF16)
        identb = fw.sb("identb", [128, 128], BF16)
        G16 = [Buf(B[5 + i // 2].t[:, (i % 2) * 1024:(i % 2) * 1024 + 1024].bitcast(BF16), "g16_%d" % i) for i in range(4)]

        def alias_in(views, bases):
            for i, v in enumerate(views):
                b = bases[i // 2]
                v.lw = b.lw
                v.rd = dict(b.rd)

        def alias_out(views, bases):
            for i, v in enumerate(views):
                b = bases[i // 2]
                if v.lw is not None:
                    b.rd[v.lw[0]] = max(b.rd.get(v.lw[0], 0), v.lw[1])
                for kk, vv in v.rd.items():
                    b.rd[kk] = max(b.rd.get(kk, 0), vv)

        Wb = {"in": [fw.dram("Wb_in%d" % i, [D, DIN], BF16) for i in range(L)],
              "uq": [fw.dram("Wb_uq%d" % i, [512, 1536], BF16) for i in range(L)],
              "ukv": [fw.dram("Wb_ukv%d" % i, [256, 2048], BF16) for i in range(L)],
              "out": [fw.dram("Wb_out%d" % i, [D, D], BF16) for i in range(L)],
              "pq": [fw.dram("Wb_pq%d" % i, [D, D], BF16) for i in range(L)]}
        alias_in(G16, [B[5], B[6]])
        itw = 0
        for i in range(L):
            for (srcb, key, R_, C_) in ((w_in, "in", D, DIN), (w_uq, "uq", 512, 1536), (w_ukv, "ukv", 256, 2048), (w_out, "out", D, D), (peer_wq, "pq", D, D)):
                dstb = Wb[key][i]
                for r in range(R_ // 128):
                    c0 = 0
                    while c0 < C_:
                        w = min(2048, C_ - c0)
                        fb = B[itw % 4]
                        gb = G16[itw % 4]
                        fw.dma("sp", lambda e, srcb=srcb, i=i, r=r, c0=c0, w=w, fb=fb: e.dma_start(out=fb[:, 0:w], in_=srcb[i, r * 128:(r + 1) * 128, c0:c0 + w]), [srcb], [fb])
                        if itw % 2 == 0:
                            fw.op("act", lambda e, fb=fb, gb=gb, w=w: e.copy(out=gb[:, 0:w], in_=fb[:, 0:w]), [fb], [gb])
                        else:
                            fw.op("dve", lambda e, fb=fb, gb=gb, w=w: e.tensor_copy(out=gb[:, 0:w], in_=fb[:, 0:w]), [fb], [gb])
                        fw.dma("pool", lambda e, dstb=dstb, r=r, c0=c0, w=w, gb=gb: e.dma_start(out=dstb[r * 128:(r + 1) * 128, c0:c0 + w], in_=gb[:, 0:w]), [gb], [dstb])
                        itw += 1
                        c0 += w
        alias_out(G16, [B[5], B[6]])
        UB = [fw.dram("UB%d" % i, [16384, D], BF16) for i in range(L)]
        VB = [fw.dram("VB%d" % i, [16384, D], BF16) for i in range(L)]
        if cfg.stages > 2:
            alias_in(G16, [B[5], B[6]])
            it = 0
            for (src, dst) in [(peer_u[i], UB[i]) for i in range(L)] + [(peer_v[i], VB[i]) for i in range(L)]:
                for c in range(128):
                    fb = B[it % 4]
                    gb = G16[it % 4]
                    fw.dma("sp", lambda e, src=src, c=c, fb=fb: e.dma_start(out=fb[:], in_=src[c * 128:(c + 1) * 128, :]), [src], [fb])
                    if it % 2 == 0:
                        fw.op("act", lambda e, fb=fb, gb=gb: e.copy(out=gb[:], in_=fb[:]), [fb], [gb])
                    else:
                        fw.op("dve", lambda e, fb=fb, gb=gb: e.tensor_copy(out=gb[:], in_=fb[:]), [fb], [gb])
                    fw.dma("pool", lambda e, dst=dst, c=c, gb=gb: e.dma_start(out=dst[c * 128:(c + 1) * 128, :], in_=gb[:]), [gb], [dst])
                    it += 1
            alias_out(G16, [B[5], B[6]])
        fw.op("dve", lambda e: e.tensor_copy(out=identb[:], in_=ident[:]), [ident], [identb])
        wsl = [fw.sb("wsl%d" % i, [128, 16, CBW], BF16) for i in range(2)]
        wi = [0]

        def WSL():
            b = wsl[wi[0] % 2]
            wi[0] += 1
            return b

        modT = fw.sb("modT", [128, 96, 2])
        abT = fw.sb("abT", [128, 96])
        rowb = fw.sb("rowb", [2, ABW])
        rowr = fw.sb("rowr", [2, ABW])
        gmix = fw.sb("gmix", [128, 16])
        gffn = fw.sb("gffn", [128, 16])
        A1 = fw.sb("A1", [128, 16, 2])
        A2 = fw.sb("A2", [128, 16, 2])
        gv = fw.sb("gv", [128, GLEN])
        cw = fw.sb("cw", [128, 3, 512])
        hT = fw.sb("hT", [128, 16, 128], BF16)
        z = fw.sb("z", [128, DIN])
        ss = fw.sb("ss", [128, 16])
        rs = fw.sb("rs", [128, 16])
        tT = fw.sb("tT", [128, 4, 128])
        tTb = fw.sb("tTb", [128, 4, 128], BF16)
        stg = B[5].t[:, :].rearrange("p (a h t) -> p a h t", a=2, h=8)
        stgb = B[5]
        kTs = fw.sb("kTs", [128, 2, 1536])
        qTs = fw.sb("qTs", [128, 2, 128])
        mx = fw.sb("mx", [128, 4])
        subk = fw.sb("subk", [128, 16, 128])
        TV = fw.sb("TV", [128, 16, 16])
        TIu = fw.sb("TIu", [128, 16, 16], U32)
        TIf = fw.sb("TIf", [128, 16, 16])
        TS = fw.sb("TS", [128, 8, 16])
        PIu = fw.sb("PIu", [128, 8, 16], U32)
        PIf = fw.sb("PIf", [128, 8, 16])
        iota = fw.sb("iota", [128, 256])
        fw.dma("sp", lambda e: e.dma_start(out=iota[:], in_=iota_d[:, :]), [iota_d], [iota])
        EI = fw.sb("EI", [128, 128])
        GT = fw.sb("GT", [128, 128])
        IDX = fw.sb("IDX", [128, 128], I32)
        IDX2 = fw.sb("IDX2", [128, 128], I32)
        GTT = fw.sb("GTT", [128, 128])
        GTT2 = fw.sb("GTT2", [128, 128])
        ACTV = fw.sb("ACTV", [128, 128])
        ACTV2 = fw.sb("ACTV2", [128, 128])
        xbs = [fw.sb("xbs%d" % i, [128, 512], BF16) for i in range(2)]
        ptmp = fw.sb("ptmp", [128, 512])
        ptmp2 = fw.sb("ptmp2", [128, 512])
        WT = fw.sb("WT", [128, 128])
        ZW = [fw.sb("ZW%d" % i, [128, 256], BF16) for i in range(4)]
        for i in range(4):
            fw.op("pool", lambda e, i=i: e.memset(ZW[i][:], 0.0), [], [ZW[i]])
        zrow = fw.sb("zrow", [1, 512])
        fw.op("pool", lambda e: e.memset(zrow[:], 0.0), [], [zrow])
        for si in range(NSEQ):
            st = si * 256 if si < NS else NS * 256
            ln = 256 if si < NS else TL
            for rr in (st + 2 * si, st + 2 * si + 1 + ln):
                fw.dma("pool", lambda e, rr=rr: e.dma_start(out=CU[rr:rr + 1, :], in_=zrow[:, :]), [zrow], [CU])
        if TL:
            ropet = fw.sb("ropet", [128, NTL, 64])
            fw.dma("sp", lambda e: e.dma_start(out=ropet[:], in_=rope_d.t.ap().rearrange("(n p) c -> p n c", p=128)), [rope_d], [ropet])
            relTs = fw.sb("relTs", [31, 4, 15])
            cm = fw.sb("cm", [128, 64])
            fw.dma("sp", lambda e: e.dma_start(out=cm[:], in_=cm_d[:, :]), [cm_d], [cm])
            CB2 = fw.sb("CB2", [128, 15, 64])
            Bt = B[8]

        def rms_rstd(src_ap, srcbuf, n, col, junk):
            fw.op("act", lambda e: e.activation(out=junk[:, 0:n], in_=src_ap, func=AF.Square, scale=float(n ** -0.5),
                                                accum_out=ss[:, col:col + 1]), [srcbuf], [junk, ss])
            fw.op("act", lambda e: e.activation(out=rs[:, col:col + 1], in_=ss[:, col:col + 1], func=AF.Sqrt, bias=1e-6, scale=1.0), [ss], [rs])
            fw.op("dve", lambda e: e.reciprocal(out=rs[:, col:col + 1], in_=rs[:, col:col + 1]), [rs], [rs])

        def heads_norm(src_ap, srcbuf, nh, dh, gname, dst_ap, dstbuf, junk, extra_scale=1.0):
            jv = junk[:, 0:nh * dh].rearrange("p (h d) -> p h d", h=nh)
            fw.op("dve", lambda e: e.tensor_tensor(out=jv, in0=src_ap, in1=src_ap, op=ALU.mult), [srcbuf], [junk])
            fw.op("dve", lambda e: e.tensor_reduce(out=ss[:, 0:nh], in_=jv, axis=AX.X, op=ALU.add), [junk], [ss])
            fw.op("act", lambda e: e.activation(out=rs[:, 0:nh], in_=ss[:, 0:nh], func=AF.Sqrt, bias=1e-6, scale=float(1.0 / dh)), [ss], [rs])
            fw.op("dve", lambda e: e.reciprocal(out=rs[:, 0:nh], in_=rs[:, 0:nh]), [rs], [rs])
            fw.op("dve", lambda e: e.tensor_tensor(out=dst_ap, in0=src_ap, in1=rs[:, 0:nh].unsqueeze(2).to_broadcast([128, nh, dh]), op=ALU.mult),
                  [srcbuf, rs], [dstbuf])
            g0 = GOFF[gname]
            fw.op("dve", lambda e: e.scalar_tensor_tensor(out=dst_ap, in0=dst_ap, scalar=float(extra_scale),
                                                          in1=gv[:, g0:g0 + dh].unsqueeze(1).to_broadcast([128, nh, dh]), op0=ALU.mult, op1=ALU.mult),
                  [dstbuf, gv], [dstbuf])

        def rope(buf, nh, lt, junk):
            v = buf[:, 0:nh * 192].rearrange("p (h d) -> p h d", h=nh)
            x1, x2 = v[:, :, 128:160], v[:, :, 160:192]
            cs = ropet[:, lt, 0:32].unsqueeze(1).to_broadcast([128, nh, 32])
            sn = ropet[:, lt, 32:64].unsqueeze(1).to_broadcast([128, nh, 32])
            j = junk[:, 0:nh * 128].rearrange("p (h d) -> p h d", h=nh)
            a, b_, c, d_ = j[:, :, 0:32], j[:, :, 32:64], j[:, :, 64:96], j[:, :, 96:128]
            fw.op("dve", lambda e: e.tensor_tensor(out=a, in0=x1, in1=cs, op=ALU.mult), [buf, ropet], [junk])
            fw.op("dve", lambda e: e.tensor_tensor(out=b_, in0=x2, in1=sn, op=ALU.mult), [buf, ropet], [junk])
            fw.op("dve", lambda e: e.tensor_tensor(out=c, in0=x1, in1=sn, op=ALU.mult), [buf, ropet], [junk])
            fw.op("dve", lambda e: e.tensor_tensor(out=d_, in0=x2, in1=cs, op=ALU.mult), [buf, ropet], [junk])
            fw.op("dve", lambda e: e.tensor_tensor(out=x1, in0=a, in1=b_, op=ALU.subtract), [junk], [buf])
            fw.op("dve", lambda e: e.tensor_tensor(out=x2, in0=c, in1=d_, op=ALU.add), [junk], [buf])

        def transpose_chunks(src_ap_fn, srcbuf, nch, dst_fn, dstbuf, width=128, scale=None, bias=None, rows=128):
            for k in range(nch):
                p = PS()
                fw.op("pe", lambda e, k=k, p=p: e.transpose(out=p[0:width, 0:rows], in_=src_ap_fn(k), identity=ident[0:rows, 0:rows]), [srcbuf, ident], [p])
                if scale is None:
                    fw.op("act", lambda e, k=k, p=p: e.copy(out=dst_fn(k), in_=p[0:width, 0:rows]), [p], [dstbuf])
                else:
                    sc, bi = scale(k), bias(k)
                    fw.op("act", lambda e, k=k, p=p, sc=sc, bi=bi: e.activation(out=dst_fn(k), in_=p[0:width, 0:rows], func=AF.Identity,
                                                                                 bias=bi, scale=sc), [p, A1, A2, modT], [dstbuf])

        def gemm(lhsT_fn, lbuf, nk, Wd, w_ap_fn, ncols, out_fn, kparts=128, c_lo=0):
            c0 = c_lo
            while c0 < ncols:
                w = min(CBW, ncols - c0)
                slab = WSL()
                fw.dma("sp", lambda e, c0=c0, w=w, slab=slab: e.dma_start(out=slab[0:kparts, 0:nk, 0:w], in_=w_ap_fn(c0, w)), [Wd], [slab])
                p = PS()
                for k in range(nk):
                    fw.op("pe", lambda e, k=k, p=p, slab=slab, w=w: e.matmul(p[:, 0:w], lhsT=lhsT_fn(k), rhs=slab[0:kparts, k, 0:w],
                                                                            start=(k == 0), stop=(k == nk - 1)), [lbuf, slab], [p])
                out_fn(p, c0, w)
                c0 += w

        def store_T(src, nh, dq, dstT, tok0):
            sv = src[:, 0:nh * dq].rearrange("p (h d) -> p h d", h=nh)
            for h in range(nh):
                transpose_chunks(lambda k, h=h: sv[:, h, 0:128], src, 1, lambda k, h=h: stg[:, 0, h, :], stgb)
                if dq > 128:
                    transpose_chunks(lambda k, h=h: sv[:, h, 128:dq], src, 1, lambda k, h=h: stg[0:dq - 128, 1, h, :], stgb, width=dq - 128)
            fw.dma("pool", lambda e: e.dma_start(out=dstT[:, 0:128, tok0:tok0 + 128].rearrange("h d t -> d h t"), in_=stg[:, 0, 0:nh, :]), [stgb], [dstT])
            if dq > 128:
                fw.dma("pool", lambda e: e.dma_start(out=dstT[:, 128:dq, tok0:tok0 + 128].rearrange("h d t -> d h t"), in_=stg[0:dq - 128, 1, 0:nh, :]), [stgb], [dstT])

        def kv_pipeline(l, ckvn_ap, ckvn_buf, kpe_ap, kpe_buf, tok0, lt, t1, t2, junk):
            transpose_chunks(lambda k: ckvn_ap[:, k * 128:(k + 1) * 128], ckvn_buf, 2, lambda k: tTb[:, k, :], tTb)
            gemm(lambda k: tTb[:, k, :], tTb, 2, Wb["ukv"][l], lambda c0, w, l=l: Wb["ukv"][l][:, c0:c0 + w].rearrange("(k p) n -> p k n", p=128), 2048,
                 lambda p, c0, w: fw.op("act", lambda e, p=p, c0=c0, w=w: e.copy(out=t2[:, c0:c0 + w], in_=p[:, 0:w]), [p], [t2]))
            kvv = t2[:, 0:2048].rearrange("p (h d) -> p h d", h=8)
            kf = t1[:, 0:1536].rearrange("p (h d) -> p h d", h=8)
            fw.op("dve", lambda e: e.tensor_copy(out=kf[:, :, 0:128], in_=kvv[:, :, 0:128]), [t2], [t1])
            fw.op("dve", lambda e: e.tensor_copy(out=kf[:, :, 128:192], in_=kpe_ap.unsqueeze(1).to_broadcast([128, 8, 64])), [kpe_buf], [t1])
            fw.dma("pool", lambda e: e.dma_start(out=VM[tok0:tok0 + 128, :].rearrange("t (h d) -> t h d", h=8), in_=kvv[:, :, 128:256]), [t2], [VM])
            heads_norm(kf, t1, 8, 192, "kh", kf, t1, junk)
            if lt is not None:
                rope(t1, 8, lt, junk)
            store_T(t1, 8, 192, KMT, tok0)

        def attention(QT, KT, V, vw, h, dq, dv, r0, keyblocks, mix, mixcol, Ssb, PTs, Vs, bias=None):
            nkb = len(keyblocks)
            nk = sum(n for _, n in keyblocks)
            nch = 2 if dq > 128 else 1
            fw.dma("sp", lambda e: e.dma_start(out=qTs[:, 0, :], in_=QT[h, 0:128, r0:r0 + 128]), [QT], [qTs])
            if nch == 2:
                fw.dma("sp", lambda e: e.dma_start(out=qTs[0:dq - 128, 1, :], in_=QT[h, 128:dq, r0:r0 + 128]), [QT], [qTs])
            off = 0
            i = 0
            while i < nkb:
                t0, n = keyblocks[i]
                j = i + 1
                tot = n
                while j < nkb and keyblocks[j][0] == t0 + tot:
                    tot += keyblocks[j][1]
                    j += 1
                fw.dma("sp", lambda e, t0=t0, tot=tot, off=off: e.dma_start(out=kTs[:, 0, off:off + tot], in_=KT[h, 0:128, t0:t0 + tot]), [KT], [kTs])
                if nch == 2:
                    fw.dma("sp", lambda e, t0=t0, tot=tot, off=off: e.dma_start(out=kTs[0:dq - 128, 1, off:off + tot], in_=KT[h, 128:dq, t0:t0 + tot]), [KT], [kTs])
                off += tot
                i = j
            Vv = Vs[:, 0:12 * 128].rearrange("p (k d) -> p k d", k=12)
            for kb, (t0, n) in enumerate(keyblocks):
                fw.dma("pool", lambda e, kb=kb, t0=t0, n=n: e.dma_start(out=Vv[0:n, kb, 0:dv], in_=V[t0:t0 + n, h * dv:(h + 1) * dv]), [V], [Vs])
            banks, _ = PSG()
            c0 = 0
            while c0 < nk:
                w = min(512, nk - c0)
                b = banks[c0 // 512]
                fw.op("pe", lambda e, b=b, c0=c0, w=w: e.matmul(b[:, 0:w], lhsT=qTs[:, 0, :], rhs=kTs[:, 0, c0:c0 + w], start=True, stop=(nch == 1)), [qTs, kTs], [b])
                if nch == 2:
                    fw.op("pe", lambda e, b=b, c0=c0, w=w: e.matmul(b[:, 0:w], lhsT=qTs[0:dq - 128, 1, :], rhs=kTs[0:dq - 128, 1, c0:c0 + w], start=False, stop=True),
                          [qTs, kTs], [b])
                if bias is not None:
                    fw.op("dve", lambda e, b=b, c0=c0, w=w: e.tensor_tensor(out=Ssb[:, c0:c0 + w], in0=b[:, 0:w], in1=bias[:, c0:c0 + w], op=ALU.add), [b, bias], [Ssb])
                else:
                    fw.op("act", lambda e, b=b, c0=c0, w=w: e.copy(out=Ssb[:, c0:c0 + w], in_=b[:, 0:w]), [b], [Ssb])
                c0 += w
            fw.op("dve", lambda e: e.reduce_max(out=mx[:, 0:1], in_=Ssb[:, 0:nk], axis=AX.X), [Ssb], [mx])
            fw.op("dve", lambda e: e.tensor_scalar(out=mx[:, 1:2], in0=mx[:, 0:1], scalar1=-1.0, scalar2=None, op0=ALU.mult), [mx], [mx])
            fw.op("act", lambda e: e.activation(out=Ssb[:, 0:nk], in_=Ssb[:, 0:nk], func=AF.Exp, bias=mx[:, 1:2], scale=1.0, accum_out=mx[:, 2:3]), [Ssb, mx], [Ssb, mx])
            fw.op("dve", lambda e: e.reciprocal(out=mx[:, 3:4], in_=mx[:, 2:3]), [mx], [mx])
            PTv = PTs[:, 0:12 * 128].rearrange("p (k d) -> p k d", k=12)
            off = 0
            for kb, (t0, n) in enumerate(keyblocks):
                p = PS()
                fw.op("pe", lambda e, p=p, off=off, n=n: e.transpose(out=p[0:n, 0:128], in_=Ssb[:, off:off + n], identity=ident[:]), [Ssb, ident], [p])
                fw.op("act", lambda e, p=p, kb=kb, n=n: e.copy(out=PTv[0:n, kb, :], in_=p[0:n, 0:128]), [p], [PTs])
                off += n
            po = PS()
            for kb, (t0, n) in enumerate(keyblocks):
                fw.op("pe", lambda e, kb=kb, n=n, po=po: e.matmul(po[:, 0:dv], lhsT=PTv[0:n, kb, :], rhs=Vv[0:n, kb, 0:dv], start=(kb == 0), stop=(kb == nkb - 1)),
                      [PTs, Vs], [po])
            fw.op("dve", lambda e, po=po: e.tensor_scalar(out=mix[:, mixcol:mixcol + dv], in0=po[:, 0:dv], scalar1=mx[:, 3:4], scalar2=None, op0=ALU.mult), [po, mx], [mix])


        def peer_phase(l):
            xn, junk, t2, G2t, pj = B[1], B[2], B[4], B[9], B[8]
            xts = [B[0], B[3]]
            IDXs = [IDX, IDX2]
            alias_in(G16, [B[5], B[6]])
            SCv = z[:, 0:2048].rearrange("p (a n) -> p a n", a=16)
            TVv = TV[:].rearrange("p (h two) k -> p h two k", two=2)
            TIv = TIf[:].rearrange("p (h two) k -> p h two k", two=2)
            cand = xn[:, 0:2048].rearrange("p (h a b) -> p h a b", h=8, a=16)
            cidx = junk[:, 0:2048].rearrange("p (h a b) -> p h a b", h=8, a=16)
            candf = xn[:, 0:2048].rearrange("p (h c) -> p h c", h=8)
            cidxf = junk[:, 0:2048].rearrange("p (h c) -> p h c", h=8)
            scr = z[:, 2048:2304]
            scr2 = z[:, 2304:2560]
            GTv = GT[:].rearrange("p (h k) -> p h k", h=8)

            def prep_chunks(t):
                grp = 0 if t < NTC else 1
                r0 = t * 128
                xt = xts[t % 2]
                idx = IDXs[t % 2]
                ch = []

                def c_norm():
                    fw.dma("sp", lambda e: e.dma_start(out=xt[:], in_=X[r0:r0 + 128, :]), [X], [xt])
                    rms_rstd(xt[:], xt, D, 0, junk)
                    fw.op("dve", lambda e: e.tensor_scalar(out=xn[:], in0=xt[:], scalar1=rs[:, 0:1], scalar2=None, op0=ALU.mult), [xt, rs], [xn])
                    transpose_chunks(lambda k: xn[:, k * 128:(k + 1) * 128], xn, 16, lambda k: hT[:, k, :], hT,
                                     scale=lambda k: A2[:, k, grp:grp + 1], bias=lambda k: modT[:, 48 + k, grp:grp + 1])
                ch.append(c_norm)
                for cb in range(D // CBW):
                    def c_gemm(cb=cb):
                        gemm(lambda k: hT[:, k, :], hT, 16, Wb["pq"][l], lambda c0, w: Wb["pq"][l][:, c0:c0 + w].rearrange("(k p) n -> p k n", p=128), (cb + 1) * CBW,
                             lambda p, c0, w: fw.op("act", lambda e, p=p, c0=c0, w=w: e.copy(out=t2[:, c0:c0 + w], in_=p[:, 0:w]), [p], [t2]), c_lo=cb * CBW)
                    ch.append(c_gemm)

                def c_back():
                    for kk in range(16):
                        p = PS()
                        pv = p.t[:, 0:64].bitcast(BF16)
                        fw.op("pe", lambda e, kk=kk, pv=pv: e.transpose(out=pv, in_=hT[:, kk, :], identity=identb[:]), [hT, identb], [p])
                        fw.op("act", lambda e, kk=kk, pv=pv: e.copy(out=Hhb[:, kk * 128:(kk + 1) * 128], in_=pv), [p], [Hhb])
                ch.append(c_back)
                for g4 in range(4):
                    def c_scores(g4=g4):
                        for hp in range(g4 * 4, g4 * 4 + 4):
                            transpose_chunks(lambda k, hp=hp: t2[:, hp * 128:(hp + 1) * 128], t2, 1, lambda k, hp=hp: tT[:, hp % 4, :], tT)
                            p2 = PS()
                            fw.op("pe", lambda e, hp=hp, p2=p2: e.matmul(p2[:, 0:128], lhsT=tT[:, hp % 4, :], rhs=subk[:, hp, :], start=True, stop=True), [tT, subk], [p2])
                            fw.op("act", lambda e, hp=hp, p2=p2: e.copy(out=SCv[:, hp, :], in_=p2[:, 0:128]), [p2], [z])
                    ch.append(c_scores)
                for g4 in range(4):
                    def c_topk(g4=g4):
                        for hp in range(g4 * 4, g4 * 4 + 4):
                            fw.op("dve", lambda e, hp=hp: e.max(out=TV[:, hp, 0:8], in_=SCv[:, hp, :]), [z], [TV])
                            fw.op("dve", lambda e, hp=hp: e.max_index(out=TIu[:, hp, 0:8], in_max=TV[:, hp, 0:8], in_values=SCv[:, hp, :]), [z, TV], [TIu])
                            fw.op("dve", lambda e, hp=hp: e.match_replace(out=junk[:, 0:128], in_to_replace=TV[:, hp, 0:8], in_values=SCv[:, hp, :], imm_value=NEG2), [z, TV], [junk])
                            fw.op("dve", lambda e, hp=hp: e.max(out=TV[:, hp, 8:16], in_=junk[:, 0:128]), [junk], [TV])
                            fw.op("dve", lambda e, hp=hp: e.max_index(out=TIu[:, hp, 8:16], in_max=TV[:, hp, 8:16], in_values=junk[:, 0:128]), [junk, TV], [TIu])
                    ch.append(c_topk)

                def c_cand():
                    fw.op("dve", lambda e: e.tensor_copy(out=TIf[:], in_=TIu[:]), [TIu], [TIf])
                    for h in range(8):
                        fw.op("dve", lambda e, h=h: e.tensor_tensor(out=cand[:, h, :, :], in0=TVv[:, h, 0, :].unsqueeze(2).to_broadcast([128, 16, 16]),
                                                                    in1=TVv[:, h, 1, :].unsqueeze(1).to_broadcast([128, 16, 16]), op=ALU.add), [TV], [xn])
                        fw.op("dve", lambda e, h=h: e.scalar_tensor_tensor(out=cidx[:, h, :, :], in0=TIv[:, h, 0, :].unsqueeze(2).to_broadcast([128, 16, 16]), scalar=128.0,
                                                                           in1=TIv[:, h, 1, :].unsqueeze(1).to_broadcast([128, 16, 16]), op0=ALU.mult, op1=ALU.add), [TIf], [junk])
                ch.append(c_cand)
                for g2 in range(4):
                    def c_top16(g2=g2):
                        for h in range(g2 * 2, g2 * 2 + 2):
                            fw.op("dve", lambda e, h=h: e.max(out=TS[:, h, 0:8], in_=candf[:, h, :]), [xn], [TS])
                            fw.op("dve", lambda e, h=h: e.max_index(out=PIu[:, h, 0:8], in_max=TS[:, h, 0:8], in_values=candf[:, h, :]), [xn, TS], [PIu])
                            fw.op("dve", lambda e, h=h: e.match_replace(out=scr, in_to_replace=TS[:, h, 0:8], in_values=candf[:, h, :], imm_value=NEG2), [xn, TS], [z])
                            fw.op("dve", lambda e, h=h: e.max(out=TS[:, h, 8:16], in_=scr), [z], [TS])
                            fw.op("dve", lambda e, h=h: e.max_index(out=PIu[:, h, 8:16], in_max=TS[:, h, 8:16], in_values=scr), [z, TS], [PIu])
                    ch.append(c_top16)
                ch.append(lambda: fw.op("dve", lambda e: e.tensor_copy(out=PIf[:], in_=PIu[:]), [PIu], [PIf]))
                for h in range(8):
                    def c_eidx(h=h):
                        for k in range(16):
                            fw.op("dve", lambda e, k=k: e.scalar_tensor_tensor(out=scr2, in0=iota[:, :], scalar=PIf[:, h, k:k + 1], in1=cidxf[:, h, :],
                                                                                op0=ALU.is_equal, op1=ALU.mult, accum_out=EI[:, h * 16 + k:h * 16 + k + 1]), [iota, PIf, junk], [z, EI])
                    ch.append(c_eidx)

                def c_gates():
                    fw.op("dve", lambda e: e.tensor_tensor(out=GTv, in0=TS[:], in1=TS[:, :, 0:1].to_broadcast([128, 8, 16]), op=ALU.subtract), [TS], [GT])
                    fw.op("act", lambda e: e.activation(out=GT[:], in_=GT[:], func=AF.Exp), [GT], [GT])
                    fw.op("dve", lambda e: e.tensor_reduce(out=ss[:, 0:8], in_=GTv, axis=AX.X, op=ALU.add), [GT], [ss])
                    fw.op("dve", lambda e: e.reciprocal(out=rs[:, 0:8], in_=ss[:, 0:8]), [ss], [rs])
                    fw.op("dve", lambda e: e.tensor_tensor(out=GTv, in0=GTv, in1=rs[:, 0:8].unsqueeze(2).to_broadcast([128, 8, 16]), op=ALU.mult), [GT, rs], [GT])
                    pe_ = PS()
                    fw.op("pe", lambda e: e.transpose(out=pe_[:, 0:128], in_=EI[:], identity=ident[:]), [EI, ident], [pe_])
                    fw.op("dve", lambda e: e.tensor_copy(out=idx[:], in_=pe_[:, 0:128]), [pe_], [idx])
                    pg = PS()
                    fw.op("pe", lambda e: e.transpose(out=pg[:, 0:128], in_=GT[:], identity=ident[:]), [GT, ident], [pg])
                    fw.op("act", lambda e: e.copy(out=GTT2[:], in_=pg[:, 0:128]), [pg], [GTT2])
                ch.append(c_gates)
                return ch

            def passes(t, nxt):
                grp = 0 if t < NTC else 1
                r0 = t * 128
                xt = xts[t % 2]
                idx = IDXs[t % 2]
                fw.op("act", lambda e: e.copy(out=GTT[:], in_=GTT2[:]), [GTT2], [GTT])
                def issue_ug(tok):
                    ug = G16[tok % 4]
                    fw.dma("pool", lambda e: e.indirect_dma_start(
                        out=ug[:, :], out_offset=None, in_=UB[l][:, :],
                        in_offset=bass.IndirectOffsetOnAxis(ap=idx[:, tok:tok + 1], axis=0)), [UB[l], idx], [ug])
                for tok in range(3):
                    issue_ug(tok)
                for tok in range(128):
                    if tok + 3 < 128:
                        issue_ug(tok + 3)
                    ug = G16[tok % 4]
                    xb_ = xbs[tok % 2]
                    banks, gap = PSG()
                    for c in range(4):
                        fw.op("pe", lambda e, c=c, tok=tok, banks=banks: e.matmul(banks[c][:, :], lhsT=identb[:, tok:tok + 1].to_broadcast([128, 128]),
                                                                                  rhs=Hhb[:, c * 512:(c + 1) * 512], start=True, stop=True), [identb, Hhb], [banks[c]])
                    fw.op("act", lambda e, banks=banks, xb_=xb_: e.copy(out=xb_[:], in_=banks[3][:, :]), [banks[3]], [xb_])
                    fw.op("dve", lambda e, ug=ug, gap=gap, tok=tok: e.scalar_tensor_tensor(out=pj[:, 0:1536], in0=ug[:, 0:1536], scalar=1.0, in1=gap[:, 0:1536], op0=ALU.mult, op1=ALU.mult,
                                                                                           accum_out=ACTV[:, tok:tok + 1]), [ug] + banks[0:3], [pj, ACTV])
                    fw.op("pool", lambda e, ug=ug, xb_=xb_: e.tensor_tensor(out=ptmp[:], in0=ug[:, 1536:2048], in1=xb_[:], op=ALU.mult), [ug, xb_], [ptmp])
                    fw.op("act", lambda e, tok=tok: e.activation(out=ptmp2[:], in_=ptmp[:], func=AF.Copy, accum_out=ACTV2[:, tok:tok + 1]), [ptmp], [ptmp2, ACTV2])
                fw.op("dve", lambda e: e.tensor_tensor(out=ACTV[:], in0=ACTV[:], in1=ACTV2[:], op=ALU.add), [ACTV, ACTV2], [ACTV])
                fw.op("act", lambda e: e.activation(out=WT[:], in_=ACTV[:], func=AF.Gelu), [ACTV], [WT])
                fw.op("dve", lambda e: e.tensor_tensor(out=WT[:], in0=WT[:], in1=GTT[:], op=ALU.mult), [WT, GTT], [WT])
                banks = psb[0:4]
                gap = pbig.t[:, 0:2048]
                ps_hi[0] = True
                ci = 0
                for tok in range(128):
                    vg = G16[tok % 4]
                    zw = ZW[tok % 4]
                    fw.dma("pool", lambda e, vg=vg, tok=tok: e.indirect_dma_start(
                        out=vg[:, :], out_offset=None, in_=VB[l][:, :],
                        in_offset=bass.IndirectOffsetOnAxis(ap=idx[:, tok:tok + 1], axis=0)), [VB[l], idx], [vg])
                    fw.op("act", lambda e, zw=zw, tok=tok: e.copy(out=zw[:, 127:128], in_=WT[:, tok:tok + 1]), [WT], [zw])
                    for c in range(4):
                        fw.op("pe", lambda e, c=c, tok=tok, zw=zw, vg=vg: e.matmul(banks[c][:, :], lhsT=zw[:, 127 - tok:255 - tok], rhs=vg[:, c * 512:(c + 1) * 512],
                                                                                 start=(tok == 0), stop=(tok == 127)), [zw, vg], [banks[c]])
                    if nxt is not None and tok % 3 == 2 and ci < len(nxt):
                        nxt[ci]()
                        ci += 1
                while nxt is not None and ci < len(nxt):
                    nxt[ci]()
                    ci += 1
                ps_hi[0] = False
                fw.dma("sp", lambda e: e.dma_start(out=G2t[:], in_=GD[1, grp, :, :]), [GD], [G2t])
                fw.op("dve", lambda e: e.tensor_tensor(out=pj[:], in0=gap, in1=G2t[:], op=ALU.mult), banks + [G2t], [pj])
                fw.op("dve", lambda e: e.tensor_tensor(out=xt[:], in0=xt[:], in1=pj[:], op=ALU.add), [xt, pj], [xt])
                fw.dma("sp", lambda e: e.dma_start(out=X[r0:r0 + 128, :], in_=xt[:]), [xt], [X])

            for c_ in prep_chunks(0):
                c_()
            for t in range(NT):
                passes(t, prep_chunks(t + 1) if t + 1 < NT else None)
            alias_out(G16, [B[5], B[6]])

        for l in range(L):
            abrow0, abrow1 = B[8], B[9]
            fw.dma("sp", lambda e, l=l: e.dma_start(out=abT[:], in_=ada_bT[l, :, :]), [ada_bT], [abT])
            fw.dma("sp", lambda e, l=l: e.dma_start(out=abrow0[0:2, :], in_=ada_b[l:l + 1, 2 * D:3 * D].partition_broadcast(2)), [ada_b], [abrow0])
            fw.dma("sp", lambda e, l=l: e.dma_start(out=abrow1[0:2, :], in_=ada_b[l:l + 1, 5 * D:6 * D].partition_broadcast(2)), [ada_b], [abrow1])
            fw.dma("sp", lambda e, l=l: e.dma_start(out=gmix[:], in_=gmixT[l, :, :]), [gmixT], [gmix])
            fw.dma("sp", lambda e, l=l: e.dma_start(out=gffn[:], in_=gffnT[l, :, :]), [gffnT], [gffn])
            fw.dma("sp", lambda e, l=l: e.dma_start(out=gv[:], in_=gvec[l:l + 1, :].partition_broadcast(128)), [gvec], [gv])
            fw.dma("sp", lambda e, l=l: e.dma_start(out=cw[:], in_=conv_wT[l:l + 1, :, :].partition_broadcast(128)), [conv_wT], [cw])
            fw.dma("sp", lambda e, l=l: e.dma_start(out=subk[:], in_=subT_d[l, :, :, :]), [subT_d], [subk])
            Gst = B[7]
            NJB = 6 * D // ABW
            for jb in range(NJB):
                slab = WSL()
                s32 = slab.t[:, :, :].bitcast(F32)
                fw.dma("sp", lambda e, l=l, jb=jb, s32=s32: e.dma_start(
                    out=s32, in_=ada_w[l, :, jb * ABW:(jb + 1) * ABW].rearrange("(k p) n -> p k n", p=128)), [ada_w], [slab])
                pr = PS()
                for k in range(16):
                    fw.op("pe", lambda e, k=k, pr=pr, s32=s32: e.matmul(pr[0:2, 0:ABW], lhsT=sT[:, k, :], rhs=s32[:, k, :], start=(k == 0), stop=(k == 15)),
                          [slab, sT], [pr])
                fw.op("act", lambda e, pr=pr: e.copy(out=rowr[:], in_=pr[0:2, 0:ABW]), [pr], [rowr])
                for sub in range(ABW // 128):
                    j = jb * (ABW // 128) + sub
                    pm = PS()
                    fw.op("pe", lambda e, pm=pm, sub=sub: e.transpose(out=pm[:, 0:2], in_=rowr[0:2, sub * 128:(sub + 1) * 128], identity=ident[0:2, 0:2]), [rowr, ident], [pm])
                    fw.op("dve", lambda e, j=j, pm=pm: e.tensor_scalar(out=modT[:, j, :], in0=pm[:, 0:2], scalar1=abT[:, j:j + 1], scalar2=None, op0=ALU.add),
                          [pm, abT], [modT])
                c0 = jb * ABW
                which = 0 if 2 * D <= c0 < 3 * D else (1 if c0 >= 5 * D else None)
                if which is not None:
                    off = c0 - (2 * D if which == 0 else 5 * D)
                    abr = abrow0 if which == 0 else abrow1
                    fw.op("dve", lambda e, abr=abr, off=off: e.tensor_tensor(out=rowb[:], in0=rowr[:], in1=abr[0:2, off:off + ABW], op=ALU.add),
                          [rowr, abr], [rowb])
                    for grp in range(2):
                        pb = PS()
                        fw.op("pe", lambda e, pb=pb, grp=grp: e.matmul(pb[:, 0:ABW], lhsT=sel[:, grp, :], rhs=rowb[:, :], start=True, stop=True), [sel, rowb], [pb])
                        fw.op("act", lambda e, pb=pb: e.copy(out=Gst[:, 0:ABW], in_=pb[:, 0:ABW]), [pb], [Gst])
                        fw.dma("sp", lambda e, which=which, grp=grp, off=off: e.dma_start(out=GD[which, grp, :, off:off + ABW], in_=Gst[:, 0:ABW]), [Gst], [GD])
            for (A, g, j0) in ((A1, gmix, 16), (A2, gffn, 64)):
                fw.op("dve", lambda e, A=A, j0=j0: e.tensor_scalar(out=A[:], in0=modT[:, j0:j0 + 16, :], scalar1=1.0, scalar2=None, op0=ALU.add), [modT], [A])
                fw.op("dve", lambda e, A=A, g=g: e.tensor_tensor(out=A[:], in0=A[:], in1=g[:].unsqueeze(2).to_broadcast([128, 16, 2]), op=ALU.mult), [A, g], [A])

            xt, xn, junk, t1, t2 = B[0], B[1], B[2], B[3], B[4]
            for t in range(NT):
                grp = 0 if t < NTC else 1
                lt = None if t < NTC else t - NTC
                r0 = t * 128
                fw.dma("sp", lambda e, r0=r0: e.dma_start(out=xt[:], in_=X[r0:r0 + 128, :]), [X], [xt])
                rms_rstd(xt[:], xt, D, 0, junk)
                fw.op("dve", lambda e: e.tensor_scalar(out=xn[:], in0=xt[:], scalar1=rs[:, 0:1], scalar2=None, op0=ALU.mult), [xt, rs], [xn])
                transpose_chunks(lambda k: xn[:, k * 128:(k + 1) * 128], xn, 16, lambda k: hT[:, k, :], hT,
                                 scale=lambda k, grp=grp: A1[:, k, grp:grp + 1], bias=lambda k, grp=grp: modT[:, k, grp:grp + 1])
                gemm(lambda k: hT[:, k, :], hT, 16, Wb["in"][l], lambda c0, w, l=l: Wb["in"][l][:, c0:c0 + w].rearrange("(k p) n -> p k n", p=128), DIN,
                     lambda p, c0, w: fw.op("act", lambda e, p=p, c0=c0, w=w: e.copy(out=z[:, c0:c0 + w], in_=p[:, 0:w]), [p], [z]))
                rms_rstd(z[:, 0:512], z, 512, 1, junk)
                fw.op("dve", lambda e: e.scalar_tensor_tensor(out=t1[:, 0:512], in0=z[:, 0:512], scalar=rs[:, 1:2], in1=gv[:, 0:512], op0=ALU.mult, op1=ALU.mult),
                      [z, rs, gv], [t1])
                transpose_chunks(lambda k: t1[:, k * 128:(k + 1) * 128], t1, 4, lambda k: tTb[:, k, :], tTb)
                gemm(lambda k: tTb[:, k, :], tTb, 4, Wb["uq"][l], lambda c0, w, l=l: Wb["uq"][l][:, c0:c0 + w].rearrange("(k p) n -> p k n", p=128), 1536,
                     lambda p, c0, w: fw.op("act", lambda e, p=p, c0=c0, w=w: e.copy(out=t2[:, c0:c0 + w], in_=p[:, 0:w]), [p], [t2]))
                heads_norm(t2[:, 0:1536].rearrange("p (h d) -> p h d", h=8), t2, 8, 192, "qh",
                           t1[:, 0:1536].rearrange("p (h d) -> p h d", h=8), t1, junk, extra_scale=192 ** -0.5)
                if lt is not None:
                    rope(t1, 8, lt, junk)
                store_T(t1, 8, 192, QMT, r0)
                rms_rstd(z[:, 512:768], z, 256, 2, junk)
                fw.op("dve", lambda e: e.scalar_tensor_tensor(out=xn[:, 0:256], in0=z[:, 512:768], scalar=rs[:, 2:3], in1=gv[:, 512:768], op0=ALU.mult, op1=ALU.mult),
                      [z, rs, gv], [xn])
                if t < NTC:
                    s_, h_ = t // 2, t % 2
                    fw.dma("pool", lambda e, s_=s_, h_=h_, l=l: e.dma_start(out=o_ckv[s_, l, h_ * 128:(h_ + 1) * 128, :], in_=xn[:, 0:256]), [xn], [o_ckv])
                    fw.dma("pool", lambda e, s_=s_, h_=h_, l=l: e.dma_start(out=o_kpe[s_, l, h_ * 128:(h_ + 1) * 128, :], in_=z[:, 768:832]), [z], [o_kpe])
                    fw.dma("pool", lambda e, s_=s_, h_=h_, l=l: e.dma_start(
                        out=o_nav[s_, l, :, h_ * 128:(h_ + 1) * 128, :].rearrange("h t d -> t h d"),
                        in_=z[:, 1856:2368].rearrange("p (h d) -> p h d", h=4)), [z], [o_nav])
                kv_pipeline(l, xn[:, 0:256], xn, z[:, 768:832], z, r0, lt, t1, t2, junk)
                fw.dma("pool", lambda e, r0=r0: e.dma_start(out=VN[r0:r0 + 128, :], in_=z[:, 1856:2368]), [z], [VN])
                heads_norm(z[:, 832:1344].rearrange("p (h d) -> p h d", h=4), z, 4, 128, "naq",
                           t1[:, 0:512].rearrange("p (h d) -> p h d", h=4), t1, junk, extra_scale=128 ** -0.5)
                store_T(t1, 4, 128, QNT, r0)
                heads_norm(z[:, 1344:1856].rearrange("p (h d) -> p h d", h=4), z, 4, 128, "nak",
                           t2[:, 0:512].rearrange("p (h d) -> p h d", h=4), t2, junk)
                if t < NTC:
                    fw.dma("pool", lambda e, s_=s_, h_=h_, l=l: e.dma_start(
                        out=o_nak[s_, l, :, h_ * 128:(h_ + 1) * 128, :].rearrange("h t d -> t h d"),
                        in_=t2[:, 0:512].rearrange("p (h d) -> p h d", h=4)), [t2], [o_nak])
                store_T(t2, 4, 128, KNT, r0)
                si, cur = seq_of_tile(t)
                fw.op("dve", lambda e: e.tensor_tensor(out=t1[:, 1024:1536], in0=z[:, 2880:3392], in1=z[:, 3392:3904], op=ALU.mult), [z], [t1])
                fw.dma("pool", lambda e, cur=cur: e.dma_start(out=CU[cur:cur + 128, :], in_=t1[:, 1024:1536]), [t1], [CU])
                fw.dma("pool", lambda e, r0=r0: e.dma_start(out=GBs[r0:r0 + 128, :], in_=z[:, 2368:2880]), [z], [GBs])
            if TL:
                for ct in range(4):
                    tok0 = NTOK + ct * 128
                    fw.dma("sp", lambda e, ct=ct, l=l: e.dma_start(out=xn[:, 0:256], in_=c_ckv[l, ct * 128:(ct + 1) * 128, :]), [c_ckv], [xn])
                    fw.dma("sp", lambda e, ct=ct, l=l: e.dma_start(out=xn[:, 256:320], in_=c_kpe[l, ct * 128:(ct + 1) * 128, :]), [c_kpe], [xn])
                    kv_pipeline(l, xn[:, 0:256], xn, xn[:, 256:320], xn, tok0, None, t1, t2, junk)
                    fw.dma("sp", lambda e, ct=ct, l=l: e.dma_start(out=t2[:, 0:512].rearrange("p (h d) -> p h d", h=4),
                                                                   in_=c_nak[l, :, ct * 128:(ct + 1) * 128, :].rearrange("h t d -> t h d")), [c_nak], [t2])
                    store_T(t2, 4, 128, KNT, tok0)
                    fw.dma("pool", lambda e, ct=ct, l=l, tok0=tok0: e.dma_start(out=VN[tok0:tok0 + 128, :].rearrange("t (h d) -> t h d", h=4),
                                                                                 in_=c_nav[l, :, ct * 128:(ct + 1) * 128, :].rearrange("h t d -> t h d")), [c_nav], [VN])
            if cfg.stages <= 1:
                continue

            mix, Ssb, Vs, PTs, xt2, G1t = B[0], B[1], B[3], B[4], B[5], B[6]
            cb_ = B[7]
            junk = B[2]
            if TL:
                fw.dma("sp", lambda e, l=l: e.dma_start(out=relTs[:], in_=relT_d[l, :, :, :]), [relT_d], [relTs])
                fw.op("pool", lambda e: e.memset(Bt[:, 576:1088], 0.0), [], [Bt])

            def na_bias(h):
                pbk = [PS(), PS()]
                for sl in range(4):
                    slab = WSL()
                    ov = slab.t[:, :, :].bitcast(F32)[0:31, 0:8, :].rearrange("p a (b c) -> p (a b) c", c=128)
                    fw.dma("sp", lambda e, sl=sl, ov=ov: e.dma_start(out=ov, in_=OH_d[:, sl * 16:(sl + 1) * 16, :]), [OH_d], [slab])
                    for kci in range(16):
                        kc = sl * 16 + kci
                        pb = pbk[kc // 32]
                        fw.op("pe", lambda e, pb=pb, kc=kc, kci=kci, ov=ov: e.matmul(pb[:, (kc % 32) * 15:(kc % 32) * 15 + 15], lhsT=ov[:, kci, :], rhs=relTs[:, h, :],
                                                                                  start=True, stop=True), [slab, relTs], [pb])
                for half in range(2):
                    pb = pbk[half]
                    fw.op("dve", lambda e, pb=pb, half=half: e.tensor_tensor(
                        out=CB2[:, :, half * 32:(half + 1) * 32], in0=pb[:, 0:480].rearrange("p (k a) -> p a k", a=15),
                        in1=cm[:, half * 32:(half + 1) * 32].unsqueeze(1).to_broadcast([128, 15, 32]), op=ALU.add), [pb, cm], [CB2])

            def phaseB_tail(t):
                grp = 0 if t < NTC else 1
                r0 = t * 128
                si, cur = seq_of_tile(t)
                fw.dma("sp", lambda e: e.dma_start(out=cb_[:, 0:512], in_=CU[cur - 1:cur + 127, :]), [CU], [cb_])
                fw.dma("sp", lambda e: e.dma_start(out=cb_[:, 1024:1536], in_=CU[cur:cur + 128, :]), [CU], [cb_])
                fw.dma("sp", lambda e: e.dma_start(out=cb_[:, 1536:2048], in_=CU[cur + 1:cur + 129, :]), [CU], [cb_])
                fw.dma("sp", lambda e: e.dma_start(out=cb_[:, 512:1024], in_=GBs[r0:r0 + 128, :]), [GBs], [cb_])
                fw.op("dve", lambda e: e.tensor_tensor(out=cb_[:, 0:512], in0=cb_[:, 0:512], in1=cw[:, 0, :], op=ALU.mult), [cb_, cw], [cb_])
                fw.op("dve", lambda e: e.tensor_tensor(out=cb_[:, 1024:1536], in0=cb_[:, 1024:1536], in1=cw[:, 1, :], op=ALU.mult), [cb_, cw], [cb_])
                fw.op("dve", lambda e: e.tensor_tensor(out=cb_[:, 1536:2048], in0=cb_[:, 1536:2048], in1=cw[:, 2, :], op=ALU.mult), [cb_, cw], [cb_])
                fw.op("dve", lambda e: e.tensor_tensor(out=cb_[:, 1024:1536], in0=cb_[:, 1024:1536], in1=cb_[:, 0:512], op=ALU.add), [cb_], [cb_])
                fw.op("dve", lambda e: e.tensor_tensor(out=cb_[:, 1024:1536], in0=cb_[:, 1024:1536], in1=cb_[:, 1536:2048], op=ALU.add), [cb_], [cb_])
                fw.op("dve", lambda e: e.tensor_tensor(out=mix[:, 1536:2048], in0=cb_[:, 1024:1536], in1=cb_[:, 512:1024], op=ALU.mult), [cb_], [mix])
                transpose_chunks(lambda k: mix[:, k * 128:(k + 1) * 128], mix, 16, lambda k: hT[:, k, :], hT)
                fw.dma("sp", lambda e: e.dma_start(out=xt2[:], in_=X[r0:r0 + 128, :]), [X], [xt2])
                fw.dma("sp", lambda e: e.dma_start(out=G1t[:], in_=GD[0, grp, :, :]), [GD], [G1t])

                def ofn(p, c0, w):
                    fw.op("dve", lambda e, p=p, c0=c0, w=w: e.tensor_tensor(out=junk[:, c0:c0 + w], in0=p[:, 0:w], in1=G1t[:, c0:c0 + w], op=ALU.mult), [p, G1t], [junk])
                    fw.op("dve", lambda e, c0=c0, w=w: e.tensor_tensor(out=xt2[:, c0:c0 + w], in0=xt2[:, c0:c0 + w], in1=junk[:, c0:c0 + w], op=ALU.add), [xt2, junk], [xt2])
                gemm(lambda k: hT[:, k, :], hT, 16, Wb["out"][l], lambda c0, w, l=l: Wb["out"][l][:, c0:c0 + w].rearrange("(k p) n -> p k n", p=128), D, ofn)
                fw.dma("sp", lambda e: e.dma_start(out=X[r0:r0 + 128, :], in_=xt2[:]), [xt2], [X])
            if TL:
                for h in range(4):
                    na_bias(h)
                    fw.dma("sp", lambda e, h=h: e.dma_start(out=CBD[h, :, :], in_=CB2[:].rearrange("p a k -> p (a k)")), [CB2], [CBD])
            for t in range(NT):
                r0 = t * 128
                if t < NTC:
                    sq0 = (t // 2) * 256
                    kb_m = [(sq0, 128), (sq0 + 128, 128)]
                    for h in range(8):
                        attention(QMT, KMT, VM, 1024, h, 192, 128, r0, kb_m, mix, h * 128, Ssb, PTs, Vs)
                    for h in range(4):
                        attention(QNT, KNT, VN, 512, h, 128, 128, r0, kb_m, mix, 1024 + h * 128, Ssb, PTs, Vs)
                else:
                    l0 = NTC * 128
                    kb_c = [(NTOK + i * 128, 128) for i in range(4)]
                    kb_m = kb_c + [(l0 + i * 128, 128) for i in range(NTL)]
                    for h in range(8):
                        attention(QMT, KMT, VM, 1024, h, 192, 128, r0, kb_m, mix, h * 128, Ssb, PTs, Vs)
                    rows = TL // 64
                    rg = 2 * (t - NTC)
                    rsf = lambda r: min(max(r - 4, 0), rows - 8)
                    ks = min(rsf(rg), rows - 9)
                    kb_l = [(l0 + ks * 64 + i * 128, 128) for i in range(4)] + [(l0 + ks * 64 + 512, 64)]
                    for h in range(4):
                        fw.op("pool", lambda e: e.memset(Bt[:, 0:576], NEG), [], [Bt])
                        for dr in range(2):
                            r = rg + dr
                            j0 = rsf(r) - ks
                            a0 = rsf(r) - r + 7
                            fw.dma("sp", lambda e, h=h, dr=dr, j0=j0, a0=a0: e.dma_start(out=Bt[dr * 64:(dr + 1) * 64, j0 * 64:(j0 + 8) * 64],
                                                                                          in_=CBD[h, dr * 64:(dr + 1) * 64, a0 * 64:(a0 + 8) * 64]), [CBD], [Bt])
                        attention(QNT, KNT, VN, 512, h, 128, 128, r0, kb_l + kb_c, mix, 1024 + h * 128, Ssb, PTs, Vs, bias=Bt)
                phaseB_tail(t)
            if cfg.stages <= 2:
                continue
            peer_phase(l)
        for t in range(NT):
            fw.dma("pool", lambda e, t=t: e.dma_start(out=xout[t * 128:(t + 1) * 128, :], in_=X[t * 128:(t + 1) * 128, :]), [X], [xout])
        fw.finish()
        fw.emit()
        print("instructions:", fw.ninst)
    return nc


def host_inputs(cfg, core, inp):
    L, NS = cfg.depth, cfg.nseq
    f = np.float32
    b = core // 4
    xs = [np.asarray(inp["x_prompt"][core * NS:(core + 1) * NS], f).reshape(NS * 256, D)]
    if cfg.lat:
        xs.append(np.asarray(inp["x_sample"][b], f)[:cfg.lat])
    m = {}
    m["xin"] = np.ascontiguousarray(np.concatenate(xs, 0))
    cv = np.stack([np.asarray(inp["c_ctx"], f), np.asarray(inp["c"][b], f)], 0)
    m["cvecT"] = np.ascontiguousarray(cv.reshape(2, 16, 128).transpose(2, 1, 0))
    m["ada_w"] = np.asarray(inp["ada_w"][:L], f)
    m["ada_b"] = np.asarray(inp["ada_b"][:L], f)
    m["ada_bT"] = np.ascontiguousarray(np.asarray(inp["ada_b"][:L], f).reshape(L, 96, 128).transpose(0, 2, 1))
    m["gmixT"] = np.ascontiguousarray(np.asarray(inp["norm_mix_g"][:L], f).reshape(L, 16, 128).transpose(0, 2, 1))
    m["gffnT"] = np.ascontiguousarray(np.asarray(inp["norm_ffn_g"][:L], f).reshape(L, 16, 128).transpose(0, 2, 1))
    m["w_in"] = np.asarray(inp["w_in"][:L], f)
    m["w_uq"] = np.asarray(inp["mla_w_uq"][:L], f)
    m["w_ukv"] = np.asarray(inp["mla_w_ukv"][:L], f)
    m["w_out"] = np.asarray(inp["w_out"][:L], f)
    m["gvec"] = np.ascontiguousarray(np.concatenate([np.asarray(inp[k][:L], f) for k in
                                                     ("mla_q_norm_g", "mla_kv_norm_g", "mla_q_head_g", "mla_k_head_g", "na_q_head_g", "na_k_head_g")], 1))
    m["conv_wT"] = np.ascontiguousarray(np.asarray(inp["conv_w"][:L], f).transpose(0, 2, 1))
    m["ident"] = np.eye(128, dtype=f)
    sel = np.zeros((2, 2, 128), f)
    sel[0, 0, :] = 1
    sel[1, 1, :] = 1
    m["sel"] = sel
    m["peer_wq"] = np.asarray(inp["peer_w_q"][:L], f)
    m["iota"] = np.ascontiguousarray(np.broadcast_to(np.arange(256, dtype=f)[None, :], (128, 256)))
    m["subT"] = np.ascontiguousarray(np.asarray(inp["peer_sub_keys"][:L], f).transpose(0, 4, 1, 2, 3).reshape(L, 128, 16, 128))
    for i in range(L):
        m["peer_u%d" % i] = np.asarray(inp["peer_u"][i], f)
        m["peer_v%d" % i] = np.asarray(inp["peer_v"][i], f)
    if cfg.lat:
        TL = cfg.lat
        tt = np.arange(TL)
        row = (tt // 64).astype(f)
        col = (tt % 64).astype(f)
        inv = (np.float32(10000.0) ** (-np.arange(16, dtype=f) / np.float32(16))).astype(f)
        ang = np.concatenate([row[:, None] * inv, col[:, None] * inv], -1).astype(f)
        m["rope"] = np.ascontiguousarray(np.concatenate([np.cos(ang), np.sin(ang)], -1).astype(f))
        m["c_ckv"] = np.ascontiguousarray(np.asarray(inp["cache_mla_ckv"][b, :L], f))
        m["c_kpe"] = np.ascontiguousarray(np.asarray(inp["cache_mla_kpe"][b, :L], f))
        m["c_nak"] = np.ascontiguousarray(np.asarray(inp["cache_na_k"][b, :L], f))
        m["c_nav"] = np.ascontiguousarray(np.asarray(inp["cache_na_v"][b, :L], f))
        m["relT"] = np.ascontiguousarray(np.asarray(inp["na_rel_bias"][:L], f).transpose(0, 3, 1, 2))
        OH = np.zeros((31, 64, 128), f)
        cmk = np.full((128, 64), NEG, f)
        for qc in range(64):
            cs = min(max(qc - 8, 0), 48)
            for kc in range(64):
                bb = kc - qc + 15
                if 0 <= bb < 31:
                    OH[bb, kc, qc] = 1
                    OH[bb, kc, 64 + qc] = 1
                if cs <= kc < cs + 16:
                    cmk[qc, kc] = 0
                    cmk[64 + qc, kc] = 0
        m["OH"] = OH
        m["cmask"] = cmk
    return m


_NC_CACHE = {}


def kernel(**inp):
    cfg = Cfg()
    n = 8
    if "nc" not in _NC_CACHE:
        _NC_CACHE["nc"] = build(cfg)
    nc = _NC_CACHE["nc"]
    in_maps = [host_inputs(cfg, c, inp) for c in range(n)]
    res = run_bass_kernel_spmd(nc, in_maps, core_ids=list(range(n)))
    R = res.results
    NS, L = cfg.nseq, cfg.depth
    y_prompt = np.concatenate([R[c]["xout"][:NS * 256].reshape(NS, 256, D) for c in range(n)], 0)
    y_sample = np.stack([np.concatenate([R[b * 4 + q]["xout"][NS * 256 + q * 256:NS * 256 + (q + 1) * 256] for q in range(4)], 0) for b in range(2)], 0)
    ckv = np.concatenate([R[c]["o_ckv"] for c in range(n)], 0)
    kpe = np.concatenate([R[c]["o_kpe"] for c in range(n)], 0)
    nak = np.concatenate([R[c]["o_nak"] for c in range(n)], 0)
    nav = np.concatenate([R[c]["o_nav"] for c in range(n)], 0)
    return (y_prompt.astype(np.float32), y_sample.astype(np.float32), ckv, kpe, nak, nav)
```

```python
from contextlib import ExitStack
import numpy as np
import concourse.bass as bass
import concourse.mybir as mybir
from concourse.bass_utils import run_bass_kernel_spmd

F32 = mybir.dt.float32
I32 = mybir.dt.int32
AF = mybir.ActivationFunctionType
ALU = mybir.AluOpType
AX = mybir.AxisListType

D = 2048
DIN = 3904
NEG = -30000.0


class Buf:
    __slots__ = ("t", "lw", "rd", "name")

    def __init__(self, t, name=""):
        self.t = t
        self.lw = None
        self.rd = {}
        self.name = name

    def __getitem__(self, k):
        return self.t[k]


class FW:
    ENG = ("pe", "act", "dve", "pool", "sp")
    NDMA = 6

    def __init__(self, nc, es, same_engine_sync=True):
        self.nc = nc
        self.es = es
        self.same = same_engine_sync
        self.sem = {}
        self.cnt = {}
        for e in self.ENG:
            self.sem[e] = es.enter_context(nc.semaphore("s_" + e))
            self.cnt[e] = 0
        self.dq = {}
        for q in ("sp", "pool", "act"):
            slots = []
            for i in range(self.NDMA):
                k = "d_%s%d" % (q, i)
                self.sem[k] = es.enter_context(nc.semaphore(k))
                self.cnt[k] = 0
                slots.append(k)
            self.dq[q] = [slots, 0]
        self.seen = {e: {} for e in self.ENG}
        self.prog = {e: [] for e in self.ENG}
        self.ninst = 0
        self._psi = 0

    def sb(self, name, shape, dt=F32):
        t = self.es.enter_context(self.nc.sbuf_tensor("sb_" + name, list(shape), dt))
        return Buf(t, name)

    def ps(self, name, shape, dt=F32):
        t = self.es.enter_context(self.nc.psum_tensor(name, list(shape), dt))
        return Buf(t, name)

    def dram(self, name, shape, dt=F32, kind="Internal"):
        t = self.nc.dram_tensor(name, list(shape), dt, kind=kind)
        return Buf(t, name)

    def _wait(self, eng, key, val):
        if key == eng and not self.same:
            return
        if self.seen[eng].get(key, 0) >= val:
            return
        self.prog[eng].append(("w", key, val))
        self.seen[eng][key] = val

    def _deps(self, eng, reads, writes):
        for b in reads:
            if b.lw is not None:
                self._wait(eng, *b.lw)
        for b in writes:
            if b.lw is not None:
                self._wait(eng, *b.lw)
            for k, v in b.rd.items():
                self._wait(eng, k, v)

    def _mark(self, key, val, reads, writes):
        for b in reads:
            if b.rd.get(key, 0) < val:
                b.rd[key] = val
        for b in writes:
            b.lw = (key, val)
            b.rd = {}

    def op(self, eng, fn, reads=(), writes=()):
        self._deps(eng, reads, writes)
        self.cnt[eng] += 1
        self.prog[eng].append(("i", fn, eng, 1))
        self._mark(eng, self.cnt[eng], reads, writes)
        self.ninst += 1

    def dma(self, q, fn, reads=(), writes=()):
        slots, i = self.dq[q]
        k = slots[i % len(slots)]
        self.dq[q][1] = i + 1
        if self.cnt[k] > 0:
            self._wait(q, k, self.cnt[k])
        self._deps(q, reads, writes)
        self.cnt[k] += 16
        self.prog[q].append(("i", fn, k, 16))
        self._mark(k, self.cnt[k], reads, writes)
        self.ninst += 1

    def finish(self):
        for e in self.ENG:
            if e != "sp" and self.cnt[e] > 0:
                self._wait("sp", e, self.cnt[e])
        for q in self.dq:
            for k in self.dq[q][0]:
                if self.cnt[k] > 0:
                    self._wait("sp", k, self.cnt[k])

    def emit(self):
        nc = self.nc
        prog = self.prog
        sem = self.sem

        def run(e, items):
            for it in items:
                if it[0] == "w":
                    e.wait_ge(sem[it[1]], it[2])
                else:
                    it[1](e).then_inc(sem[it[2]], it[3])

        with nc.Block() as block:
            @block.tensor
            def _(e):
                run(e, prog["pe"])

            @block.scalar
            def _(e):
                run(e, prog["act"])

            @block.vector
            def _(e):
                run(e, prog["dve"])

            @block.gpsimd
            def _(e):
                run(e, prog["pool"])

            @block.sync
            def _(e):
                run(e, prog["sp"])


class Cfg:
    def __init__(self, depth=4, nseq=4, lat=1024, stages=99):
        self.depth = depth
        self.nseq = nseq
        self.lat = lat
        self.stages = stages


GOFF = {"qn": 0, "kvn": 512, "qh": 768, "kh": 960, "naq": 1152, "nak": 1280}
GLEN = 1408


CBW = 512
ABW = 256
NEG2 = -1.0e30
U32 = mybir.dt.uint32
BF16 = mybir.dt.bfloat16


def build(cfg):
    L = cfg.depth
    NS = cfg.nseq
    NTC = NS * 2
    TL = cfg.lat
    NTL = TL // 128
    NT = NTC + NTL
    NTOK = NT * 128
    NCACHE = 512 if TL else 0
    nc = bass.Bass("TRN2", target_bir_lowering=False)

    def din(name, shape, dt=F32):
        return Buf(nc.dram_tensor(name, list(shape), dt, kind="ExternalInput"), name)

    def dout(name, shape, dt=F32):
        return Buf(nc.dram_tensor(name, list(shape), dt, kind="ExternalOutput"), name)

    xin = din("xin", [NTOK, D])
    cvecT = din("cvecT", [128, 16, 2])
    ada_w = din("ada_w", [L, D, 6 * D])
    ada_bT = din("ada_bT", [L, 128, 96])
    ada_b = din("ada_b", [L, 6 * D])
    gmixT = din("gmixT", [L, 128, 16])
    gffnT = din("gffnT", [L, 128, 16])
    w_in = din("w_in", [L, D, DIN])
    w_uq = din("w_uq", [L, 512, 1536])
    w_ukv = din("w_ukv", [L, 256, 2048])
    w_out = din("w_out", [L, D, D])
    gvec = din("gvec", [L, GLEN])
    conv_wT = din("conv_wT", [L, 3, 512])
    ident_d = din("ident", [128, 128])
    sel_d = din("sel", [2, 2, 128])
    peer_wq = din("peer_wq", [L, D, D])
    subT_d = din("subT", [L, 128, 16, 128])
    peer_u = [din("peer_u%d" % i, [16384, D]) for i in range(L)]
    peer_v = [din("peer_v%d" % i, [16384, D]) for i in range(L)]
    iota_d = din("iota", [128, 256])
    if TL:
        rope_d = din("rope", [TL, 64])
        c_ckv = din("c_ckv", [L, 512, 256])
        c_kpe = din("c_kpe", [L, 512, 64])
        c_nak = din("c_nak", [L, 4, 512, 128])
        c_nav = din("c_nav", [L, 4, 512, 128])
        relT_d = din("relT", [L, 31, 4, 15])
        OH_d = din("OH", [31, 64, 128])
        cm_d = din("cmask", [128, 64])

    xout = dout("xout", [NTOK, D])
    o_ckv = dout("o_ckv", [NS, L, 256, 256])
    o_kpe = dout("o_kpe", [NS, L, 256, 64])
    o_nak = dout("o_nak", [NS, L, 4, 256, 128])
    o_nav = dout("o_nav", [NS, L, 4, 256, 128])

    with ExitStack() as es:
        fw = FW(nc, es)
        NKT = NTOK + NCACHE
        X = fw.dram("Xs", [NTOK, D])
        QMT = fw.dram("QMT", [8, 192, NTOK])
        KMT = fw.dram("KMT", [8, 192, NKT])
        VM = fw.dram("VM", [NKT, 1024])
        QNT = fw.dram("QNT", [4, 128, NTOK])
        KNT = fw.dram("KNT", [4, 128, NKT])
        VN = fw.dram("VN", [NKT, 512])
        NSEQ = NS + (1 if TL else 0)
        CU = fw.dram("CU", [NTOK + 2 * NSEQ, 512])
        GBs = fw.dram("GBs", [NTOK, 512])
        GD = fw.dram("GD", [2, 2, 128, D])
        CBD = fw.dram("CBD", [4, 128, 15 * 64])

        def seq_of_tile(t):
            if t < NTC:
                si = t // 2
            else:
                si = NS
            return si, t * 128 + 2 * si + 1

        pbig = fw.ps("pbig", [128, 4096])
        psb = [Buf(pbig.t[:, i * 512:(i + 1) * 512], "ps%d" % i) for i in range(8)]
        grp_i = [0]

        ps_hi = [False]

        def PS():
            if ps_hi[0]:
                b = psb[4 + fw._psi % 4]
            else:
                b = psb[fw._psi % 8]
            fw._psi += 1
            return b

        def PSG():
            g = grp_i[0] % 2
            grp_i[0] += 1
            return psb[g * 4:(g + 1) * 4], pbig.t[:, g * 2048:(g + 1) * 2048]

        ident = fw.sb("ident", [128, 128])
        sel = fw.sb("sel", [2, 2, 128])
        fw.dma("sp", lambda e: e.dma_start(out=ident[:], in_=ident_d[:, :]), [ident_d], [ident])
        fw.dma("sp", lambda e: e.dma_start(out=sel[:], in_=sel_d[:, :, :]), [sel_d], [sel])
        sT = fw.sb("sT", [128, 16, 2])
        fw.dma("sp", lambda e: e.dma_start(out=sT[:], in_=cvecT[:, :, :]), [cvecT], [sT])
        fw.op("act", lambda e: e.activation(out=sT[:], in_=sT[:], func=AF.Silu), [sT], [sT])

        for t in range(NT):
            fw.dma("pool", lambda e, t=t: e.dma_start(out=X[t * 128:(t + 1) * 128, :], in_=xin[t * 128:(t + 1) * 128, :]), [xin], [X])

        B = [fw.sb("B%d" % i, [128, D]) for i in range(10)]
        Hhb = fw.sb("Hhb", [128, D], BF16)
        identb = fw.sb("identb", [128, 128], BF16)
        G16 = [Buf(B[5 + i // 2].t[:, (i % 2) * 1024:(i % 2) * 1024 + 1024].bitcast(BF16), "g16_%d" % i) for i in range(6)]

        def alias_in(views, bases):
            for i, v in enumerate(views):
                b = bases[i // 2]
                v.lw = b.lw
                v.rd = dict(b.rd)

        def alias_out(views, bases):
            for i, v in enumerate(views):
                b = bases[i // 2]
                if v.lw is not None:
                    b.rd[v.lw[0]] = max(b.rd.get(v.lw[0], 0), v.lw[1])
                for kk, vv in v.rd.items():
                    b.rd[kk] = max(b.rd.get(kk, 0), vv)

        Wb = {"in": [fw.dram("Wb_in%d" % i, [D, DIN], BF16) for i in range(L)],
              "uq": [fw.dram("Wb_uq%d" % i, [512, 1536], BF16) for i in range(L)],
              "ukv": [fw.dram("Wb_ukv%d" % i, [256, 2048], BF16) for i in range(L)],
              "out": [fw.dram("Wb_out%d" % i, [D, D], BF16) for i in range(L)],
              "pq": [fw.dram("Wb_pq%d" % i, [D, D], BF16) for i in range(L)]}
        alias_in(G16, [B[5], B[6], B[7]])
        itw = 0
        for i in range(L):
            for (srcb, key, R_, C_) in ((w_in, "in", D, DIN), (w_uq, "uq", 512, 1536), (w_ukv, "ukv", 256, 2048), (w_out, "out", D, D), (peer_wq, "pq", D, D)):
                dstb = Wb[key][i]
                for r in range(R_ // 128):
                    c0 = 0
                    while c0 < C_:
                        w = min(2048, C_ - c0)
                        fb = B[itw % 4]
                        gb = G16[itw % 4]
                        fw.dma("sp", lambda e, srcb=srcb, i=i, r=r, c0=c0, w=w, fb=fb: e.dma_start(out=fb[:, 0:w], in_=srcb[i, r * 128:(r + 1) * 128, c0:c0 + w]), [srcb], [fb])
                        if itw % 2 == 0:
                            fw.op("act", lambda e, fb=fb, gb=gb, w=w: e.copy(out=gb[:, 0:w], in_=fb[:, 0:w]), [fb], [gb])
                        else:
                            fw.op("dve", lambda e, fb=fb, gb=gb, w=w: e.tensor_copy(out=gb[:, 0:w], in_=fb[:, 0:w]), [fb], [gb])
                        fw.dma("pool", lambda e, dstb=dstb, r=r, c0=c0, w=w, gb=gb: e.dma_start(out=dstb[r * 128:(r + 1) * 128, c0:c0 + w], in_=gb[:, 0:w]), [gb], [dstb])
                        itw += 1
                        c0 += w
        alias_out(G16, [B[5], B[6], B[7]])
        UB = [fw.dram("UB%d" % i, [16384, D], BF16) for i in range(L)]
        VB = [fw.dram("VB%d" % i, [16384, D], BF16) for i in range(L)]
        if cfg.stages > 2:
            alias_in(G16, [B[5], B[6], B[7]])
            it = 0
            for (src, dst) in [(peer_u[i], UB[i]) for i in range(L)] + [(peer_v[i], VB[i]) for i in range(L)]:
                for c in range(128):
                    fb = B[it % 4]
                    gb = G16[it % 4]
                    fw.dma("sp", lambda e, src=src, c=c, fb=fb: e.dma_start(out=fb[:], in_=src[c * 128:(c + 1) * 128, :]), [src], [fb])
                    if it % 2 == 0:
                        fw.op("act", lambda e, fb=fb, gb=gb: e.copy(out=gb[:], in_=fb[:]), [fb], [gb])
                    else:
                        fw.op("dve", lambda e, fb=fb, gb=gb: e.tensor_copy(out=gb[:], in_=fb[:]), [fb], [gb])
                    fw.dma("pool", lambda e, dst=dst, c=c, gb=gb: e.dma_start(out=dst[c * 128:(c + 1) * 128, :], in_=gb[:]), [gb], [dst])
                    it += 1
            alias_out(G16, [B[5], B[6], B[7]])
        fw.op("dve", lambda e: e.tensor_copy(out=identb[:], in_=ident[:]), [ident], [identb])
        wsl = [fw.sb("wsl%d" % i, [128, 16, CBW], BF16) for i in range(2)]
        wi = [0]

        def WSL():
            b = wsl[wi[0] % 2]
            wi[0] += 1
            return b

        modT = fw.sb("modT", [128, 96, 2])
        abT = fw.sb("abT", [128, 96])
        rowb = fw.sb("rowb", [2, ABW])
        rowr = fw.sb("rowr", [2, ABW])
        gmix = fw.sb("gmix", [128, 16])
        gffn = fw.sb("gffn", [128, 16])
        A1 = fw.sb("A1", [128, 16, 2])
        A2 = fw.sb("A2", [128, 16, 2])
        gv = fw.sb("gv", [128, GLEN])
        cw = fw.sb("cw", [128, 3, 512])
        hT = fw.sb("hT", [128, 16, 128], BF16)
        z = fw.sb("z", [128, DIN])
        ss = fw.sb("ss", [128, 16])
        rs = fw.sb("rs", [128, 16])
        tT = fw.sb("tT", [128, 4, 128])
        tTb = fw.sb("tTb", [128, 4, 128], BF16)
        stg = B[5].t[:, :].rearrange("p (a h t) -> p a h t", a=2, h=8)
        stgb = B[5]
        kTs = fw.sb("kTs", [128, 2, 1536])
        qTs = fw.sb("qTs", [128, 2, 128])
        mx = fw.sb("mx", [128, 4])
        subk = fw.sb("subk", [128, 16, 128])
        TV = fw.sb("TV", [128, 16, 16])
        TIu = fw.sb("TIu", [128, 16, 16], U32)
        TIf = fw.sb("TIf", [128, 16, 16])
        TS = fw.sb("TS", [128, 8, 16])
        PIu = fw.sb("PIu", [128, 8, 16], U32)
        PIf = fw.sb("PIf", [128, 8, 16])
        iota = fw.sb("iota", [128, 256])
        fw.dma("sp", lambda e: e.dma_start(out=iota[:], in_=iota_d[:, :]), [iota_d], [iota])
        EI = fw.sb("EI", [128, 128])
        GT = fw.sb("GT", [128, 128])
        IDX = fw.sb("IDX", [128, 128], I32)
        IDX2 = fw.sb("IDX2", [128, 128], I32)
        GTT = fw.sb("GTT", [128, 128])
        GTT2 = fw.sb("GTT2", [128, 128])
        ACTV = fw.sb("ACTV", [128, 128])
        WT = fw.sb("WT", [128, 128])
        ZW = [fw.sb("ZW%d" % i, [128, 256], BF16) for i in range(4)]
        for i in range(4):
            fw.op("pool", lambda e, i=i: e.memset(ZW[i][:], 0.0), [], [ZW[i]])
        zrow = fw.sb("zrow", [1, 512])
        fw.op("pool", lambda e: e.memset(zrow[:], 0.0), [], [zrow])
        for si in range(NSEQ):
            st = si * 256 if si < NS else NS * 256
            ln = 256 if si < NS else TL
            for rr in (st + 2 * si, st + 2 * si + 1 + ln):
                fw.dma("pool", lambda e, rr=rr: e.dma_start(out=CU[rr:rr + 1, :], in_=zrow[:, :]), [zrow], [CU])
        if TL:
            ropet = fw.sb("ropet", [128, NTL, 64])
            fw.dma("sp", lambda e: e.dma_start(out=ropet[:], in_=rope_d.t.ap().rearrange("(n p) c -> p n c", p=128)), [rope_d], [ropet])
            relTs = fw.sb("relTs", [31, 4, 15])
            cm = fw.sb("cm", [128, 64])
            fw.dma("sp", lambda e: e.dma_start(out=cm[:], in_=cm_d[:, :]), [cm_d], [cm])
            CB2 = fw.sb("CB2", [128, 15, 64])
            Bt = B[8]

        def rms_rstd(src_ap, srcbuf, n, col, junk):
            fw.op("act", lambda e: e.activation(out=junk[:, 0:n], in_=src_ap, func=AF.Square, scale=float(n ** -0.5),
                                                accum_out=ss[:, col:col + 1]), [srcbuf], [junk, ss])
            fw.op("act", lambda e: e.activation(out=rs[:, col:col + 1], in_=ss[:, col:col + 1], func=AF.Sqrt, bias=1e-6, scale=1.0), [ss], [rs])
            fw.op("dve", lambda e: e.reciprocal(out=rs[:, col:col + 1], in_=rs[:, col:col + 1]), [rs], [rs])

        def heads_norm(src_ap, srcbuf, nh, dh, gname, dst_ap, dstbuf, junk, extra_scale=1.0):
            jv = junk[:, 0:nh * dh].rearrange("p (h d) -> p h d", h=nh)
            fw.op("dve", lambda e: e.tensor_tensor(out=jv, in0=src_ap, in1=src_ap, op=ALU.mult), [srcbuf], [junk])
            fw.op("dve", lambda e: e.tensor_reduce(out=ss[:, 0:nh], in_=jv, axis=AX.X, op=ALU.add), [junk], [ss])
            fw.op("act", lambda e: e.activation(out=rs[:, 0:nh], in_=ss[:, 0:nh], func=AF.Sqrt, bias=1e-6, scale=float(1.0 / dh)), [ss], [rs])
            fw.op("dve", lambda e: e.reciprocal(out=rs[:, 0:nh], in_=rs[:, 0:nh]), [rs], [rs])
            fw.op("dve", lambda e: e.tensor_tensor(out=dst_ap, in0=src_ap, in1=rs[:, 0:nh].unsqueeze(2).to_broadcast([128, nh, dh]), op=ALU.mult),
                  [srcbuf, rs], [dstbuf])
            g0 = GOFF[gname]
            fw.op("dve", lambda e: e.scalar_tensor_tensor(out=dst_ap, in0=dst_ap, scalar=float(extra_scale),
                                                          in1=gv[:, g0:g0 + dh].unsqueeze(1).to_broadcast([128, nh, dh]), op0=ALU.mult, op1=ALU.mult),
                  [dstbuf, gv], [dstbuf])

        def rope(buf, nh, lt, junk):
            v = buf[:, 0:nh * 192].rearrange("p (h d) -> p h d", h=nh)
            x1, x2 = v[:, :, 128:160], v[:, :, 160:192]
            cs = ropet[:, lt, 0:32].unsqueeze(1).to_broadcast([128, nh, 32])
            sn = ropet[:, lt, 32:64].unsqueeze(1).to_broadcast([128, nh, 32])
            j = junk[:, 0:nh * 128].rearrange("p (h d) -> p h d", h=nh)
            a, b_, c, d_ = j[:, :, 0:32], j[:, :, 32:64], j[:, :, 64:96], j[:, :, 96:128]
            fw.op("dve", lambda e: e.tensor_tensor(out=a, in0=x1, in1=cs, op=ALU.mult), [buf, ropet], [junk])
            fw.op("dve", lambda e: e.tensor_tensor(out=b_, in0=x2, in1=sn, op=ALU.mult), [buf, ropet], [junk])
            fw.op("dve", lambda e: e.tensor_tensor(out=c, in0=x1, in1=sn, op=ALU.mult), [buf, ropet], [junk])
            fw.op("dve", lambda e: e.tensor_tensor(out=d_, in0=x2, in1=cs, op=ALU.mult), [buf, ropet], [junk])
            fw.op("dve", lambda e: e.tensor_tensor(out=x1, in0=a, in1=b_, op=ALU.subtract), [junk], [buf])
            fw.op("dve", lambda e: e.tensor_tensor(out=x2, in0=c, in1=d_, op=ALU.add), [junk], [buf])

        def transpose_chunks(src_ap_fn, srcbuf, nch, dst_fn, dstbuf, width=128, scale=None, bias=None, rows=128):
            for k in range(nch):
                p = PS()
                fw.op("pe", lambda e, k=k, p=p: e.transpose(out=p[0:width, 0:rows], in_=src_ap_fn(k), identity=ident[0:rows, 0:rows]), [srcbuf, ident], [p])
                if scale is None:
                    fw.op("act", lambda e, k=k, p=p: e.copy(out=dst_fn(k), in_=p[0:width, 0:rows]), [p], [dstbuf])
                else:
                    sc, bi = scale(k), bias(k)
                    fw.op("act", lambda e, k=k, p=p, sc=sc, bi=bi: e.activation(out=dst_fn(k), in_=p[0:width, 0:rows], func=AF.Identity,
                                                                                 bias=bi, scale=sc), [p, A1, A2, modT], [dstbuf])

        def gemm(lhsT_fn, lbuf, nk, Wd, w_ap_fn, ncols, out_fn, kparts=128, c_lo=0):
            c0 = c_lo
            while c0 < ncols:
                w = min(CBW, ncols - c0)
                slab = WSL()
                fw.dma("sp", lambda e, c0=c0, w=w, slab=slab: e.dma_start(out=slab[0:kparts, 0:nk, 0:w], in_=w_ap_fn(c0, w)), [Wd], [slab])
                p = PS()
                for k in range(nk):
                    fw.op("pe", lambda e, k=k, p=p, slab=slab, w=w: e.matmul(p[:, 0:w], lhsT=lhsT_fn(k), rhs=slab[0:kparts, k, 0:w],
                                                                            start=(k == 0), stop=(k == nk - 1)), [lbuf, slab], [p])
                out_fn(p, c0, w)
                c0 += w

        def store_T(src, nh, dq, dstT, tok0):
            sv = src[:, 0:nh * dq].rearrange("p (h d) -> p h d", h=nh)
            for h in range(nh):
                transpose_chunks(lambda k, h=h: sv[:, h, 0:128], src, 1, lambda k, h=h: stg[:, 0, h, :], stgb)
                if dq > 128:
                    transpose_chunks(lambda k, h=h: sv[:, h, 128:dq], src, 1, lambda k, h=h: stg[0:dq - 128, 1, h, :], stgb, width=dq - 128)
            fw.dma("pool", lambda e: e.dma_start(out=dstT[:, 0:128, tok0:tok0 + 128].rearrange("h d t -> d h t"), in_=stg[:, 0, 0:nh, :]), [stgb], [dstT])
            if dq > 128:
                fw.dma("pool", lambda e: e.dma_start(out=dstT[:, 128:dq, tok0:tok0 + 128].rearrange("h d t -> d h t"), in_=stg[0:dq - 128, 1, 0:nh, :]), [stgb], [dstT])

        def kv_pipeline(l, ckvn_ap, ckvn_buf, kpe_ap, kpe_buf, tok0, lt, t1, t2, junk):
            transpose_chunks(lambda k: ckvn_ap[:, k * 128:(k + 1) * 128], ckvn_buf, 2, lambda k: tTb[:, k, :], tTb)
            gemm(lambda k: tTb[:, k, :], tTb, 2, Wb["ukv"][l], lambda c0, w, l=l: Wb["ukv"][l][:, c0:c0 + w].rearrange("(k p) n -> p k n", p=128), 2048,
                 lambda p, c0, w: fw.op("act", lambda e, p=p, c0=c0, w=w: e.copy(out=t2[:, c0:c0 + w], in_=p[:, 0:w]), [p], [t2]))
            kvv = t2[:, 0:2048].rearrange("p (h d) -> p h d", h=8)
            kf = t1[:, 0:1536].rearrange("p (h d) -> p h d", h=8)
            fw.op("dve", lambda e: e.tensor_copy(out=kf[:, :, 0:128], in_=kvv[:, :, 0:128]), [t2], [t1])
            fw.op("dve", lambda e: e.tensor_copy(out=kf[:, :, 128:192], in_=kpe_ap.unsqueeze(1).to_broadcast([128, 8, 64])), [kpe_buf], [t1])
            fw.dma("pool", lambda e: e.dma_start(out=VM[tok0:tok0 + 128, :].rearrange("t (h d) -> t h d", h=8), in_=kvv[:, :, 128:256]), [t2], [VM])
            heads_norm(kf, t1, 8, 192, "kh", kf, t1, junk)
            if lt is not None:
                rope(t1, 8, lt, junk)
            store_T(t1, 8, 192, KMT, tok0)

        def attention(QT, KT, V, vw, h, dq, dv, r0, keyblocks, mix, mixcol, Ssb, PTs, Vs, bias=None):
            nkb = len(keyblocks)
            nk = sum(n for _, n in keyblocks)
            nch = 2 if dq > 128 else 1
            fw.dma("sp", lambda e: e.dma_start(out=qTs[:, 0, :], in_=QT[h, 0:128, r0:r0 + 128]), [QT], [qTs])
            if nch == 2:
                fw.dma("sp", lambda e: e.dma_start(out=qTs[0:dq - 128, 1, :], in_=QT[h, 128:dq, r0:r0 + 128]), [QT], [qTs])
            off = 0
            i = 0
            while i < nkb:
                t0, n = keyblocks[i]
                j = i + 1
                tot = n
                while j < nkb and keyblocks[j][0] == t0 + tot:
                    tot += keyblocks[j][1]
                    j += 1
                fw.dma("sp", lambda e, t0=t0, tot=tot, off=off: e.dma_start(out=kTs[:, 0, off:off + tot], in_=KT[h, 0:128, t0:t0 + tot]), [KT], [kTs])
                if nch == 2:
                    fw.dma("sp", lambda e, t0=t0, tot=tot, off=off: e.dma_start(out=kTs[0:dq - 128, 1, off:off + tot], in_=KT[h, 128:dq, t0:t0 + tot]), [KT], [kTs])
                off += tot
                i = j
            Vv = Vs[:, 0:12 * 128].rearrange("p (k d) -> p k d", k=12)
            kb = 0
            while kb < nkb:
                t0, n = keyblocks[kb]
                j = kb + 1
                while n == 128 and j < nkb and keyblocks[j][1] == 128 and keyblocks[j][0] == t0 + (j - kb) * 128:
                    j += 1
                cnt = j - kb
                if cnt > 1:
                    fw.dma("pool", lambda e, kb=kb, t0=t0, cnt=cnt: e.dma_start(out=Vv[:, kb:kb + cnt, 0:dv],
                                                                                in_=V[t0:t0 + cnt * 128, h * dv:(h + 1) * dv].rearrange("(k p) d -> p k d", p=128)), [V], [Vs])
                else:
                    fw.dma("pool", lambda e, kb=kb, t0=t0, n=n: e.dma_start(out=Vv[0:n, kb, 0:dv], in_=V[t0:t0 + n, h * dv:(h + 1) * dv]), [V], [Vs])
                kb = j
            banks, _ = PSG()
            c0 = 0
            while c0 < nk:
                w = min(512, nk - c0)
                b = banks[c0 // 512]
                fw.op("pe", lambda e, b=b, c0=c0, w=w: e.matmul(b[:, 0:w], lhsT=qTs[:, 0, :], rhs=kTs[:, 0, c0:c0 + w], start=True, stop=(nch == 1)), [qTs, kTs], [b])
                if nch == 2:
                    fw.op("pe", lambda e, b=b, c0=c0, w=w: e.matmul(b[:, 0:w], lhsT=qTs[0:dq - 128, 1, :], rhs=kTs[0:dq - 128, 1, c0:c0 + w], start=False, stop=True),
                          [qTs, kTs], [b])
                if bias is not None:
                    fw.op("dve", lambda e, b=b, c0=c0, w=w: e.tensor_tensor(out=Ssb[:, c0:c0 + w], in0=b[:, 0:w], in1=bias[:, c0:c0 + w], op=ALU.add), [b, bias], [Ssb])
                else:
                    fw.op("act", lambda e, b=b, c0=c0, w=w: e.copy(out=Ssb[:, c0:c0 + w], in_=b[:, 0:w]), [b], [Ssb])
                c0 += w
            fw.op("dve", lambda e: e.reduce_max(out=mx[:, 0:1], in_=Ssb[:, 0:nk], axis=AX.X), [Ssb], [mx])
            fw.op("dve", lambda e: e.tensor_scalar(out=mx[:, 1:2], in0=mx[:, 0:1], scalar1=-1.0, scalar2=None, op0=ALU.mult), [mx], [mx])
            fw.op("act", lambda e: e.activation(out=Ssb[:, 0:nk], in_=Ssb[:, 0:nk], func=AF.Exp, bias=mx[:, 1:2], scale=1.0, accum_out=mx[:, 2:3]), [Ssb, mx], [Ssb, mx])
            fw.op("dve", lambda e: e.reciprocal(out=mx[:, 3:4], in_=mx[:, 2:3]), [mx], [mx])
            PTv = PTs[:, 0:12 * 128].rearrange("p (k d) -> p k d", k=12)
            off = 0
            for kb, (t0, n) in enumerate(keyblocks):
                p = PS()
                fw.op("pe", lambda e, p=p, off=off, n=n: e.transpose(out=p[0:n, 0:128], in_=Ssb[:, off:off + n], identity=ident[:]), [Ssb, ident], [p])
                fw.op("act", lambda e, p=p, kb=kb, n=n: e.copy(out=PTv[0:n, kb, :], in_=p[0:n, 0:128]), [p], [PTs])
                off += n
            po = PS()
            for kb, (t0, n) in enumerate(keyblocks):
                fw.op("pe", lambda e, kb=kb, n=n, po=po: e.matmul(po[:, 0:dv], lhsT=PTv[0:n, kb, :], rhs=Vv[0:n, kb, 0:dv], start=(kb == 0), stop=(kb == nkb - 1)),
                      [PTs, Vs], [po])
            fw.op("dve", lambda e, po=po: e.tensor_scalar(out=mix[:, mixcol:mixcol + dv], in0=po[:, 0:dv], scalar1=mx[:, 3:4], scalar2=None, op0=ALU.mult), [po, mx], [mix])


        def peer_phase(l):
            xn, junk, t2, G2t, pj = B[1], B[2], B[4], B[9], B[8]
            xts = [B[0], B[3]]
            IDXs = [IDX, IDX2]
            alias_in(G16, [B[5], B[6], B[7]])
            SCv = z[:, 0:2048].rearrange("p (a n) -> p a n", a=16)
            TVv = TV[:].rearrange("p (h two) k -> p h two k", two=2)
            TIv = TIf[:].rearrange("p (h two) k -> p h two k", two=2)
            cand = xn[:, 0:2048].rearrange("p (h a b) -> p h a b", h=8, a=16)
            cidx = junk[:, 0:2048].rearrange("p (h a b) -> p h a b", h=8, a=16)
            candf = xn[:, 0:2048].rearrange("p (h c) -> p h c", h=8)
            cidxf = junk[:, 0:2048].rearrange("p (h c) -> p h c", h=8)
            scr = z[:, 2048:2304]
            scr2 = z[:, 2304:2560]
            GTv = GT[:].rearrange("p (h k) -> p h k", h=8)

            def prep_chunks(t):
                grp = 0 if t < NTC else 1
                r0 = t * 128
                xt = xts[t % 2]
                idx = IDXs[t % 2]
                ch = []

                def c_norm():
                    fw.dma("sp", lambda e: e.dma_start(out=xt[:], in_=X[r0:r0 + 128, :]), [X], [xt])
                    rms_rstd(xt[:], xt, D, 0, junk)
                    fw.op("dve", lambda e: e.tensor_scalar(out=xn[:], in0=xt[:], scalar1=rs[:, 0:1], scalar2=None, op0=ALU.mult), [xt, rs], [xn])
                    transpose_chunks(lambda k: xn[:, k * 128:(k + 1) * 128], xn, 16, lambda k: hT[:, k, :], hT,
                                     scale=lambda k: A2[:, k, grp:grp + 1], bias=lambda k: modT[:, 48 + k, grp:grp + 1])
                ch.append(c_norm)
                for cb in range(D // CBW):
                    def c_gemm(cb=cb):
                        gemm(lambda k: hT[:, k, :], hT, 16, Wb["pq"][l], lambda c0, w: Wb["pq"][l][:, c0:c0 + w].rearrange("(k p) n -> p k n", p=128), (cb + 1) * CBW,
                             lambda p, c0, w: fw.op("act", lambda e, p=p, c0=c0, w=w: e.copy(out=t2[:, c0:c0 + w], in_=p[:, 0:w]), [p], [t2]), c_lo=cb * CBW)
                    ch.append(c_gemm)

                def c_back():
                    for kk in range(16):
                        p = PS()
                        pv = p.t[:, 0:64].bitcast(BF16)
                        fw.op("pe", lambda e, kk=kk, pv=pv: e.transpose(out=pv, in_=hT[:, kk, :], identity=identb[:]), [hT, identb], [p])
                        fw.op("act", lambda e, kk=kk, pv=pv: e.copy(out=Hhb[:, kk * 128:(kk + 1) * 128], in_=pv), [p], [Hhb])
                ch.append(c_back)
                for g4 in range(4):
                    def c_scores(g4=g4):
                        for hp in range(g4 * 4, g4 * 4 + 4):
                            transpose_chunks(lambda k, hp=hp: t2[:, hp * 128:(hp + 1) * 128], t2, 1, lambda k, hp=hp: tT[:, hp % 4, :], tT)
                            p2 = PS()
                            fw.op("pe", lambda e, hp=hp, p2=p2: e.matmul(p2[:, 0:128], lhsT=tT[:, hp % 4, :], rhs=subk[:, hp, :], start=True, stop=True), [tT, subk], [p2])
                            fw.op("act", lambda e, hp=hp, p2=p2: e.copy(out=SCv[:, hp, :], in_=p2[:, 0:128]), [p2], [z])
                    ch.append(c_scores)
                for g4 in range(4):
                    def c_topk(g4=g4):
                        for hp in range(g4 * 4, g4 * 4 + 4):
                            fw.op("dve", lambda e, hp=hp: e.max(out=TV[:, hp, 0:8], in_=SCv[:, hp, :]), [z], [TV])
                            fw.op("dve", lambda e, hp=hp: e.max_index(out=TIu[:, hp, 0:8], in_max=TV[:, hp, 0:8], in_values=SCv[:, hp, :]), [z, TV], [TIu])
                            fw.op("dve", lambda e, hp=hp: e.match_replace(out=junk[:, 0:128], in_to_replace=TV[:, hp, 0:8], in_values=SCv[:, hp, :], imm_value=NEG2), [z, TV], [junk])
                            fw.op("dve", lambda e, hp=hp: e.max(out=TV[:, hp, 8:16], in_=junk[:, 0:128]), [junk], [TV])
                            fw.op("dve", lambda e, hp=hp: e.max_index(out=TIu[:, hp, 8:16], in_max=TV[:, hp, 8:16], in_values=junk[:, 0:128]), [junk, TV], [TIu])
                    ch.append(c_topk)

                def c_cand():
                    fw.op("dve", lambda e: e.tensor_copy(out=TIf[:], in_=TIu[:]), [TIu], [TIf])
                    for h in range(8):
                        fw.op("dve", lambda e, h=h: e.tensor_tensor(out=cand[:, h, :, :], in0=TVv[:, h, 0, :].unsqueeze(2).to_broadcast([128, 16, 16]),
                                                                    in1=TVv[:, h, 1, :].unsqueeze(1).to_broadcast([128, 16, 16]), op=ALU.add), [TV], [xn])
                        fw.op("dve", lambda e, h=h: e.scalar_tensor_tensor(out=cidx[:, h, :, :], in0=TIv[:, h, 0, :].unsqueeze(2).to_broadcast([128, 16, 16]), scalar=128.0,
                                                                           in1=TIv[:, h, 1, :].unsqueeze(1).to_broadcast([128, 16, 16]), op0=ALU.mult, op1=ALU.add), [TIf], [junk])
                ch.append(c_cand)
                for g2 in range(4):
                    def c_top16(g2=g2):
                        for h in range(g2 * 2, g2 * 2 + 2):
                            fw.op("dve", lambda e, h=h: e.max(out=TS[:, h, 0:8], in_=candf[:, h, :]), [xn], [TS])
                            fw.op("dve", lambda e, h=h: e.max_index(out=PIu[:, h, 0:8], in_max=TS[:, h, 0:8], in_values=candf[:, h, :]), [xn, TS], [PIu])
                            fw.op("dve", lambda e, h=h: e.match_replace(out=scr, in_to_replace=TS[:, h, 0:8], in_values=candf[:, h, :], imm_value=NEG2), [xn, TS], [z])
                            fw.op("dve", lambda e, h=h: e.max(out=TS[:, h, 8:16], in_=scr), [z], [TS])
                            fw.op("dve", lambda e, h=h: e.max_index(out=PIu[:, h, 8:16], in_max=TS[:, h, 8:16], in_values=scr), [z, TS], [PIu])
                    ch.append(c_top16)
                ch.append(lambda: fw.op("dve", lambda e: e.tensor_copy(out=PIf[:], in_=PIu[:]), [PIu], [PIf]))
                for h in range(8):
                    def c_eidx(h=h):
                        for k in range(16):
                            fw.op("dve", lambda e, k=k: e.scalar_tensor_tensor(out=scr2, in0=iota[:, :], scalar=PIf[:, h, k:k + 1], in1=cidxf[:, h, :],
                                                                                op0=ALU.is_equal, op1=ALU.mult, accum_out=EI[:, h * 16 + k:h * 16 + k + 1]), [iota, PIf, junk], [z, EI])
                    ch.append(c_eidx)

                def c_gates():
                    fw.op("dve", lambda e: e.tensor_tensor(out=GTv, in0=TS[:], in1=TS[:, :, 0:1].to_broadcast([128, 8, 16]), op=ALU.subtract), [TS], [GT])
                    fw.op("act", lambda e: e.activation(out=GT[:], in_=GT[:], func=AF.Exp), [GT], [GT])
                    fw.op("dve", lambda e: e.tensor_reduce(out=ss[:, 0:8], in_=GTv, axis=AX.X, op=ALU.add), [GT], [ss])
                    fw.op("dve", lambda e: e.reciprocal(out=rs[:, 0:8], in_=ss[:, 0:8]), [ss], [rs])
                    fw.op("dve", lambda e: e.tensor_tensor(out=GTv, in0=GTv, in1=rs[:, 0:8].unsqueeze(2).to_broadcast([128, 8, 16]), op=ALU.mult), [GT, rs], [GT])
                    pe_ = PS()
                    fw.op("pe", lambda e: e.transpose(out=pe_[:, 0:128], in_=EI[:], identity=ident[:]), [EI, ident], [pe_])
                    fw.op("dve", lambda e: e.tensor_copy(out=idx[:], in_=pe_[:, 0:128]), [pe_], [idx])
                    pg = PS()
                    fw.op("pe", lambda e: e.transpose(out=pg[:, 0:128], in_=GT[:], identity=ident[:]), [GT, ident], [pg])
                    fw.op("act", lambda e: e.copy(out=GTT2[:], in_=pg[:, 0:128]), [pg], [GTT2])
                ch.append(c_gates)
                return ch

            def passes(t, nxt):
                grp = 0 if t < NTC else 1
                r0 = t * 128
                xt = xts[t % 2]
                idx = IDXs[t % 2]
                fw.op("act", lambda e: e.copy(out=GTT[:], in_=GTT2[:]), [GTT2], [GTT])
                for tok in range(128):
                    ug = G16[tok % 6]
                    fw.dma("pool", lambda e, ug=ug, tok=tok: e.indirect_dma_start(
                        out=ug[:, :], out_offset=None, in_=UB[l][:, :],
                        in_offset=bass.IndirectOffsetOnAxis(ap=idx[:, tok:tok + 1], axis=0)), [UB[l], idx], [ug])
                    banks, gap = PSG()
                    for c in range(4):
                        fw.op("pe", lambda e, c=c, tok=tok, banks=banks: e.matmul(banks[c][:, :], lhsT=identb[:, tok:tok + 1].to_broadcast([128, 128]),
                                                                                  rhs=Hhb[:, c * 512:(c + 1) * 512], start=True, stop=True), [identb, Hhb], [banks[c]])
                    fw.op("dve", lambda e, ug=ug, gap=gap, tok=tok: e.scalar_tensor_tensor(out=pj[:], in0=ug[:], scalar=1.0, in1=gap, op0=ALU.mult, op1=ALU.mult,
                                                                                           accum_out=ACTV[:, tok:tok + 1]), [ug] + banks, [pj, ACTV])
                fw.op("act", lambda e: e.activation(out=WT[:], in_=ACTV[:], func=AF.Gelu), [ACTV], [WT])
                fw.op("dve", lambda e: e.tensor_tensor(out=WT[:], in0=WT[:], in1=GTT[:], op=ALU.mult), [WT, GTT], [WT])
                banks = psb[0:4]
                gap = pbig.t[:, 0:2048]
                ps_hi[0] = True
                ci = 0
                for tok in range(128):
                    vg = G16[tok % 6]
                    zw = ZW[tok % 4]
                    fw.dma("pool", lambda e, vg=vg, tok=tok: e.indirect_dma_start(
                        out=vg[:, :], out_offset=None, in_=VB[l][:, :],
                        in_offset=bass.IndirectOffsetOnAxis(ap=idx[:, tok:tok + 1], axis=0)), [VB[l], idx], [vg])
                    fw.op("act", lambda e, zw=zw, tok=tok: e.copy(out=zw[:, 127:128], in_=WT[:, tok:tok + 1]), [WT], [zw])
                    for c in range(4):
                        fw.op("pe", lambda e, c=c, tok=tok, zw=zw, vg=vg: e.matmul(banks[c][:, :], lhsT=zw[:, 127 - tok:255 - tok], rhs=vg[:, c * 512:(c + 1) * 512],
                                                                                 start=(tok == 0), stop=(tok == 127)), [zw, vg], [banks[c]])
                    if nxt is not None and tok % 3 == 2 and ci < len(nxt):
                        nxt[ci]()
                        ci += 1
                while nxt is not None and ci < len(nxt):
                    nxt[ci]()
                    ci += 1
                ps_hi[0] = False
                fw.dma("sp", lambda e: e.dma_start(out=G2t[:], in_=GD[1, grp, :, :]), [GD], [G2t])
                fw.op("dve", lambda e: e.tensor_tensor(out=pj[:], in0=gap, in1=G2t[:], op=ALU.mult), banks + [G2t], [pj])
                fw.op("dve", lambda e: e.tensor_tensor(out=xt[:], in0=xt[:], in1=pj[:], op=ALU.add), [xt, pj], [xt])
                fw.dma("sp", lambda e: e.dma_start(out=X[r0:r0 + 128, :], in_=xt[:]), [xt], [X])

            for c_ in prep_chunks(0):
                c_()
            for t in range(NT):
                passes(t, prep_chunks(t + 1) if t + 1 < NT else None)
            alias_out(G16, [B[5], B[6], B[7]])

        for l in range(L):
            abrow0, abrow1 = B[8], B[9]
            fw.dma("sp", lambda e, l=l: e.dma_start(out=abT[:], in_=ada_bT[l, :, :]), [ada_bT], [abT])
            fw.dma("sp", lambda e, l=l: e.dma_start(out=abrow0[0:2, :], in_=ada_b[l:l + 1, 2 * D:3 * D].partition_broadcast(2)), [ada_b], [abrow0])
            fw.dma("sp", lambda e, l=l: e.dma_start(out=abrow1[0:2, :], in_=ada_b[l:l + 1, 5 * D:6 * D].partition_broadcast(2)), [ada_b], [abrow1])
            fw.dma("sp", lambda e, l=l: e.dma_start(out=gmix[:], in_=gmixT[l, :, :]), [gmixT], [gmix])
            fw.dma("sp", lambda e, l=l: e.dma_start(out=gffn[:], in_=gffnT[l, :, :]), [gffnT], [gffn])
            fw.dma("sp", lambda e, l=l: e.dma_start(out=gv[:], in_=gvec[l:l + 1, :].partition_broadcast(128)), [gvec], [gv])
            fw.dma("sp", lambda e, l=l: e.dma_start(out=cw[:], in_=conv_wT[l:l + 1, :, :].partition_broadcast(128)), [conv_wT], [cw])
            fw.dma("sp", lambda e, l=l: e.dma_start(out=subk[:], in_=subT_d[l, :, :, :]), [subT_d], [subk])
            Gst = B[7]
            NJB = 6 * D // ABW
            for jb in range(NJB):
                slab = WSL()
                s32 = slab.t[:, :, :].bitcast(F32)
                fw.dma("sp", lambda e, l=l, jb=jb, s32=s32: e.dma_start(
                    out=s32, in_=ada_w[l, :, jb * ABW:(jb + 1) * ABW].rearrange("(k p) n -> p k n", p=128)), [ada_w], [slab])
                pr = PS()
                for k in range(16):
                    fw.op("pe", lambda e, k=k, pr=pr, s32=s32: e.matmul(pr[0:2, 0:ABW], lhsT=sT[:, k, :], rhs=s32[:, k, :], start=(k == 0), stop=(k == 15)),
                          [slab, sT], [pr])
                fw.op("act", lambda e, pr=pr: e.copy(out=rowr[:], in_=pr[0:2, 0:ABW]), [pr], [rowr])
                for sub in range(ABW // 128):
                    j = jb * (ABW // 128) + sub
                    pm = PS()
                    fw.op("pe", lambda e, pm=pm, sub=sub: e.transpose(out=pm[:, 0:2], in_=rowr[0:2, sub * 128:(sub + 1) * 128], identity=ident[0:2, 0:2]), [rowr, ident], [pm])
                    fw.op("dve", lambda e, j=j, pm=pm: e.tensor_scalar(out=modT[:, j, :], in0=pm[:, 0:2], scalar1=abT[:, j:j + 1], scalar2=None, op0=ALU.add),
                          [pm, abT], [modT])
                c0 = jb * ABW
                which = 0 if 2 * D <= c0 < 3 * D else (1 if c0 >= 5 * D else None)
                if which is not None:
                    off = c0 - (2 * D if which == 0 else 5 * D)
                    abr = abrow0 if which == 0 else abrow1
                    fw.op("dve", lambda e, abr=abr, off=off: e.tensor_tensor(out=rowb[:], in0=rowr[:], in1=abr[0:2, off:off + ABW], op=ALU.add),
                          [rowr, abr], [rowb])
                    for grp in range(2):
                        pb = PS()
                        fw.op("pe", lambda e, pb=pb, grp=grp: e.matmul(pb[:, 0:ABW], lhsT=sel[:, grp, :], rhs=rowb[:, :], start=True, stop=True), [sel, rowb], [pb])
                        fw.op("act", lambda e, pb=pb: e.copy(out=Gst[:, 0:ABW], in_=pb[:, 0:ABW]), [pb], [Gst])
                        fw.dma("sp", lambda e, which=which, grp=grp, off=off: e.dma_start(out=GD[which, grp, :, off:off + ABW], in_=Gst[:, 0:ABW]), [Gst], [GD])
            for (A, g, j0) in ((A1, gmix, 16), (A2, gffn, 64)):
                fw.op("dve", lambda e, A=A, j0=j0: e.tensor_scalar(out=A[:], in0=modT[:, j0:j0 + 16, :], scalar1=1.0, scalar2=None, op0=ALU.add), [modT], [A])
                fw.op("dve", lambda e, A=A, g=g: e.tensor_tensor(out=A[:], in0=A[:], in1=g[:].unsqueeze(2).to_broadcast([128, 16, 2]), op=ALU.mult), [A, g], [A])

            xt, xn, junk, t1, t2 = B[0], B[1], B[2], B[3], B[4]
            for t in range(NT):
                grp = 0 if t < NTC else 1
                lt = None if t < NTC else t - NTC
                r0 = t * 128
                fw.dma("sp", lambda e, r0=r0: e.dma_start(out=xt[:], in_=X[r0:r0 + 128, :]), [X], [xt])
                rms_rstd(xt[:], xt, D, 0, junk)
                fw.op("dve", lambda e: e.tensor_scalar(out=xn[:], in0=xt[:], scalar1=rs[:, 0:1], scalar2=None, op0=ALU.mult), [xt, rs], [xn])
                transpose_chunks(lambda k: xn[:, k * 128:(k + 1) * 128], xn, 16, lambda k: hT[:, k, :], hT,
                                 scale=lambda k, grp=grp: A1[:, k, grp:grp + 1], bias=lambda k, grp=grp: modT[:, k, grp:grp + 1])
                gemm(lambda k: hT[:, k, :], hT, 16, Wb["in"][l], lambda c0, w, l=l: Wb["in"][l][:, c0:c0 + w].rearrange("(k p) n -> p k n", p=128), DIN,
                     lambda p, c0, w: fw.op("act", lambda e, p=p, c0=c0, w=w: e.copy(out=z[:, c0:c0 + w], in_=p[:, 0:w]), [p], [z]))
                rms_rstd(z[:, 0:512], z, 512, 1, junk)
                fw.op("dve", lambda e: e.scalar_tensor_tensor(out=t1[:, 0:512], in0=z[:, 0:512], scalar=rs[:, 1:2], in1=gv[:, 0:512], op0=ALU.mult, op1=ALU.mult),
                      [z, rs, gv], [t1])
                transpose_chunks(lambda k: t1[:, k * 128:(k + 1) * 128], t1, 4, lambda k: tTb[:, k, :], tTb)
                gemm(lambda k: tTb[:, k, :], tTb, 4, Wb["uq"][l], lambda c0, w, l=l: Wb["uq"][l][:, c0:c0 + w].rearrange("(k p) n -> p k n", p=128), 1536,
                     lambda p, c0, w: fw.op("act", lambda e, p=p, c0=c0, w=w: e.copy(out=t2[:, c0:c0 + w], in_=p[:, 0:w]), [p], [t2]))
                heads_norm(t2[:, 0:1536].rearrange("p (h d) -> p h d", h=8), t2, 8, 192, "qh",
                           t1[:, 0:1536].rearrange("p (h d) -> p h d", h=8), t1, junk, extra_scale=192 ** -0.5)
                if lt is not None:
                    rope(t1, 8, lt, junk)
                store_T(t1, 8, 192, QMT, r0)
                rms_rstd(z[:, 512:768], z, 256, 2, junk)
                fw.op("dve", lambda e: e.scalar_tensor_tensor(out=xn[:, 0:256], in0=z[:, 512:768], scalar=rs[:, 2:3], in1=gv[:, 512:768], op0=ALU.mult, op1=ALU.mult),
                      [z, rs, gv], [xn])
                if t < NTC:
                    s_, h_ = t // 2, t % 2
                    fw.dma("pool", lambda e, s_=s_, h_=h_, l=l: e.dma_start(out=o_ckv[s_, l, h_ * 128:(h_ + 1) * 128, :], in_=xn[:, 0:256]), [xn], [o_ckv])
                    fw.dma("pool", lambda e, s_=s_, h_=h_, l=l: e.dma_start(out=o_kpe[s_, l, h_ * 128:(h_ + 1) * 128, :], in_=z[:, 768:832]), [z], [o_kpe])
                    fw.dma("pool", lambda e, s_=s_, h_=h_, l=l: e.dma_start(
                        out=o_nav[s_, l, :, h_ * 128:(h_ + 1) * 128, :].rearrange("h t d -> t h d"),
                        in_=z[:, 1856:2368].rearrange("p (h d) -> p h d", h=4)), [z], [o_nav])
                kv_pipeline(l, xn[:, 0:256], xn, z[:, 768:832], z, r0, lt, t1, t2, junk)
                fw.dma("pool", lambda e, r0=r0: e.dma_start(out=VN[r0:r0 + 128, :], in_=z[:, 1856:2368]), [z], [VN])
                heads_norm(z[:, 832:1344].rearrange("p (h d) -> p h d", h=4), z, 4, 128, "naq",
                           t1[:, 0:512].rearrange("p (h d) -> p h d", h=4), t1, junk, extra_scale=128 ** -0.5)
                store_T(t1, 4, 128, QNT, r0)
                heads_norm(z[:, 1344:1856].rearrange("p (h d) -> p h d", h=4), z, 4, 128, "nak",
                           t2[:, 0:512].rearrange("p (h d) -> p h d", h=4), t2, junk)
                if t < NTC:
                    fw.dma("pool", lambda e, s_=s_, h_=h_, l=l: e.dma_start(
                        out=o_nak[s_, l, :, h_ * 128:(h_ + 1) * 128, :].rearrange("h t d -> t h d"),
                        in_=t2[:, 0:512].rearrange("p (h d) -> p h d", h=4)), [t2], [o_nak])
                store_T(t2, 4, 128, KNT, r0)
                si, cur = seq_of_tile(t)
                fw.op("dve", lambda e: e.tensor_tensor(out=t1[:, 1024:1536], in0=z[:, 2880:3392], in1=z[:, 3392:3904], op=ALU.mult), [z], [t1])
                fw.dma("pool", lambda e, cur=cur: e.dma_start(out=CU[cur:cur + 128, :], in_=t1[:, 1024:1536]), [t1], [CU])
                fw.dma("pool", lambda e, r0=r0: e.dma_start(out=GBs[r0:r0 + 128, :], in_=z[:, 2368:2880]), [z], [GBs])
            if TL:
                for ct in range(4):
                    tok0 = NTOK + ct * 128
                    fw.dma("sp", lambda e, ct=ct, l=l: e.dma_start(out=xn[:, 0:256], in_=c_ckv[l, ct * 128:(ct + 1) * 128, :]), [c_ckv], [xn])
                    fw.dma("sp", lambda e, ct=ct, l=l: e.dma_start(out=xn[:, 256:320], in_=c_kpe[l, ct * 128:(ct + 1) * 128, :]), [c_kpe], [xn])
                    kv_pipeline(l, xn[:, 0:256], xn, xn[:, 256:320], xn, tok0, None, t1, t2, junk)
                    fw.dma("sp", lambda e, ct=ct, l=l: e.dma_start(out=t2[:, 0:512].rearrange("p (h d) -> p h d", h=4),
                                                                   in_=c_nak[l, :, ct * 128:(ct + 1) * 128, :].rearrange("h t d -> t h d")), [c_nak], [t2])
                    store_T(t2, 4, 128, KNT, tok0)
                    fw.dma("pool", lambda e, ct=ct, l=l, tok0=tok0: e.dma_start(out=VN[tok0:tok0 + 128, :].rearrange("t (h d) -> t h d", h=4),
                                                                                 in_=c_nav[l, :, ct * 128:(ct + 1) * 128, :].rearrange("h t d -> t h d")), [c_nav], [VN])
            if cfg.stages <= 1:
                continue

            mix, Ssb, Vs, PTs, xt2, G1t = B[0], B[1], B[3], B[4], B[5], B[6]
            cb_ = B[7]
            junk = B[2]
            if TL:
                fw.dma("sp", lambda e, l=l: e.dma_start(out=relTs[:], in_=relT_d[l, :, :, :]), [relT_d], [relTs])
                fw.op("pool", lambda e: e.memset(Bt[:, 576:1088], 0.0), [], [Bt])

            def na_bias(h):
                pbk = [PS(), PS()]
                for sl in range(4):
                    slab = WSL()
                    ov = slab.t[:, :, :].bitcast(F32)[0:31, 0:8, :].rearrange("p a (b c) -> p (a b) c", c=128)
                    fw.dma("sp", lambda e, sl=sl, ov=ov: e.dma_start(out=ov, in_=OH_d[:, sl * 16:(sl + 1) * 16, :]), [OH_d], [slab])
                    for kci in range(16):
                        kc = sl * 16 + kci
                        pb = pbk[kc // 32]
                        fw.op("pe", lambda e, pb=pb, kc=kc, kci=kci, ov=ov: e.matmul(pb[:, (kc % 32) * 15:(kc % 32) * 15 + 15], lhsT=ov[:, kci, :], rhs=relTs[:, h, :],
                                                                                  start=True, stop=True), [slab, relTs], [pb])
                for half in range(2):
                    pb = pbk[half]
                    fw.op("dve", lambda e, pb=pb, half=half: e.tensor_tensor(
                        out=CB2[:, :, half * 32:(half + 1) * 32], in0=pb[:, 0:480].rearrange("p (k a) -> p a k", a=15),
                        in1=cm[:, half * 32:(half + 1) * 32].unsqueeze(1).to_broadcast([128, 15, 32]), op=ALU.add), [pb, cm], [CB2])

            def phaseB_tail(t):
                grp = 0 if t < NTC else 1
                r0 = t * 128
                si, cur = seq_of_tile(t)
                fw.dma("sp", lambda e: e.dma_start(out=cb_[:, 0:512], in_=CU[cur - 1:cur + 127, :]), [CU], [cb_])
                fw.dma("sp", lambda e: e.dma_start(out=cb_[:, 1024:1536], in_=CU[cur:cur + 128, :]), [CU], [cb_])
                fw.dma("sp", lambda e: e.dma_start(out=cb_[:, 1536:2048], in_=CU[cur + 1:cur + 129, :]), [CU], [cb_])
                fw.dma("sp", lambda e: e.dma_start(out=cb_[:, 512:1024], in_=GBs[r0:r0 + 128, :]), [GBs], [cb_])
                fw.op("dve", lambda e: e.tensor_tensor(out=cb_[:, 0:512], in0=cb_[:, 0:512], in1=cw[:, 0, :], op=ALU.mult), [cb_, cw], [cb_])
                fw.op("dve", lambda e: e.tensor_tensor(out=cb_[:, 1024:1536], in0=cb_[:, 1024:1536], in1=cw[:, 1, :], op=ALU.mult), [cb_, cw], [cb_])
                fw.op("dve", lambda e: e.tensor_tensor(out=cb_[:, 1536:2048], in0=cb_[:, 1536:2048], in1=cw[:, 2, :], op=ALU.mult), [cb_, cw], [cb_])
                fw.op("dve", lambda e: e.tensor_tensor(out=cb_[:, 1024:1536], in0=cb_[:, 1024:1536], in1=cb_[:, 0:512], op=ALU.add), [cb_], [cb_])
                fw.op("dve", lambda e: e.tensor_tensor(out=cb_[:, 1024:1536], in0=cb_[:, 1024:1536], in1=cb_[:, 1536:2048], op=ALU.add), [cb_], [cb_])
                fw.op("dve", lambda e: e.tensor_tensor(out=mix[:, 1536:2048], in0=cb_[:, 1024:1536], in1=cb_[:, 512:1024], op=ALU.mult), [cb_], [mix])
                transpose_chunks(lambda k: mix[:, k * 128:(k + 1) * 128], mix, 16, lambda k: hT[:, k, :], hT)
                fw.dma("sp", lambda e: e.dma_start(out=xt2[:], in_=X[r0:r0 + 128, :]), [X], [xt2])
                fw.dma("sp", lambda e: e.dma_start(out=G1t[:], in_=GD[0, grp, :, :]), [GD], [G1t])

                def ofn(p, c0, w):
                    fw.op("dve", lambda e, p=p, c0=c0, w=w: e.tensor_tensor(out=junk[:, c0:c0 + w], in0=p[:, 0:w], in1=G1t[:, c0:c0 + w], op=ALU.mult), [p, G1t], [junk])
                    fw.op("dve", lambda e, c0=c0, w=w: e.tensor_tensor(out=xt2[:, c0:c0 + w], in0=xt2[:, c0:c0 + w], in1=junk[:, c0:c0 + w], op=ALU.add), [xt2, junk], [xt2])
                gemm(lambda k: hT[:, k, :], hT, 16, Wb["out"][l], lambda c0, w, l=l: Wb["out"][l][:, c0:c0 + w].rearrange("(k p) n -> p k n", p=128), D, ofn)
                fw.dma("sp", lambda e: e.dma_start(out=X[r0:r0 + 128, :], in_=xt2[:]), [xt2], [X])
            if TL:
                for h in range(4):
                    na_bias(h)
                    fw.dma("sp", lambda e, h=h: e.dma_start(out=CBD[h, :, :], in_=CB2[:].rearrange("p a k -> p (a k)")), [CB2], [CBD])
            for t in range(NT):
                r0 = t * 128
                if t < NTC:
                    sq0 = (t // 2) * 256
                    kb_m = [(sq0, 128), (sq0 + 128, 128)]
                    for h in range(8):
                        attention(QMT, KMT, VM, 1024, h, 192, 128, r0, kb_m, mix, h * 128, Ssb, PTs, Vs)
                    for h in range(4):
                        attention(QNT, KNT, VN, 512, h, 128, 128, r0, kb_m, mix, 1024 + h * 128, Ssb, PTs, Vs)
                else:
                    l0 = NTC * 128
                    kb_c = [(NTOK + i * 128, 128) for i in range(4)]
                    kb_m = kb_c + [(l0 + i * 128, 128) for i in range(NTL)]
                    for h in range(8):
                        attention(QMT, KMT, VM, 1024, h, 192, 128, r0, kb_m, mix, h * 128, Ssb, PTs, Vs)
                    rows = TL // 64
                    rg = 2 * (t - NTC)
                    rsf = lambda r: min(max(r - 4, 0), rows - 8)
                    ks = min(rsf(rg), rows - 9)
                    kb_l = [(l0 + ks * 64 + i * 128, 128) for i in range(4)] + [(l0 + ks * 64 + 512, 64)]
                    for h in range(4):
                        fw.op("pool", lambda e: e.memset(Bt[:, 0:576], NEG), [], [Bt])
                        for dr in range(2):
                            r = rg + dr
                            j0 = rsf(r) - ks
                            a0 = rsf(r) - r + 7
                            fw.dma("sp", lambda e, h=h, dr=dr, j0=j0, a0=a0: e.dma_start(out=Bt[dr * 64:(dr + 1) * 64, j0 * 64:(j0 + 8) * 64],
                                                                                          in_=CBD[h, dr * 64:(dr + 1) * 64, a0 * 64:(a0 + 8) * 64]), [CBD], [Bt])
                        attention(QNT, KNT, VN, 512, h, 128, 128, r0, kb_l + kb_c, mix, 1024 + h * 128, Ssb, PTs, Vs, bias=Bt)
                phaseB_tail(t)
            if cfg.stages <= 2:
                continue
            peer_phase(l)
        for t in range(NT):
            fw.dma("pool", lambda e, t=t: e.dma_start(out=xout[t * 128:(t + 1) * 128, :], in_=X[t * 128:(t + 1) * 128, :]), [X], [xout])
        fw.finish()
        fw.emit()
        print("instructions:", fw.ninst)
    return nc


def host_inputs(cfg, core, inp):
    L, NS = cfg.depth, cfg.nseq
    f = np.float32
    b = core // 4
    xs = [np.asarray(inp["x_prompt"][core * NS:(core + 1) * NS], f).reshape(NS * 256, D)]
    if cfg.lat:
        xs.append(np.asarray(inp["x_sample"][b], f)[:cfg.lat])
    m = {}
    m["xin"] = np.ascontiguousarray(np.concatenate(xs, 0))
    cv = np.stack([np.asarray(inp["c_ctx"], f), np.asarray(inp["c"][b], f)], 0)
    m["cvecT"] = np.ascontiguousarray(cv.reshape(2, 16, 128).transpose(2, 1, 0))
    m["ada_w"] = np.asarray(inp["ada_w"][:L], f)
    m["ada_b"] = np.asarray(inp["ada_b"][:L], f)
    m["ada_bT"] = np.ascontiguousarray(np.asarray(inp["ada_b"][:L], f).reshape(L, 96, 128).transpose(0, 2, 1))
    m["gmixT"] = np.ascontiguousarray(np.asarray(inp["norm_mix_g"][:L], f).reshape(L, 16, 128).transpose(0, 2, 1))
    m["gffnT"] = np.ascontiguousarray(np.asarray(inp["norm_ffn_g"][:L], f).reshape(L, 16, 128).transpose(0, 2, 1))
    m["w_in"] = np.asarray(inp["w_in"][:L], f)
    m["w_uq"] = np.asarray(inp["mla_w_uq"][:L], f)
    m["w_ukv"] = np.asarray(inp["mla_w_ukv"][:L], f)
    m["w_out"] = np.asarray(inp["w_out"][:L], f)
    m["gvec"] = np.ascontiguousarray(np.concatenate([np.asarray(inp[k][:L], f) for k in
                                                     ("mla_q_norm_g", "mla_kv_norm_g", "mla_q_head_g", "mla_k_head_g", "na_q_head_g", "na_k_head_g")], 1))
    m["conv_wT"] = np.ascontiguousarray(np.asarray(inp["conv_w"][:L], f).transpose(0, 2, 1))
    m["ident"] = np.eye(128, dtype=f)
    sel = np.zeros((2, 2, 128), f)
    sel[0, 0, :] = 1
    sel[1, 1, :] = 1
    m["sel"] = sel
    m["peer_wq"] = np.asarray(inp["peer_w_q"][:L], f)
    m["iota"] = np.ascontiguousarray(np.broadcast_to(np.arange(256, dtype=f)[None, :], (128, 256)))
    m["subT"] = np.ascontiguousarray(np.asarray(inp["peer_sub_keys"][:L], f).transpose(0, 4, 1, 2, 3).reshape(L, 128, 16, 128))
    for i in range(L):
        m["peer_u%d" % i] = np.asarray(inp["peer_u"][i], f)
        m["peer_v%d" % i] = np.asarray(inp["peer_v"][i], f)
    if cfg.lat:
        TL = cfg.lat
        tt = np.arange(TL)
        row = (tt // 64).astype(f)
        col = (tt % 64).astype(f)
        inv = (np.float32(10000.0) ** (-np.arange(16, dtype=f) / np.float32(16))).astype(f)
        ang = np.concatenate([row[:, None] * inv, col[:, None] * inv], -1).astype(f)
        m["rope"] = np.ascontiguousarray(np.concatenate([np.cos(ang), np.sin(ang)], -1).astype(f))
        m["c_ckv"] = np.ascontiguousarray(np.asarray(inp["cache_mla_ckv"][b, :L], f))
        m["c_kpe"] = np.ascontiguousarray(np.asarray(inp["cache_mla_kpe"][b, :L], f))
        m["c_nak"] = np.ascontiguousarray(np.asarray(inp["cache_na_k"][b, :L], f))
        m["c_nav"] = np.ascontiguousarray(np.asarray(inp["cache_na_v"][b, :L], f))
        m["relT"] = np.ascontiguousarray(np.asarray(inp["na_rel_bias"][:L], f).transpose(0, 3, 1, 2))
        OH = np.zeros((31, 64, 128), f)
        cmk = np.full((128, 64), NEG, f)
        for qc in range(64):
            cs = min(max(qc - 8, 0), 48)
            for kc in range(64):
                bb = kc - qc + 15
                if 0 <= bb < 31:
                    OH[bb, kc, qc] = 1
                    OH[bb, kc, 64 + qc] = 1
                if cs <= kc < cs + 16:
                    cmk[qc, kc] = 0
                    cmk[64 + qc, kc] = 0
        m["OH"] = OH
        m["cmask"] = cmk
    return m


_NC_CACHE = {}


def kernel(**inp):
    cfg = Cfg()
    n = 8
    if "nc" not in _NC_CACHE:
        _NC_CACHE["nc"] = build(cfg)
    nc = _NC_CACHE["nc"]
    in_maps = [host_inputs(cfg, c, inp) for c in range(n)]
    res = run_bass_kernel_spmd(nc, in_maps, core_ids=list(range(n)))
    R = res.results
    NS, L = cfg.nseq, cfg.depth
    y_prompt = np.concatenate([R[c]["xout"][:NS * 256].reshape(NS, 256, D) for c in range(n)], 0)
    y_sample = np.stack([np.concatenate([R[b * 4 + q]["xout"][NS * 256 + q * 256:NS * 256 + (q + 1) * 256] for q in range(4)], 0) for b in range(2)], 0)
    ckv = np.concatenate([R[c]["o_ckv"] for c in range(n)], 0)
    kpe = np.concatenate([R[c]["o_kpe"] for c in range(n)], 0)
    nak = np.concatenate([R[c]["o_nak"] for c in range(n)], 0)
    nav = np.concatenate([R[c]["o_nav"] for c in range(n)], 0)
    return (y_prompt.astype(np.float32), y_sample.astype(np.float32), ckv, kpe, nak, nav)
```

```python
from contextlib import ExitStack
import numpy as np
import concourse.bass as bass
import concourse.mybir as mybir
from concourse.bass_utils import run_bass_kernel_spmd

F32 = mybir.dt.float32
I32 = mybir.dt.int32
AF = mybir.ActivationFunctionType
ALU = mybir.AluOpType
AX = mybir.AxisListType

D = 2048
DIN = 3904
NEG = -30000.0


class Buf:
    __slots__ = ("t", "lw", "rd", "name")

    def __init__(self, t, name=""):
        self.t = t
        self.lw = None
        self.rd = {}
        self.name = name

    def __getitem__(self, k):
        return self.t[k]


class FW:
    ENG = ("pe", "act", "dve", "pool", "sp")
    NDMA = 6

    def __init__(self, nc, es, same_engine_sync=True):
        self.nc = nc
        self.es = es
        self.same = same_engine_sync
        self.sem = {}
        self.cnt = {}
        for e in self.ENG:
            self.sem[e] = es.enter_context(nc.semaphore("s_" + e))
            self.cnt[e] = 0
        self.dq = {}
        for q in ("sp", "pool", "act"):
            slots = []
            for i in range(self.NDMA):
                k = "d_%s%d" % (q, i)
                self.sem[k] = es.enter_context(nc.semaphore(k))
                self.cnt[k] = 0
                slots.append(k)
            self.dq[q] = [slots, 0]
        self.seen = {e: {} for e in self.ENG}
        self.prog = {e: [] for e in self.ENG}
        self.ninst = 0
        self._psi = 0

    def sb(self, name, shape, dt=F32):
        t = self.es.enter_context(self.nc.sbuf_tensor("sb_" + name, list(shape), dt))
        return Buf(t, name)

    def ps(self, name, shape, dt=F32):
        t = self.es.enter_context(self.nc.psum_tensor(name, list(shape), dt))
        return Buf(t, name)

    def dram(self, name, shape, dt=F32, kind="Internal"):
        t = self.nc.dram_tensor(name, list(shape), dt, kind=kind)
        return Buf(t, name)

    def _wait(self, eng, key, val):
        if key == eng and not self.same:
            return
        if self.seen[eng].get(key, 0) >= val:
            return
        self.prog[eng].append(("w", key, val))
        self.seen[eng][key] = val

    def _deps(self, eng, reads, writes):
        for b in reads:
            if b.lw is not None:
                self._wait(eng, *b.lw)
        for b in writes:
            if b.lw is not None:
                self._wait(eng, *b.lw)
            for k, v in b.rd.items():
                self._wait(eng, k, v)

    def _mark(self, key, val, reads, writes):
        for b in reads:
            if b.rd.get(key, 0) < val:
                b.rd[key] = val
        for b in writes:
            b.lw = (key, val)
            b.rd = {}

    def op(self, eng, fn, reads=(), writes=()):
        self._deps(eng, reads, writes)
        self.cnt[eng] += 1
        self.prog[eng].append(("i", fn, eng, 1))
        self._mark(eng, self.cnt[eng], reads, writes)
        self.ninst += 1

    def dma(self, q, fn, reads=(), writes=()):
        slots, i = self.dq[q]
        k = slots[i % len(slots)]
        self.dq[q][1] = i + 1
        if self.cnt[k] > 0:
            self._wait(q, k, self.cnt[k])
        self._deps(q, reads, writes)
        self.cnt[k] += 16
        self.prog[q].append(("i", fn, k, 16))
        self._mark(k, self.cnt[k], reads, writes)
        self.ninst += 1

    def finish(self):
        for e in self.ENG:
            if e != "sp" and self.cnt[e] > 0:
                self._wait("sp", e, self.cnt[e])
        for q in self.dq:
            for k in self.dq[q][0]:
                if self.cnt[k] > 0:
                    self._wait("sp", k, self.cnt[k])

    def emit(self):
        nc = self.nc
        prog = self.prog
        sem = self.sem

        def run(e, items):
            for it in items:
                if it[0] == "w":
                    e.wait_ge(sem[it[1]], it[2])
                else:
                    it[1](e).then_inc(sem[it[2]], it[3])

        with nc.Block() as block:
            @block.tensor
            def _(e):
                run(e, prog["pe"])

            @block.scalar
            def _(e):
                run(e, prog["act"])

            @block.vector
            def _(e):
                run(e, prog["dve"])

            @block.gpsimd
            def _(e):
                run(e, prog["pool"])

            @block.sync
            def _(e):
                run(e, prog["sp"])


class Cfg:
    def __init__(self, depth=4, nseq=4, lat=1024, stages=99):
        self.depth = depth
        self.nseq = nseq
        self.lat = lat
        self.stages = stages


GOFF = {"qn": 0, "kvn": 512, "qh": 768, "kh": 960, "naq": 1152, "nak": 1280}
GLEN = 1408


CBW = 512
ABW = 256
NEG2 = -1.0e30
U32 = mybir.dt.uint32
BF16 = mybir.dt.bfloat16


def build(cfg):
    L = cfg.depth
    NS = cfg.nseq
    NTC = NS * 2
    TL = cfg.lat
    NTL = TL // 128
    NT = NTC + NTL
    NTOK = NT * 128
    NCACHE = 512 if TL else 0
    nc = bass.Bass("TRN2", target_bir_lowering=False)

    def din(name, shape, dt=F32):
        return Buf(nc.dram_tensor(name, list(shape), dt, kind="ExternalInput"), name)

    def dout(name, shape, dt=F32):
        return Buf(nc.dram_tensor(name, list(shape), dt, kind="ExternalOutput"), name)

    xin = din("xin", [NTOK, D])
    cvecT = din("cvecT", [128, 16, 2])
    ada_w = din("ada_w", [L, D, 6 * D])
    ada_bT = din("ada_bT", [L, 128, 96])
    ada_b = din("ada_b", [L, 6 * D])
    gmixT = din("gmixT", [L, 128, 16])
    gffnT = din("gffnT", [L, 128, 16])
    w_in = din("w_in", [L, D, DIN])
    w_uq = din("w_uq", [L, 512, 1536])
    w_ukv = din("w_ukv", [L, 256, 2048])
    w_out = din("w_out", [L, D, D])
    gvec = din("gvec", [L, GLEN])
    conv_wT = din("conv_wT", [L, 3, 512])
    ident_d = din("ident", [128, 128])
    sel_d = din("sel", [2, 2, 128])
    peer_wq = din("peer_wq", [L, D, D])
    subT_d = din("subT", [L, 128, 16, 128])
    peer_u = [din("peer_u%d" % i, [16384, D]) for i in range(L)]
    peer_v = [din("peer_v%d" % i, [16384, D]) for i in range(L)]
    iota_d = din("iota", [128, 256])
    if TL:
        rope_d = din("rope", [TL, 64])
        c_ckv = din("c_ckv", [L, 512, 256])
        c_kpe = din("c_kpe", [L, 512, 64])
        c_nak = din("c_nak", [L, 4, 512, 128])
        c_nav = din("c_nav", [L, 4, 512, 128])
        relT_d = din("relT", [L, 31, 4, 15])
        OH_d = din("OH", [31, 64, 128])
        cm_d = din("cmask", [128, 64])

    xout = dout("xout", [NTOK, D])
    o_ckv = dout("o_ckv", [NS, L, 256, 256])
    o_kpe = dout("o_kpe", [NS, L, 256, 64])
    o_nak = dout("o_nak", [NS, L, 4, 256, 128])
    o_nav = dout("o_nav", [NS, L, 4, 256, 128])

    with ExitStack() as es:
        fw = FW(nc, es)
        NKT = NTOK + NCACHE
        X = fw.dram("Xs", [NTOK, D])
        QMT = fw.dram("QMT", [8, 192, NTOK])
        KMT = fw.dram("KMT", [8, 192, NKT])
        VM = fw.dram("VM", [NKT, 1024])
        QNT = fw.dram("QNT", [4, 128, NTOK])
        KNT = fw.dram("KNT", [4, 128, NKT])
        VN = fw.dram("VN", [NKT, 512])
        NSEQ = NS + (1 if TL else 0)
        CU = fw.dram("CU", [NTOK + 2 * NSEQ, 512])
        GBs = fw.dram("GBs", [NTOK, 512])
        GD = fw.dram("GD", [2, 2, 128, D])
        CBD = fw.dram("CBD", [4, 128, 15 * 64])

        def seq_of_tile(t):
            if t < NTC:
                si = t // 2
            else:
                si = NS
            return si, t * 128 + 2 * si + 1

        pbig = fw.ps("pbig", [128, 4096])
        psb = [Buf(pbig.t[:, i * 512:(i + 1) * 512], "ps%d" % i) for i in range(8)]
        grp_i = [0]

        ps_hi = [False]

        def PS():
            if ps_hi[0]:
                b = psb[4 + fw._psi % 4]
            else:
                b = psb[fw._psi % 8]
            fw._psi += 1
            return b

        def PSG():
            g = grp_i[0] % 2
            grp_i[0] += 1
            return psb[g * 4:(g + 1) * 4], pbig.t[:, g * 2048:(g + 1) * 2048]

        ident = fw.sb("ident", [128, 128])
        sel = fw.sb("sel", [2, 2, 128])
        fw.dma("sp", lambda e: e.dma_start(out=ident[:], in_=ident_d[:, :]), [ident_d], [ident])
        fw.dma("sp", lambda e: e.dma_start(out=sel[:], in_=sel_d[:, :, :]), [sel_d], [sel])
        sT = fw.sb("sT", [128, 16, 2])
        fw.dma("sp", lambda e: e.dma_start(out=sT[:], in_=cvecT[:, :, :]), [cvecT], [sT])
        fw.op("act", lambda e: e.activation(out=sT[:], in_=sT[:], func=AF.Silu), [sT], [sT])

        for t in range(NT):
            fw.dma("pool", lambda e, t=t: e.dma_start(out=X[t * 128:(t + 1) * 128, :], in_=xin[t * 128:(t + 1) * 128, :]), [xin], [X])

        B = [fw.sb("B%d" % i, [128, D]) for i in range(10)]
        Hhb = fw.sb("Hhb", [128, D], BF16)
        identb = fw.sb("identb", [128, 128], BF16)
        G16 = [Buf(B[5 + i // 2].t[:, (i % 2) * 1024:(i % 2) * 1024 + 1024].bitcast(BF16), "g16_%d" % i) for i in range(6)]

        def alias_in(views, bases):
            for i, v in enumerate(views):
                b = bases[i // 2]
                v.lw = b.lw
                v.rd = dict(b.rd)

        def alias_out(views, bases):
            for i, v in enumerate(views):
                b = bases[i // 2]
                if v.lw is not None:
                    b.rd[v.lw[0]] = max(b.rd.get(v.lw[0], 0), v.lw[1])
                for kk, vv in v.rd.items():
                    b.rd[kk] = max(b.rd.get(kk, 0), vv)

        Wb = {"in": [fw.dram("Wb_in%d" % i, [D, DIN], BF16) for i in range(L)],
              "uq": [fw.dram("Wb_uq%d" % i, [512, 1536], BF16) for i in range(L)],
              "ukv": [fw.dram("Wb_ukv%d" % i, [256, 2048], BF16) for i in range(L)],
              "out": [fw.dram("Wb_out%d" % i, [D, D], BF16) for i in range(L)],
              "pq": [fw.dram("Wb_pq%d" % i, [D, D], BF16) for i in range(L)]}
        alias_in(G16, [B[5], B[6], B[7]])
        itw = 0
        for i in range(L):
            for (srcb, key, R_, C_) in ((w_in, "in", D, DIN), (w_uq, "uq", 512, 1536), (w_ukv, "ukv", 256, 2048), (w_out, "out", D, D), (peer_wq, "pq", D, D)):
                dstb = Wb[key][i]
                for r in range(R_ // 128):
                    c0 = 0
                    while c0 < C_:
                        w = min(2048, C_ - c0)
                        fb = B[itw % 4]
                        gb = G16[itw % 4]
                        fw.dma("sp", lambda e, srcb=srcb, i=i, r=r, c0=c0, w=w, fb=fb: e.dma_start(out=fb[:, 0:w], in_=srcb[i, r * 128:(r + 1) * 128, c0:c0 + w]), [srcb], [fb])
                        if itw % 2 == 0:
                            fw.op("act", lambda e, fb=fb, gb=gb, w=w: e.copy(out=gb[:, 0:w], in_=fb[:, 0:w]), [fb], [gb])
                        else:
                            fw.op("dve", lambda e, fb=fb, gb=gb, w=w: e.tensor_copy(out=gb[:, 0:w], in_=fb[:, 0:w]), [fb], [gb])
                        fw.dma("pool", lambda e, dstb=dstb, r=r, c0=c0, w=w, gb=gb: e.dma_start(out=dstb[r * 128:(r + 1) * 128, c0:c0 + w], in_=gb[:, 0:w]), [gb], [dstb])
                        itw += 1
                        c0 += w
        alias_out(G16, [B[5], B[6], B[7]])
        UB = [fw.dram("UB%d" % i, [16384, D], BF16) for i in range(L)]
        VB = [fw.dram("VB%d" % i, [16384, D], BF16) for i in range(L)]
        if cfg.stages > 2:
            alias_in(G16, [B[5], B[6], B[7]])
            it = 0
            for (src, dst) in [(peer_u[i], UB[i]) for i in range(L)] + [(peer_v[i], VB[i]) for i in range(L)]:
                for c in range(128):
                    fb = B[it % 4]
                    gb = G16[it % 4]
                    fw.dma("sp", lambda e, src=src, c=c, fb=fb: e.dma_start(out=fb[:], in_=src[c * 128:(c + 1) * 128, :]), [src], [fb])
                    if it % 2 == 0:
                        fw.op("act", lambda e, fb=fb, gb=gb: e.copy(out=gb[:], in_=fb[:]), [fb], [gb])
                    else:
                        fw.op("dve", lambda e, fb=fb, gb=gb: e.tensor_copy(out=gb[:], in_=fb[:]), [fb], [gb])
                    fw.dma("pool", lambda e, dst=dst, c=c, gb=gb: e.dma_start(out=dst[c * 128:(c + 1) * 128, :], in_=gb[:]), [gb], [dst])
                    it += 1
            alias_out(G16, [B[5], B[6], B[7]])
        fw.op("dve", lambda e: e.tensor_copy(out=identb[:], in_=ident[:]), [ident], [identb])
        wsl = [fw.sb("wsl%d" % i, [128, 16, CBW], BF16) for i in range(2)]
        wi = [0]

        def WSL():
            b = wsl[wi[0] % 2]
            wi[0] += 1
            return b

        modT = fw.sb("modT", [128, 96, 2])
        abT = fw.sb("abT", [128, 96])
        rowb = fw.sb("rowb", [2, ABW])
        rowr = fw.sb("rowr", [2, ABW])
        gmix = fw.sb("gmix", [128, 16])
        gffn = fw.sb("gffn", [128, 16])
        A1 = fw.sb("A1", [128, 16, 2])
        A2 = fw.sb("A2", [128, 16, 2])
        gv = fw.sb("gv", [128, GLEN])
        cw = fw.sb("cw", [128, 3, 512])
        hT = fw.sb("hT", [128, 16, 128], BF16)
        z = fw.sb("z", [128, DIN])
        ss = fw.sb("ss", [128, 16])
        rs = fw.sb("rs", [128, 16])
        tT = fw.sb("tT", [128, 4, 128])
        tTb = fw.sb("tTb", [128, 4, 128], BF16)
        stg = B[5].t[:, :].rearrange("p (a h t) -> p a h t", a=2, h=8)
        stgb = B[5]
        kTs = fw.sb("kTs", [128, 2, 1536])
        G16.append(Buf(kTs.t[:, 0, 0:1024].bitcast(BF16), "g16_6"))
        G16.append(Buf(kTs.t[:, 1, 0:1024].bitcast(BF16), "g16_7"))
        qTs = fw.sb("qTs", [128, 2, 128])
        mx = fw.sb("mx", [128, 4])
        subk = fw.sb("subk", [128, 16, 128])
        TV = fw.sb("TV", [128, 16, 16])
        TIu = fw.sb("TIu", [128, 16, 16], U32)
        TIf = fw.sb("TIf", [128, 16, 16])
        TS = fw.sb("TS", [128, 8, 16])
        PIu = fw.sb("PIu", [128, 8, 16], U32)
        PIf = fw.sb("PIf", [128, 8, 16])
        iota = fw.sb("iota", [128, 256])
        fw.dma("sp", lambda e: e.dma_start(out=iota[:], in_=iota_d[:, :]), [iota_d], [iota])
        EI = fw.sb("EI", [128, 128])
        GT = fw.sb("GT", [128, 128])
        IDX = fw.sb("IDX", [128, 128], I32)
        IDX2 = fw.sb("IDX2", [128, 128], I32)
        GTT = fw.sb("GTT", [128, 128])
        GTT2 = fw.sb("GTT2", [128, 128])
        ACTV = fw.sb("ACTV", [128, 128])
        WT = fw.sb("WT", [128, 128])
        ZW = [fw.sb("ZW%d" % i, [128, 256], BF16) for i in range(4)]
        for i in range(4):
            fw.op("pool", lambda e, i=i: e.memset(ZW[i][:], 0.0), [], [ZW[i]])
        zrow = fw.sb("zrow", [1, 512])
        fw.op("pool", lambda e: e.memset(zrow[:], 0.0), [], [zrow])
        for si in range(NSEQ):
            st = si * 256 if si < NS else NS * 256
            ln = 256 if si < NS else TL
            for rr in (st + 2 * si, st + 2 * si + 1 + ln):
                fw.dma("pool", lambda e, rr=rr: e.dma_start(out=CU[rr:rr + 1, :], in_=zrow[:, :]), [zrow], [CU])
        if TL:
            ropet = fw.sb("ropet", [128, NTL, 64])
            fw.dma("sp", lambda e: e.dma_start(out=ropet[:], in_=rope_d.t.ap().rearrange("(n p) c -> p n c", p=128)), [rope_d], [ropet])
            relTs = fw.sb("relTs", [31, 4, 15])
            cm = fw.sb("cm", [128, 64])
            fw.dma("sp", lambda e: e.dma_start(out=cm[:], in_=cm_d[:, :]), [cm_d], [cm])
            CB2 = fw.sb("CB2", [128, 15, 64])
            Bt = B[8]

        def rms_rstd(src_ap, srcbuf, n, col, junk):
            fw.op("act", lambda e: e.activation(out=junk[:, 0:n], in_=src_ap, func=AF.Square, scale=float(n ** -0.5),
                                                accum_out=ss[:, col:col + 1]), [srcbuf], [junk, ss])
            fw.op("act", lambda e: e.activation(out=rs[:, col:col + 1], in_=ss[:, col:col + 1], func=AF.Sqrt, bias=1e-6, scale=1.0), [ss], [rs])
            fw.op("dve", lambda e: e.reciprocal(out=rs[:, col:col + 1], in_=rs[:, col:col + 1]), [rs], [rs])

        def heads_norm(src_ap, srcbuf, nh, dh, gname, dst_ap, dstbuf, junk, extra_scale=1.0):
            jv = junk[:, 0:nh * dh].rearrange("p (h d) -> p h d", h=nh)
            fw.op("dve", lambda e: e.tensor_tensor(out=jv, in0=src_ap, in1=src_ap, op=ALU.mult), [srcbuf], [junk])
            fw.op("dve", lambda e: e.tensor_reduce(out=ss[:, 0:nh], in_=jv, axis=AX.X, op=ALU.add), [junk], [ss])
            fw.op("act", lambda e: e.activation(out=rs[:, 0:nh], in_=ss[:, 0:nh], func=AF.Sqrt, bias=1e-6, scale=float(1.0 / dh)), [ss], [rs])
            fw.op("dve", lambda e: e.reciprocal(out=rs[:, 0:nh], in_=rs[:, 0:nh]), [rs], [rs])
            fw.op("dve", lambda e: e.tensor_tensor(out=dst_ap, in0=src_ap, in1=rs[:, 0:nh].unsqueeze(2).to_broadcast([128, nh, dh]), op=ALU.mult),
                  [srcbuf, rs], [dstbuf])
            g0 = GOFF[gname]
            fw.op("dve", lambda e: e.scalar_tensor_tensor(out=dst_ap, in0=dst_ap, scalar=float(extra_scale),
                                                          in1=gv[:, g0:g0 + dh].unsqueeze(1).to_broadcast([128, nh, dh]), op0=ALU.mult, op1=ALU.mult),
                  [dstbuf, gv], [dstbuf])

        def rope(buf, nh, lt, junk):
            v = buf[:, 0:nh * 192].rearrange("p (h d) -> p h d", h=nh)
            x1, x2 = v[:, :, 128:160], v[:, :, 160:192]
            cs = ropet[:, lt, 0:32].unsqueeze(1).to_broadcast([128, nh, 32])
            sn = ropet[:, lt, 32:64].unsqueeze(1).to_broadcast([128, nh, 32])
            j = junk[:, 0:nh * 128].rearrange("p (h d) -> p h d", h=nh)
            a, b_, c, d_ = j[:, :, 0:32], j[:, :, 32:64], j[:, :, 64:96], j[:, :, 96:128]
            fw.op("dve", lambda e: e.tensor_tensor(out=a, in0=x1, in1=cs, op=ALU.mult), [buf, ropet], [junk])
            fw.op("dve", lambda e: e.tensor_tensor(out=b_, in0=x2, in1=sn, op=ALU.mult), [buf, ropet], [junk])
            fw.op("dve", lambda e: e.tensor_tensor(out=c, in0=x1, in1=sn, op=ALU.mult), [buf, ropet], [junk])
            fw.op("dve", lambda e: e.tensor_tensor(out=d_, in0=x2, in1=cs, op=ALU.mult), [buf, ropet], [junk])
            fw.op("dve", lambda e: e.tensor_tensor(out=x1, in0=a, in1=b_, op=ALU.subtract), [junk], [buf])
            fw.op("dve", lambda e: e.tensor_tensor(out=x2, in0=c, in1=d_, op=ALU.add), [junk], [buf])

        def transpose_chunks(src_ap_fn, srcbuf, nch, dst_fn, dstbuf, width=128, scale=None, bias=None, rows=128):
            for k in range(nch):
                p = PS()
                fw.op("pe", lambda e, k=k, p=p: e.transpose(out=p[0:width, 0:rows], in_=src_ap_fn(k), identity=ident[0:rows, 0:rows]), [srcbuf, ident], [p])
                if scale is None:
                    fw.op("act", lambda e, k=k, p=p: e.copy(out=dst_fn(k), in_=p[0:width, 0:rows]), [p], [dstbuf])
                else:
                    sc, bi = scale(k), bias(k)
                    fw.op("act", lambda e, k=k, p=p, sc=sc, bi=bi: e.activation(out=dst_fn(k), in_=p[0:width, 0:rows], func=AF.Identity,
                                                                                 bias=bi, scale=sc), [p, A1, A2, modT], [dstbuf])

        def gemm(lhsT_fn, lbuf, nk, Wd, w_ap_fn, ncols, out_fn, kparts=128, c_lo=0):
            c0 = c_lo
            while c0 < ncols:
                w = min(CBW, ncols - c0)
                slab = WSL()
                fw.dma("sp", lambda e, c0=c0, w=w, slab=slab: e.dma_start(out=slab[0:kparts, 0:nk, 0:w], in_=w_ap_fn(c0, w)), [Wd], [slab])
                p = PS()
                for k in range(nk):
                    fw.op("pe", lambda e, k=k, p=p, slab=slab, w=w: e.matmul(p[:, 0:w], lhsT=lhsT_fn(k), rhs=slab[0:kparts, k, 0:w],
                                                                            start=(k == 0), stop=(k == nk - 1)), [lbuf, slab], [p])
                out_fn(p, c0, w)
                c0 += w

        def store_T(src, nh, dq, dstT, tok0):
            sv = src[:, 0:nh * dq].rearrange("p (h d) -> p h d", h=nh)
            for h in range(nh):
                transpose_chunks(lambda k, h=h: sv[:, h, 0:128], src, 1, lambda k, h=h: stg[:, 0, h, :], stgb)
                if dq > 128:
                    transpose_chunks(lambda k, h=h: sv[:, h, 128:dq], src, 1, lambda k, h=h: stg[0:dq - 128, 1, h, :], stgb, width=dq - 128)
            fw.dma("pool", lambda e: e.dma_start(out=dstT[:, 0:128, tok0:tok0 + 128].rearrange("h d t -> d h t"), in_=stg[:, 0, 0:nh, :]), [stgb], [dstT])
            if dq > 128:
                fw.dma("pool", lambda e: e.dma_start(out=dstT[:, 128:dq, tok0:tok0 + 128].rearrange("h d t -> d h t"), in_=stg[0:dq - 128, 1, 0:nh, :]), [stgb], [dstT])

        def kv_pipeline(l, ckvn_ap, ckvn_buf, kpe_ap, kpe_buf, tok0, lt, t1, t2, junk):
            transpose_chunks(lambda k: ckvn_ap[:, k * 128:(k + 1) * 128], ckvn_buf, 2, lambda k: tTb[:, k, :], tTb)
            gemm(lambda k: tTb[:, k, :], tTb, 2, Wb["ukv"][l], lambda c0, w, l=l: Wb["ukv"][l][:, c0:c0 + w].rearrange("(k p) n -> p k n", p=128), 2048,
                 lambda p, c0, w: fw.op("act", lambda e, p=p, c0=c0, w=w: e.copy(out=t2[:, c0:c0 + w], in_=p[:, 0:w]), [p], [t2]))
            kvv = t2[:, 0:2048].rearrange("p (h d) -> p h d", h=8)
            kf = t1[:, 0:1536].rearrange("p (h d) -> p h d", h=8)
            fw.op("dve", lambda e: e.tensor_copy(out=kf[:, :, 0:128], in_=kvv[:, :, 0:128]), [t2], [t1])
            fw.op("dve", lambda e: e.tensor_copy(out=kf[:, :, 128:192], in_=kpe_ap.unsqueeze(1).to_broadcast([128, 8, 64])), [kpe_buf], [t1])
            fw.dma("pool", lambda e: e.dma_start(out=VM[tok0:tok0 + 128, :].rearrange("t (h d) -> t h d", h=8), in_=kvv[:, :, 128:256]), [t2], [VM])
            heads_norm(kf, t1, 8, 192, "kh", kf, t1, junk)
            if lt is not None:
                rope(t1, 8, lt, junk)
            store_T(t1, 8, 192, KMT, tok0)

        def attention(QT, KT, V, vw, h, dq, dv, r0, keyblocks, mix, mixcol, Ssb, PTs, Vs, bias=None):
            nkb = len(keyblocks)
            nk = sum(n for _, n in keyblocks)
            nch = 2 if dq > 128 else 1
            fw.dma("sp", lambda e: e.dma_start(out=qTs[:, 0, :], in_=QT[h, 0:128, r0:r0 + 128]), [QT], [qTs])
            if nch == 2:
                fw.dma("sp", lambda e: e.dma_start(out=qTs[0:dq - 128, 1, :], in_=QT[h, 128:dq, r0:r0 + 128]), [QT], [qTs])
            off = 0
            i = 0
            while i < nkb:
                t0, n = keyblocks[i]
                j = i + 1
                tot = n
                while j < nkb and keyblocks[j][0] == t0 + tot:
                    tot += keyblocks[j][1]
                    j += 1
                fw.dma("sp", lambda e, t0=t0, tot=tot, off=off: e.dma_start(out=kTs[:, 0, off:off + tot], in_=KT[h, 0:128, t0:t0 + tot]), [KT], [kTs])
                if nch == 2:
                    fw.dma("sp", lambda e, t0=t0, tot=tot, off=off: e.dma_start(out=kTs[0:dq - 128, 1, off:off + tot], in_=KT[h, 128:dq, t0:t0 + tot]), [KT], [kTs])
                off += tot
                i = j
            Vv = Vs[:, 0:12 * 128].rearrange("p (k d) -> p k d", k=12)
            kb = 0
            while kb < nkb:
                t0, n = keyblocks[kb]
                j = kb + 1
                while n == 128 and j < nkb and keyblocks[j][1] == 128 and keyblocks[j][0] == t0 + (j - kb) * 128:
                    j += 1
                cnt = j - kb
                if cnt > 1:
                    fw.dma("pool", lambda e, kb=kb, t0=t0, cnt=cnt: e.dma_start(out=Vv[:, kb:kb + cnt, 0:dv],
                                                                                in_=V[t0:t0 + cnt * 128, h * dv:(h + 1) * dv].rearrange("(k p) d -> p k d", p=128)), [V], [Vs])
                else:
                    fw.dma("pool", lambda e, kb=kb, t0=t0, n=n: e.dma_start(out=Vv[0:n, kb, 0:dv], in_=V[t0:t0 + n, h * dv:(h + 1) * dv]), [V], [Vs])
                kb = j
            banks, _ = PSG()
            c0 = 0
            while c0 < nk:
                w = min(512, nk - c0)
                b = banks[c0 // 512]
                fw.op("pe", lambda e, b=b, c0=c0, w=w: e.matmul(b[:, 0:w], lhsT=qTs[:, 0, :], rhs=kTs[:, 0, c0:c0 + w], start=True, stop=(nch == 1)), [qTs, kTs], [b])
                if nch == 2:
                    fw.op("pe", lambda e, b=b, c0=c0, w=w: e.matmul(b[:, 0:w], lhsT=qTs[0:dq - 128, 1, :], rhs=kTs[0:dq - 128, 1, c0:c0 + w], start=False, stop=True),
                          [qTs, kTs], [b])
                if bias is not None:
                    fw.op("dve", lambda e, b=b, c0=c0, w=w: e.tensor_tensor(out=Ssb[:, c0:c0 + w], in0=b[:, 0:w], in1=bias[:, c0:c0 + w], op=ALU.add), [b, bias], [Ssb])
                else:
                    fw.op("act", lambda e, b=b, c0=c0, w=w: e.copy(out=Ssb[:, c0:c0 + w], in_=b[:, 0:w]), [b], [Ssb])
                c0 += w
            fw.op("dve", lambda e: e.reduce_max(out=mx[:, 0:1], in_=Ssb[:, 0:nk], axis=AX.X), [Ssb], [mx])
            fw.op("dve", lambda e: e.tensor_scalar(out=mx[:, 1:2], in0=mx[:, 0:1], scalar1=-1.0, scalar2=None, op0=ALU.mult), [mx], [mx])
            fw.op("act", lambda e: e.activation(out=Ssb[:, 0:nk], in_=Ssb[:, 0:nk], func=AF.Exp, bias=mx[:, 1:2], scale=1.0, accum_out=mx[:, 2:3]), [Ssb, mx], [Ssb, mx])
            fw.op("dve", lambda e: e.reciprocal(out=mx[:, 3:4], in_=mx[:, 2:3]), [mx], [mx])
            PTv = PTs[:, 0:12 * 128].rearrange("p (k d) -> p k d", k=12)
            off = 0
            for kb, (t0, n) in enumerate(keyblocks):
                p = PS()
                fw.op("pe", lambda e, p=p, off=off, n=n: e.transpose(out=p[0:n, 0:128], in_=Ssb[:, off:off + n], identity=ident[:]), [Ssb, ident], [p])
                fw.op("act", lambda e, p=p, kb=kb, n=n: e.copy(out=PTv[0:n, kb, :], in_=p[0:n, 0:128]), [p], [PTs])
                off += n
            po = PS()
            for kb, (t0, n) in enumerate(keyblocks):
                fw.op("pe", lambda e, kb=kb, n=n, po=po: e.matmul(po[:, 0:dv], lhsT=PTv[0:n, kb, :], rhs=Vv[0:n, kb, 0:dv], start=(kb == 0), stop=(kb == nkb - 1)),
                      [PTs, Vs], [po])
            fw.op("dve", lambda e, po=po: e.tensor_scalar(out=mix[:, mixcol:mixcol + dv], in0=po[:, 0:dv], scalar1=mx[:, 3:4], scalar2=None, op0=ALU.mult), [po, mx], [mix])


        def peer_phase(l):
            xn, junk, t2, G2t, pj = B[1], B[2], B[4], B[9], B[8]
            xts = [B[0], B[3]]
            IDXs = [IDX, IDX2]
            alias_in(G16, [B[5], B[6], B[7], kTs])
            SCv = z[:, 0:2048].rearrange("p (a n) -> p a n", a=16)
            TVv = TV[:].rearrange("p (h two) k -> p h two k", two=2)
            TIv = TIf[:].rearrange("p (h two) k -> p h two k", two=2)
            cand = xn[:, 0:2048].rearrange("p (h a b) -> p h a b", h=8, a=16)
            cidx = junk[:, 0:2048].rearrange("p (h a b) -> p h a b", h=8, a=16)
            candf = xn[:, 0:2048].rearrange("p (h c) -> p h c", h=8)
            cidxf = junk[:, 0:2048].rearrange("p (h c) -> p h c", h=8)
            scr = z[:, 2048:2304]
            scr2 = z[:, 2304:2560]
            GTv = GT[:].rearrange("p (h k) -> p h k", h=8)

            def prep_chunks(t):
                grp = 0 if t < NTC else 1
                r0 = t * 128
                xt = xts[t % 2]
                idx = IDXs[t % 2]
                ch = []

                def c_norm():
                    fw.dma("sp", lambda e: e.dma_start(out=xt[:], in_=X[r0:r0 + 128, :]), [X], [xt])
                    rms_rstd(xt[:], xt, D, 0, junk)
                    fw.op("dve", lambda e: e.tensor_scalar(out=xn[:], in0=xt[:], scalar1=rs[:, 0:1], scalar2=None, op0=ALU.mult), [xt, rs], [xn])
                    transpose_chunks(lambda k: xn[:, k * 128:(k + 1) * 128], xn, 16, lambda k: hT[:, k, :], hT,
                                     scale=lambda k: A2[:, k, grp:grp + 1], bias=lambda k: modT[:, 48 + k, grp:grp + 1])
                ch.append(c_norm)
                for cb in range(D // CBW):
                    def c_gemm(cb=cb):
                        gemm(lambda k: hT[:, k, :], hT, 16, Wb["pq"][l], lambda c0, w: Wb["pq"][l][:, c0:c0 + w].rearrange("(k p) n -> p k n", p=128), (cb + 1) * CBW,
                             lambda p, c0, w: fw.op("act", lambda e, p=p, c0=c0, w=w: e.copy(out=t2[:, c0:c0 + w], in_=p[:, 0:w]), [p], [t2]), c_lo=cb * CBW)
                    ch.append(c_gemm)

                def c_back():
                    for kk in range(16):
                        p = PS()
                        pv = p.t[:, 0:64].bitcast(BF16)
                        fw.op("pe", lambda e, kk=kk, pv=pv: e.transpose(out=pv, in_=hT[:, kk, :], identity=identb[:]), [hT, identb], [p])
                        fw.op("act", lambda e, kk=kk, pv=pv: e.copy(out=Hhb[:, kk * 128:(kk + 1) * 128], in_=pv), [p], [Hhb])
                ch.append(c_back)
                for g4 in range(4):
                    def c_scores(g4=g4):
                        for hp in range(g4 * 4, g4 * 4 + 4):
                            transpose_chunks(lambda k, hp=hp: t2[:, hp * 128:(hp + 1) * 128], t2, 1, lambda k, hp=hp: tT[:, hp % 4, :], tT)
                            p2 = PS()
                            fw.op("pe", lambda e, hp=hp, p2=p2: e.matmul(p2[:, 0:128], lhsT=tT[:, hp % 4, :], rhs=subk[:, hp, :], start=True, stop=True), [tT, subk], [p2])
                            fw.op("act", lambda e, hp=hp, p2=p2: e.copy(out=SCv[:, hp, :], in_=p2[:, 0:128]), [p2], [z])
                    ch.append(c_scores)
                for g4 in range(4):
                    def c_topk(g4=g4):
                        for hp in range(g4 * 4, g4 * 4 + 4):
                            fw.op("dve", lambda e, hp=hp: e.max(out=TV[:, hp, 0:8], in_=SCv[:, hp, :]), [z], [TV])
                            fw.op("dve", lambda e, hp=hp: e.max_index(out=TIu[:, hp, 0:8], in_max=TV[:, hp, 0:8], in_values=SCv[:, hp, :]), [z, TV], [TIu])
                            fw.op("dve", lambda e, hp=hp: e.match_replace(out=junk[:, 0:128], in_to_replace=TV[:, hp, 0:8], in_values=SCv[:, hp, :], imm_value=NEG2), [z, TV], [junk])
                            fw.op("dve", lambda e, hp=hp: e.max(out=TV[:, hp, 8:16], in_=junk[:, 0:128]), [junk], [TV])
                            fw.op("dve", lambda e, hp=hp: e.max_index(out=TIu[:, hp, 8:16], in_max=TV[:, hp, 8:16], in_values=junk[:, 0:128]), [junk, TV], [TIu])
                    ch.append(c_topk)

                def c_cand():
                    fw.op("dve", lambda e: e.tensor_copy(out=TIf[:], in_=TIu[:]), [TIu], [TIf])
                    for h in range(8):
                        fw.op("dve", lambda e, h=h: e.tensor_tensor(out=cand[:, h, :, :], in0=TVv[:, h, 0, :].unsqueeze(2).to_broadcast([128, 16, 16]),
                                                                    in1=TVv[:, h, 1, :].unsqueeze(1).to_broadcast([128, 16, 16]), op=ALU.add), [TV], [xn])
                        fw.op("dve", lambda e, h=h: e.scalar_tensor_tensor(out=cidx[:, h, :, :], in0=TIv[:, h, 0, :].unsqueeze(2).to_broadcast([128, 16, 16]), scalar=128.0,
                                                                           in1=TIv[:, h, 1, :].unsqueeze(1).to_broadcast([128, 16, 16]), op0=ALU.mult, op1=ALU.add), [TIf], [junk])
                ch.append(c_cand)
                for g2 in range(4):
                    def c_top16(g2=g2):
                        for h in range(g2 * 2, g2 * 2 + 2):
                            fw.op("dve", lambda e, h=h: e.max(out=TS[:, h, 0:8], in_=candf[:, h, :]), [xn], [TS])
                            fw.op("dve", lambda e, h=h: e.max_index(out=PIu[:, h, 0:8], in_max=TS[:, h, 0:8], in_values=candf[:, h, :]), [xn, TS], [PIu])
                            fw.op("dve", lambda e, h=h: e.match_replace(out=scr, in_to_replace=TS[:, h, 0:8], in_values=candf[:, h, :], imm_value=NEG2), [xn, TS], [z])
                            fw.op("dve", lambda e, h=h: e.max(out=TS[:, h, 8:16], in_=scr), [z], [TS])
                            fw.op("dve", lambda e, h=h: e.max_index(out=PIu[:, h, 8:16], in_max=TS[:, h, 8:16], in_values=scr), [z, TS], [PIu])
                    ch.append(c_top16)
                ch.append(lambda: fw.op("dve", lambda e: e.tensor_copy(out=PIf[:], in_=PIu[:]), [PIu], [PIf]))
                for h in range(8):
                    def c_eidx(h=h):
                        for k in range(16):
                            fw.op("dve", lambda e, k=k: e.scalar_tensor_tensor(out=scr2, in0=iota[:, :], scalar=PIf[:, h, k:k + 1], in1=cidxf[:, h, :],
                                                                                op0=ALU.is_equal, op1=ALU.mult, accum_out=EI[:, h * 16 + k:h * 16 + k + 1]), [iota, PIf, junk], [z, EI])
                    ch.append(c_eidx)

                def c_gates():
                    fw.op("dve", lambda e: e.tensor_tensor(out=GTv, in0=TS[:], in1=TS[:, :, 0:1].to_broadcast([128, 8, 16]), op=ALU.subtract), [TS], [GT])
                    fw.op("act", lambda e: e.activation(out=GT[:], in_=GT[:], func=AF.Exp), [GT], [GT])
                    fw.op("dve", lambda e: e.tensor_reduce(out=ss[:, 0:8], in_=GTv, axis=AX.X, op=ALU.add), [GT], [ss])
                    fw.op("dve", lambda e: e.reciprocal(out=rs[:, 0:8], in_=ss[:, 0:8]), [ss], [rs])
                    fw.op("dve", lambda e: e.tensor_tensor(out=GTv, in0=GTv, in1=rs[:, 0:8].unsqueeze(2).to_broadcast([128, 8, 16]), op=ALU.mult), [GT, rs], [GT])
                    pe_ = PS()
                    fw.op("pe", lambda e: e.transpose(out=pe_[:, 0:128], in_=EI[:], identity=ident[:]), [EI, ident], [pe_])
                    fw.op("dve", lambda e: e.tensor_copy(out=idx[:], in_=pe_[:, 0:128]), [pe_], [idx])
                    pg = PS()
                    fw.op("pe", lambda e: e.transpose(out=pg[:, 0:128], in_=GT[:], identity=ident[:]), [GT, ident], [pg])
                    fw.op("act", lambda e: e.copy(out=GTT2[:], in_=pg[:, 0:128]), [pg], [GTT2])
                ch.append(c_gates)
                return ch

            def passes(t, nxt):
                grp = 0 if t < NTC else 1
                r0 = t * 128
                xt = xts[t % 2]
                idx = IDXs[t % 2]
                fw.op("act", lambda e: e.copy(out=GTT[:], in_=GTT2[:]), [GTT2], [GTT])
                fw.dma("sp", lambda e: e.dma_start(out=G2t[:], in_=GD[1, grp, :, :]), [GD], [G2t])
                for tok in range(128):
                    ug = G16[tok % 8]
                    fw.dma("pool", lambda e, ug=ug, tok=tok: e.indirect_dma_start(
                        out=ug[:, :], out_offset=None, in_=UB[l][:, :],
                        in_offset=bass.IndirectOffsetOnAxis(ap=idx[:, tok:tok + 1], axis=0)), [UB[l], idx], [ug])
                    banks, gap = PSG()
                    for c in range(4):
                        fw.op("pe", lambda e, c=c, tok=tok, banks=banks: e.matmul(banks[c][:, :], lhsT=identb[:, tok:tok + 1].to_broadcast([128, 128]),
                                                                                  rhs=Hhb[:, c * 512:(c + 1) * 512], start=True, stop=True), [identb, Hhb], [banks[c]])
                    fw.op("dve", lambda e, ug=ug, gap=gap, tok=tok: e.scalar_tensor_tensor(out=pj[:], in0=ug[:], scalar=1.0, in1=gap, op0=ALU.mult, op1=ALU.mult,
                                                                                           accum_out=ACTV[:, tok:tok + 1]), [ug] + banks, [pj, ACTV])
                fw.op("act", lambda e: e.activation(out=WT[:], in_=ACTV[:], func=AF.Gelu), [ACTV], [WT])
                fw.op("dve", lambda e: e.tensor_tensor(out=WT[:], in0=WT[:], in1=GTT[:], op=ALU.mult), [WT, GTT], [WT])
                banks = psb[0:4]
                gap = pbig.t[:, 0:2048]
                ps_hi[0] = True
                ci = 0
                for tok in range(128):
                    vg = G16[tok % 8]
                    zw = ZW[tok % 4]
                    fw.dma("pool", lambda e, vg=vg, tok=tok: e.indirect_dma_start(
                        out=vg[:, :], out_offset=None, in_=VB[l][:, :],
                        in_offset=bass.IndirectOffsetOnAxis(ap=idx[:, tok:tok + 1], axis=0)), [VB[l], idx], [vg])
                    fw.op("act", lambda e, zw=zw, tok=tok: e.copy(out=zw[:, 127:128], in_=WT[:, tok:tok + 1]), [WT], [zw])
                    for c in range(4):
                        fw.op("pe", lambda e, c=c, tok=tok, zw=zw, vg=vg: e.matmul(banks[c][:, :], lhsT=zw[:, 127 - tok:255 - tok], rhs=vg[:, c * 512:(c + 1) * 512],
                                                                                 start=(tok == 0), stop=(tok == 127)), [zw, vg], [banks[c]])
                    if nxt is not None and tok % 3 == 2 and ci < len(nxt):
                        nxt[ci]()
                        ci += 1
                while nxt is not None and ci < len(nxt):
                    nxt[ci]()
                    ci += 1
                ps_hi[0] = False
                fw.op("dve", lambda e: e.tensor_tensor(out=pj[:], in0=gap, in1=G2t[:], op=ALU.mult), banks + [G2t], [pj])
                fw.op("dve", lambda e: e.tensor_tensor(out=xt[:], in0=xt[:], in1=pj[:], op=ALU.add), [xt, pj], [xt])
                fw.dma("sp", lambda e: e.dma_start(out=X[r0:r0 + 128, :], in_=xt[:]), [xt], [X])

            for c_ in prep_chunks(0):
                c_()
            for t in range(NT):
                passes(t, prep_chunks(t + 1) if t + 1 < NT else None)
            alias_out(G16, [B[5], B[6], B[7], kTs])

        for l in range(L):
            abrow0, abrow1 = B[8], B[9]
            fw.dma("sp", lambda e, l=l: e.dma_start(out=abT[:], in_=ada_bT[l, :, :]), [ada_bT], [abT])
            fw.dma("sp", lambda e, l=l: e.dma_start(out=abrow0[0:2, :], in_=ada_b[l:l + 1, 2 * D:3 * D].partition_broadcast(2)), [ada_b], [abrow0])
            fw.dma("sp", lambda e, l=l: e.dma_start(out=abrow1[0:2, :], in_=ada_b[l:l + 1, 5 * D:6 * D].partition_broadcast(2)), [ada_b], [abrow1])
            fw.dma("sp", lambda e, l=l: e.dma_start(out=gmix[:], in_=gmixT[l, :, :]), [gmixT], [gmix])
            fw.dma("sp", lambda e, l=l: e.dma_start(out=gffn[:], in_=gffnT[l, :, :]), [gffnT], [gffn])
            fw.dma("sp", lambda e, l=l: e.dma_start(out=gv[:], in_=gvec[l:l + 1, :].partition_broadcast(128)), [gvec], [gv])
            fw.dma("sp", lambda e, l=l: e.dma_start(out=cw[:], in_=conv_wT[l:l + 1, :, :].partition_broadcast(128)), [conv_wT], [cw])
            fw.dma("sp", lambda e, l=l: e.dma_start(out=subk[:], in_=subT_d[l, :, :, :]), [subT_d], [subk])
            Gst = B[7]
            NJB = 6 * D // ABW
            for jb in range(NJB):
                slab = WSL()
                s32 = slab.t[:, :, :].bitcast(F32)
                fw.dma("sp", lambda e, l=l, jb=jb, s32=s32: e.dma_start(
                    out=s32, in_=ada_w[l, :, jb * ABW:(jb + 1) * ABW].rearrange("(k p) n -> p k n", p=128)), [ada_w], [slab])
                pr = PS()
                for k in range(16):
                    fw.op("pe", lambda e, k=k, pr=pr, s32=s32: e.matmul(pr[0:2, 0:ABW], lhsT=sT[:, k, :], rhs=s32[:, k, :], start=(k == 0), stop=(k == 15)),
                          [slab, sT], [pr])
                fw.op("act", lambda e, pr=pr: e.copy(out=rowr[:], in_=pr[0:2, 0:ABW]), [pr], [rowr])
                for sub in range(ABW // 128):
                    j = jb * (ABW // 128) + sub
                    pm = PS()
                    fw.op("pe", lambda e, pm=pm, sub=sub: e.transpose(out=pm[:, 0:2], in_=rowr[0:2, sub * 128:(sub + 1) * 128], identity=ident[0:2, 0:2]), [rowr, ident], [pm])
                    fw.op("dve", lambda e, j=j, pm=pm: e.tensor_scalar(out=modT[:, j, :], in0=pm[:, 0:2], scalar1=abT[:, j:j + 1], scalar2=None, op0=ALU.add),
                          [pm, abT], [modT])
                c0 = jb * ABW
                which = 0 if 2 * D <= c0 < 3 * D else (1 if c0 >= 5 * D else None)
                if which is not None:
                    off = c0 - (2 * D if which == 0 else 5 * D)
                    abr = abrow0 if which == 0 else abrow1
                    fw.op("dve", lambda e, abr=abr, off=off: e.tensor_tensor(out=rowb[:], in0=rowr[:], in1=abr[0:2, off:off + ABW], op=ALU.add),
                          [rowr, abr], [rowb])
                    for grp in range(2):
                        pb = PS()
                        fw.op("pe", lambda e, pb=pb, grp=grp: e.matmul(pb[:, 0:ABW], lhsT=sel[:, grp, :], rhs=rowb[:, :], start=True, stop=True), [sel, rowb], [pb])
                        fw.op("act", lambda e, pb=pb: e.copy(out=Gst[:, 0:ABW], in_=pb[:, 0:ABW]), [pb], [Gst])
                        fw.dma("sp", lambda e, which=which, grp=grp, off=off: e.dma_start(out=GD[which, grp, :, off:off + ABW], in_=Gst[:, 0:ABW]), [Gst], [GD])
            for (A, g, j0) in ((A1, gmix, 16), (A2, gffn, 64)):
                fw.op("dve", lambda e, A=A, j0=j0: e.tensor_scalar(out=A[:], in0=modT[:, j0:j0 + 16, :], scalar1=1.0, scalar2=None, op0=ALU.add), [modT], [A])
                fw.op("dve", lambda e, A=A, g=g: e.tensor_tensor(out=A[:], in0=A[:], in1=g[:].unsqueeze(2).to_broadcast([128, 16, 2]), op=ALU.mult), [A, g], [A])

            xt, xn, junk, t1, t2 = B[0], B[1], B[2], B[3], B[4]
            for t in range(NT):
                grp = 0 if t < NTC else 1
                lt = None if t < NTC else t - NTC
                r0 = t * 128
                fw.dma("sp", lambda e, r0=r0: e.dma_start(out=xt[:], in_=X[r0:r0 + 128, :]), [X], [xt])
                rms_rstd(xt[:], xt, D, 0, junk)
                fw.op("dve", lambda e: e.tensor_scalar(out=xn[:], in0=xt[:], scalar1=rs[:, 0:1], scalar2=None, op0=ALU.mult), [xt, rs], [xn])
                transpose_chunks(lambda k: xn[:, k * 128:(k + 1) * 128], xn, 16, lambda k: hT[:, k, :], hT,
                                 scale=lambda k, grp=grp: A1[:, k, grp:grp + 1], bias=lambda k, grp=grp: modT[:, k, grp:grp + 1])
                gemm(lambda k: hT[:, k, :], hT, 16, Wb["in"][l], lambda c0, w, l=l: Wb["in"][l][:, c0:c0 + w].rearrange("(k p) n -> p k n", p=128), DIN,
                     lambda p, c0, w: fw.op("act", lambda e, p=p, c0=c0, w=w: e.copy(out=z[:, c0:c0 + w], in_=p[:, 0:w]), [p], [z]))
                rms_rstd(z[:, 0:512], z, 512, 1, junk)
                fw.op("dve", lambda e: e.scalar_tensor_tensor(out=t1[:, 0:512], in0=z[:, 0:512], scalar=rs[:, 1:2], in1=gv[:, 0:512], op0=ALU.mult, op1=ALU.mult),
                      [z, rs, gv], [t1])
                transpose_chunks(lambda k: t1[:, k * 128:(k + 1) * 128], t1, 4, lambda k: tTb[:, k, :], tTb)
                gemm(lambda k: tTb[:, k, :], tTb, 4, Wb["uq"][l], lambda c0, w, l=l: Wb["uq"][l][:, c0:c0 + w].rearrange("(k p) n -> p k n", p=128), 1536,
                     lambda p, c0, w: fw.op("act", lambda e, p=p, c0=c0, w=w: e.copy(out=t2[:, c0:c0 + w], in_=p[:, 0:w]), [p], [t2]))
                heads_norm(t2[:, 0:1536].rearrange("p (h d) -> p h d", h=8), t2, 8, 192, "qh",
                           t1[:, 0:1536].rearrange("p (h d) -> p h d", h=8), t1, junk, extra_scale=192 ** -0.5)
                if lt is not None:
                    rope(t1, 8, lt, junk)
                store_T(t1, 8, 192, QMT, r0)
                rms_rstd(z[:, 512:768], z, 256, 2, junk)
                fw.op("dve", lambda e: e.scalar_tensor_tensor(out=xn[:, 0:256], in0=z[:, 512:768], scalar=rs[:, 2:3], in1=gv[:, 512:768], op0=ALU.mult, op1=ALU.mult),
                      [z, rs, gv], [xn])
                if t < NTC:
                    s_, h_ = t // 2, t % 2
                    fw.dma("pool", lambda e, s_=s_, h_=h_, l=l: e.dma_start(out=o_ckv[s_, l, h_ * 128:(h_ + 1) * 128, :], in_=xn[:, 0:256]), [xn], [o_ckv])
                    fw.dma("pool", lambda e, s_=s_, h_=h_, l=l: e.dma_start(out=o_kpe[s_, l, h_ * 128:(h_ + 1) * 128, :], in_=z[:, 768:832]), [z], [o_kpe])
                    fw.dma("pool", lambda e, s_=s_, h_=h_, l=l: e.dma_start(
                        out=o_nav[s_, l, :, h_ * 128:(h_ + 1) * 128, :].rearrange("h t d -> t h d"),
                        in_=z[:, 1856:2368].rearrange("p (h d) -> p h d", h=4)), [z], [o_nav])
                kv_pipeline(l, xn[:, 0:256], xn, z[:, 768:832], z, r0, lt, t1, t2, junk)
                fw.dma("pool", lambda e, r0=r0: e.dma_start(out=VN[r0:r0 + 128, :], in_=z[:, 1856:2368]), [z], [VN])
                heads_norm(z[:, 832:1344].rearrange("p (h d) -> p h d", h=4), z, 4, 128, "naq",
                           t1[:, 0:512].rearrange("p (h d) -> p h d", h=4), t1, junk, extra_scale=128 ** -0.5)
                store_T(t1, 4, 128, QNT, r0)
                heads_norm(z[:, 1344:1856].rearrange("p (h d) -> p h d", h=4), z, 4, 128, "nak",
                           t2[:, 0:512].rearrange("p (h d) -> p h d", h=4), t2, junk)
                if t < NTC:
                    fw.dma("pool", lambda e, s_=s_, h_=h_, l=l: e.dma_start(
                        out=o_nak[s_, l, :, h_ * 128:(h_ + 1) * 128, :].rearrange("h t d -> t h d"),
                        in_=t2[:, 0:512].rearrange("p (h d) -> p h d", h=4)), [t2], [o_nak])
                store_T(t2, 4, 128, KNT, r0)
                si, cur = seq_of_tile(t)
                fw.op("dve", lambda e: e.tensor_tensor(out=t1[:, 1024:1536], in0=z[:, 2880:3392], in1=z[:, 3392:3904], op=ALU.mult), [z], [t1])
                fw.dma("pool", lambda e, cur=cur: e.dma_start(out=CU[cur:cur + 128, :], in_=t1[:, 1024:1536]), [t1], [CU])
                fw.dma("pool", lambda e, r0=r0: e.dma_start(out=GBs[r0:r0 + 128, :], in_=z[:, 2368:2880]), [z], [GBs])
            if TL:
                for ct in range(4):
                    tok0 = NTOK + ct * 128
                    fw.dma("sp", lambda e, ct=ct, l=l: e.dma_start(out=xn[:, 0:256], in_=c_ckv[l, ct * 128:(ct + 1) * 128, :]), [c_ckv], [xn])
                    fw.dma("sp", lambda e, ct=ct, l=l: e.dma_start(out=xn[:, 256:320], in_=c_kpe[l, ct * 128:(ct + 1) * 128, :]), [c_kpe], [xn])
                    kv_pipeline(l, xn[:, 0:256], xn, xn[:, 256:320], xn, tok0, None, t1, t2, junk)
                    fw.dma("sp", lambda e, ct=ct, l=l: e.dma_start(out=t2[:, 0:512].rearrange("p (h d) -> p h d", h=4),
                                                                   in_=c_nak[l, :, ct * 128:(ct + 1) * 128, :].rearrange("h t d -> t h d")), [c_nak], [t2])
                    store_T(t2, 4, 128, KNT, tok0)
                    fw.dma("pool", lambda e, ct=ct, l=l, tok0=tok0: e.dma_start(out=VN[tok0:tok0 + 128, :].rearrange("t (h d) -> t h d", h=4),
                                                                                 in_=c_nav[l, :, ct * 128:(ct + 1) * 128, :].rearrange("h t d -> t h d")), [c_nav], [VN])
            if cfg.stages <= 1:
                continue

            mix, Ssb, Vs, PTs, xt2, G1t = B[0], B[1], B[3], B[4], B[5], B[6]
            cb_ = B[7]
            junk = B[2]
            if TL:
                fw.dma("sp", lambda e, l=l: e.dma_start(out=relTs[:], in_=relT_d[l, :, :, :]), [relT_d], [relTs])
                fw.op("pool", lambda e: e.memset(Bt[:, 576:1088], 0.0), [], [Bt])

            def na_bias(h):
                pbk = [PS(), PS()]
                for sl in range(4):
                    slab = WSL()
                    ov = slab.t[:, :, :].bitcast(F32)[0:31, 0:8, :].rearrange("p a (b c) -> p (a b) c", c=128)
                    fw.dma("sp", lambda e, sl=sl, ov=ov: e.dma_start(out=ov, in_=OH_d[:, sl * 16:(sl + 1) * 16, :]), [OH_d], [slab])
                    for kci in range(16):
                        kc = sl * 16 + kci
                        pb = pbk[kc // 32]
                        fw.op("pe", lambda e, pb=pb, kc=kc, kci=kci, ov=ov: e.matmul(pb[:, (kc % 32) * 15:(kc % 32) * 15 + 15], lhsT=ov[:, kci, :], rhs=relTs[:, h, :],
                                                                                  start=True, stop=True), [slab, relTs], [pb])
                for half in range(2):
                    pb = pbk[half]
                    fw.op("dve", lambda e, pb=pb, half=half: e.tensor_tensor(
                        out=CB2[:, :, half * 32:(half + 1) * 32], in0=pb[:, 0:480].rearrange("p (k a) -> p a k", a=15),
                        in1=cm[:, half * 32:(half + 1) * 32].unsqueeze(1).to_broadcast([128, 15, 32]), op=ALU.add), [pb, cm], [CB2])

            def phaseB_tail(t):
                grp = 0 if t < NTC else 1
                r0 = t * 128
                si, cur = seq_of_tile(t)
                fw.dma("sp", lambda e: e.dma_start(out=cb_[:, 0:512], in_=CU[cur - 1:cur + 127, :]), [CU], [cb_])
                fw.dma("sp", lambda e: e.dma_start(out=cb_[:, 1024:1536], in_=CU[cur:cur + 128, :]), [CU], [cb_])
                fw.dma("sp", lambda e: e.dma_start(out=cb_[:, 1536:2048], in_=CU[cur + 1:cur + 129, :]), [CU], [cb_])
                fw.dma("sp", lambda e: e.dma_start(out=cb_[:, 512:1024], in_=GBs[r0:r0 + 128, :]), [GBs], [cb_])
                fw.op("dve", lambda e: e.tensor_tensor(out=cb_[:, 0:512], in0=cb_[:, 0:512], in1=cw[:, 0, :], op=ALU.mult), [cb_, cw], [cb_])
                fw.op("dve", lambda e: e.tensor_tensor(out=cb_[:, 1024:1536], in0=cb_[:, 1024:1536], in1=cw[:, 1, :], op=ALU.mult), [cb_, cw], [cb_])
                fw.op("dve", lambda e: e.tensor_tensor(out=cb_[:, 1536:2048], in0=cb_[:, 1536:2048], in1=cw[:, 2, :], op=ALU.mult), [cb_, cw], [cb_])
                fw.op("dve", lambda e: e.tensor_tensor(out=cb_[:, 1024:1536], in0=cb_[:, 1024:1536], in1=cb_[:, 0:512], op=ALU.add), [cb_], [cb_])
                fw.op("dve", lambda e: e.tensor_tensor(out=cb_[:, 1024:1536], in0=cb_[:, 1024:1536], in1=cb_[:, 1536:2048], op=ALU.add), [cb_], [cb_])
                fw.op("dve", lambda e: e.tensor_tensor(out=mix[:, 1536:2048], in0=cb_[:, 1024:1536], in1=cb_[:, 512:1024], op=ALU.mult), [cb_], [mix])
                transpose_chunks(lambda k: mix[:, k * 128:(k + 1) * 128], mix, 16, lambda k: hT[:, k, :], hT)
                fw.dma("sp", lambda e: e.dma_start(out=xt2[:], in_=X[r0:r0 + 128, :]), [X], [xt2])
                fw.dma("sp", lambda e: e.dma_start(out=G1t[:], in_=GD[0, grp, :, :]), [GD], [G1t])

                def ofn(p, c0, w):
                    fw.op("dve", lambda e, p=p, c0=c0, w=w: e.tensor_tensor(out=junk[:, c0:c0 + w], in0=p[:, 0:w], in1=G1t[:, c0:c0 + w], op=ALU.mult), [p, G1t], [junk])
                    fw.op("dve", lambda e, c0=c0, w=w: e.tensor_tensor(out=xt2[:, c0:c0 + w], in0=xt2[:, c0:c0 + w], in1=junk[:, c0:c0 + w], op=ALU.add), [xt2, junk], [xt2])
                gemm(lambda k: hT[:, k, :], hT, 16, Wb["out"][l], lambda c0, w, l=l: Wb["out"][l][:, c0:c0 + w].rearrange("(k p) n -> p k n", p=128), D, ofn)
                fw.dma("sp", lambda e: e.dma_start(out=X[r0:r0 + 128, :], in_=xt2[:]), [xt2], [X])
            if TL:
                for h in range(4):
                    na_bias(h)
                    fw.dma("sp", lambda e, h=h: e.dma_start(out=CBD[h, :, :], in_=CB2[:].rearrange("p a k -> p (a k)")), [CB2], [CBD])
            for t in range(NT):
                r0 = t * 128
                if t < NTC:
                    sq0 = (t // 2) * 256
                    kb_m = [(sq0, 128), (sq0 + 128, 128)]
                    for h in range(8):
                        attention(QMT, KMT, VM, 1024, h, 192, 128, r0, kb_m, mix, h * 128, Ssb, PTs, Vs)
                    for h in range(4):
                        attention(QNT, KNT, VN, 512, h, 128, 128, r0, kb_m, mix, 1024 + h * 128, Ssb, PTs, Vs)
                else:
                    l0 = NTC * 128
                    kb_c = [(NTOK + i * 128, 128) for i in range(4)]
                    kb_m = kb_c + [(l0 + i * 128, 128) for i in range(NTL)]
                    for h in range(8):
                        attention(QMT, KMT, VM, 1024, h, 192, 128, r0, kb_m, mix, h * 128, Ssb, PTs, Vs)
                    rows = TL // 64
                    rg = 2 * (t - NTC)
                    rsf = lambda r: min(max(r - 4, 0), rows - 8)
                    ks = min(rsf(rg), rows - 9)
                    kb_l = [(l0 + ks * 64 + i * 128, 128) for i in range(4)] + [(l0 + ks * 64 + 512, 64)]
                    for h in range(4):
                        fw.op("pool", lambda e: e.memset(Bt[:, 0:576], NEG), [], [Bt])
                        for dr in range(2):
                            r = rg + dr
                            j0 = rsf(r) - ks
                            a0 = rsf(r) - r + 7
                            fw.dma("sp", lambda e, h=h, dr=dr, j0=j0, a0=a0: e.dma_start(out=Bt[dr * 64:(dr + 1) * 64, j0 * 64:(j0 + 8) * 64],
                                                                                          in_=CBD[h, dr * 64:(dr + 1) * 64, a0 * 64:(a0 + 8) * 64]), [CBD], [Bt])
                        attention(QNT, KNT, VN, 512, h, 128, 128, r0, kb_l + kb_c, mix, 1024 + h * 128, Ssb, PTs, Vs, bias=Bt)
                phaseB_tail(t)
            if cfg.stages <= 2:
                continue
            peer_phase(l)
        for t in range(NT):
            fw.dma("pool", lambda e, t=t: e.dma_start(out=xout[t * 128:(t + 1) * 128, :], in_=X[t * 128:(t + 1) * 128, :]), [X], [xout])
        fw.finish()
        fw.emit()
        print("instructions:", fw.ninst)
    return nc


def host_inputs(cfg, core, inp):
    L, NS = cfg.depth, cfg.nseq
    f = np.float32
    b = core // 4
    xs = [np.asarray(inp["x_prompt"][core * NS:(core + 1) * NS], f).reshape(NS * 256, D)]
    if cfg.lat:
        xs.append(np.asarray(inp["x_sample"][b], f)[:cfg.lat])
    m = {}
    m["xin"] = np.ascontiguousarray(np.concatenate(xs, 0))
    cv = np.stack([np.asarray(inp["c_ctx"], f), np.asarray(inp["c"][b], f)], 0)
    m["cvecT"] = np.ascontiguousarray(cv.reshape(2, 16, 128).transpose(2, 1, 0))
    m["ada_w"] = np.asarray(inp["ada_w"][:L], f)
    m["ada_b"] = np.asarray(inp["ada_b"][:L], f)
    m["ada_bT"] = np.ascontiguousarray(np.asarray(inp["ada_b"][:L], f).reshape(L, 96, 128).transpose(0, 2, 1))
    m["gmixT"] = np.ascontiguousarray(np.asarray(inp["norm_mix_g"][:L], f).reshape(L, 16, 128).transpose(0, 2, 1))
    m["gffnT"] = np.ascontiguousarray(np.asarray(inp["norm_ffn_g"][:L], f).reshape(L, 16, 128).transpose(0, 2, 1))
    m["w_in"] = np.asarray(inp["w_in"][:L], f)
    m["w_uq"] = np.asarray(inp["mla_w_uq"][:L], f)
    m["w_ukv"] = np.asarray(inp["mla_w_ukv"][:L], f)
    m["w_out"] = np.asarray(inp["w_out"][:L], f)
    m["gvec"] = np.ascontiguousarray(np.concatenate([np.asarray(inp[k][:L], f) for k in
                                                     ("mla_q_norm_g", "mla_kv_norm_g", "mla_q_head_g", "mla_k_head_g", "na_q_head_g", "na_k_head_g")], 1))
    m["conv_wT"] = np.ascontiguousarray(np.asarray(inp["conv_w"][:L], f).transpose(0, 2, 1))
    m["ident"] = np.eye(128, dtype=f)
    sel = np.zeros((2, 2, 128), f)
    sel[0, 0, :] = 1
    sel[1, 1, :] = 1
    m["sel"] = sel
    m["peer_wq"] = np.asarray(inp["peer_w_q"][:L], f)
    m["iota"] = np.ascontiguousarray(np.broadcast_to(np.arange(256, dtype=f)[None, :], (128, 256)))
    m["subT"] = np.ascontiguousarray(np.asarray(inp["peer_sub_keys"][:L], f).transpose(0, 4, 1, 2, 3).reshape(L, 128, 16, 128))
    for i in range(L):
        m["peer_u%d" % i] = np.asarray(inp["peer_u"][i], f)
        m["peer_v%d" % i] = np.asarray(inp["peer_v"][i], f)
    if cfg.lat:
        TL = cfg.lat
        tt = np.arange(TL)
        row = (tt // 64).astype(f)
        col = (tt % 64).astype(f)
        inv = (np.float32(10000.0) ** (-np.arange(16, dtype=f) / np.float32(16))).astype(f)
        ang = np.concatenate([row[:, None] * inv, col[:, None] * inv], -1).astype(f)
        m["rope"] = np.ascontiguousarray(np.concatenate([np.cos(ang), np.sin(ang)], -1).astype(f))
        m["c_ckv"] = np.ascontiguousarray(np.asarray(inp["cache_mla_ckv"][b, :L], f))
        m["c_kpe"] = np.ascontiguousarray(np.asarray(inp["cache_mla_kpe"][b, :L], f))
        m["c_nak"] = np.ascontiguousarray(np.asarray(inp["cache_na_k"][b, :L], f))
        m["c_nav"] = np.ascontiguousarray(np.asarray(inp["cache_na_v"][b, :L], f))
        m["relT"] = np.ascontiguousarray(np.asarray(inp["na_rel_bias"][:L], f).transpose(0, 3, 1, 2))
        OH = np.zeros((31, 64, 128), f)
        cmk = np.full((128, 64), NEG, f)
        for qc in range(64):
            cs = min(max(qc - 8, 0), 48)
            for kc in range(64):
                bb = kc - qc + 15
                if 0 <= bb < 31:
                    OH[bb, kc, qc] = 1
                    OH[bb, kc, 64 + qc] = 1
                if cs <= kc < cs + 16:
                    cmk[qc, kc] = 0
                    cmk[64 + qc, kc] = 0
        m["OH"] = OH
        m["cmask"] = cmk
    return m


_NC_CACHE = {}


def kernel(**inp):
    cfg = Cfg()
    n = 8
    if "nc" not in _NC_CACHE:
        _NC_CACHE["nc"] = build(cfg)
    nc = _NC_CACHE["nc"]
    in_maps = [host_inputs(cfg, c, inp) for c in range(n)]
    res = run_bass_kernel_spmd(nc, in_maps, core_ids=list(range(n)))
    R = res.results
    NS, L = cfg.nseq, cfg.depth
    y_prompt = np.concatenate([R[c]["xout"][:NS * 256].reshape(NS, 256, D) for c in range(n)], 0)
    y_sample = np.stack([np.concatenate([R[b * 4 + q]["xout"][NS * 256 + q * 256:NS * 256 + (q + 1) * 256] for q in range(4)], 0) for b in range(2)], 0)
    ckv = np.concatenate([R[c]["o_ckv"] for c in range(n)], 0)
    kpe = np.concatenate([R[c]["o_kpe"] for c in range(n)], 0)
    nak = np.concatenate([R[c]["o_nak"] for c in range(n)], 0)
    nav = np.concatenate([R[c]["o_nav"] for c in range(n)], 0)
    return (y_prompt.astype(np.float32), y_sample.astype(np.float32), ckv, kpe, nak, nav)
```
